# Optimizing a Trainium2 kernel written in Bass

```python
import math
import jax, jax.numpy as jnp
from jax import lax
import numpy as np

D_MODEL = 1024
BATCH = 32
SEQ = 256
DEPTH = 2
DEC_BATCH = 4
DEC_SEQ = 2048
PAST_LEN = 512

GRID_W = 64
MIX_W = D_MODEL
FOURIER_W = D_MODEL // 4
FOURIER_HEADS = 4
FOURIER_DH = FOURIER_W // FOURIER_HEADS
MLSTM_W = 3 * D_MODEL // 8
MLSTM_HEADS = 4
MLSTM_DH = MLSTM_W // MLSTM_HEADS
S5_W = MIX_W - FOURIER_W - MLSTM_W
S5_GROUP_CH = 16
S5_GROUPS = S5_W // S5_GROUP_CH
S5_STATE = 64
N_GATES = 2 * 2 * MLSTM_HEADS
P_IN = FOURIER_W + 3 * MLSTM_W + N_GATES + MLSTM_W + S5_W
D_FF = -(-8 * D_MODEL // (3 * 256)) * 256
CHUNK = 128
EPS = 1e-6

kernel_name = "hybrid_fnet_mlstm_s5_prefix_dit_step"

F32 = jnp.float32


def rmsnorm(x, w):
    x32 = x.astype(F32)
    y = x32 * lax.rsqrt(jnp.mean(x32 * x32, axis=-1, keepdims=True) + EPS)
    return (y * w.astype(F32)).astype(x.dtype)


def ada_mod(cvec, w_ada, b_ada):
    m = jax.nn.silu(cvec) @ w_ada + b_ada
    return jnp.split(m, 6, axis=-1)


def fourier_mix(zf, grid):
    B, S, _ = zf.shape
    z = zf.astype(F32).reshape(B, S, FOURIER_HEADS, FOURIER_DH)
    if grid:
        rows = S // GRID_W
        z = z.reshape(B, rows, GRID_W, FOURIER_HEADS, FOURIER_DH)
        f = jnp.fft.fftn(z, axes=(1, 2, 4), norm="ortho")
    else:
        f = jnp.fft.fftn(z, axes=(1, 3), norm="ortho")
    return f.real.reshape(B, S, FOURIER_W)


def mlstm_chunked(q, k, v, ig, lf, c0, n0, m0):
    B, H, S, DH = q.shape
    nc = S // CHUNK

    def chunks(t):
        return jnp.moveaxis(t.reshape((B, H, nc, CHUNK) + t.shape[3:]), 2, 0)

    causal = jnp.tril(jnp.ones((CHUNK, CHUNK), dtype=bool))

    def step(carry, xs):
        c, n, m = carry
        qc, kc, vc, igc, lfc = xs
        b = jnp.cumsum(lfc, axis=-1)
        log_d = jnp.where(causal, b[..., :, None] - b[..., None, :] + igc[..., None, :], -jnp.inf)
        m_inter = b + m[..., None]
        m_row = jnp.maximum(m_inter, jnp.max(log_d, axis=-1))
        d = jnp.exp(log_d - m_row[..., None])
        inter = jnp.exp(m_inter - m_row)
        s = jnp.einsum("bhjd,bhsd->bhjs", qc, kc) * d
        num = jnp.einsum("bhjs,bhse->bhje", s, vc) + inter[..., None] * jnp.einsum("bhjd,bhde->bhje", qc, c)
        den = jnp.sum(s, axis=-1) + inter * jnp.einsum("bhjd,bhd->bhj", qc, n)
        h = num / jnp.maximum(jnp.abs(den), jnp.exp(-m_row))[..., None]
        m_new = m_row[..., -1]
        w = jnp.exp(b[..., -1:] - b + igc - m_new[..., None])
        decay = jnp.exp(b[..., -1] + m - m_new)
        c_new = decay[..., None, None] * c + jnp.einsum("bhs,bhsd,bhse->bhde", w, kc, vc)
        n_new = decay[..., None] * n + jnp.einsum("bhs,bhsd->bhd", w, kc)
        return (c_new, n_new, m_new), h

    (c, n, m), hs = lax.scan(step, (c0, n0, m0),
                             (chunks(q), chunks(k), chunks(v), chunks(ig), chunks(lf)))
    h = jnp.moveaxis(hs, 0, 2).reshape(B, H, S, DH)
    return h, (c, n, m)


def s5_zoh(lam_re, lam_im, log_step, b_re, b_im):
    lam = lax.complex(lam_re.astype(F32), lam_im.astype(F32))
    step = jnp.exp(log_step.astype(F32))[:, None]
    lam_bar = jnp.exp(lam * step)
    bmat = lax.complex(b_re.astype(F32), b_im.astype(F32))
    b_bar = ((lam_bar - 1.0) / lam)[..., None] * bmat
    return lam_bar, b_bar


def s5_scan(u, lam_bar, b_bar, x0):
    bu = jnp.einsum("bsgc,gpc->bsgp", u, b_bar)
    bu = bu.at[:, 0].add(lam_bar * x0)
    a = jnp.broadcast_to(lam_bar, bu.shape)
    _, xs = lax.associative_scan(lambda e, l: (e[0] * l[0], l[0] * e[1] + l[1]), (a, bu), axis=1)
    return xs


def mixer(xn, p, init, grid):
    B, S, _ = xn.shape
    dt = xn.dtype
    z = xn @ p["w_in"]
    o = FOURIER_W
    idx = [o, o + MLSTM_W, o + 2 * MLSTM_W, o + 3 * MLSTM_W, o + 3 * MLSTM_W + N_GATES,
           o + 4 * MLSTM_W + N_GATES]
    zf, zq, zk, zv, zg, zo, zu = jnp.split(z, idx, axis=-1)

    f_out = fourier_mix(zf, grid).astype(dt) @ p["w_fourier"]

    def heads(t):
        return t.astype(F32).reshape(B, S, MLSTM_HEADS, MLSTM_DH).transpose(0, 2, 1, 3)

    q = heads(zq)
    k = heads(zk) * (MLSTM_DH ** -0.5)
    v = heads(zv)
    g = (zg.astype(F32) + p["b_gates"].astype(F32)).reshape(B, S, 2, 2, MLSTM_HEADS).transpose(2, 3, 0, 4, 1)
    cf, nf, mf, cb, nb, mb, sf, sb = init
    h_f, (cf2, nf2, mf2) = mlstm_chunked(q, k, v, g[0, 0], jax.nn.log_sigmoid(g[0, 1]), cf, nf, mf)

    def rev(t):
        return jnp.flip(t, axis=2)

    h_b, (cb2, nb2, mb2) = mlstm_chunked(rev(q), rev(k), rev(v), rev(g[1, 0]),
                                         rev(jax.nn.log_sigmoid(g[1, 1])), cb, nb, mb)
    h = (h_f + rev(h_b)).transpose(0, 2, 1, 3)
    h = h * lax.rsqrt(jnp.mean(h * h, axis=-1, keepdims=True) + EPS)
    h = h.reshape(B, S, MLSTM_W) * p["mlstm_norm_w"].astype(F32)
    m_out = (h * jax.nn.sigmoid(zo.astype(F32))).astype(dt)

    u = zu.astype(F32).reshape(B, S, S5_GROUPS, S5_GROUP_CH)
    uc = u.astype(jnp.complex64)
    lam_f, bb_f = s5_zoh(p["s5_lambda_re"][0], p["s5_lambda_im"][0], p["s5_log_step"][0], p["s5_b_re"], p["s5_b_im"])
    lam_b, bb_b = s5_zoh(p["s5_lambda_re"][1], p["s5_lambda_im"][1], p["s5_log_step"][1], p["s5_b_re"], p["s5_b_im"])
    xs_f = s5_scan(uc, lam_f, bb_f, sf)
    xs_b = s5_scan(jnp.flip(uc, axis=1), lam_b, bb_b, sb)
    cmat = lax.complex(p["s5_c_re"].astype(F32), p["s5_c_im"].astype(F32))
    y = jnp.einsum("bsgp,gcp->bsgc", xs_f + jnp.flip(xs_b, axis=1), cmat).real + p["s5_d"].astype(F32) * u
    y = jax.nn.gelu(y.reshape(B, S, S5_W)).astype(dt)
    ga, gb = jnp.split(y @ p["w_glu"], 2, axis=-1)
    s_out = ga * jax.nn.sigmoid(gb)

    out = jnp.concatenate([f_out, m_out, s_out], axis=-1) @ p["w_out"]
    final = (cf2, nf2, mf2, cb2, nb2, mb2, xs_f[:, -1], xs_b[:, -1])
    return out, final


def block(x, mods, p, init, grid):
    sh1, sc1, g1, sh2, sc2, g2 = mods
    xn = rmsnorm(x, p["norm1_w"]) * (1 + sc1) + sh1
    mix, final = mixer(xn, p, init, grid)
    x = x + g1 * mix
    xn = rmsnorm(x, p["norm2_w"]) * (1 + sc2) + sh2
    ff = (jax.nn.silu(xn @ p["w_gate"]) * (xn @ p["w_up"])) @ p["w_down"]
    x = x + g2 * ff
    return x, final


def setup_inputs(seed: int = 0) -> dict:
    key = jax.random.key(seed)
    ks = iter(jax.random.split(key, 48))

    def nrm(shape, scale):
        return scale * jax.random.normal(next(ks), shape, F32)

    H, DH, G, P, GC = MLSTM_HEADS, MLSTM_DH, S5_GROUPS, S5_STATE, S5_GROUP_CH
    x_prompt = nrm((BATCH, SEQ, D_MODEL), 1.0)
    x_sample = nrm((DEC_BATCH, DEC_SEQ, D_MODEL), 1.0)
    state_mlstm_C = nrm((DEC_BATCH, DEPTH, 2, H, DH, DH), 0.3)
    state_mlstm_n = nrm((DEC_BATCH, DEPTH, 2, H, DH), 0.3)
    state_mlstm_m = jax.random.uniform(next(ks), (DEC_BATCH, DEPTH, 2, H), F32, 0.0, 2.0)
    state_s5_re = nrm((DEC_BATCH, DEPTH, 2, G, P), 0.3)
    state_s5_im = nrm((DEC_BATCH, DEPTH, 2, G, P), 0.3)
    c = nrm((DEC_BATCH, D_MODEL), 1.0)
    c_ctx = nrm((D_MODEL,), 1.0)
    w_ada = nrm((DEPTH, D_MODEL, 6 * D_MODEL), 0.5 * D_MODEL ** -0.5)
    b_ada = nrm((DEPTH, 6 * D_MODEL), 0.02)
    norm1_w = 1.0 + nrm((DEPTH, D_MODEL), 0.02)
    norm2_w = 1.0 + nrm((DEPTH, D_MODEL), 0.02)
    w_in = nrm((DEPTH, D_MODEL, P_IN), D_MODEL ** -0.5)
    ig_b = nrm((DEPTH, 2, 1, H), 0.1)
    fg_b = jnp.linspace(3.0, 6.0, H, dtype=F32) + nrm((DEPTH, 2, 1, H), 0.1)
    b_gates = jnp.concatenate([ig_b, fg_b], axis=2).reshape(DEPTH, N_GATES)
    w_fourier = nrm((DEPTH, FOURIER_W, FOURIER_W), FOURIER_W ** -0.5)
    mlstm_norm_w = 1.0 + nrm((DEPTH, MLSTM_W), 0.02)
    s5_lambda_re = -0.5 + nrm((DEPTH, 2, G, P), 0.01)
    s5_lambda_im = math.pi * jnp.arange(P, dtype=F32) + nrm((DEPTH, 2, G, P), 0.01)
    s5_log_step = jax.random.uniform(next(ks), (DEPTH, 2, G), F32, math.log(1e-3), math.log(1e-1))
    s5_b_re = nrm((DEPTH, G, P, GC), (2 * GC) ** -0.5)
    s5_b_im = nrm((DEPTH, G, P, GC), (2 * GC) ** -0.5)
    s5_c_re = nrm((DEPTH, G, GC, P), P ** -0.5)
    s5_c_im = nrm((DEPTH, G, GC, P), P ** -0.5)
    s5_d = nrm((DEPTH, G, GC), 1.0)
    w_glu = nrm((DEPTH, S5_W, 2 * S5_W), S5_W ** -0.5)
    w_out = nrm((DEPTH, MIX_W, D_MODEL), MIX_W ** -0.5)
    w_gate = nrm((DEPTH, D_MODEL, D_FF), D_MODEL ** -0.5)
    w_up = nrm((DEPTH, D_MODEL, D_FF), D_MODEL ** -0.5)
    w_down = nrm((DEPTH, D_FF, D_MODEL), D_FF ** -0.5)
    norm_f = 1.0 + nrm((D_MODEL,), 0.02)
    return {"x_prompt": x_prompt, "x_sample": x_sample,
            "state_mlstm_C": state_mlstm_C, "state_mlstm_n": state_mlstm_n, "state_mlstm_m": state_mlstm_m,
            "state_s5_re": state_s5_re, "state_s5_im": state_s5_im, "c": c, "c_ctx": c_ctx,
            "w_ada": w_ada, "b_ada": b_ada, "norm1_w": norm1_w, "norm2_w": norm2_w, "w_in": w_in,
            "b_gates": b_gates, "w_fourier": w_fourier, "mlstm_norm_w": mlstm_norm_w,
            "s5_lambda_re": s5_lambda_re, "s5_lambda_im": s5_lambda_im, "s5_log_step": s5_log_step,
            "s5_b_re": s5_b_re, "s5_b_im": s5_b_im, "s5_c_re": s5_c_re, "s5_c_im": s5_c_im, "s5_d": s5_d,
            "w_glu": w_glu, "w_out": w_out, "w_gate": w_gate, "w_up": w_up, "w_down": w_down,
            "norm_f": norm_f}


def reference(x_prompt, x_sample, state_mlstm_C, state_mlstm_n, state_mlstm_m, state_s5_re, state_s5_im,
              c, c_ctx, w_ada, b_ada, norm1_w, norm2_w, w_in, b_gates, w_fourier, mlstm_norm_w,
              s5_lambda_re, s5_lambda_im, s5_log_step, s5_b_re, s5_b_im, s5_c_re, s5_c_im, s5_d,
              w_glu, w_out, w_gate, w_up, w_down, norm_f):
    params = [{"w_ada": w_ada[l], "b_ada": b_ada[l], "norm1_w": norm1_w[l], "norm2_w": norm2_w[l],
               "w_in": w_in[l], "b_gates": b_gates[l], "w_fourier": w_fourier[l],
               "mlstm_norm_w": mlstm_norm_w[l], "s5_lambda_re": s5_lambda_re[l],
               "s5_lambda_im": s5_lambda_im[l], "s5_log_step": s5_log_step[l], "s5_b_re": s5_b_re[l],
               "s5_b_im": s5_b_im[l], "s5_c_re": s5_c_re[l], "s5_c_im": s5_c_im[l], "s5_d": s5_d[l],
               "w_glu": w_glu[l], "w_out": w_out[l], "w_gate": w_gate[l], "w_up": w_up[l],
               "w_down": w_down[l]} for l in range(DEPTH)]

    bp = x_prompt.shape[0]
    zc = jnp.zeros((bp, MLSTM_HEADS, MLSTM_DH, MLSTM_DH), F32)
    zn = jnp.zeros((bp, MLSTM_HEADS, MLSTM_DH), F32)
    zm = jnp.zeros((bp, MLSTM_HEADS), F32)
    zs = jnp.zeros((bp, S5_GROUPS, S5_STATE), jnp.complex64)
    xp = x_prompt
    finals = []
    for l in range(DEPTH):
        p = params[l]
        mods = ada_mod(c_ctx, p["w_ada"], p["b_ada"])
        xp, fin = block(xp, mods, p, (zc, zn, zm, zc, zn, zm, zs, zs), False)
        finals.append(fin)
    y_prompt = rmsnorm(xp, norm_f)

    xs = x_sample
    for l in range(DEPTH):
        p = params[l]
        mods = [t[:, None, :] for t in ada_mod(c, p["w_ada"], p["b_ada"])]
        init = (state_mlstm_C[:, l, 0].astype(F32), state_mlstm_n[:, l, 0].astype(F32),
                state_mlstm_m[:, l, 0].astype(F32),
                state_mlstm_C[:, l, 1].astype(F32), state_mlstm_n[:, l, 1].astype(F32),
                state_mlstm_m[:, l, 1].astype(F32),
                lax.complex(state_s5_re[:, l, 0].astype(F32), state_s5_im[:, l, 0].astype(F32)),
                lax.complex(state_s5_re[:, l, 1].astype(F32), state_s5_im[:, l, 1].astype(F32)))
        xs, _ = block(xs, mods, p, init, True)
    y_sample = rmsnorm(xs, norm_f)

    new_mlstm_C = jnp.stack([jnp.stack([f[0], f[3]], axis=1) for f in finals], axis=1)
    new_mlstm_n = jnp.stack([jnp.stack([f[1], f[4]], axis=1) for f in finals], axis=1)
    new_mlstm_m = jnp.stack([jnp.stack([f[2], f[5]], axis=1) for f in finals], axis=1)
    s_states = jnp.stack([jnp.stack([f[6], f[7]], axis=1) for f in finals], axis=1)
    new_s5_re = jnp.real(s_states)
    new_s5_im = jnp.imag(s_states)
    return (y_prompt, y_sample, new_mlstm_C, new_mlstm_n, new_mlstm_m, new_s5_re, new_s5_im)
```

```python
import math
from contextlib import ExitStack

import numpy as np
import ml_dtypes

import concourse.bass as bass
import concourse.mybir as mybir
from concourse.ap import AP
from concourse.bass_utils import run_bass_kernel_spmd

F32 = mybir.dt.float32
BF16 = mybir.dt.bfloat16
AF = mybir.ActivationFunctionType
ALU = mybir.AluOpType
AX = mybir.AxisListType

P = 128
T = 2048
D = 1024
KC = 8
NT = 16
NB = 4
L = 2
DFF = 2816
NHC = DFF // 128
PIN = 2192
H = 4
DH = 96
G = 24
GC = 16
SP_ = 64
EPS = 1e-6
NSB = 256
DMA_SCRATCH = 4096
ARENA_BYTES = 116 * 1024

FLAGS = {"fourier": True, "mlstm": True, "s5": True, "ffn": True, "layers": 2}


class KB:
    def __init__(self):
        self.nc = bass.Bass("TRN2", target_bir_lowering=False, dynamic_dma_scratch_size=DMA_SCRATCH)
        self.es = ExitStack()
        nc = self.nc
        self.engs = {"pe": nc.tensor, "dve": nc.vector, "act": nc.scalar, "pool": nc.gpsimd, "sp": nc.sync}
        self.sem = {e: self.es.enter_context(nc.semaphore("s_" + e)) for e in self.engs}
        self.cnt = {e: 0 for e in self.engs}
        self.waited = {}
        self.ND = 32
        self.dsem = [self.es.enter_context(nc.semaphore("d%d" % i)) for i in range(self.ND)]
        self.dcnt = [0] * self.ND
        self.dnext = 0
        self.dnext_sw = 0
        self.res = {}
        self.ninstr = 0
        self.barrier_hooks = []
        self.bg = set()

    def sb(self, name, shape, dtype):
        return self.es.enter_context(self.nc.sbuf_tensor(name, list(shape), dtype))

    def ps(self, name, shape, dtype):
        return self.es.enter_context(self.nc.psum_tensor(name, list(shape), dtype))

    def dram(self, name, shape, dtype, kind):
        return self.nc.dram_tensor(name, list(shape), dtype, kind=kind).ap()

    def _wait(self, e, tok):
        if tok is None:
            return
        kind, src, val = tok
        key = (e, kind, src)
        if self.waited.get(key, 0) >= val:
            return
        self.waited[key] = val
        sem = self.sem[src] if kind == "e" else self.dsem[src]
        self.engs[e].wait_ge(sem, val)

    def _deps(self, e, reads, writes, pe_acc):
        for r in reads:
            st = self.res.get(r)
            if st is not None:
                self._wait(e, st["w"])
        for w in writes:
            st = self.res.get(w)
            if st is not None:
                if not (pe_acc and st["w"] is not None and st["w"][0] == "e" and st["w"][1] == "pe"):
                    self._wait(e, st["w"])
                for t in st["r"]:
                    if not (pe_acc and t[0] == "e" and t[1] == "pe"):
                        self._wait(e, t)

    def _update(self, tok, reads, writes):
        for r in reads:
            st = self.res.setdefault(r, {"w": None, "r": []})
            st["r"].append(tok)
            if len(st["r"]) > 48:
                latest = {}
                for t in st["r"]:
                    latest[(t[0], t[1])] = t
                st["r"] = list(latest.values())
        for w in writes:
            self.res[w] = {"w": tok, "r": []}

    def op(self, e, fn, reads=(), writes=(), pe_acc=False):
        if e != "pe":
            for r in reads:
                if isinstance(r, tuple) and r[0] == "ps":
                    st = self.res.get(("psx", r[1]))
                    if st is not None and st["w"] is not None and st["w"][1] != e:
                        self._wait(e, st["w"])
        self._deps(e, reads, writes, pe_acc)
        ins = fn(self.engs[e])
        ins.then_inc(self.sem[e], 1)
        self.cnt[e] += 1
        tok = ("e", e, self.cnt[e])
        self._update(tok, reads, writes)
        if e != "pe":
            for r in reads:
                if isinstance(r, tuple) and r[0] == "ps":
                    self.res[("psx", r[1])] = {"w": tok, "r": []}
        self.ninstr += 1
        return tok

    def dma(self, q, out, in_, reads=(), writes=(), bg=False, **kw):
        self._deps(q, reads, writes, False)
        half = self.ND // 2
        if q == "pool":
            i = half + self.dnext_sw
            self.dnext_sw = (self.dnext_sw + 1) % half
        else:
            i = self.dnext
            self.dnext = (self.dnext + 1) % half
        if self.dcnt[i] > 0:
            self._wait(q, ("d", i, self.dcnt[i]))
        self.bg.discard(i)
        if bg:
            self.bg.add(i)
        ins = self.engs[q].dma_start(out=out, in_=in_, **kw)
        ins.then_inc(self.dsem[i], 16)
        self.dcnt[i] += 16
        tok = ("d", i, self.dcnt[i])
        self._update(tok, reads, writes)
        self.ninstr += 1
        return tok

    def barrier(self):
        for e in self.engs:
            for e2 in self.engs:
                if self.cnt[e2] > 0:
                    self._wait(e, ("e", e2, self.cnt[e2]))
            for i in range(self.ND):
                if self.dcnt[i] > 0 and i not in self.bg:
                    self._wait(e, ("d", i, self.dcnt[i]))
        self.res = {k: v for k, v in self.res.items() if isinstance(k, tuple) and k[0] == "WA"}
        for h in self.barrier_hooks:
            h()

    def finish(self):
        self.bg = set()
        self.barrier()
        self.es.close()
        return self.nc


class Arena:
    def __init__(self, kb, nbytes):
        self.n = nbytes // 2
        self.t = kb.sb("arena", [P, self.n], BF16)
        self.free_list = [(0, self.n)]
        self.pending = []
        self.live = {}
        self.used = 0
        self.peak = 0
        kb.barrier_hooks.append(self.commit)

    def alloc(self, free_shape, dtype, top=False):
        nel = 1
        for s in free_shape:
            nel *= s
        units = nel * (2 if dtype == F32 else 1)
        units = (units + 31) // 32 * 32
        order = range(len(self.free_list) - 1, -1, -1) if top else range(len(self.free_list))
        for idx in order:
            off, size = self.free_list[idx]
            if size >= units:
                if size == units:
                    self.free_list.pop(idx)
                elif top:
                    self.free_list[idx] = (off, size - units)
                    off = off + size - units
                else:
                    self.free_list[idx] = (off + units, size - units)
                break
        else:
            raise RuntimeError("arena overflow: need %d units, free=%s" % (units, self.free_list))
        self.used += units
        self.peak = max(self.peak, self.used)
        v = self.t[:, off:off + units]
        if dtype == F32:
            v = v.bitcast(F32)[:, 0:nel]
        else:
            v = v[:, 0:nel]
        if len(free_shape) > 1:
            names = " ".join("a%d" % i for i in range(len(free_shape)))
            kw = {"a%d" % i: free_shape[i] for i in range(len(free_shape))}
            v = v.rearrange("p (%s) -> p %s" % (names, names), **kw)
        self.live[id(v)] = (v, off, units)
        return v

    def free(self, *aps):
        for ap in aps:
            v, off, units = self.live.pop(id(ap))
            self.pending.append((off, units))

    def commit(self):
        for off, units in self.pending:
            self.used -= units
            self.free_list.append((off, units))
        self.pending = []
        self.free_list.sort()
        merged = []
        for off, size in self.free_list:
            if merged and merged[-1][0] + merged[-1][1] == off:
                merged[-1] = (merged[-1][0], merged[-1][1] + size)
            else:
                merged.append((off, size))
        self.free_list = merged


def bcast(ap, shape):
    return ap.to_broadcast(list(shape))


def rev_axis(ap, axis):
    a = [list(x) for x in ap.ap]
    st, n = a[axis]
    a[axis] = [-st, n]
    return AP(ap.tensor, ap.offset + st * (n - 1), a)


def swap_ri(ap):
    a = [list(x) for x in ap.ap]
    assert len(a) == 3 and a[1][1] == 2
    st = a[1][0]
    return AP(ap.tensor, ap.offset + st, [a[0], [-st, 2], a[2]])

def build_program(flags=FLAGS, debug=None):
    kb = KB()
    nc = kb.nc
    NL = flags["layers"]

    def din(name, shape, dt=F32):
        return kb.dram(name, shape, dt, "ExternalInput")

    def dout(name, shape, dt=F32):
        return kb.dram(name, shape, dt, "ExternalOutput")

    x_d = din("x", [T, D])
    cvec_d = din("cvec", [P, KC])
    w_ada_d = din("w_ada", [L, D, 6 * D])
    b_ada_d = din("b_ada_fm", [P, L, 48])
    n1w_d = din("n1w", [P, L, KC])
    n2w_d = din("n2w", [P, L, KC])
    nfw_d = din("nfw", [P, KC])
    w_in_d = din("w_in", [L, D, PIN])
    w_fo_d = din("w_fourier", [L, 256, 256])
    w_glu_d = din("w_glu", [L, 384, 768])
    w_out_d = din("w_out", [L, D, D])
    w_gate_d = din("w_gate", [L, D, DFF])
    w_up_d = din("w_up", [L, D, DFF])
    w_down_d = din("w_down", [L, DFF, D])
    identf_d = din("identf", [P, P])
    csc_d = din("csc", [P, 256], BF16)
    posmat_d = din("posmat", [NT, P, 2, T], BF16)
    cmask_d = din("cmask", [P, 5, P])
    keepcol_d = din("keepcol", [P, 1])
    keeprow_d = din("keeprow", [1, NT])
    bg_d = din("bg_bc", [P, L, 16])
    mnw_d = din("mnw_bc", [P, L, 384])
    mC0_d = din("mC0", [DH, L, 2, H, 97])
    mM0_d = din("mM0", [1, L, 8])
    s5p_d = din("s5p", [P, L, 3, G])
    s5bc_d = din("s5bc", [P, L, 4, G, GC])
    s5dcol_d = din("s5dcol", [P, L, G])
    s5st0_d = din("s5st0", [P, L, 2, G])
    y_d = dout("y", [T, D])
    newC_d = dout("newC", [8, L, 2, H, DH, DH])
    newn_d = dout("newn", [8, L, 2, H, DH])
    newm_d = dout("newm", [8, L, 2, H])
    news5_d = [dout("news5re", [8, L, 2, G, SP_]), dout("news5im", [8, L, 2, G, SP_])]

    dbg_d = {}
    if debug:
        for name, (shape, dt) in debug.items():
            dbg_d[name] = dout("dbg_" + name, shape, dt)

    X = kb.sb("X", [P, KC, T], F32)
    identf = kb.sb("identf_sb", [P, P], F32)
    identb = kb.sb("identb", [P, P], BF16)
    onesb = kb.sb("onesb", [P, P], BF16)
    onesf = kb.sb("onesf", [P, P], F32)
    csc = kb.sb("csc_sb", [P, 256], BF16)
    cmask = kb.sb("cmask_sb", [P, 5, P], F32)
    maskb = kb.sb("maskb", [P, 2, P], BF16)
    Jb = kb.sb("Jb", [P, P], BF16)
    keepcol = kb.sb("keepcol_sb", [P, 1], F32)
    keeprow = kb.sb("keeprow_sb", [1, NT], F32)
    cvec = kb.sb("cvec_sb", [P, KC], F32)
    csil = kb.sb("csil", [P, KC], BF16)
    b_ada = kb.sb("b_ada_sb", [P, L, 48], F32)
    n1w = kb.sb("n1w_sb", [P, L, KC], F32)
    n2w = kb.sb("n2w_sb", [P, L, KC], F32)
    nfw = kb.sb("nfw_sb", [P, KC], F32)
    mods = kb.sb("mods", [P, L, 6, KC], F32)
    wn = kb.sb("wn", [P, L, 2, KC], F32)
    banks = [kb.ps("bank%d" % i, [P, 512], F32) for i in range(8)]
    ar = Arena(kb, ARENA_BYTES)
    trile, trige, s5mF, s5mB = cmask[:, 0, :], cmask[:, 1, :], cmask[:, 2, :], cmask[:, 3, :]

    bstate = {"i": 0}

    def nextbank():
        b = bstate["i"]
        bstate["i"] = (b + 1) % 8
        return b

    evq = {"i": 0}

    def evac_eng():
        evq["i"] ^= 1
        return "act" if evq["i"] else "dve"

    def mm(out, lhsT, rhs, start, stop, reads, writes):
        return kb.op("pe", lambda e: e.matmul(out, lhsT, rhs, start=start, stop=stop), reads=reads, writes=writes,
                     pe_acc=True)

    def tr(out, in_, ident, reads, writes):
        return kb.op("pe", lambda e: e.transpose(out, in_, ident), reads=reads, writes=writes, pe_acc=True)

    import os as _os
    PSUB = _os.environ.get("KPOOL", "pool")

    def copy_to(eng, out, in_, reads, writes):
        if eng == "pool":
            eng = PSUB
        if eng == "act":
            return kb.op("act", lambda e: e.activation(out=out, in_=in_, func=AF.Copy), reads=reads, writes=writes)
        return kb.op(eng, lambda e: e.tensor_copy(out=out, in_=in_), reads=reads, writes=writes)

    def tt(eng, out, in0, in1, op, reads, writes):
        if eng == "pool":
            eng = PSUB
        return kb.op(eng, lambda e: e.tensor_tensor(out=out, in0=in0, in1=in1, op=op), reads=reads, writes=writes)

    def ts(eng, out, in0, s1, s2, op0, op1, reads, writes):
        if eng == "pool":
            eng = PSUB
        if op1 is None:
            return kb.op(eng, lambda e: e.tensor_scalar(out=out, in0=in0, scalar1=s1, scalar2=None, op0=op0),
                         reads=reads, writes=writes)
        return kb.op(eng, lambda e: e.tensor_scalar(out=out, in0=in0, scalar1=s1, scalar2=s2, op0=op0, op1=op1),
                     reads=reads, writes=writes)

    def stt(out, in0, scalar, in1, op0, op1, reads, writes):
        return kb.op("dve", lambda e: e.scalar_tensor_tensor(out=out, in0=in0, scalar=scalar, in1=in1, op0=op0,
                                                             op1=op1), reads=reads, writes=writes)

    def act(out, in_, func, reads, writes, scale=1.0, bias=None):
        if bias is None:
            return kb.op("act", lambda e: e.activation(out=out, in_=in_, func=func, scale=scale), reads=reads,
                         writes=writes)
        return kb.op("act", lambda e: e.activation(out=out, in_=in_, func=func, scale=scale, bias=bias),
                     reads=reads, writes=writes)

    def dbg_out(name, ap_sb, keyreads=()):
        if debug and name in dbg_d:
            kb.barrier()
            kb.dma("sp", dbg_d[name], ap_sb, reads=list(keyreads), writes=[("dbg", name)])

    kb.dma("sp", identf[:], identf_d, writes=["identf"])
    kb.dma("sp", csc[:], csc_d, writes=["csc"])
    kb.dma("sp", cvec[:], cvec_d, writes=["cvec"])
    kb.dma("sp", b_ada[:], b_ada_d, writes=["b_ada"])
    kb.dma("sp", n1w[:], n1w_d, writes=["n1w"])
    kb.dma("sp", n2w[:], n2w_d, writes=["n2w"])
    kb.dma("sp", nfw[:], nfw_d, writes=["nfw"])
    kb.dma("sp", cmask[:], cmask_d, writes=["cmask"])
    kb.dma("sp", keepcol[:], keepcol_d, writes=["keepcol"])
    kb.dma("sp", keeprow[:], keeprow_d, writes=["keeprow"])
    kb.op("dve", lambda e: e.memset(onesb[:], 1.0), writes=["onesb"])
    kb.op("dve", lambda e: e.memset(onesf[:], 1.0), writes=["onesf"])
    copy_to("dve", identb[:], identf[:], ["identf"], ["identb"])
    copy_to("dve", maskb[:], cmask[:, 0:2, :], ["cmask"], ["maskb"])
    copy_to("dve", Jb[:], cmask[:, 4, :], ["cmask"], ["Jb"])
    act(csil[:], cvec[:], AF.Silu, ["cvec"], ["csil"])

    def ada_dma(l, i, WA, bg=False):
        src = w_ada_d[l, :, i * D:(i + 1) * D].rearrange("(k p) n -> p k n", p=P)
        kb.dma("pool", WA, src, writes=[("WA", id(WA))], bg=bg)

    def ada_compute(l, i, WA):
        b = nextbank()
        for m in range(KC):
            for k in range(KC):
                mm(banks[b][:, m:m + 1], WA[:, k, m * P:(m + 1) * P], csil[:, k:k + 1], k == 0, k == KC - 1,
                   reads=[("WA", id(WA)), "csil"], writes=[("ps", b)])
        tt("dve", mods[:, l, i, :], banks[b][:, 0:KC], b_ada[:, l, i * KC:(i + 1) * KC], ALU.add,
           [("ps", b), "b_ada"], [("mods", l, i)])
        if i in (1, 4):
            j = 0 if i == 1 else 1
            nw = n1w if i == 1 else n2w
            stt(wn[:, l, j, :], mods[:, l, i, :], 1.0, nw[:, l, :], ALU.add, ALU.mult,
                [("mods", l, i), "n1w", "n2w"], [("wn", l, j)])

    def rms_norm(l, j, XN):
        xsq = ar.alloc((KC, 512), BF16)
        rstd = ar.alloc((512,), F32)
        tmp = [ar.alloc((512,), F32) for _ in range(2)]
        for nb in range(NB):
            sl = slice(nb * 512, (nb + 1) * 512)
            for k in range(KC):
                act(xsq[:, k, :], X[:, k, sl], AF.Square, [("X", nb)], [("xsq", k)])
            b = nextbank()
            for k in range(KC):
                mm(banks[b][:], onesb[:], xsq[:, k, :], k == 0, k == KC - 1, reads=["onesb", ("xsq", k)],
                   writes=[("ps", b)])
            act(rstd, banks[b][:], AF.Sqrt, [("ps", b)], ["rstd"], scale=1.0 / D, bias=EPS)
            kb.op("dve", lambda e: e.reciprocal(out=rstd, in_=rstd), reads=["rstd"], writes=["rstd"])
            for k in range(KC):
                if l is None:
                    stt(X[:, k, sl], X[:, k, sl], nfw[:, k:k + 1], rstd, ALU.mult, ALU.mult,
                        [("X", nb), "nfw", "rstd"], [("X", nb)])
                else:
                    tq = tmp[k % 2]
                    stt(tq, X[:, k, sl], wn[:, l, j, k:k + 1], rstd, ALU.mult, ALU.mult,
                        [("X", nb), ("wn", l, j), "rstd"], [("ntmp", k % 2)])
                    act(XN[:, k, sl], tq, AF.Identity, [("ntmp", k % 2), ("mods", l, 3 * j)], [("XN", k, nb)],
                        bias=mods[:, l, 3 * j, k:k + 1])
        ar.free(xsq, rstd, *tmp)
        kb.barrier()

    def load_w(dst, src, key, q="pool"):
        return kb.dma(q, dst, src, writes=[key])

    def proj_fm(W, wkey, nk, actf, akey, mchunks, evac):
        for nb in range(NB):
            for mi in mchunks:
                b = nextbank()
                for k in range(nk):
                    mm(banks[b][:], W[:, k, mi * P:(mi + 1) * P], actf(k, nb), k == 0, k == nk - 1,
                       reads=[wkey, akey(k, nb)], writes=[("ps", b)])
                evac(b, mi, nb)

    def resid_evac(l, gi):
        def f(b, mi, nb):
            sl = slice(nb * 512, (nb + 1) * 512)
            stt(X[:, mi, sl], banks[b][:], mods[:, l, gi, mi:mi + 1], X[:, mi, sl], ALU.mult, ALU.add,
                [("ps", b), ("mods", l, gi), ("X", nb)], [("X", nb)])
        return f

    def fourier_inproj(l, XN):
        Wf = ar.alloc((KC, 256), BF16)
        load_w(Wf, w_in_d[l, :, 0:256].rearrange("(k p) n -> p k n", p=P), "Wf")
        zfT = ar.alloc((2, T), BF16)

        def ev_zf(b, mi, nb):
            copy_to(evac_eng(), zfT[:, mi, nb * 512:(nb + 1) * 512], banks[b][:], [("ps", b)], [("zfT", nb)])
        proj_fm(Wf, "Wf", KC, lambda k, nb: XN[:, k, nb * 512:(nb + 1) * 512], lambda k, nb: ("XN", k, nb), range(2),
                ev_zf)
        ar.free(Wf)
        return zfT

    def fourier_core(l, zfT):
        foT = ar.alloc((2, T), BF16, top=True)
        Wfo = ar.alloc((2, 256), BF16)
        load_w(Wfo, w_fo_d[l].rearrange("(k p) n -> p k n", p=P), "Wfo")
        ZCS = ar.alloc((NT, 512), BF16)
        NR = 2
        PM = [ar.alloc((2, T // 2), BF16) for _ in range(NR)]
        for t in range(NT):
            b = nextbank()
            for kc in range(2):
                mm(banks[b][:, kc * 256:(kc + 1) * 256], zfT[:, kc, t * P:(t + 1) * P], csc[:], True, True,
                   reads=[("zfT", t // 4), "csc"], writes=[("ps", b)])
            copy_to(evac_eng(), ZCS[:, t, :], banks[b][:], [("ps", b)], [("ZCS", t)])
        kb.barrier()
        yT = zfT
        it = 0
        for hp in range(2):
            for ti in range(NT):
                pm = PM[it % NR]
                kb.dma("sp", pm, posmat_d[ti][:, :, hp * 1024:(hp + 1) * 1024], writes=[("PM", it % NR)])
                for cs in range(2):
                    for fc in range(2):
                        for nbl in range(2):
                            b = hp * 4 + fc * 2 + nbl
                            mm(banks[b][:], ZCS[:, ti, fc * 256 + cs * P: fc * 256 + (cs + 1) * P],
                               pm[:, cs, nbl * 512:(nbl + 1) * 512], ti == 0 and cs == 0, ti == NT - 1 and cs == 1,
                               reads=[("ZCS", ti), ("PM", it % NR)], writes=[("ps", b)])
                it += 1
            for fc in range(2):
                for nbl in range(2):
                    b = hp * 4 + fc * 2 + nbl
                    nb = hp * 2 + nbl
                    copy_to(evac_eng(), yT[:, fc, nb * 512:(nb + 1) * 512], banks[b][:], [("ps", b)], [("zfT", nb)])
        bstate["i"] = 0

        def ev_fo(b, mi, nb):
            copy_to(evac_eng(), foT[:, mi, nb * 512:(nb + 1) * 512], banks[b][:], [("ps", b)], [("foT", nb)])
        proj_fm(Wfo, "Wfo", 2, lambda k, nb: yT[:, k, nb * 512:(nb + 1) * 512], lambda k, nb: ("zfT", nb), range(2),
                ev_fo)
        ar.free(Wfo, ZCS, *PM, zfT)
        kb.barrier()
        return foT

    def mlstm_inproj(l, XN):
        Qt = ar.alloc((NT, 384), BF16)
        Kt = ar.alloc((NT, 384), BF16)
        Vt = ar.alloc((NT, 384), BF16)
        ZO = ar.alloc((NT, 384), BF16)
        GT = ar.alloc((NT, 16), F32)
        bg = ar.alloc((16,), F32)
        mnw = ar.alloc((384,), F32)
        kb.dma("sp", bg, bg_d[:, l, :], writes=["bg"])
        kb.dma("sp", mnw, mnw_d[:, l, :], writes=["mnw"])
        groups = [(256, 640, "q"), (640, 1024, "k"), (1024, 1408, "v"), (1408, 1808, "go")]
        Wp = [ar.alloc((KC, 400), BF16) for _ in range(2)]
        import os
        groups = groups[:int(os.environ.get("KIPG", "4"))]
        for gi, (c0, c1, kind) in enumerate(groups):
            W = Wp[gi % 2]
            n = c1 - c0
            load_w(W[:, :, 0:n], w_in_d[l, :, c0:c1].rearrange("(k p) n -> p k n", p=P), ("Wp", gi % 2))
            for t in range(NT):
                b = nextbank()
                for k in range(KC):
                    mm(banks[b][:, 0:n], XN[:, k, t * P:(t + 1) * P], W[:, k, 0:n], k == 0, k == KC - 1,
                       reads=[("Wp", gi % 2), ("XN", k, t // 4)], writes=[("ps", b)])
                if kind == "q":
                    copy_to(evac_eng(), Qt[:, t, :], banks[b][:, 0:384], [("ps", b)], [("Qt", t)])
                elif kind == "k":
                    act(Kt[:, t, :], banks[b][:, 0:384], AF.Copy, [("ps", b)], [("Kt", t)], scale=float(DH) ** -0.5)
                elif kind == "v":
                    copy_to(evac_eng(), Vt[:, t, :], banks[b][:, 0:384], [("ps", b)], [("Vt", t)])
                else:
                    KGO = int(os.environ.get("KGO", "7"))
                    if KGO & 1:
                        tt("dve", GT[:, t, :], banks[b][:, 0:16], bg, ALU.add, [("ps", b), "bg"], [("GT", t)])
                    if KGO & 2:
                        act(ZO[:, t, :], banks[b][:, 16:400], AF.Sigmoid, [("ps", b)], [("ZO", t)])
                    if KGO & 4:
                        tt("pool", ZO[:, t, :], ZO[:, t, :], mnw, ALU.mult, [("ZO", t), "mnw"], [("ZO", t)])
        ar.free(*Wp, bg, mnw)
        kb.barrier()
        return dict(Qt=Qt, Kt=Kt, Vt=Vt, ZO=ZO, GT=GT)

    def mlstm_core(l, mb):
        Qt, Kt, Vt, ZO, GT = mb["Qt"], mb["Kt"], mb["Vt"], mb["ZO"], mb["GT"]
        import os
        STAGE = int(os.environ.get("KSTAGE", "9"))

        def bail(bufs):
            ar.free(*bufs)
            kb.barrier()
            moT_ = ar.alloc((3, T), BF16)
            kb.op("dve", lambda e: e.memset(moT_, 0.0), writes=[("moT", i) for i in range(NB)])
            kb.barrier()
            return moT_
        if STAGE < 1:
            return bail([Qt, Kt, Vt, ZO, GT])
        allT = [("GT", t) for t in range(NT)]
        SPl = ar.alloc((2, NT, H), F32)
        IG = ar.alloc((2, NT, H), F32)
        A = ar.alloc((128,), F32)
        NBc = ar.alloc((128,), F32)
        Wt = ar.alloc((128,), F32)
        FL = ar.alloc((128,), F32)
        MMbc = ar.alloc((128,), F32)
        INbc = ar.alloc((128,), F32)
        COLS = ar.alloc((2,), F32)
        ROW = ar.alloc((256,), F32)
        MP = ar.alloc((128,), F32)
        MMr = ar.alloc((128,), F32)
        MN = ar.alloc((128,), F32)
        INr = ar.alloc((128,), F32)
        MI = ar.alloc((8,), F32)
        kb.dma("sp", MI[0:1, :], mM0_d[:, l, :], writes=["MI"])
        GTv = GT
        for d in range(2):
            act(SPl[:, d, :, :], GTv[:, :, d * 8 + 4:d * 8 + 8], AF.Exp, allT, [("SPl", d)], scale=-1.0)
            act(SPl[:, d, :, :], SPl[:, d, :, :], AF.Ln, [("SPl", d)], [("SPl", d)], bias=1.0)
            copy_to("dve", IG[:, d, :, :], GTv[:, :, d * 8:d * 8 + 4], allT, [("IG", d)])
        SPf = SPl.rearrange("p d t h -> p (d t h)")
        IGf = IG.rearrange("p d t h -> p (d t h)")
        b = nextbank()
        mm(banks[b][:, 0:64], trile, SPf[:, 0:64], True, True, ["cmask", ("SPl", 0)], [("ps", b)])
        mm(banks[b][:, 64:128], trige, SPf[:, 64:128], True, True, ["cmask", ("SPl", 1)], [("ps", b)])
        copy_to("dve", NBc, banks[b][:, 0:128], [("ps", b)], ["NBc"])
        tt("dve", A, IGf, NBc, ALU.add, [("IG", 0), ("IG", 1), "NBc"], ["A"])
        b = nextbank()
        tr(banks[b][:, 0:128], A, identf[:], ["A", "identf"], [("ps", b)])
        kb.op("dve", lambda e: e.tensor_reduce(out=COLS[:, 0:1], in_=banks[b][:, 0:128], axis=AX.X, op=ALU.max),
              reads=[("ps", b)], writes=[("COLS", 0)])
        b2 = nextbank()
        mm(banks[b2][:, 0:1], SPf, onesf[:, 0:1], True, True, [("SPl", 0), ("SPl", 1), "onesf"], [("ps", b2)])
        ts("dve", COLS[:, 1:2], banks[b2][:, 0:1], -1.0, None, ALU.mult, None, [("ps", b2)], [("COLS", 1)])
        b = nextbank()
        tr(banks[b][0:1, 0:128], COLS[:, 0:1], identf[:], [("COLS", 0), "identf"], [("ps", b)])
        tr(banks[b][0:1, 128:256], COLS[:, 1:2], identf[:], [("COLS", 1), "identf"], [("ps", b)])
        copy_to("dve", ROW[0:1, :], banks[b][0:1, 0:256], [("ps", b)], ["ROW"])
        for d in range(2):
            for n in range(NT):
                t = n if d == 0 else NT - 1 - n
                c = (d * NT + t) * H
                if n == 0:
                    copy_to("dve", MP[0:1, c:c + H], MI[0:1, d * H:(d + 1) * H], ["MI"], [("MP", d)])
                tt("dve", MMr[0:1, c:c + H], MP[0:1, c:c + H], ROW[0:1, c:c + H], ALU.max, [("MP", d), "ROW"],
                   [("MMr", d)])
                tt("dve", MN[0:1, c:c + H], MMr[0:1, c:c + H], ROW[0:1, 128 + c:128 + c + H], ALU.add,
                   [("MMr", d), "ROW"], [("MN", d)])
                if n + 1 < NT:
                    t2 = n + 1 if d == 0 else NT - 2 - n
                    c2 = (d * NT + t2) * H
                    ts("dve", MP[0:1, c2:c2 + H], MN[0:1, c:c + H], keeprow[0:1, n + 1:n + 2], None, ALU.mult, None,
                       [("MN", d), "keeprow"], [("MP", d)])
        chain = [("MP", 0), ("MP", 1), ("MMr", 0), ("MMr", 1)]
        tt("dve", INr[0:1, :], MP[0:1, :], MMr[0:1, :], ALU.subtract, chain, ["INr"])
        act(INr[0:1, :], INr[0:1, :], AF.Exp, ["INr"], ["INr"])
        b = nextbank()
        mm(banks[b][:, 0:128], onesf[0:1, :], MMr[0:1, :], True, True, ["onesf"] + chain, [("ps", b)])
        mm(banks[b][:, 128:256], onesf[0:1, :], INr[0:1, :], True, True, ["onesf", "INr"], [("ps", b)])
        copy_to("dve", MMbc, banks[b][:, 0:128], [("ps", b)], ["MMbc"])
        copy_to("act", INbc, banks[b][:, 128:256], [("ps", b)], ["INbc"])
        tt("dve", Wt, A, MMbc, ALU.subtract, ["A", "MMbc"], ["Wt"])
        act(Wt, Wt, AF.Exp, ["Wt"], ["Wt"])
        tt("dve", FL, NBc, MMbc, ALU.subtract, ["NBc", "MMbc"], ["FL"])
        act(FL, FL, AF.Exp, ["FL"], ["FL"])
        import os
        LVL = int(os.environ.get("KLVL", "9"))
        for d in range(2):
            if LVL < 3:
                break
            src = MN[0:1, d * 64:(d + 1) * 64].rearrange("p (s two h) -> p s two h", two=2, h=H)[:, :, 1 - d, :]
            if d == 0:
                dst = newm_d[:, l, d, :].rearrange("(o s) h -> o s h", o=1)
                kb.dma("sp", dst, src, reads=[("MN", d)], writes=[("newm", l, d)])
            else:
                dst = newm_d[:, l, d, :].rearrange("(o s) h -> o s h", o=1)
                kb.dma("sp", dst, src, reads=[("MN", d)], writes=[("newm", l, d)])

        if STAGE < 2:
            return bail([Qt, Kt, Vt, ZO, GT, SPl, IG, A, NBc, Wt, FL, MMbc, INbc, COLS, ROW, MP, MMr, MN, INr, MI])
        HS = ar.alloc((NT, 384), BF16)
        CST = ar.alloc((2, H, 97), F32)
        kb.dma("sp", CST[0:DH], mC0_d[:, l], writes=[("CST", 0), ("CST", 1)])
        CSb = [ar.alloc((H, 97), BF16) for _ in range(2)]
        QT = [ar.alloc((H, 128), BF16) for _ in range(2)]
        KT = [ar.alloc((H, 128), BF16) for _ in range(2)]
        SM = [ar.alloc((H, 128), BF16) for _ in range(2)]
        VE = [ar.alloc((H, 97), BF16) for _ in range(2)]
        dd = [ar.alloc((H,), F32) for _ in range(2)]
        hp = [ar.alloc((H, DH), F32) for _ in range(2)]
        CAP = [ar.alloc((H, 97), F32) for _ in range(2)]
        bcs = {}

        def stage_a(n, d):
                t = n if d == 0 else NT - 1 - n
                c = (d * NT + t) * H
                bq = nextbank()
                bqv = banks[bq][:].bitcast(BF16)
                for h in range(H):
                    tr(bqv[0:DH, h * P:(h + 1) * P], Qt[:, t, h * DH:(h + 1) * DH], identb[:], [("Qt", t), "identb"],
                       [("ps", bq)])
                    tr(bqv[0:DH, 512 + h * P:512 + (h + 1) * P], Kt[:, t, h * DH:(h + 1) * DH], identb[:],
                       [("Kt", t), "identb"], [("ps", bq)])
                copy_to("act", QT[d][0:DH].rearrange("p h n -> p (h n)"), bqv[0:DH, 0:512], [("ps", bq)], [("QT", d)])
                copy_to("dve", KT[d][0:DH].rearrange("p h n -> p (h n)"), bqv[0:DH, 512:1024], [("ps", bq)],
                        [("KT", d)])
                bs = nextbank()
                for h in range(H):
                    mm(banks[bs][:, h * P:(h + 1) * P], KT[d][0:DH, h, :], QT[d][0:DH, h, :], True, True,
                       [("KT", d), ("QT", d)], [("ps", bs)])
                tt("dve", SM[d], banks[bs][:].rearrange("p (h n) -> p h n", h=H),
                   bcast(maskb[:, d:d + 1, :], [P, H, P]), ALU.mult, [("ps", bs), "maskb"], [("SM", d)])
                tt("pool", VE[d][:, :, 0:DH], Vt[:, t, :].rearrange("p (h e) -> p h e", h=H),
                   bcast(Wt[:, c:c + H].unsqueeze(2), [P, H, DH]), ALU.mult, [("Vt", t), "Wt"], [("VE", d)])
                copy_to("pool", VE[d][:, :, DH:DH + 1], Wt[:, c:c + H].unsqueeze(2), ["Wt", ("VE", d)], [("VE", d)])
                bc = nextbank()
                bcs[d] = bc
                for h in range(H):
                    mm(banks[bc][0:DH, h * 97:(h + 1) * 97], Kt[:, t, h * DH:(h + 1) * DH], VE[d][:, h, :], True, True,
                       [("Kt", t), ("VE", d)], [("ps", bc)])

        def stage_b(n, d):
                t = n if d == 0 else NT - 1 - n
                c = (d * NT + t) * H
                for h in range(H):
                    act(CSb[d][0:DH, h, :], CST[0:DH, d, h, :], AF.Copy, [("CST", d), "INbc"], [("CSb", d)],
                        scale=INbc[0:DH, c + h:c + h + 1])
                bc = bcs[d]
                for h in range(H):
                    stt(CST[0:DH, d, h, :], CST[0:DH, d, h, :], INbc[0:DH, c + h:c + h + 1],
                        banks[bc][0:DH, h * 97:(h + 1) * 97], ALU.mult, ALU.add,
                        [("CST", d), "INbc", ("ps", bc)], [("CST", d)])
                if n % 2 == 1:
                    kcap = n // 2
                    seq = kcap if d == 0 else 7 - kcap
                    copy_to("act", CAP[d][0:DH], CST[0:DH, d], [("CST", d)], [("CAP", d)])
                    if LVL >= 1:
                        kb.dma("sp", newC_d[seq, l, d].rearrange("h k e -> k h e"), CAP[d][0:DH, :, 0:DH],
                               reads=[("CAP", d)], writes=[("newC", seq, l, d)])
                    if LVL >= 2:
                        with nc.allow_non_contiguous_dma(reason="small state vector"):
                            kb.dma("sp", newn_d[seq, l, d].rearrange("h k -> k h"), CAP[d][0:DH, :, DH],
                                   reads=[("CAP", d)], writes=[("newn", seq, l, d)])
                    if n + 1 < NT:
                        ts("dve", CST[0:DH, d], CST[0:DH, d], keepcol[0:DH, 0:1], None, ALU.mult, None,
                           [("CST", d), "keepcol"], [("CST", d)])
                bn = nextbank()
                for h in range(H):
                    mm(banks[bn][:, h * 97:(h + 1) * 97], SM[d][:, h, :], VE[d][:, h, :], True, False,
                       [("SM", d), ("VE", d)], [("ps", bn)])
                    mm(banks[bn][:, h * 97:(h + 1) * 97], QT[d][0:DH, h, :], CSb[d][0:DH, h, :], False, True,
                       [("QT", d), ("CSb", d)], [("ps", bn)])
                ndv = banks[bn][:, 0:H * 97].rearrange("p (h e) -> p h e", h=H)
                act(dd[d], ndv[:, :, DH], AF.Abs, [("ps", bn)], [("dd", d)])
                tt("dve", dd[d], dd[d], FL[:, c:c + H], ALU.max, [("dd", d), "FL"], [("dd", d)])
                kb.op("dve", lambda e: e.reciprocal(out=dd[d], in_=dd[d]), reads=[("dd", d)], writes=[("dd", d)])
                if n < NT // 2:
                    tt("dve", HS[:, t, :].rearrange("p (h e) -> p h e", h=H), ndv[:, :, 0:DH],
                       bcast(dd[d].unsqueeze(2), [P, H, DH]), ALU.mult, [("ps", bn), ("dd", d)], [("HS", t)])
                else:
                    tt("dve", hp[d], ndv[:, :, 0:DH], bcast(dd[d].unsqueeze(2), [P, H, DH]), ALU.mult,
                       [("ps", bn), ("dd", d)], [("hp", d)])
                    tt("pool", HS[:, t, :].rearrange("p (h e) -> p h e", h=H),
                       HS[:, t, :].rearrange("p (h e) -> p h e", h=H), hp[d], ALU.add, [("hp", d), ("HS", t)],
                       [("HS", t)])

        its = [(n, d) for n in range(NT) for d in range(2)]
        stage_a(*its[0])
        for i_ in range(len(its)):
            if i_ + 1 < len(its):
                stage_a(*its[i_ + 1])
            stage_b(*its[i_])
        ar.free(Qt, Kt, Vt, SPl, IG, A, NBc, Wt, FL, MMbc, INbc, COLS, ROW, MP, MMr, MN, INr, MI, CST, *CSb, *QT, *KT,
                *SM, *VE, *dd, *hp, *CAP)
        kb.barrier()
        if STAGE < 3:
            return bail([HS, ZO, GT])
        moT = ar.alloc((3, T), BF16)
        NQ = 4
        sq = [ar.alloc((384,), F32) for _ in range(NQ)]
        ss = [ar.alloc((H,), F32) for _ in range(NQ)]
        mo = [ar.alloc((384,), BF16) for _ in range(NQ)]
        for t in range(NT):
            q = t % NQ
            hs = HS[:, t, :]
            tt("pool", sq[q], hs, hs, ALU.mult, [("HS", t)], [("sq", q)])
            kb.op("dve", lambda e: e.tensor_reduce(out=ss[q], in_=sq[q].rearrange("p (h e) -> p h e", h=H), axis=AX.X,
                                                   op=ALU.add), reads=[("sq", q)], writes=[("ss", q)])
            act(ss[q], ss[q], AF.Sqrt, [("ss", q)], [("ss", q)], scale=1.0 / DH, bias=EPS)
            kb.op("dve", lambda e: e.reciprocal(out=ss[q], in_=ss[q]), reads=[("ss", q)], writes=[("ss", q)])
            tt("dve", sq[q].rearrange("p (h e) -> p h e", h=H), hs.rearrange("p (h e) -> p h e", h=H),
               bcast(ss[q].unsqueeze(2), [P, H, DH]), ALU.mult, [("HS", t), ("ss", q), ("sq", q)], [("sq", q)])
            tt("dve", mo[q], sq[q], ZO[:, t, :], ALU.mult, [("sq", q), ("ZO", t)], [("mo", q)])
            bq = nextbank()
            bqv = banks[bq][:].bitcast(BF16)
            for cc in range(3):
                tr(bqv[:, cc * P:(cc + 1) * P], mo[q][:, cc * P:(cc + 1) * P], identb[:], [("mo", q), "identb"],
                   [("ps", bq)])
            copy_to("act", moT[:, :, t * P:(t + 1) * P], bqv[:, 0:384].rearrange("p (c n) -> p c n", c=3),
                    [("ps", bq)], [("moT", t // 4)])
        ar.free(HS, ZO, GT, *sq, *ss, *mo)
        kb.barrier()
        return moT

    prep_cache = {}

    def s5_prep(l, tick=None):
        HALF_PI = math.pi / 2
        GH = G // 2

        def _tick():
            if tick is not None:
                tick()
        prm = ar.alloc((3, G), F32)
        bc4 = ar.alloc((4, G, GC), F32)
        dcol = ar.alloc((G,), F32)
        kb.dma("sp", prm, s5p_d[:, l], writes=["prm"])
        kb.dma("sp", bc4, s5bc_d[:, l], writes=["bc4"])
        kb.dma("sp", dcol, s5dcol_d[:, l], writes=["dcol"])
        sm = {}

        def sv(name):
            if name not in sm:
                sm[name] = ar.alloc((G,), F32)
            return sm[name]
        K1 = ["s5tmp"]

        def e_tt(out, a, bb, op, eng="dve"):
            tt(eng, out, a, bb, op, K1 + ["prm", "bc4"], K1)

        def e_ts(out, a, s1, op0, s2=None, op1=None):
            ts("dve", out, a, s1, s2, op0, op1, K1 + ["prm"], K1)

        lre, lim, lst = prm[:, 0, :], prm[:, 1, :], prm[:, 2, :]
        step = sv("step")
        act(step, lst, AF.Exp, ["prm"], K1)
        mag = sv("mag")
        e_tt(mag, lre, step, ALU.mult)
        act(mag, mag, AF.Exp, K1, K1)
        ang = sv("ang")
        stt(ang, lim, 1.0 / 16, step, ALU.mult, ALU.mult, K1 + ["prm"], K1)
        cs_, sn_ = sv("c"), sv("s")
        halfpi = sv("halfpi")
        kb.op("dve", lambda e: e.memset(halfpi, HALF_PI), writes=K1)
        act(sn_, ang, AF.Sin, K1, K1)
        act(cs_, ang, AF.Sin, K1, K1, bias=halfpi[:, 0:1])
        t1, t2, t3 = sv("t1"), sv("t2"), sv("t3")
        for _ in range(4):
            e_tt(t1, cs_, cs_, ALU.mult)
            e_tt(t2, sn_, sn_, ALU.mult)
            e_tt(t3, cs_, sn_, ALU.mult)
            e_tt(cs_, t1, t2, ALU.subtract)
            e_ts(sn_, t3, 2.0, ALU.mult)
        PW = ar.alloc((9, 2, G), F32)
        kb.op("dve", lambda e: e.memset(PW[:, 0, 0, :], 1.0), writes=K1)
        kb.op("dve", lambda e: e.memset(PW[:, 0, 1, :], 0.0), writes=K1)
        e_tt(PW[:, 1, 0, :], mag, cs_, ALU.mult)
        e_tt(PW[:, 1, 1, :], mag, sn_, ALU.mult)
        lbre, lbim = PW[:, 1, 0, :], PW[:, 1, 1, :]

        def cmul(ore, oim, are, aim, bre, bim, conj_neg_im=False):
            e_tt(t1x(ore), are, bre, ALU.mult)
            e_tt(t2x(ore), aim, bim, ALU.mult)
            e_tt(ore, t1x(ore), t2x(ore), ALU.subtract)
            e_tt(t1x(ore), are, bim, ALU.mult)
            e_tt(t2x(ore), aim, bre, ALU.mult)
            e_tt(oim, t1x(ore), t2x(ore), ALU.add)

        GQ = 6
        big1 = ar.alloc((GQ, 8, GC), F32)
        big2 = ar.alloc((GQ, 8, GC), F32)

        def t1x(like):
            n = 1
            for s_ in like.shape[1:]:
                n *= s_
            v = big1.rearrange("p a b c -> p (a b c)")[:, 0:n]
            return reshape_like(v, like)

        def t2x(like):
            n = 1
            for s_ in like.shape[1:]:
                n *= s_
            v = big2.rearrange("p a b c -> p (a b c)")[:, 0:n]
            return reshape_like(v, like)

        def reshape_like(v, like):
            sh = like.shape[1:]
            if len(sh) == 1:
                return v
            names = " ".join("a%d" % i for i in range(len(sh)))
            kw = {"a%d" % i: sh[i] for i in range(len(sh))}
            v = v.rearrange("p (%s) -> p %s" % (names, names), **kw)
            if like.shape[0] != P:
                v = v[0:like.shape[0]]
            return v

        for k in range(2, 9):
            cmul(PW[:, k, 0, :], PW[:, k, 1, :], PW[:, k - 1, 0, :], PW[:, k - 1, 1, :], lbre, lbim)
        i8re, i8im, den = sv("i8re"), sv("i8im"), sv("den")
        e_tt(t1, PW[:, 8, 0, :], PW[:, 8, 0, :], ALU.mult)
        e_tt(t2, PW[:, 8, 1, :], PW[:, 8, 1, :], ALU.mult)
        e_tt(den, t1, t2, ALU.add)
        kb.op("dve", lambda e: e.reciprocal(out=den, in_=den), reads=K1, writes=K1)
        e_tt(i8re, PW[:, 8, 0, :], den, ALU.mult)
        e_tt(i8im, PW[:, 8, 1, :], den, ALU.mult)
        e_ts(i8im, i8im, -1.0, ALU.mult)
        cfre, cfim, ar_ = sv("cfre"), sv("cfim"), sv("ar_")
        e_ts(ar_, lbre, -1.0, ALU.add)
        e_tt(t1, lre, lre, ALU.mult)
        e_tt(t2, lim, lim, ALU.mult)
        e_tt(den, t1, t2, ALU.add)
        kb.op("dve", lambda e: e.reciprocal(out=den, in_=den), reads=K1, writes=K1)
        e_tt(t1, ar_, lre, ALU.mult)
        e_tt(t2, lbim, lim, ALU.mult)
        e_tt(cfre, t1, t2, ALU.add)
        e_tt(cfre, cfre, den, ALU.mult)
        e_tt(t1, lbim, lre, ALU.mult)
        e_tt(t2, ar_, lim, ALU.mult)
        e_tt(cfim, t1, t2, ALU.subtract)
        e_tt(cfim, cfim, den, ALU.mult)
        _tick()
        Bb = ar.alloc((2, G, GC), F32)
        Cm2 = ar.alloc((2, G, GC), F32)
        gshape = [P, G, GC]
        cmul(Bb[:, 0], Bb[:, 1], bcast(cfre.unsqueeze(2), gshape), bcast(cfim.unsqueeze(2), gshape), bc4[:, 0], bc4[:, 1])
        cmul(Cm2[:, 0], Cm2[:, 1], bcast(i8re.unsqueeze(2), gshape), bcast(i8im.unsqueeze(2), gshape), bc4[:, 2],
             bc4[:, 3])
        PWe = ar.alloc((8, 2, G), F32)
        PWc = ar.alloc((8, 2, G), F32)
        for r in range(8):
            copy_to("dve", PWe[0:64, r], PW[0:64, 7 - r], K1, K1)
            copy_to("dve", PWe[64:128, r], PW[64:128, r], K1, K1)
            copy_to("dve", PWc[0:64, r], PW[0:64, r + 1], K1, K1)
            copy_to("dve", PWc[64:128, r], PW[64:128, 8 - r], K1, K1)
        ET = ar.alloc((G, 2, 128), BF16)
        CPn = ar.alloc((G, 2, 128), BF16)
        GH = G // 2

        def expand(dst, pw, mat_re, mat_im, neg_im):
            for gh in range(G // GQ):
                gs = slice(gh * GQ, (gh + 1) * GQ)
                shp = [P, GQ, 8, GC]
                pre = bcast(pw[:, :, 0, gs].rearrange("p r g -> p g r").unsqueeze(3), shp)
                pim = bcast(pw[:, :, 1, gs].rearrange("p r g -> p g r").unsqueeze(3), shp)
                mre = bcast(mat_re[:, gs, :].unsqueeze(2), shp)
                mim = bcast(mat_im[:, gs, :].unsqueeze(2), shp)
                dre = dst[:, gs, 0, :].rearrange("p g (r c) -> p g r c", r=8)
                dim_ = dst[:, gs, 1, :].rearrange("p g (r c) -> p g r c", r=8)
                e_tt(big1, pre, mre, ALU.mult)
                e_tt(big2, pim, mim, ALU.mult)
                e_tt(dre, big1, big2, ALU.subtract)
                e_tt(big1, pre, mim, ALU.mult)
                e_tt(big2, pim, mre, ALU.mult)
                if neg_im:
                    e_tt(big1, big1, big2, ALU.add)
                    e_ts(dim_, big1, -1.0, ALU.mult)
                else:
                    e_tt(dim_, big1, big2, ALU.add)
        expand(ET, PWe, Bb[:, 0], Bb[:, 1], False)
        _tick()
        expand(CPn, PWc, Cm2[:, 0], Cm2[:, 1], True)
        _tick()
        Emat = ar.alloc((G, 2, 128), BF16, top=True)
        T0 = ar.alloc((G, 128), BF16, top=True)
        for g in range(G):
            if g % 4 == 3:
                _tick()
            bq = nextbank()
            bqv = banks[bq][:].bitcast(BF16)
            for ri in range(2):
                tr(bqv[:, ri * P:(ri + 1) * P], ET[:, g, ri, :], identb[:], K1 + ["identb"], [("ps", bq)])
            copy_to(evac_eng(), Emat[:, g, :, :], bqv[:, 0:256].rearrange("p (r n) -> p r n", r=2), [("ps", bq)],
                    ["Emat"])
            bts = [nextbank(), nextbank()]
            for dd_ in range(2):
                ps_ = slice(dd_ * 64, (dd_ + 1) * 64)
                for ri in range(2):
                    mm(banks[bts[dd_]][:, 0:P], ET[ps_, g, ri, :], CPn[ps_, g, ri, :], ri == 0, ri == 1,
                       K1, [("ps", bts[dd_])])
            ta, tb = big1.rearrange("p a b c -> p (a b c)")[:, 0:128], big2.rearrange("p a b c -> p (a b c)")[:, 0:128]
            tt("dve", ta, banks[bts[0]][:, 0:128], s5mF, ALU.mult, [("ps", bts[0]), "cmask"] + K1, K1)
            tt("dve", tb, banks[bts[1]][:, 0:128], s5mB, ALU.mult, [("ps", bts[1]), "cmask"] + K1, K1)
            tt("dve", ta, ta, tb, ALU.add, K1, K1)
            stt(T0[:, g, :], identf[:], dcol[:, g:g + 1], ta, ALU.mult, ALU.add, K1 + ["identf", "dcol"], ["T0"])
        ar.free(ET, CPn, Bb, Cm2, PWe)
        kb.barrier()
        CP = ar.alloc((G, 2, 128), BF16, top=True)
        expand(CP, PWc, bc4[:, 2], bc4[:, 3], True)
        LreB = ar.alloc((2, G), F32, top=True)
        LimS = ar.alloc((2, G), F32, top=True)
        copy_to("dve", LreB[:, 0, :], PW[:, 8, 0, :], K1, ["Lmul"])
        copy_to("dve", LreB[:, 1, :], PW[:, 8, 0, :], K1, ["Lmul"])
        ts("dve", LimS[:, 0, :], PW[:, 8, 1, :], -1.0, None, ALU.mult, None, K1, ["Lmul"])
        copy_to("dve", LimS[:, 1, :], PW[:, 8, 1, :], K1, ["Lmul"])
        LreB16 = ar.alloc((2, G), F32, top=True)
        LimS16 = ar.alloc((2, G), F32, top=True)
        e_tt(t1, PW[:, 8, 0, :], PW[:, 8, 0, :], ALU.mult)
        e_tt(t2, PW[:, 8, 1, :], PW[:, 8, 1, :], ALU.mult)
        tt("dve", LreB16[:, 0, :], t1, t2, ALU.subtract, K1, ["Lmul"])
        tt("dve", LreB16[:, 1, :], t1, t2, ALU.subtract, K1, ["Lmul"])
        e_tt(t3, PW[:, 8, 0, :], PW[:, 8, 1, :], ALU.mult)
        ts("dve", LimS16[:, 1, :], t3, 2.0, None, ALU.mult, None, K1, ["Lmul"])
        ts("dve", LimS16[:, 0, :], t3, -2.0, None, ALU.mult, None, K1, ["Lmul"])
        LreB32 = ar.alloc((2, G), F32, top=True)
        LimS32 = ar.alloc((2, G), F32, top=True)
        tt("dve", t1, LreB16[:, 0, :], LreB16[:, 0, :], ALU.mult, K1 + ["Lmul"], K1)
        tt("dve", t2, LimS16[:, 1, :], LimS16[:, 1, :], ALU.mult, K1 + ["Lmul"], K1)
        tt("dve", LreB32[:, 0, :], t1, t2, ALU.subtract, K1, ["Lmul"])
        tt("dve", LreB32[:, 1, :], t1, t2, ALU.subtract, K1, ["Lmul"])
        tt("dve", t3, LreB16[:, 0, :], LimS16[:, 1, :], ALU.mult, K1 + ["Lmul"], K1)
        ts("dve", LimS32[:, 1, :], t3, 2.0, None, ALU.mult, None, K1, ["Lmul"])
        ts("dve", LimS32[:, 0, :], t3, -2.0, None, ALU.mult, None, K1, ["Lmul"])
        ar.free(prm, bc4, dcol, PW, PWc, big1, big2, *sm.values())
        kb.barrier()

        return dict(Emat=Emat, T0=T0, CP=CP, LreB=LreB, LimS=LimS, LreB16=LreB16, LimS16=LimS16, LreB32=LreB32,
                    LimS32=LimS32)

    def s5_all(l):
        HALF_PI = math.pi / 2
        import os
        KS5 = int(os.environ.get("KS5", "9"))
        base_live = set(ar.live.keys())

        def bail5():
            for k_ in list(ar.live.keys()):
                if k_ not in base_live:
                    ar.free(ar.live[k_][0])
            kb.barrier()
            so_ = ar.alloc((3, T), BF16)
            kb.op("dve", lambda e: e.memset(so_, 0.0), writes=[("soT", i) for i in range(NB)])
            kb.barrier()
            return so_
        XN = ar.alloc((KC, T), BF16)
        rms_norm(l, 0, XN)
        Ws = ar.alloc((KC, 384), BF16)
        load_w(Ws, w_in_d[l, :, 1808:2192].rearrange("(k p) n -> p k n", p=P), "Ws")
        U = ar.alloc((G, NSB), BF16, top=True)
        Urev = ar.alloc((G, NSB), BF16, top=True)
        Utok = [ar.alloc((G, 128), BF16) for _ in range(2)]
        for half in range(2):
            ut = Utok[half]
            for r in range(8):
                b = nextbank()
                for k in range(KC):
                    lhsT = XN[:, k, half * 1024 + r:(half + 1) * 1024:8]
                    mm(banks[b][:, 0:384], lhsT, Ws[:, k, :], k == 0, k == KC - 1,
                       reads=["Ws"] + [("XN", k, nb) for nb in (2 * half, 2 * half + 1)], writes=[("ps", b)])
                copy_to(evac_eng(), ut[:, :, r * GC:(r + 1) * GC], banks[b][:, 0:384].rearrange("p (g c) -> p g c", g=G),
                        [("ps", b)], [("Utok", half)])
            for g0 in range(0, G, 4):
                b = nextbank()
                for gg in range(4):
                    mm(banks[b][:, gg * P:(gg + 1) * P], ut[:, g0 + gg, :], identb[:], True, True,
                       [("Utok", half), "identb"], [("ps", b)])
                copy_to(evac_eng(), U[:, g0:g0 + 4, half * P:(half + 1) * P],
                        banks[b][:].rearrange("p (g n) -> p g n", g=4), [("ps", b)], [("U", half)])
                b = nextbank()
                for gg in range(4):
                    mm(banks[b][:, gg * P:(gg + 1) * P], ut[:, g0 + gg, :], Jb[:], True, True,
                       [("Utok", half), "Jb"], [("ps", b)])
                copy_to(evac_eng(), Urev[:, g0:g0 + 4, (1 - half) * P:(2 - half) * P],
                        banks[b][:].rearrange("p (g n) -> p g n", g=4), [("ps", b)], [("Urev", 1 - half)])
        ar.free(XN, Ws, *Utok)
        kb.barrier()

        if KS5 < 2:
            return bail5()
        pp = prep_cache.pop(l, None)
        if pp is None:
            pp = s5_prep(l)
        Emat, T0, CP, LreB, LimS = pp["Emat"], pp["T0"], pp["CP"], pp["LreB"], pp["LimS"]
        LreB16, LimS16, LreB32, LimS32 = pp["LreB16"], pp["LimS16"], pp["LreB32"], pp["LimS32"]
        GH = G // 2

        if KS5 < 3:
            return bail5()
        Gs = ar.alloc((NSB, 2, G), BF16)
        for g in range(G):
            b = nextbank()
            for ri in range(2):
                mm(banks[b][0:64, ri * NSB:(ri + 1) * NSB], Emat[:, g, ri, 0:64], U[:, g, :], True, True,
                   ["Emat", ("U", 0), ("U", 1)], [("ps", b)])
                mm(banks[b][64:128, ri * NSB:(ri + 1) * NSB], Emat[:, g, ri, 64:128], Urev[:, g, :], True, True,
                   ["Emat", ("Urev", 0), ("Urev", 1)], [("ps", b)])
            copy_to(evac_eng(), Gs[:, :, :, g].rearrange("p n r -> p r n"),
                    banks[b][:].rearrange("p (r n) -> p r n", r=2), [("ps", b)], ["Gs"])
        ar.free(Emat, Urev)
        kb.barrier()

        if KS5 < 4:
            return bail5()
        HIST = ar.alloc((NSB, 2, G), BF16)
        NST = 4
        NK = NSB // 2
        ST = [ar.alloc((2, G), F32) for _ in range(NST)]
        TA = [ar.alloc((2, G), F32) for _ in range(2)]
        TB = [ar.alloc((2, G), F32) for _ in range(2)]
        CAPs = ar.alloc((8, 2, G), F32)
        G2 = ar.alloc((NK, 2, G), BF16)
        tb1 = ar.alloc((32, 2, G), F32)
        tb2 = ar.alloc((32, 2, G), F32)
        kb.dma("sp", ST[0], s5st0_d[:, l], writes=[("ST", 0, 0), ("ST", 0, 1)])
        shp4 = [P, 32, 2, G]
        for c4 in range(4):
            ge = Gs[:, 64 * c4:64 * c4 + 64:2]
            go = Gs[:, 64 * c4 + 1:64 * c4 + 64:2]
            tt("dve", tb1, ge, bcast(LreB.unsqueeze(1), shp4), ALU.mult, ["Gs", "Lmul"], ["tb1"])
            tt("dve", tb2, rev_axis(ge, 2), bcast(LimS.unsqueeze(1), shp4), ALU.mult, ["Gs", "Lmul"], ["tb2"])
            tt("dve", tb1, tb1, tb2, ALU.add, ["tb1", "tb2"], ["tb1"])
            tt("dve", G2[:, 32 * c4:32 * c4 + 32], tb1, go, ALU.add, ["tb1", "Gs"], ["G2"])
        NJ = NSB // 4
        G4 = ar.alloc((NJ, 2, G), BF16)
        for c2 in range(2):
            ge = G2[:, 64 * c2:64 * c2 + 64:2]
            go = G2[:, 64 * c2 + 1:64 * c2 + 64:2]
            tt("dve", tb1, ge, bcast(LreB16.unsqueeze(1), shp4), ALU.mult, ["G2", "Lmul"], ["tb1"])
            tt("dve", tb2, rev_axis(ge, 2), bcast(LimS16.unsqueeze(1), shp4), ALU.mult, ["G2", "Lmul"], ["tb2"])
            tt("dve", tb1, tb1, tb2, ALU.add, ["tb1", "tb2"], ["tb1"])
            tt("dve", G4[:, 32 * c2:32 * c2 + 32], tb1, go, ALU.add, ["tb1", "G2"], ["G4"])
        gss = [slice(hh * GH, (hh + 1) * GH) for hh in range(2)]
        for j in range(NJ):
            n = 4 * j
            ci, ni = j % NST, (j + 1) % NST
            cur, nxt = ST[ci], ST[ni]
            if n % 32 == 0 and n > 0:
                for hh in range(2):
                    ts("dve", cur[:, :, gss[hh]], cur[:, :, gss[hh]], keepcol[:, 0:1], None, ALU.mult, None,
                       [("ST", ci, hh), "keepcol"], [("ST", ci, hh)])
            copy_to("act", HIST[0:64, n], cur[0:64], [("ST", ci, 0), ("ST", ci, 1)], ["HIST"])
            copy_to("act", HIST[64:128, NSB - 1 - n], cur[64:128], [("ST", ci, 0), ("ST", ci, 1)], ["HIST"])
            for hh in range(2):
                tt("dve", TA[hh][:, :, 0:GH], cur[:, :, gss[hh]], LreB32[:, :, gss[hh]], ALU.mult,
                   [("ST", ci, hh), "Lmul"], [("TA", hh)])
            for hh in range(2):
                tt("dve", TB[hh][:, :, 0:GH], swap_ri(cur[:, :, gss[hh]]), LimS32[:, :, gss[hh]], ALU.mult,
                   [("ST", ci, hh), "Lmul"], [("TB", hh)])
            for hh in range(2):
                tt("dve", TA[hh][:, :, 0:GH], TA[hh][:, :, 0:GH], TB[hh][:, :, 0:GH], ALU.add,
                   [("TA", hh), ("TB", hh)], [("TA", hh)])
            for hh in range(2):
                tt("dve", nxt[:, :, gss[hh]], TA[hh][:, :, 0:GH], G4[:, j, :, gss[hh]], ALU.add,
                   [("TA", hh), "G4"], [("ST", ni, hh)])
            if (j + 1) % 8 == 0:
                copy_to("act", CAPs[:, (j + 1) // 8 - 1], nxt, [("ST", ni, 0), ("ST", ni, 1)], ["CAPs"])
        hshp = [64, 32, 2, G]

        def bulk(xin, xout, gin, Lr, Li, ps_, gkey):
            tt("dve", tb1[ps_], xin, bcast(Lr[ps_].unsqueeze(1), hshp), ALU.mult, ["HIST", "Lmul"], ["tb1"])
            tt("dve", tb2[ps_], rev_axis(xin, 2), bcast(Li[ps_].unsqueeze(1), hshp), ALU.mult, ["HIST", "Lmul"],
               ["tb2"])
            tt("dve", tb1[ps_], tb1[ps_], tb2[ps_], ALU.add, ["tb1", "tb2"], ["tb1"])
            tt("dve", xout, tb1[ps_], gin, ALU.add, ["tb1", gkey], ["HIST"])
        fw, bw = slice(0, 64), slice(64, 128)
        for c2 in range(2):
            lo = 128 * c2
            bulk(HIST[fw, lo:lo + 128:4], HIST[fw, lo + 2:lo + 128:4], G2[fw, 64 * c2:64 * c2 + 64:2], LreB16, LimS16,
                 fw, "G2")
            kmin = 64 - 64 * c2
            bulk(HIST[bw, lo + 3:lo + 128:4], HIST[bw, lo + 1:lo + 128:4],
                 rev_axis(G2[bw, kmin:kmin + 64:2], 1), LreB16, LimS16, bw, "G2")
        for c4 in range(4):
            lo = 64 * c4
            bulk(HIST[fw, lo:lo + 64:2], HIST[fw, lo + 1:lo + 64:2], Gs[fw, lo:lo + 64:2], LreB, LimS, fw, "Gs")
            nmin = 192 - 64 * c4
            bulk(HIST[bw, lo + 1:lo + 64:2], HIST[bw, lo:lo + 64:2], rev_axis(Gs[bw, nmin:nmin + 64:2], 1), LreB, LimS,
                 bw, "Gs")
        ar.free(Gs, G2, G4, tb1, tb2)
        kb.barrier()
        OUTS = ar.alloc((16, 128), F32)
        for k4 in range(4):
            b = nextbank()
            for j in range(4):
                kk = k4 * 4 + j
                kcap, ri = kk // 2, kk % 2
                tr(banks[b][0:G, j * P:(j + 1) * P], CAPs[:, kcap, ri, :], identf[:], ["CAPs", "identf"], [("ps", b)])
            copy_to(evac_eng(), OUTS[0:G, k4 * 4:(k4 + 1) * 4, :], banks[b][0:G, :].rearrange("p (j n) -> p j n", j=4),
                    [("ps", b)], ["OUTS"])
        for kcap in range(8):
            for ri in range(2):
                for d in range(2):
                    seq = kcap if d == 0 else 7 - kcap
                    kb.dma("sp", news5_d[ri][seq, l, d], OUTS[0:G, kcap * 2 + ri, d * 64:(d + 1) * 64],
                           reads=["OUTS"], writes=[("news5", ri, seq, l, d)])
        ar.free(*ST, *TA, *TB, CAPs, LreB, LimS, LreB16, LimS16, LreB32, LimS32)
        kb.barrier()

        if KS5 < 5:
            return bail5()
        Wgl = ar.alloc((3, 768), BF16)
        load_w(Wgl, w_glu_d[l].rearrange("(k p) n -> p k n", p=P), "Wgl")
        yT = ar.alloc((3, T), BF16)
        Yt = ar.alloc((8, 384), BF16)
        gt = [ar.alloc((512,), F32) for _ in range(2)]
        for half in range(2):
            asl = slice(half * P, (half + 1) * P)
            for g0 in range(0, G, 4):
                b = nextbank()
                for gg in range(4):
                    g = g0 + gg
                    osl = banks[b][:, gg * P:(gg + 1) * P]
                    mm(osl, HIST[:, asl, 0, g], CP[:, g, 0, :], True, False, ["HIST", "CP"],
                       [("ps", b)])
                    mm(osl, HIST[:, asl, 1, g], CP[:, g, 1, :], False, False, ["HIST", "CP"],
                       [("ps", b)])
                    mm(osl, U[:, g, asl], T0[:, g, :], False, True, [("U", half), "T0"], [("ps", b)])
                q = (g0 // 4) % 2
                act(gt[q], banks[b][:], AF.Square, [("ps", b)], [("gt", q)])
                ts("dve", gt[q], gt[q], 0.044715, 1.0, ALU.mult, ALU.add, [("gt", q)], [("gt", q)])
                tt("dve", gt[q], gt[q], banks[b][:], ALU.mult, [("gt", q), ("ps", b)], [("gt", q)])
                act(gt[q], gt[q], AF.Sigmoid, [("gt", q)], [("gt", q)], scale=1.5957691216)
                tt("dve", Yt[:, :, g0 * GC:(g0 + 4) * GC].rearrange("p r (g c) -> p g r c", g=4),
                   gt[q].rearrange("p (g r c) -> p g r c", g=4, r=8), banks[b][:].rearrange("p (g r c) -> p g r c", g=4, r=8),
                   ALU.mult, [("gt", q), ("ps", b)], ["Yt"])
            for cc in range(3):
                bq = nextbank()
                bqv = banks[bq][:].bitcast(BF16)
                for r in range(8):
                    tr(bqv[:, r * P:(r + 1) * P], Yt[:, r, cc * P:(cc + 1) * P], identb[:], ["Yt", "identb"],
                       [("ps", bq)])
                copy_to(evac_eng(), yT[:, cc, half * 1024:(half + 1) * 1024].rearrange("p (a r) -> p r a", r=8),
                        bqv.rearrange("p (r a) -> p r a", r=8), [("ps", bq)], [("yT", 2 * half), ("yT", 2 * half + 1)])
        soT = ar.alloc((3, T), BF16, top=True)
        sg = [ar.alloc((512,), BF16) for _ in range(2)]
        for nb in range(NB):
            sl = slice(nb * 512, (nb + 1) * 512)
            for mi in range(3):
                ba, bb_ = nextbank(), nextbank()
                for k in range(3):
                    mm(banks[ba][:], Wgl[:, k, mi * P:(mi + 1) * P], yT[:, k, sl], k == 0, k == 2, ["Wgl", ("yT", nb)],
                       [("ps", ba)])
                for k in range(3):
                    mm(banks[bb_][:], Wgl[:, k, 384 + mi * P:384 + (mi + 1) * P], yT[:, k, sl], k == 0, k == 2,
                       ["Wgl", ("yT", nb)], [("ps", bb_)])
                q = mi % 2
                act(sg[q], banks[bb_][:], AF.Sigmoid, [("ps", bb_)], [("sg", q)])
                tt("dve", soT[:, mi, sl], banks[ba][:], sg[q], ALU.mult, [("ps", ba), ("sg", q)], [("soT", nb)])
        ar.free(Wgl, yT, Yt, *gt, *sg, U, HIST, CP, T0, OUTS)
        kb.barrier()
        return soT

    def outproj(l, foT, moT, soT):
        Wo = ar.alloc((KC, D), BF16)
        load_w(Wo, w_out_d[l].rearrange("(k p) n -> p k n", p=P), "Wo")
        srcs = []
        if foT is not None:
            srcs += [(0, foT, 0, "foT"), (1, foT, 1, "foT")]
        if moT is not None:
            srcs += [(2 + i, moT, i, "moT") for i in range(3)]
        if soT is not None:
            srcs += [(5 + i, soT, i, "soT") for i in range(3)]
        for nb in range(NB):
            sl = slice(nb * 512, (nb + 1) * 512)
            for mi in range(KC):
                b = nextbank()
                for j, (kc, buf, ci, key) in enumerate(srcs):
                    mm(banks[b][:], Wo[:, kc, mi * P:(mi + 1) * P], buf[:, ci, sl], j == 0, j == len(srcs) - 1,
                       ["Wo", (key, nb)], [("ps", b)])
                resid_evac(l, 2)(b, mi, nb)
        ar.free(Wo)
        for buf in (foT, moT, soT):
            if buf is not None:
                ar.free(buf)
        kb.barrier()

    def ffn(l, XN, extra=None):
        NG = NHC // 2
        Wg = [ar.alloc((KC, 256), BF16) for _ in range(2)]
        Wu = [ar.alloc((KC, 256), BF16) for _ in range(2)]
        Wd = [ar.alloc((2, D), BF16) for _ in range(2)]
        Hh = [ar.alloc((2, T), BF16) for _ in range(2)]
        sg = [ar.alloc((512,), BF16) for _ in range(2)]

        def issue(gi):
            s = gi % 2
            c0 = gi * 256
            load_w(Wg[s], w_gate_d[l, :, c0:c0 + 256].rearrange("(k p) n -> p k n", p=P), ("Wg", s))
            load_w(Wu[s], w_up_d[l, :, c0:c0 + 256].rearrange("(k p) n -> p k n", p=P), ("Wu", s))
            load_w(Wd[s], w_down_d[l, c0:c0 + 256, :].rearrange("(k p) n -> p k n", p=P), ("Wd", s))
        issue(0)
        for gi in range(NG):
            s = gi % 2
            if gi + 1 < NG:
                issue(gi + 1)
            if extra is not None:
                extra(gi)
            for j in range(2):
                for nb in range(NB):
                    sl = slice(nb * 512, (nb + 1) * 512)
                    bg_ = nextbank()
                    for k in range(KC):
                        mm(banks[bg_][:], Wg[s][:, k, j * P:(j + 1) * P], XN[:, k, sl], k == 0, k == KC - 1,
                           reads=[("Wg", s), ("XN", k, nb)], writes=[("ps", bg_)])
                    bu = nextbank()
                    for k in range(KC):
                        mm(banks[bu][:], Wu[s][:, k, j * P:(j + 1) * P], XN[:, k, sl], k == 0, k == KC - 1,
                           reads=[("Wu", s), ("XN", k, nb)], writes=[("ps", bu)])
                    q = (j * NB + nb) % 2
                    act(sg[q], banks[bg_][:], AF.Silu, [("ps", bg_)], [("sg", q)])
                    tt("dve", Hh[s][:, j, sl], banks[bu][:], sg[q], ALU.mult, [("ps", bu), ("sg", q)],
                       [("Hh", s, j, nb)])
            for nb in range(NB):
                for mi in range(KC):
                    b = nextbank()
                    for j in range(2):
                        mm(banks[b][:], Wd[s][:, j, mi * P:(mi + 1) * P], Hh[s][:, j, nb * 512:(nb + 1) * 512], j == 0,
                           j == 1, reads=[("Wd", s), ("Hh", s, j, nb)], writes=[("ps", b)])
                    resid_evac(l, 5)(b, mi, nb)
        ar.free(*Wg, *Wu, *Wd, *Hh, *sg)
        kb.barrier()

    WAs = [ar.alloc((KC, D), BF16) for _ in range(2)]
    stage = [ar.alloc((D,), F32) for _ in range(2)]
    ada_dma(0, 0, WAs[0], bg=True)
    ada_dma(0, 1, WAs[1], bg=True)
    for t in range(NT):
        s = t % 2
        kb.dma("sp", stage[s], x_d[t * P:(t + 1) * P, :], writes=[("stage", s)])
        for half in range(2):
            b = nextbank()
            for j in range(4):
                k = half * 4 + j
                tr(banks[b][:, j * P:(j + 1) * P], stage[s][:, k * P:(k + 1) * P], identf[:],
                   [("stage", s), "identf"], [("ps", b)])
            copy_to(evac_eng(), X[:, half * 4:(half + 1) * 4, t * P:(t + 1) * P],
                    banks[b][:].rearrange("p (j n) -> p j n", j=4), [("ps", b)], [("X", t // 4)])
    ar.free(*stage)
    ada_i = [0]

    def ada_tick():
        if ada_i[0] < 6:
            i_ = ada_i[0]
            ada_compute(0, i_, WAs[i_ % 2])
            if i_ + 2 < 6:
                ada_dma(0, i_ + 2, WAs[i_ % 2], bg=True)
            ada_i[0] += 1
    if flags["s5"]:
        prep_cache[0] = s5_prep(0, tick=ada_tick)
    while ada_i[0] < 6:
        ada_tick()
    ar.free(*WAs)
    kb.barrier()

    for l in range(NL):
        foT = moT = soT = None
        if flags["s5"]:
            soT = s5_all(l)
        if flags["fourier"] or flags["mlstm"]:
            XN = ar.alloc((KC, T), BF16)
            rms_norm(l, 0, XN)
            mb = mlstm_inproj(l, XN) if flags["mlstm"] else None
            zfT = fourier_inproj(l, XN) if flags["fourier"] else None
            ar.free(XN)
            kb.barrier()
            if flags["fourier"]:
                foT = fourier_core(l, zfT)
            if flags["mlstm"]:
                moT = mlstm_core(l, mb)
        if foT is not None or moT is not None or soT is not None:
            outproj(l, foT, moT, soT)
        if flags["ffn"]:
            XN = ar.alloc((KC, T), BF16)
            rms_norm(l, 1, XN)
            if l + 1 < NL:
                WAs = [ar.alloc((KC, D), BF16) for _ in range(2)]

                def extra(gi, l=l, WAs=WAs):
                    if gi < 6:
                        ada_dma(l + 1, gi, WAs[gi % 2])
                    if 1 <= gi < 7:
                        ada_compute(l + 1, gi - 1, WAs[(gi - 1) % 2])
                ffn(l, XN, extra)
                ar.free(*WAs)
            else:
                ffn(l, XN)
            ar.free(XN)
            kb.barrier()
        elif l + 1 < NL:
            WAs = [ar.alloc((KC, D), BF16) for _ in range(2)]
            for i in range(6):
                ada_dma(l + 1, i, WAs[i % 2])
                ada_compute(l + 1, i, WAs[i % 2])
            ar.free(*WAs)
            kb.barrier()

    rms_norm(None, None, None)
    stage = [ar.alloc((D,), F32) for _ in range(2)]
    for t in range(NT):
        s = t % 2
        for half in range(2):
            b = nextbank()
            for j in range(4):
                k = half * 4 + j
                tr(banks[b][:, j * P:(j + 1) * P], X[:, k, t * P:(t + 1) * P], identf[:], [("X", t // 4), "identf"],
                   [("ps", b)])
            copy_to(evac_eng(), stage[s][:, half * 512:(half + 1) * 512], banks[b][:], [("ps", b)], [("stage", s, half)])
        kb.dma("sp", y_d[t * P:(t + 1) * P, :], stage[s], reads=[("stage", s, 0), ("stage", s, 1)],
               writes=[("y", t)])
    nc_done = kb.finish()
    build_program.last_peak = ar.peak * 2
    build_program.ninstr = kb.ninstr
    build_program.counts = dict(kb.cnt)
    return nc_done


def fm(v):
    v = np.asarray(v, np.float32)
    n = v.shape[-1] // P
    r = v.reshape(v.shape[:-1] + (n, P))
    return np.ascontiguousarray(np.moveaxis(r, -1, 0))


def dft_consts():
    i = np.arange(64)
    ang = 2 * np.pi * np.outer(i, i) / 64.0
    C64, S64 = np.cos(ang), np.sin(ang)
    Z = np.zeros((64, 64))
    BDC = np.block([[C64, Z], [Z, C64]])
    BDS = np.block([[S64, Z], [Z, S64]])
    csc = np.concatenate([BDC, BDS], axis=1).astype(ml_dtypes.bfloat16)
    t = np.arange(T)
    pos = t % 256
    same = (t[:, None] // 256) == (t[None, :] // 256)
    angp = 2 * np.pi * np.outer(pos, pos) / 256.0
    nrm = 1.0 / math.sqrt(256 * 64)
    pc = np.where(same, np.cos(angp), 0.0) * nrm
    psn = np.where(same, -np.sin(angp), 0.0) * nrm
    pm_prompt = np.stack([pc, psn], axis=1).reshape(NT, P, 2, T).astype(ml_dtypes.bfloat16)
    r, c = t // 64, t % 64
    angs = 2 * np.pi * (np.outer(r, r) / 32.0 + np.outer(c, c) / 64.0)
    nrm = 1.0 / math.sqrt(32 * 64 * 64)
    pm_sample = np.stack([np.cos(angs) * nrm, -np.sin(angs) * nrm], axis=1).reshape(NT, P, 2, T).astype(ml_dtypes.bfloat16)
    return csc, pm_prompt, pm_sample


def mask_consts():
    i = np.arange(P)
    trile = (i[:, None] <= i[None, :]).astype(np.float32)
    trige = (i[:, None] >= i[None, :]).astype(np.float32)
    r = i // GC
    mF = (r[:, None] <= r[None, :]).astype(np.float32)
    mB = (r[:, None] >= r[None, :]).astype(np.float32)
    J = np.eye(P, dtype=np.float32)[::-1].copy()
    return np.ascontiguousarray(np.stack([trile, trige, mF, mB, J], axis=1))


def rep_dp(a):
    a = np.asarray(a, np.float32)
    return np.ascontiguousarray(a.transpose(1, 3, 0, 2).reshape(P, L, G))


def make_in_maps(inp):
    csc, pm_prompt, pm_sample = dft_consts()
    f32 = lambda a: np.ascontiguousarray(np.asarray(a, np.float32))
    lre, lim = rep_dp(inp["s5_lambda_re"]), rep_dp(inp["s5_lambda_im"])
    lst = np.ascontiguousarray(np.broadcast_to(np.asarray(inp["s5_log_step"], np.float32).transpose(1, 0, 2)[:, None],
                                               (2, 64, L, G)).reshape(P, L, G))
    s5p = np.ascontiguousarray(np.stack([lre, lim, lst], axis=2))

    def rep_b(a):
        a = np.asarray(a, np.float32).transpose(2, 0, 1, 3)
        return np.broadcast_to(a[None], (2, 64, L, G, GC)).reshape(P, L, G, GC)

    def rep_c(a):
        a = np.asarray(a, np.float32).transpose(3, 0, 1, 2)
        return np.broadcast_to(a[None], (2, 64, L, G, GC)).reshape(P, L, G, GC)
    s5bc = np.ascontiguousarray(np.stack([rep_b(inp["s5_b_re"]), rep_b(inp["s5_b_im"]), rep_c(inp["s5_c_re"]),
                                          rep_c(inp["s5_c_im"])], axis=2))
    dcol = np.asarray(inp["s5_d"], np.float32).transpose(2, 0, 1)
    s5dcol = np.ascontiguousarray(np.broadcast_to(dcol[None], (8, GC, L, G)).reshape(P, L, G))
    shared = {
        "w_ada": f32(inp["w_ada"]),
        "b_ada_fm": fm(inp["b_ada"]),
        "n1w": fm(inp["norm1_w"]), "n2w": fm(inp["norm2_w"]), "nfw": fm(inp["norm_f"]),
        "w_in": f32(inp["w_in"]), "w_fourier": f32(inp["w_fourier"]), "w_glu": f32(inp["w_glu"]),
        "w_out": f32(inp["w_out"]), "w_gate": f32(inp["w_gate"]), "w_up": f32(inp["w_up"]),
        "w_down": f32(inp["w_down"]),
        "identf": np.eye(P, dtype=np.float32), "csc": csc, "cmask": mask_consts(),
        "bg_bc": np.ascontiguousarray(np.broadcast_to(np.asarray(inp["b_gates"], np.float32)[None], (P, L, 16))),
        "mnw_bc": np.ascontiguousarray(np.broadcast_to(np.asarray(inp["mlstm_norm_w"], np.float32)[None], (P, L, 384))),
        "s5p": s5p, "s5bc": s5bc, "s5dcol": s5dcol,
    }
    maps = []
    xs = np.asarray(inp["x_sample"], np.float32)
    xp = np.asarray(inp["x_prompt"], np.float32)
    sC = np.asarray(inp["state_mlstm_C"], np.float32)
    sn = np.asarray(inp["state_mlstm_n"], np.float32)
    smm = np.asarray(inp["state_mlstm_m"], np.float32)
    sre = np.asarray(inp["state_s5_re"], np.float32)
    sim = np.asarray(inp["state_s5_im"], np.float32)
    for core in range(8):
        m = dict(shared)
        if core < 4:
            b = core
            m["x"] = np.ascontiguousarray(xs[b])
            m["cvec"] = fm(np.asarray(inp["c"])[b])
            m["posmat"] = pm_sample
            m["keepcol"] = np.ones((P, 1), np.float32)
            m["keeprow"] = np.ones((1, NT), np.float32)
            mC0 = np.concatenate([sC[b].transpose(3, 0, 1, 2, 4), sn[b].transpose(3, 0, 1, 2)[..., None]], axis=-1)
            m["mC0"] = np.ascontiguousarray(mC0)
            m["mM0"] = np.ascontiguousarray(smm[b].reshape(1, L, 8))
            st = np.stack([sre[b], sim[b]], axis=0)
            m["s5st0"] = np.ascontiguousarray(st.transpose(2, 4, 1, 0, 3).reshape(P, L, 2, G))
        else:
            j = core - 4
            m["x"] = np.ascontiguousarray(xp[8 * j:8 * j + 8].reshape(T, D))
            m["cvec"] = fm(inp["c_ctx"])
            m["posmat"] = pm_prompt
            m["keepcol"] = np.zeros((P, 1), np.float32)
            kr = np.zeros((1, NT), np.float32)
            kr[0, 1::2] = 1.0
            m["keeprow"] = kr
            m["mC0"] = np.zeros((DH, L, 2, H, 97), np.float32)
            m["mM0"] = np.zeros((1, L, 8), np.float32)
            m["s5st0"] = np.zeros((P, L, 2, G), np.float32)
        maps.append(m)
    return maps


_CACHE = {}
DEBUG = None
LAST = {}


def kernel(**inputs):
    import os
    maps = make_in_maps(inputs)
    if "nc" not in _CACHE:
        _CACHE["nc"] = build_program(debug=DEBUG)
    nc = _CACHE["nc"]
    dev_cores = os.environ.get("KDEV_CORES")
    if dev_cores:
        ids = [int(c) for c in dev_cores.split(",")]
        res = run_bass_kernel_spmd(nc, [maps[i] for i in ids], core_ids=list(range(len(ids))))
        outs = [None] * 8
        for j, i in enumerate(ids):
            outs[i] = res.results[j]
        for i in range(8):
            if outs[i] is None:
                outs[i] = outs[ids[0] if i < 4 else ids[-1]]
    else:
        res = run_bass_kernel_spmd(nc, maps, core_ids=list(range(8)))
        outs = res.results
    LAST["outs"] = outs
    f = lambda a: np.ascontiguousarray(np.asarray(a, dtype=np.float32))
    y_sample = f(np.stack([outs[c]["y"] for c in range(4)], axis=0))
    y_prompt = f(np.concatenate([np.asarray(outs[c]["y"]).reshape(8, 256, D) for c in range(4, 8)], axis=0))
    cat = lambda k: f(np.concatenate([np.asarray(outs[c][k]) for c in range(4, 8)], axis=0))
    return (y_prompt, y_sample, cat("newC"), cat("newn"), cat("newm"), cat("news5re"), cat("news5im"))
```

```python
import math
from contextlib import ExitStack

import numpy as np
import ml_dtypes

import concourse.bass as bass
import concourse.mybir as mybir
from concourse.ap import AP
from concourse.bass_utils import run_bass_kernel_spmd

F32 = mybir.dt.float32
BF16 = mybir.dt.bfloat16
AF = mybir.ActivationFunctionType
ALU = mybir.AluOpType
AX = mybir.AxisListType

P = 128
T = 2048
D = 1024
KC = 8
NT = 16
NB = 4
L = 2
DFF = 2816
NHC = DFF // 128
PIN = 2192
H = 4
DH = 96
G = 24
GC = 16
SP_ = 64
EPS = 1e-6
NSB = 256
DMA_SCRATCH = 4096
ARENA_BYTES = 116 * 1024

FLAGS = {"fourier": True, "mlstm": True, "s5": True, "ffn": True, "layers": 2}


class KB:
    def __init__(self):
        self.nc = bass.Bass("TRN2", target_bir_lowering=False, dynamic_dma_scratch_size=DMA_SCRATCH)
        self.es = ExitStack()
        nc = self.nc
        self.engs = {"pe": nc.tensor, "dve": nc.vector, "act": nc.scalar, "pool": nc.gpsimd, "sp": nc.sync}
        self.sem = {e: self.es.enter_context(nc.semaphore("s_" + e)) for e in self.engs}
        self.cnt = {e: 0 for e in self.engs}
        self.waited = {}
        self.ND = 32
        self.dsem = [self.es.enter_context(nc.semaphore("d%d" % i)) for i in range(self.ND)]
        self.dcnt = [0] * self.ND
        self.dnext = 0
        self.dnext_sw = 0
        self.res = {}
        self.ninstr = 0
        self.barrier_hooks = []
        self.bg = set()

    def sb(self, name, shape, dtype):
        return self.es.enter_context(self.nc.sbuf_tensor(name, list(shape), dtype))

    def ps(self, name, shape, dtype):
        return self.es.enter_context(self.nc.psum_tensor(name, list(shape), dtype))

    def dram(self, name, shape, dtype, kind):
        return self.nc.dram_tensor(name, list(shape), dtype, kind=kind).ap()

    def _wait(self, e, tok):
        if tok is None:
            return
        kind, src, val = tok
        key = (e, kind, src)
        if self.waited.get(key, 0) >= val:
            return
        self.waited[key] = val
        sem = self.sem[src] if kind == "e" else self.dsem[src]
        self.engs[e].wait_ge(sem, val)

    def _deps(self, e, reads, writes, pe_acc):
        for r in reads:
            st = self.res.get(r)
            if st is not None:
                self._wait(e, st["w"])
        for w in writes:
            st = self.res.get(w)
            if st is not None:
                if not (pe_acc and st["w"] is not None and st["w"][0] == "e" and st["w"][1] == "pe"):
                    self._wait(e, st["w"])
                for t in st["r"]:
                    if not (pe_acc and t[0] == "e" and t[1] == "pe"):
                        self._wait(e, t)

    def _update(self, tok, reads, writes):
        for r in reads:
            st = self.res.setdefault(r, {"w": None, "r": []})
            st["r"].append(tok)
            if len(st["r"]) > 48:
                latest = {}
                for t in st["r"]:
                    latest[(t[0], t[1])] = t
                st["r"] = list(latest.values())
        for w in writes:
            self.res[w] = {"w": tok, "r": []}

    def op(self, e, fn, reads=(), writes=(), pe_acc=False):
        if e != "pe":
            for r in reads:
                if isinstance(r, tuple) and r[0] == "ps":
                    st = self.res.get(("psx", r[1]))
                    if st is not None and st["w"] is not None and st["w"][1] != e:
                        self._wait(e, st["w"])
        self._deps(e, reads, writes, pe_acc)
        ins = fn(self.engs[e])
        ins.then_inc(self.sem[e], 1)
        self.cnt[e] += 1
        tok = ("e", e, self.cnt[e])
        self._update(tok, reads, writes)
        if e != "pe":
            for r in reads:
                if isinstance(r, tuple) and r[0] == "ps":
                    self.res[("psx", r[1])] = {"w": tok, "r": []}
        self.ninstr += 1
        return tok

    def dma(self, q, out, in_, reads=(), writes=(), bg=False, **kw):
        self._deps(q, reads, writes, False)
        half = self.ND // 2
        if q == "pool":
            i = half + self.dnext_sw
            self.dnext_sw = (self.dnext_sw + 1) % half
        else:
            i = self.dnext
            self.dnext = (self.dnext + 1) % half
        if self.dcnt[i] > 0:
            self._wait(q, ("d", i, self.dcnt[i]))
        self.bg.discard(i)
        if bg:
            self.bg.add(i)
        ins = self.engs[q].dma_start(out=out, in_=in_, **kw)
        ins.then_inc(self.dsem[i], 16)
        self.dcnt[i] += 16
        tok = ("d", i, self.dcnt[i])
        self._update(tok, reads, writes)
        self.ninstr += 1
        return tok

    def barrier(self):
        for e in self.engs:
            for e2 in self.engs:
                if self.cnt[e2] > 0:
                    self._wait(e, ("e", e2, self.cnt[e2]))
            for i in range(self.ND):
                if self.dcnt[i] > 0 and i not in self.bg:
                    self._wait(e, ("d", i, self.dcnt[i]))
        self.res = {k: v for k, v in self.res.items() if isinstance(k, tuple) and k[0] == "WA"}
        for h in self.barrier_hooks:
            h()

    def finish(self):
        self.bg = set()
        self.barrier()
        self.es.close()
        return self.nc


class Arena:
    def __init__(self, kb, nbytes):
        self.n = nbytes // 2
        self.t = kb.sb("arena", [P, self.n], BF16)
        self.free_list = [(0, self.n)]
        self.pending = []
        self.live = {}
        self.used = 0
        self.peak = 0
        kb.barrier_hooks.append(self.commit)

    def alloc(self, free_shape, dtype, top=False):
        nel = 1
        for s in free_shape:
            nel *= s
        units = nel * (2 if dtype == F32 else 1)
        units = (units + 31) // 32 * 32
        order = range(len(self.free_list) - 1, -1, -1) if top else range(len(self.free_list))
        for idx in order:
            off, size = self.free_list[idx]
            if size >= units:
                if size == units:
                    self.free_list.pop(idx)
                elif top:
                    self.free_list[idx] = (off, size - units)
                    off = off + size - units
                else:
                    self.free_list[idx] = (off + units, size - units)
                break
        else:
            raise RuntimeError("arena overflow: need %d units, free=%s" % (units, self.free_list))
        self.used += units
        self.peak = max(self.peak, self.used)
        v = self.t[:, off:off + units]
        if dtype == F32:
            v = v.bitcast(F32)[:, 0:nel]
        else:
            v = v[:, 0:nel]
        if len(free_shape) > 1:
            names = " ".join("a%d" % i for i in range(len(free_shape)))
            kw = {"a%d" % i: free_shape[i] for i in range(len(free_shape))}
            v = v.rearrange("p (%s) -> p %s" % (names, names), **kw)
        self.live[id(v)] = (v, off, units)
        return v

    def free(self, *aps):
        for ap in aps:
            v, off, units = self.live.pop(id(ap))
            self.pending.append((off, units))

    def commit(self):
        for off, units in self.pending:
            self.used -= units
            self.free_list.append((off, units))
        self.pending = []
        self.free_list.sort()
        merged = []
        for off, size in self.free_list:
            if merged and merged[-1][0] + merged[-1][1] == off:
                merged[-1] = (merged[-1][0], merged[-1][1] + size)
            else:
                merged.append((off, size))
        self.free_list = merged


def bcast(ap, shape):
    return ap.to_broadcast(list(shape))


def rev_axis(ap, axis):
    a = [list(x) for x in ap.ap]
    st, n = a[axis]
    a[axis] = [-st, n]
    return AP(ap.tensor, ap.offset + st * (n - 1), a)


def swap_ri(ap):
    a = [list(x) for x in ap.ap]
    assert len(a) == 3 and a[1][1] == 2
    st = a[1][0]
    return AP(ap.tensor, ap.offset + st, [a[0], [-st, 2], a[2]])

def build_program(flags=FLAGS, debug=None):
    kb = KB()
    nc = kb.nc
    NL = flags["layers"]

    def din(name, shape, dt=F32):
        return kb.dram(name, shape, dt, "ExternalInput")

    def dout(name, shape, dt=F32):
        return kb.dram(name, shape, dt, "ExternalOutput")

    x_d = din("x", [T, D])
    cvec_d = din("cvec", [P, KC])
    w_ada_d = din("w_ada", [L, D, 6 * D])
    b_ada_d = din("b_ada_fm", [P, L, 48])
    n1w_d = din("n1w", [P, L, KC])
    n2w_d = din("n2w", [P, L, KC])
    nfw_d = din("nfw", [P, KC])
    w_in_d = din("w_in", [L, D, PIN])
    w_fo_d = din("w_fourier", [L, 256, 256])
    w_glu_d = din("w_glu", [L, 384, 768])
    w_out_d = din("w_out", [L, D, D])
    w_gate_d = din("w_gate", [L, D, DFF])
    w_up_d = din("w_up", [L, D, DFF])
    w_down_d = din("w_down", [L, DFF, D])
    identf_d = din("identf", [P, P])
    csc_d = din("csc", [P, 256], BF16)
    posmat_d = din("posmat", [NT, P, 2, T], BF16)
    cmask_d = din("cmask", [P, 5, P])
    keepcol_d = din("keepcol", [P, 1])
    keeprow_d = din("keeprow", [1, NT])
    bg_d = din("bg_bc", [P, L, 16])
    mnw_d = din("mnw_bc", [P, L, 384])
    mC0_d = din("mC0", [DH, L, 2, H, 97])
    mM0_d = din("mM0", [1, L, 8])
    s5p_d = din("s5p", [P, L, 3, G])
    s5bc_d = din("s5bc", [P, L, 4, G, GC])
    s5dcol_d = din("s5dcol", [P, L, G])
    s5st0_d = din("s5st0", [P, L, 2, G])
    y_d = dout("y", [T, D])
    newC_d = dout("newC", [8, L, 2, H, DH, DH])
    newn_d = dout("newn", [8, L, 2, H, DH])
    newm_d = dout("newm", [8, L, 2, H])
    news5_d = [dout("news5re", [8, L, 2, G, SP_]), dout("news5im", [8, L, 2, G, SP_])]

    dbg_d = {}
    if debug:
        for name, (shape, dt) in debug.items():
            dbg_d[name] = dout("dbg_" + name, shape, dt)

    X = kb.sb("X", [P, KC, T], F32)
    identf = kb.sb("identf_sb", [P, P], F32)
    identb = kb.sb("identb", [P, P], BF16)
    onesb = kb.sb("onesb", [P, P], BF16)
    onesf = kb.sb("onesf", [P, P], F32)
    csc = kb.sb("csc_sb", [P, 256], BF16)
    cmask = kb.sb("cmask_sb", [P, 5, P], F32)
    maskb = kb.sb("maskb", [P, 2, P], BF16)
    Jb = kb.sb("Jb", [P, P], BF16)
    keepcol = kb.sb("keepcol_sb", [P, 1], F32)
    keeprow = kb.sb("keeprow_sb", [1, NT], F32)
    cvec = kb.sb("cvec_sb", [P, KC], F32)
    csil = kb.sb("csil", [P, KC], BF16)
    b_ada = kb.sb("b_ada_sb", [P, L, 48], F32)
    n1w = kb.sb("n1w_sb", [P, L, KC], F32)
    n2w = kb.sb("n2w_sb", [P, L, KC], F32)
    nfw = kb.sb("nfw_sb", [P, KC], F32)
    mods = kb.sb("mods", [P, L, 6, KC], F32)
    wn = kb.sb("wn", [P, L, 2, KC], F32)
    banks = [kb.ps("bank%d" % i, [P, 512], F32) for i in range(8)]
    ar = Arena(kb, ARENA_BYTES)
    trile, trige, s5mF, s5mB = cmask[:, 0, :], cmask[:, 1, :], cmask[:, 2, :], cmask[:, 3, :]

    bstate = {"i": 0}

    def nextbank():
        b = bstate["i"]
        bstate["i"] = (b + 1) % 8
        return b

    evq = {"i": 0}

    def evac_eng():
        evq["i"] ^= 1
        return "act" if evq["i"] else "dve"

    def mm(out, lhsT, rhs, start, stop, reads, writes):
        return kb.op("pe", lambda e: e.matmul(out, lhsT, rhs, start=start, stop=stop), reads=reads, writes=writes,
                     pe_acc=True)

    def tr(out, in_, ident, reads, writes):
        return kb.op("pe", lambda e: e.transpose(out, in_, ident), reads=reads, writes=writes, pe_acc=True)

    import os as _os
    PSUB = _os.environ.get("KPOOL", "pool")

    def copy_to(eng, out, in_, reads, writes):
        if eng == "pool":
            eng = PSUB
        if eng == "act":
            return kb.op("act", lambda e: e.activation(out=out, in_=in_, func=AF.Copy), reads=reads, writes=writes)
        return kb.op(eng, lambda e: e.tensor_copy(out=out, in_=in_), reads=reads, writes=writes)

    def tt(eng, out, in0, in1, op, reads, writes):
        if eng == "pool":
            eng = PSUB
        return kb.op(eng, lambda e: e.tensor_tensor(out=out, in0=in0, in1=in1, op=op), reads=reads, writes=writes)

    def ts(eng, out, in0, s1, s2, op0, op1, reads, writes):
        if eng == "pool":
            eng = PSUB
        if op1 is None:
            return kb.op(eng, lambda e: e.tensor_scalar(out=out, in0=in0, scalar1=s1, scalar2=None, op0=op0),
                         reads=reads, writes=writes)
        return kb.op(eng, lambda e: e.tensor_scalar(out=out, in0=in0, scalar1=s1, scalar2=s2, op0=op0, op1=op1),
                     reads=reads, writes=writes)

    def stt(out, in0, scalar, in1, op0, op1, reads, writes):
        return kb.op("dve", lambda e: e.scalar_tensor_tensor(out=out, in0=in0, scalar=scalar, in1=in1, op0=op0,
                                                             op1=op1), reads=reads, writes=writes)

    def act(out, in_, func, reads, writes, scale=1.0, bias=None):
        if bias is None:
            return kb.op("act", lambda e: e.activation(out=out, in_=in_, func=func, scale=scale), reads=reads,
                         writes=writes)
        return kb.op("act", lambda e: e.activation(out=out, in_=in_, func=func, scale=scale, bias=bias),
                     reads=reads, writes=writes)

    def dbg_out(name, ap_sb, keyreads=()):
        if debug and name in dbg_d:
            kb.barrier()
            kb.dma("sp", dbg_d[name], ap_sb, reads=list(keyreads), writes=[("dbg", name)])

    kb.dma("sp", identf[:], identf_d, writes=["identf"])
    kb.dma("sp", csc[:], csc_d, writes=["csc"])
    kb.dma("sp", cvec[:], cvec_d, writes=["cvec"])
    kb.dma("sp", b_ada[:], b_ada_d, writes=["b_ada"])
    kb.dma("sp", n1w[:], n1w_d, writes=["n1w"])
    kb.dma("sp", n2w[:], n2w_d, writes=["n2w"])
    kb.dma("sp", nfw[:], nfw_d, writes=["nfw"])
    kb.dma("sp", cmask[:], cmask_d, writes=["cmask"])
    kb.dma("sp", keepcol[:], keepcol_d, writes=["keepcol"])
    kb.dma("sp", keeprow[:], keeprow_d, writes=["keeprow"])
    kb.op("dve", lambda e: e.memset(onesb[:], 1.0), writes=["onesb"])
    kb.op("dve", lambda e: e.memset(onesf[:], 1.0), writes=["onesf"])
    copy_to("dve", identb[:], identf[:], ["identf"], ["identb"])
    copy_to("dve", maskb[:], cmask[:, 0:2, :], ["cmask"], ["maskb"])
    copy_to("dve", Jb[:], cmask[:, 4, :], ["cmask"], ["Jb"])
    act(csil[:], cvec[:], AF.Silu, ["cvec"], ["csil"])

    def ada_dma(l, i, WA, bg=False):
        src = w_ada_d[l, :, i * D:(i + 1) * D].rearrange("(k p) n -> p k n", p=P)
        kb.dma("pool", WA, src, writes=[("WA", id(WA))], bg=bg)

    def ada_compute(l, i, WA):
        b = nextbank()
        for m in range(KC):
            for k in range(KC):
                mm(banks[b][:, m:m + 1], WA[:, k, m * P:(m + 1) * P], csil[:, k:k + 1], k == 0, k == KC - 1,
                   reads=[("WA", id(WA)), "csil"], writes=[("ps", b)])
        tt("dve", mods[:, l, i, :], banks[b][:, 0:KC], b_ada[:, l, i * KC:(i + 1) * KC], ALU.add,
           [("ps", b), "b_ada"], [("mods", l, i)])
        if i in (1, 4):
            j = 0 if i == 1 else 1
            nw = n1w if i == 1 else n2w
            stt(wn[:, l, j, :], mods[:, l, i, :], 1.0, nw[:, l, :], ALU.add, ALU.mult,
                [("mods", l, i), "n1w", "n2w"], [("wn", l, j)])

    def rms_norm(l, j, XN):
        xsq = ar.alloc((KC, 512), BF16)
        rstd = ar.alloc((512,), F32)
        tmp = [ar.alloc((512,), F32) for _ in range(2)]
        for nb in range(NB):
            sl = slice(nb * 512, (nb + 1) * 512)
            for k in range(KC):
                act(xsq[:, k, :], X[:, k, sl], AF.Square, [("X", nb)], [("xsq", k)])
            b = nextbank()
            for k in range(KC):
                mm(banks[b][:], onesb[:], xsq[:, k, :], k == 0, k == KC - 1, reads=["onesb", ("xsq", k)],
                   writes=[("ps", b)])
            act(rstd, banks[b][:], AF.Sqrt, [("ps", b)], ["rstd"], scale=1.0 / D, bias=EPS)
            kb.op("dve", lambda e: e.reciprocal(out=rstd, in_=rstd), reads=["rstd"], writes=["rstd"])
            for k in range(KC):
                if l is None:
                    stt(X[:, k, sl], X[:, k, sl], nfw[:, k:k + 1], rstd, ALU.mult, ALU.mult,
                        [("X", nb), "nfw", "rstd"], [("X", nb)])
                else:
                    tq = tmp[k % 2]
                    stt(tq, X[:, k, sl], wn[:, l, j, k:k + 1], rstd, ALU.mult, ALU.mult,
                        [("X", nb), ("wn", l, j), "rstd"], [("ntmp", k % 2)])
                    act(XN[:, k, sl], tq, AF.Identity, [("ntmp", k % 2), ("mods", l, 3 * j)], [("XN", k, nb)],
                        bias=mods[:, l, 3 * j, k:k + 1])
        ar.free(xsq, rstd, *tmp)
        kb.barrier()

    def load_w(dst, src, key, q="pool"):
        return kb.dma(q, dst, src, writes=[key])

    def proj_fm(W, wkey, nk, actf, akey, mchunks, evac):
        for nb in range(NB):
            for mi in mchunks:
                b = nextbank()
                for k in range(nk):
                    mm(banks[b][:], W[:, k, mi * P:(mi + 1) * P], actf(k, nb), k == 0, k == nk - 1,
                       reads=[wkey, akey(k, nb)], writes=[("ps", b)])
                evac(b, mi, nb)

    def resid_evac(l, gi):
        def f(b, mi, nb):
            sl = slice(nb * 512, (nb + 1) * 512)
            stt(X[:, mi, sl], banks[b][:], mods[:, l, gi, mi:mi + 1], X[:, mi, sl], ALU.mult, ALU.add,
                [("ps", b), ("mods", l, gi), ("X", nb)], [("X", nb)])
        return f

    def fourier_inproj(l, XN):
        Wf = ar.alloc((KC, 256), BF16)
        load_w(Wf, w_in_d[l, :, 0:256].rearrange("(k p) n -> p k n", p=P), "Wf")
        zfT = ar.alloc((2, T), BF16)

        def ev_zf(b, mi, nb):
            copy_to(evac_eng(), zfT[:, mi, nb * 512:(nb + 1) * 512], banks[b][:], [("ps", b)], [("zfT", nb)])
        proj_fm(Wf, "Wf", KC, lambda k, nb: XN[:, k, nb * 512:(nb + 1) * 512], lambda k, nb: ("XN", k, nb), range(2),
                ev_zf)
        ar.free(Wf)
        return zfT

    def fourier_core(l, zfT):
        foT = ar.alloc((2, T), BF16, top=True)
        Wfo = ar.alloc((2, 256), BF16)
        load_w(Wfo, w_fo_d[l].rearrange("(k p) n -> p k n", p=P), "Wfo")
        ZCS = ar.alloc((NT, 512), BF16)
        NR = 2
        PM = [ar.alloc((2, T // 2), BF16) for _ in range(NR)]
        for t in range(NT):
            b = nextbank()
            for kc in range(2):
                mm(banks[b][:, kc * 256:(kc + 1) * 256], zfT[:, kc, t * P:(t + 1) * P], csc[:], True, True,
                   reads=[("zfT", t // 4), "csc"], writes=[("ps", b)])
            copy_to(evac_eng(), ZCS[:, t, :], banks[b][:], [("ps", b)], [("ZCS", t)])
        kb.barrier()
        yT = zfT
        it = 0
        for hp in range(2):
            for ti in range(NT):
                pm = PM[it % NR]
                kb.dma("sp", pm, posmat_d[ti][:, :, hp * 1024:(hp + 1) * 1024], writes=[("PM", it % NR)])
                for cs in range(2):
                    for fc in range(2):
                        for nbl in range(2):
                            b = hp * 4 + fc * 2 + nbl
                            mm(banks[b][:], ZCS[:, ti, fc * 256 + cs * P: fc * 256 + (cs + 1) * P],
                               pm[:, cs, nbl * 512:(nbl + 1) * 512], ti == 0 and cs == 0, ti == NT - 1 and cs == 1,
                               reads=[("ZCS", ti), ("PM", it % NR)], writes=[("ps", b)])
                it += 1
            for fc in range(2):
                for nbl in range(2):
                    b = hp * 4 + fc * 2 + nbl
                    nb = hp * 2 + nbl
                    copy_to(evac_eng(), yT[:, fc, nb * 512:(nb + 1) * 512], banks[b][:], [("ps", b)], [("zfT", nb)])
        bstate["i"] = 0

        def ev_fo(b, mi, nb):
            copy_to(evac_eng(), foT[:, mi, nb * 512:(nb + 1) * 512], banks[b][:], [("ps", b)], [("foT", nb)])
        proj_fm(Wfo, "Wfo", 2, lambda k, nb: yT[:, k, nb * 512:(nb + 1) * 512], lambda k, nb: ("zfT", nb), range(2),
                ev_fo)
        ar.free(Wfo, ZCS, *PM, zfT)
        kb.barrier()
        return foT

    def mlstm_inproj(l, XN):
        Qt = ar.alloc((NT, 384), BF16)
        Kt = ar.alloc((NT, 384), BF16)
        Vt = ar.alloc((NT, 384), BF16)
        ZO = ar.alloc((NT, 384), BF16)
        GT = ar.alloc((NT, 16), F32)
        bg = ar.alloc((16,), F32)
        mnw = ar.alloc((384,), F32)
        kb.dma("sp", bg, bg_d[:, l, :], writes=["bg"])
        kb.dma("sp", mnw, mnw_d[:, l, :], writes=["mnw"])
        groups = [(256, 640, "q"), (640, 1024, "k"), (1024, 1408, "v"), (1408, 1808, "go")]
        Wp = [ar.alloc((KC, 400), BF16) for _ in range(2)]
        import os
        groups = groups[:int(os.environ.get("KIPG", "4"))]
        for gi, (c0, c1, kind) in enumerate(groups):
            W = Wp[gi % 2]
            n = c1 - c0
            load_w(W[:, :, 0:n], w_in_d[l, :, c0:c1].rearrange("(k p) n -> p k n", p=P), ("Wp", gi % 2))
            for t in range(NT):
                b = nextbank()
                for k in range(KC):
                    mm(banks[b][:, 0:n], XN[:, k, t * P:(t + 1) * P], W[:, k, 0:n], k == 0, k == KC - 1,
                       reads=[("Wp", gi % 2), ("XN", k, t // 4)], writes=[("ps", b)])
                if kind == "q":
                    copy_to(evac_eng(), Qt[:, t, :], banks[b][:, 0:384], [("ps", b)], [("Qt", t)])
                elif kind == "k":
                    act(Kt[:, t, :], banks[b][:, 0:384], AF.Copy, [("ps", b)], [("Kt", t)], scale=float(DH) ** -0.5)
                elif kind == "v":
                    copy_to(evac_eng(), Vt[:, t, :], banks[b][:, 0:384], [("ps", b)], [("Vt", t)])
                else:
                    KGO = int(os.environ.get("KGO", "7"))
                    if KGO & 1:
                        tt("dve", GT[:, t, :], banks[b][:, 0:16], bg, ALU.add, [("ps", b), "bg"], [("GT", t)])
                    if KGO & 2:
                        act(ZO[:, t, :], banks[b][:, 16:400], AF.Sigmoid, [("ps", b)], [("ZO", t)])
                    if KGO & 4:
                        tt("pool", ZO[:, t, :], ZO[:, t, :], mnw, ALU.mult, [("ZO", t), "mnw"], [("ZO", t)])
        ar.free(*Wp, bg, mnw)
        kb.barrier()
        return dict(Qt=Qt, Kt=Kt, Vt=Vt, ZO=ZO, GT=GT)

    def mlstm_core(l, mb):
        Qt, Kt, Vt, ZO, GT = mb["Qt"], mb["Kt"], mb["Vt"], mb["ZO"], mb["GT"]
        import os
        STAGE = int(os.environ.get("KSTAGE", "9"))

        def bail(bufs):
            ar.free(*bufs)
            kb.barrier()
            moT_ = ar.alloc((3, T), BF16)
            kb.op("dve", lambda e: e.memset(moT_, 0.0), writes=[("moT", i) for i in range(NB)])
            kb.barrier()
            return moT_
        if STAGE < 1:
            return bail([Qt, Kt, Vt, ZO, GT])
        allT = [("GT", t) for t in range(NT)]
        SPl = ar.alloc((2, NT, H), F32)
        IG = ar.alloc((2, NT, H), F32)
        A = ar.alloc((128,), F32)
        NBc = ar.alloc((128,), F32)
        Wt = ar.alloc((128,), F32)
        FL = ar.alloc((128,), F32)
        MMbc = ar.alloc((128,), F32)
        INbc = ar.alloc((128,), F32)
        COLS = ar.alloc((2,), F32)
        ROW = ar.alloc((256,), F32)
        MP = ar.alloc((128,), F32)
        MMr = ar.alloc((128,), F32)
        MN = ar.alloc((128,), F32)
        INr = ar.alloc((128,), F32)
        MI = ar.alloc((8,), F32)
        kb.dma("sp", MI[0:1, :], mM0_d[:, l, :], writes=["MI"])
        GTv = GT
        for d in range(2):
            act(SPl[:, d, :, :], GTv[:, :, d * 8 + 4:d * 8 + 8], AF.Exp, allT, [("SPl", d)], scale=-1.0)
            act(SPl[:, d, :, :], SPl[:, d, :, :], AF.Ln, [("SPl", d)], [("SPl", d)], bias=1.0)
            copy_to("dve", IG[:, d, :, :], GTv[:, :, d * 8:d * 8 + 4], allT, [("IG", d)])
        SPf = SPl.rearrange("p d t h -> p (d t h)")
        IGf = IG.rearrange("p d t h -> p (d t h)")
        b = nextbank()
        mm(banks[b][:, 0:64], trile, SPf[:, 0:64], True, True, ["cmask", ("SPl", 0)], [("ps", b)])
        mm(banks[b][:, 64:128], trige, SPf[:, 64:128], True, True, ["cmask", ("SPl", 1)], [("ps", b)])
        copy_to("dve", NBc, banks[b][:, 0:128], [("ps", b)], ["NBc"])
        tt("dve", A, IGf, NBc, ALU.add, [("IG", 0), ("IG", 1), "NBc"], ["A"])
        b = nextbank()
        tr(banks[b][:, 0:128], A, identf[:], ["A", "identf"], [("ps", b)])
        kb.op("dve", lambda e: e.tensor_reduce(out=COLS[:, 0:1], in_=banks[b][:, 0:128], axis=AX.X, op=ALU.max),
              reads=[("ps", b)], writes=[("COLS", 0)])
        b2 = nextbank()
        mm(banks[b2][:, 0:1], SPf, onesf[:, 0:1], True, True, [("SPl", 0), ("SPl", 1), "onesf"], [("ps", b2)])
        ts("dve", COLS[:, 1:2], banks[b2][:, 0:1], -1.0, None, ALU.mult, None, [("ps", b2)], [("COLS", 1)])
        b = nextbank()
        tr(banks[b][0:1, 0:128], COLS[:, 0:1], identf[:], [("COLS", 0), "identf"], [("ps", b)])
        tr(banks[b][0:1, 128:256], COLS[:, 1:2], identf[:], [("COLS", 1), "identf"], [("ps", b)])
        copy_to("dve", ROW[0:1, :], banks[b][0:1, 0:256], [("ps", b)], ["ROW"])
        for d in range(2):
            for n in range(NT):
                t = n if d == 0 else NT - 1 - n
                c = (d * NT + t) * H
                if n == 0:
                    copy_to("dve", MP[0:1, c:c + H], MI[0:1, d * H:(d + 1) * H], ["MI"], [("MP", d)])
                tt("dve", MMr[0:1, c:c + H], MP[0:1, c:c + H], ROW[0:1, c:c + H], ALU.max, [("MP", d), "ROW"],
                   [("MMr", d)])
                tt("dve", MN[0:1, c:c + H], MMr[0:1, c:c + H], ROW[0:1, 128 + c:128 + c + H], ALU.add,
                   [("MMr", d), "ROW"], [("MN", d)])
                if n + 1 < NT:
                    t2 = n + 1 if d == 0 else NT - 2 - n
                    c2 = (d * NT + t2) * H
                    ts("dve", MP[0:1, c2:c2 + H], MN[0:1, c:c + H], keeprow[0:1, n + 1:n + 2], None, ALU.mult, None,
                       [("MN", d), "keeprow"], [("MP", d)])
        chain = [("MP", 0), ("MP", 1), ("MMr", 0), ("MMr", 1)]
        tt("dve", INr[0:1, :], MP[0:1, :], MMr[0:1, :], ALU.subtract, chain, ["INr"])
        act(INr[0:1, :], INr[0:1, :], AF.Exp, ["INr"], ["INr"])
        b = nextbank()
        mm(banks[b][:, 0:128], onesf[0:1, :], MMr[0:1, :], True, True, ["onesf"] + chain, [("ps", b)])
        mm(banks[b][:, 128:256], onesf[0:1, :], INr[0:1, :], True, True, ["onesf", "INr"], [("ps", b)])
        copy_to("dve", MMbc, banks[b][:, 0:128], [("ps", b)], ["MMbc"])
        copy_to("act", INbc, banks[b][:, 128:256], [("ps", b)], ["INbc"])
        tt("dve", Wt, A, MMbc, ALU.subtract, ["A", "MMbc"], ["Wt"])
        act(Wt, Wt, AF.Exp, ["Wt"], ["Wt"])
        tt("dve", FL, NBc, MMbc, ALU.subtract, ["NBc", "MMbc"], ["FL"])
        act(FL, FL, AF.Exp, ["FL"], ["FL"])
        import os
        LVL = int(os.environ.get("KLVL", "9"))
        for d in range(2):
            if LVL < 3:
                break
            src = MN[0:1, d * 64:(d + 1) * 64].rearrange("p (s two h) -> p s two h", two=2, h=H)[:, :, 1 - d, :]
            if d == 0:
                dst = newm_d[:, l, d, :].rearrange("(o s) h -> o s h", o=1)
                kb.dma("sp", dst, src, reads=[("MN", d)], writes=[("newm", l, d)])
            else:
                dst = newm_d[:, l, d, :].rearrange("(o s) h -> o s h", o=1)
                kb.dma("sp", dst, src, reads=[("MN", d)], writes=[("newm", l, d)])

        if STAGE < 2:
            return bail([Qt, Kt, Vt, ZO, GT, SPl, IG, A, NBc, Wt, FL, MMbc, INbc, COLS, ROW, MP, MMr, MN, INr, MI])
        HS = ar.alloc((NT, 384), BF16)
        CST = ar.alloc((2, H, 97), F32)
        kb.dma("sp", CST[0:DH], mC0_d[:, l], writes=[("CST", 0), ("CST", 1)])
        CSb = [ar.alloc((H, 97), BF16) for _ in range(2)]
        QT = [ar.alloc((H, 128), BF16) for _ in range(2)]
        KT = [ar.alloc((H, 128), BF16) for _ in range(2)]
        SM = [ar.alloc((H, 128), BF16) for _ in range(2)]
        VE = [ar.alloc((H, 97), BF16) for _ in range(2)]
        dd = [ar.alloc((H,), F32) for _ in range(2)]
        hp = [ar.alloc((H, DH), F32) for _ in range(2)]
        CAP = [ar.alloc((H, 97), F32) for _ in range(2)]
        prod = [ar.alloc((H, 97), F32) for _ in range(2)]
        bcs = {}

        def stage_a(n, d):
                t = n if d == 0 else NT - 1 - n
                c = (d * NT + t) * H
                bq = nextbank()
                bqv = banks[bq][:].bitcast(BF16)
                for h in range(H):
                    tr(bqv[0:DH, h * P:(h + 1) * P], Qt[:, t, h * DH:(h + 1) * DH], identb[:], [("Qt", t), "identb"],
                       [("ps", bq)])
                    tr(bqv[0:DH, 512 + h * P:512 + (h + 1) * P], Kt[:, t, h * DH:(h + 1) * DH], identb[:],
                       [("Kt", t), "identb"], [("ps", bq)])
                copy_to("act", QT[d][0:DH].rearrange("p h n -> p (h n)"), bqv[0:DH, 0:512], [("ps", bq)], [("QT", d)])
                copy_to("dve", KT[d][0:DH].rearrange("p h n -> p (h n)"), bqv[0:DH, 512:1024], [("ps", bq)],
                        [("KT", d)])
                bs = nextbank()
                for h in range(H):
                    mm(banks[bs][:, h * P:(h + 1) * P], KT[d][0:DH, h, :], QT[d][0:DH, h, :], True, True,
                       [("KT", d), ("QT", d)], [("ps", bs)])
                tt("dve", SM[d], banks[bs][:].rearrange("p (h n) -> p h n", h=H),
                   bcast(maskb[:, d:d + 1, :], [P, H, P]), ALU.mult, [("ps", bs), "maskb"], [("SM", d)])
                tt("pool", VE[d][:, :, 0:DH], Vt[:, t, :].rearrange("p (h e) -> p h e", h=H),
                   bcast(Wt[:, c:c + H].unsqueeze(2), [P, H, DH]), ALU.mult, [("Vt", t), "Wt"], [("VE", d)])
                copy_to("pool", VE[d][:, :, DH:DH + 1], Wt[:, c:c + H].unsqueeze(2), ["Wt", ("VE", d)], [("VE", d)])
                bc = nextbank()
                bcs[d] = bc
                for h in range(H):
                    mm(banks[bc][0:DH, h * 97:(h + 1) * 97], Kt[:, t, h * DH:(h + 1) * DH], VE[d][:, h, :], True, True,
                       [("Kt", t), ("VE", d)], [("ps", bc)])

        def stage_b(n, d):
                t = n if d == 0 else NT - 1 - n
                c = (d * NT + t) * H
                bc = bcs[d]
                tt("dve", prod[d][0:DH], CST[0:DH, d], bcast(INbc[0:DH, c:c + H].unsqueeze(2), [DH, H, 97]), ALU.mult,
                   [("CST", d), "INbc"], [("prod", d)])
                copy_to("act", CSb[d][0:DH], prod[d][0:DH], [("prod", d)], [("CSb", d)])
                tt("dve", CST[0:DH, d].rearrange("p h e -> p (h e)"), prod[d][0:DH].rearrange("p h e -> p (h e)"),
                   banks[bc][0:DH, 0:H * 97], ALU.add, [("prod", d), ("ps", bc)], [("CST", d)])
                if n % 2 == 1:
                    kcap = n // 2
                    seq = kcap if d == 0 else 7 - kcap
                    copy_to("act", CAP[d][0:DH], CST[0:DH, d], [("CST", d)], [("CAP", d)])
                    if LVL >= 1:
                        kb.dma("sp", newC_d[seq, l, d].rearrange("h k e -> k h e"), CAP[d][0:DH, :, 0:DH],
                               reads=[("CAP", d)], writes=[("newC", seq, l, d)])
                    if LVL >= 2:
                        with nc.allow_non_contiguous_dma(reason="small state vector"):
                            kb.dma("sp", newn_d[seq, l, d].rearrange("h k -> k h"), CAP[d][0:DH, :, DH],
                                   reads=[("CAP", d)], writes=[("newn", seq, l, d)])
                    if n + 1 < NT:
                        ts("dve", CST[0:DH, d], CST[0:DH, d], keepcol[0:DH, 0:1], None, ALU.mult, None,
                           [("CST", d), "keepcol"], [("CST", d)])
                bn = nextbank()
                for h in range(H):
                    mm(banks[bn][:, h * 97:(h + 1) * 97], SM[d][:, h, :], VE[d][:, h, :], True, False,
                       [("SM", d), ("VE", d)], [("ps", bn)])
                    mm(banks[bn][:, h * 97:(h + 1) * 97], QT[d][0:DH, h, :], CSb[d][0:DH, h, :], False, True,
                       [("QT", d), ("CSb", d)], [("ps", bn)])
                ndv = banks[bn][:, 0:H * 97].rearrange("p (h e) -> p h e", h=H)
                act(dd[d], ndv[:, :, DH], AF.Abs, [("ps", bn)], [("dd", d)])
                tt("dve", dd[d], dd[d], FL[:, c:c + H], ALU.max, [("dd", d), "FL"], [("dd", d)])
                kb.op("dve", lambda e: e.reciprocal(out=dd[d], in_=dd[d]), reads=[("dd", d)], writes=[("dd", d)])
                if n < NT // 2:
                    tt("dve", HS[:, t, :].rearrange("p (h e) -> p h e", h=H), ndv[:, :, 0:DH],
                       bcast(dd[d].unsqueeze(2), [P, H, DH]), ALU.mult, [("ps", bn), ("dd", d)], [("HS", t)])
                else:
                    tt("dve", hp[d], ndv[:, :, 0:DH], bcast(dd[d].unsqueeze(2), [P, H, DH]), ALU.mult,
                       [("ps", bn), ("dd", d)], [("hp", d)])
                    tt("pool", HS[:, t, :].rearrange("p (h e) -> p h e", h=H),
                       HS[:, t, :].rearrange("p (h e) -> p h e", h=H), hp[d], ALU.add, [("hp", d), ("HS", t)],
                       [("HS", t)])

        its = [(n, d) for n in range(NT) for d in range(2)]
        stage_a(*its[0])
        for i_ in range(len(its)):
            if i_ + 1 < len(its):
                stage_a(*its[i_ + 1])
            stage_b(*its[i_])
        ar.free(Qt, Kt, Vt, SPl, IG, A, NBc, Wt, FL, MMbc, INbc, COLS, ROW, MP, MMr, MN, INr, MI, CST, *CSb, *QT, *KT,
                *SM, *VE, *dd, *hp, *CAP, *prod)
        kb.barrier()
        if STAGE < 3:
            return bail([HS, ZO, GT])
        moT = ar.alloc((3, T), BF16)
        NQ = 4
        sq = [ar.alloc((384,), F32) for _ in range(NQ)]
        ss = [ar.alloc((H,), F32) for _ in range(NQ)]
        mo = [ar.alloc((384,), BF16) for _ in range(NQ)]
        for t in range(NT):
            q = t % NQ
            hs = HS[:, t, :]
            tt("pool", sq[q], hs, hs, ALU.mult, [("HS", t)], [("sq", q)])
            kb.op("dve", lambda e: e.tensor_reduce(out=ss[q], in_=sq[q].rearrange("p (h e) -> p h e", h=H), axis=AX.X,
                                                   op=ALU.add), reads=[("sq", q)], writes=[("ss", q)])
            act(ss[q], ss[q], AF.Sqrt, [("ss", q)], [("ss", q)], scale=1.0 / DH, bias=EPS)
            kb.op("dve", lambda e: e.reciprocal(out=ss[q], in_=ss[q]), reads=[("ss", q)], writes=[("ss", q)])
            tt("dve", sq[q].rearrange("p (h e) -> p h e", h=H), hs.rearrange("p (h e) -> p h e", h=H),
               bcast(ss[q].unsqueeze(2), [P, H, DH]), ALU.mult, [("HS", t), ("ss", q), ("sq", q)], [("sq", q)])
            tt("dve", mo[q], sq[q], ZO[:, t, :], ALU.mult, [("sq", q), ("ZO", t)], [("mo", q)])
            bq = nextbank()
            bqv = banks[bq][:].bitcast(BF16)
            for cc in range(3):
                tr(bqv[:, cc * P:(cc + 1) * P], mo[q][:, cc * P:(cc + 1) * P], identb[:], [("mo", q), "identb"],
                   [("ps", bq)])
            copy_to("act", moT[:, :, t * P:(t + 1) * P], bqv[:, 0:384].rearrange("p (c n) -> p c n", c=3),
                    [("ps", bq)], [("moT", t // 4)])
        ar.free(HS, ZO, GT, *sq, *ss, *mo)
        kb.barrier()
        return moT

    prep_cache = {}

    def s5_prep(l, tick=None):
        HALF_PI = math.pi / 2
        GH = G // 2

        def _tick():
            if tick is not None:
                tick()
        prm = ar.alloc((3, G), F32)
        bc4 = ar.alloc((4, G, GC), F32)
        dcol = ar.alloc((G,), F32)
        kb.dma("sp", prm, s5p_d[:, l], writes=["prm"])
        kb.dma("sp", bc4, s5bc_d[:, l], writes=["bc4"])
        kb.dma("sp", dcol, s5dcol_d[:, l], writes=["dcol"])
        sm = {}

        def sv(name):
            if name not in sm:
                sm[name] = ar.alloc((G,), F32)
            return sm[name]
        K1 = ["s5tmp"]

        def e_tt(out, a, bb, op, eng="dve"):
            tt(eng, out, a, bb, op, K1 + ["prm", "bc4"], K1)

        def e_ts(out, a, s1, op0, s2=None, op1=None):
            ts("dve", out, a, s1, s2, op0, op1, K1 + ["prm"], K1)

        lre, lim, lst = prm[:, 0, :], prm[:, 1, :], prm[:, 2, :]
        step = sv("step")
        act(step, lst, AF.Exp, ["prm"], K1)
        mag = sv("mag")
        e_tt(mag, lre, step, ALU.mult)
        act(mag, mag, AF.Exp, K1, K1)
        ang = sv("ang")
        stt(ang, lim, 1.0 / 16, step, ALU.mult, ALU.mult, K1 + ["prm"], K1)
        cs_, sn_ = sv("c"), sv("s")
        halfpi = sv("halfpi")
        kb.op("dve", lambda e: e.memset(halfpi, HALF_PI), writes=K1)
        act(sn_, ang, AF.Sin, K1, K1)
        act(cs_, ang, AF.Sin, K1, K1, bias=halfpi[:, 0:1])
        t1, t2, t3 = sv("t1"), sv("t2"), sv("t3")
        for _ in range(4):
            e_tt(t1, cs_, cs_, ALU.mult)
            e_tt(t2, sn_, sn_, ALU.mult)
            e_tt(t3, cs_, sn_, ALU.mult)
            e_tt(cs_, t1, t2, ALU.subtract)
            e_ts(sn_, t3, 2.0, ALU.mult)
        PW = ar.alloc((9, 2, G), F32)
        kb.op("dve", lambda e: e.memset(PW[:, 0, 0, :], 1.0), writes=K1)
        kb.op("dve", lambda e: e.memset(PW[:, 0, 1, :], 0.0), writes=K1)
        e_tt(PW[:, 1, 0, :], mag, cs_, ALU.mult)
        e_tt(PW[:, 1, 1, :], mag, sn_, ALU.mult)
        lbre, lbim = PW[:, 1, 0, :], PW[:, 1, 1, :]

        def cmul(ore, oim, are, aim, bre, bim, conj_neg_im=False):
            e_tt(t1x(ore), are, bre, ALU.mult)
            e_tt(t2x(ore), aim, bim, ALU.mult)
            e_tt(ore, t1x(ore), t2x(ore), ALU.subtract)
            e_tt(t1x(ore), are, bim, ALU.mult)
            e_tt(t2x(ore), aim, bre, ALU.mult)
            e_tt(oim, t1x(ore), t2x(ore), ALU.add)

        GQ = 6
        big1 = ar.alloc((GQ, 8, GC), F32)
        big2 = ar.alloc((GQ, 8, GC), F32)

        def t1x(like):
            n = 1
            for s_ in like.shape[1:]:
                n *= s_
            v = big1.rearrange("p a b c -> p (a b c)")[:, 0:n]
            return reshape_like(v, like)

        def t2x(like):
            n = 1
            for s_ in like.shape[1:]:
                n *= s_
            v = big2.rearrange("p a b c -> p (a b c)")[:, 0:n]
            return reshape_like(v, like)

        def reshape_like(v, like):
            sh = like.shape[1:]
            if len(sh) == 1:
                return v
            names = " ".join("a%d" % i for i in range(len(sh)))
            kw = {"a%d" % i: sh[i] for i in range(len(sh))}
            v = v.rearrange("p (%s) -> p %s" % (names, names), **kw)
            if like.shape[0] != P:
                v = v[0:like.shape[0]]
            return v

        for k in range(2, 9):
            cmul(PW[:, k, 0, :], PW[:, k, 1, :], PW[:, k - 1, 0, :], PW[:, k - 1, 1, :], lbre, lbim)
        i8re, i8im, den = sv("i8re"), sv("i8im"), sv("den")
        e_tt(t1, PW[:, 8, 0, :], PW[:, 8, 0, :], ALU.mult)
        e_tt(t2, PW[:, 8, 1, :], PW[:, 8, 1, :], ALU.mult)
        e_tt(den, t1, t2, ALU.add)
        kb.op("dve", lambda e: e.reciprocal(out=den, in_=den), reads=K1, writes=K1)
        e_tt(i8re, PW[:, 8, 0, :], den, ALU.mult)
        e_tt(i8im, PW[:, 8, 1, :], den, ALU.mult)
        e_ts(i8im, i8im, -1.0, ALU.mult)
        cfre, cfim, ar_ = sv("cfre"), sv("cfim"), sv("ar_")
        e_ts(ar_, lbre, -1.0, ALU.add)
        e_tt(t1, lre, lre, ALU.mult)
        e_tt(t2, lim, lim, ALU.mult)
        e_tt(den, t1, t2, ALU.add)
        kb.op("dve", lambda e: e.reciprocal(out=den, in_=den), reads=K1, writes=K1)
        e_tt(t1, ar_, lre, ALU.mult)
        e_tt(t2, lbim, lim, ALU.mult)
        e_tt(cfre, t1, t2, ALU.add)
        e_tt(cfre, cfre, den, ALU.mult)
        e_tt(t1, lbim, lre, ALU.mult)
        e_tt(t2, ar_, lim, ALU.mult)
        e_tt(cfim, t1, t2, ALU.subtract)
        e_tt(cfim, cfim, den, ALU.mult)
        _tick()
        Bb = ar.alloc((2, G, GC), F32)
        Cm2 = ar.alloc((2, G, GC), F32)
        gshape = [P, G, GC]
        cmul(Bb[:, 0], Bb[:, 1], bcast(cfre.unsqueeze(2), gshape), bcast(cfim.unsqueeze(2), gshape), bc4[:, 0], bc4[:, 1])
        cmul(Cm2[:, 0], Cm2[:, 1], bcast(i8re.unsqueeze(2), gshape), bcast(i8im.unsqueeze(2), gshape), bc4[:, 2],
             bc4[:, 3])
        PWe = ar.alloc((8, 2, G), F32)
        PWc = ar.alloc((8, 2, G), F32)
        for r in range(8):
            copy_to("dve", PWe[0:64, r], PW[0:64, 7 - r], K1, K1)
            copy_to("dve", PWe[64:128, r], PW[64:128, r], K1, K1)
            copy_to("dve", PWc[0:64, r], PW[0:64, r + 1], K1, K1)
            copy_to("dve", PWc[64:128, r], PW[64:128, 8 - r], K1, K1)
        ET = ar.alloc((G, 2, 128), BF16)
        CPn = ar.alloc((G, 2, 128), BF16)
        GH = G // 2

        def expand(dst, pw, mat_re, mat_im, neg_im):
            for gh in range(G // GQ):
                gs = slice(gh * GQ, (gh + 1) * GQ)
                shp = [P, GQ, 8, GC]
                pre = bcast(pw[:, :, 0, gs].rearrange("p r g -> p g r").unsqueeze(3), shp)
                pim = bcast(pw[:, :, 1, gs].rearrange("p r g -> p g r").unsqueeze(3), shp)
                mre = bcast(mat_re[:, gs, :].unsqueeze(2), shp)
                mim = bcast(mat_im[:, gs, :].unsqueeze(2), shp)
                dre = dst[:, gs, 0, :].rearrange("p g (r c) -> p g r c", r=8)
                dim_ = dst[:, gs, 1, :].rearrange("p g (r c) -> p g r c", r=8)
                e_tt(big1, pre, mre, ALU.mult)
                e_tt(big2, pim, mim, ALU.mult)
                e_tt(dre, big1, big2, ALU.subtract)
                e_tt(big1, pre, mim, ALU.mult)
                e_tt(big2, pim, mre, ALU.mult)
                if neg_im:
                    e_tt(big1, big1, big2, ALU.add)
                    e_ts(dim_, big1, -1.0, ALU.mult)
                else:
                    e_tt(dim_, big1, big2, ALU.add)
        expand(ET, PWe, Bb[:, 0], Bb[:, 1], False)
        _tick()
        expand(CPn, PWc, Cm2[:, 0], Cm2[:, 1], True)
        _tick()
        Emat = ar.alloc((G, 2, 128), BF16, top=True)
        T0 = ar.alloc((G, 128), BF16, top=True)
        for g in range(G):
            if g % 4 == 3:
                _tick()
            bq = nextbank()
            bqv = banks[bq][:].bitcast(BF16)
            for ri in range(2):
                tr(bqv[:, ri * P:(ri + 1) * P], ET[:, g, ri, :], identb[:], K1 + ["identb"], [("ps", bq)])
            copy_to(evac_eng(), Emat[:, g, :, :], bqv[:, 0:256].rearrange("p (r n) -> p r n", r=2), [("ps", bq)],
                    [("Emat", g)])
            bts = [nextbank(), nextbank()]
            for dd_ in range(2):
                ps_ = slice(dd_ * 64, (dd_ + 1) * 64)
                for ri in range(2):
                    mm(banks[bts[dd_]][:, 0:P], ET[ps_, g, ri, :], CPn[ps_, g, ri, :], ri == 0, ri == 1,
                       K1, [("ps", bts[dd_])])
            q4 = g % 4
            kq = [("t0t", q4)]
            ta = big1.rearrange("p a b c -> p (a b c)")[:, q4 * 128:(q4 + 1) * 128]
            tb = big2.rearrange("p a b c -> p (a b c)")[:, q4 * 128:(q4 + 1) * 128]
            tt("dve", ta, banks[bts[0]][:, 0:128], s5mF, ALU.mult, [("ps", bts[0]), "cmask"] + K1, kq)
            tt("dve", tb, banks[bts[1]][:, 0:128], s5mB, ALU.mult, [("ps", bts[1]), "cmask"] + K1 + kq, kq)
            tt("dve", ta, ta, tb, ALU.add, kq, kq)
            stt(T0[:, g, :], identf[:], dcol[:, g:g + 1], ta, ALU.mult, ALU.add, kq + ["identf", "dcol"], [("T0", g)])
        ar.free(ET, CPn, Bb, Cm2, PWe)
        kb.barrier()
        CP = ar.alloc((G, 2, 128), BF16, top=True)
        expand(CP, PWc, bc4[:, 2], bc4[:, 3], True)
        LreB = ar.alloc((2, G), F32, top=True)
        LimS = ar.alloc((2, G), F32, top=True)
        copy_to("dve", LreB[:, 0, :], PW[:, 8, 0, :], K1, ["Lmul"])
        copy_to("dve", LreB[:, 1, :], PW[:, 8, 0, :], K1, ["Lmul"])
        ts("dve", LimS[:, 0, :], PW[:, 8, 1, :], -1.0, None, ALU.mult, None, K1, ["Lmul"])
        copy_to("dve", LimS[:, 1, :], PW[:, 8, 1, :], K1, ["Lmul"])
        LreB16 = ar.alloc((2, G), F32, top=True)
        LimS16 = ar.alloc((2, G), F32, top=True)
        e_tt(t1, PW[:, 8, 0, :], PW[:, 8, 0, :], ALU.mult)
        e_tt(t2, PW[:, 8, 1, :], PW[:, 8, 1, :], ALU.mult)
        tt("dve", LreB16[:, 0, :], t1, t2, ALU.subtract, K1, ["Lmul"])
        tt("dve", LreB16[:, 1, :], t1, t2, ALU.subtract, K1, ["Lmul"])
        e_tt(t3, PW[:, 8, 0, :], PW[:, 8, 1, :], ALU.mult)
        ts("dve", LimS16[:, 1, :], t3, 2.0, None, ALU.mult, None, K1, ["Lmul"])
        ts("dve", LimS16[:, 0, :], t3, -2.0, None, ALU.mult, None, K1, ["Lmul"])
        LreB32 = ar.alloc((2, G), F32, top=True)
        LimS32 = ar.alloc((2, G), F32, top=True)
        tt("dve", t1, LreB16[:, 0, :], LreB16[:, 0, :], ALU.mult, K1 + ["Lmul"], K1)
        tt("dve", t2, LimS16[:, 1, :], LimS16[:, 1, :], ALU.mult, K1 + ["Lmul"], K1)
        tt("dve", LreB32[:, 0, :], t1, t2, ALU.subtract, K1, ["Lmul"])
        tt("dve", LreB32[:, 1, :], t1, t2, ALU.subtract, K1, ["Lmul"])
        tt("dve", t3, LreB16[:, 0, :], LimS16[:, 1, :], ALU.mult, K1 + ["Lmul"], K1)
        ts("dve", LimS32[:, 1, :], t3, 2.0, None, ALU.mult, None, K1, ["Lmul"])
        ts("dve", LimS32[:, 0, :], t3, -2.0, None, ALU.mult, None, K1, ["Lmul"])
        ar.free(prm, bc4, dcol, PW, PWc, big1, big2, *sm.values())
        kb.barrier()

        return dict(Emat=Emat, T0=T0, CP=CP, LreB=LreB, LimS=LimS, LreB16=LreB16, LimS16=LimS16, LreB32=LreB32,
                    LimS32=LimS32)

    def s5_all(l):
        HALF_PI = math.pi / 2
        import os
        KS5 = int(os.environ.get("KS5", "9"))
        base_live = set(ar.live.keys())

        def bail5():
            for k_ in list(ar.live.keys()):
                if k_ not in base_live:
                    ar.free(ar.live[k_][0])
            kb.barrier()
            so_ = ar.alloc((3, T), BF16)
            kb.op("dve", lambda e: e.memset(so_, 0.0), writes=[("soT", i) for i in range(NB)])
            kb.barrier()
            return so_
        XN = ar.alloc((KC, T), BF16)
        rms_norm(l, 0, XN)
        Ws = ar.alloc((KC, 384), BF16)
        load_w(Ws, w_in_d[l, :, 1808:2192].rearrange("(k p) n -> p k n", p=P), "Ws")
        U = ar.alloc((G, NSB), BF16, top=True)
        Urev = ar.alloc((G, NSB), BF16, top=True)
        Utok = [ar.alloc((G, 128), BF16) for _ in range(2)]
        for half in range(2):
            ut = Utok[half]
            for r in range(8):
                b = nextbank()
                for k in range(KC):
                    lhsT = XN[:, k, half * 1024 + r:(half + 1) * 1024:8]
                    mm(banks[b][:, 0:384], lhsT, Ws[:, k, :], k == 0, k == KC - 1,
                       reads=["Ws"] + [("XN", k, nb) for nb in (2 * half, 2 * half + 1)], writes=[("ps", b)])
                copy_to(evac_eng(), ut[:, :, r * GC:(r + 1) * GC], banks[b][:, 0:384].rearrange("p (g c) -> p g c", g=G),
                        [("ps", b)], [("Utok", half)])
            for g0 in range(0, G, 4):
                b = nextbank()
                for gg in range(4):
                    mm(banks[b][:, gg * P:(gg + 1) * P], ut[:, g0 + gg, :], identb[:], True, True,
                       [("Utok", half), "identb"], [("ps", b)])
                copy_to(evac_eng(), U[:, g0:g0 + 4, half * P:(half + 1) * P],
                        banks[b][:].rearrange("p (g n) -> p g n", g=4), [("ps", b)], [("U", half)])
                b = nextbank()
                for gg in range(4):
                    mm(banks[b][:, gg * P:(gg + 1) * P], ut[:, g0 + gg, :], Jb[:], True, True,
                       [("Utok", half), "Jb"], [("ps", b)])
                copy_to(evac_eng(), Urev[:, g0:g0 + 4, (1 - half) * P:(2 - half) * P],
                        banks[b][:].rearrange("p (g n) -> p g n", g=4), [("ps", b)], [("Urev", 1 - half)])
        ar.free(XN, Ws, *Utok)
        kb.barrier()

        if KS5 < 2:
            return bail5()
        pp = prep_cache.pop(l, None)
        if pp is None:
            pp = s5_prep(l)
        Emat, T0, CP, LreB, LimS = pp["Emat"], pp["T0"], pp["CP"], pp["LreB"], pp["LimS"]
        LreB16, LimS16, LreB32, LimS32 = pp["LreB16"], pp["LimS16"], pp["LreB32"], pp["LimS32"]
        GH = G // 2

        if KS5 < 3:
            return bail5()
        Gs = ar.alloc((NSB, 2, G), BF16)
        for g in range(G):
            b = nextbank()
            for ri in range(2):
                mm(banks[b][0:64, ri * NSB:(ri + 1) * NSB], Emat[:, g, ri, 0:64], U[:, g, :], True, True,
                   ["Emat", ("U", 0), ("U", 1)], [("ps", b)])
                mm(banks[b][64:128, ri * NSB:(ri + 1) * NSB], Emat[:, g, ri, 64:128], Urev[:, g, :], True, True,
                   ["Emat", ("Urev", 0), ("Urev", 1)], [("ps", b)])
            copy_to(evac_eng(), Gs[:, :, :, g].rearrange("p n r -> p r n"),
                    banks[b][:].rearrange("p (r n) -> p r n", r=2), [("ps", b)], ["Gs"])
        ar.free(Emat, Urev)
        kb.barrier()

        if KS5 < 4:
            return bail5()
        HIST = ar.alloc((NSB, 2, G), BF16)
        NST = 4
        NK = NSB // 2
        ST = [ar.alloc((2, G), F32) for _ in range(NST)]
        TA = [ar.alloc((2, G), F32) for _ in range(2)]
        TB = [ar.alloc((2, G), F32) for _ in range(2)]
        CAPs = ar.alloc((8, 2, G), F32)
        G2 = ar.alloc((NK, 2, G), BF16)
        tb1 = ar.alloc((32, 2, G), F32)
        tb2 = ar.alloc((32, 2, G), F32)
        kb.dma("sp", ST[0], s5st0_d[:, l], writes=[("ST", 0, 0), ("ST", 0, 1)])
        shp4 = [P, 32, 2, G]
        for c4 in range(4):
            ge = Gs[:, 64 * c4:64 * c4 + 64:2]
            go = Gs[:, 64 * c4 + 1:64 * c4 + 64:2]
            tt("dve", tb1, ge, bcast(LreB.unsqueeze(1), shp4), ALU.mult, ["Gs", "Lmul"], ["tb1"])
            tt("dve", tb2, rev_axis(ge, 2), bcast(LimS.unsqueeze(1), shp4), ALU.mult, ["Gs", "Lmul"], ["tb2"])
            tt("dve", tb1, tb1, tb2, ALU.add, ["tb1", "tb2"], ["tb1"])
            tt("dve", G2[:, 32 * c4:32 * c4 + 32], tb1, go, ALU.add, ["tb1", "Gs"], ["G2"])
        NJ = NSB // 4
        G4 = ar.alloc((NJ, 2, G), BF16)
        for c2 in range(2):
            ge = G2[:, 64 * c2:64 * c2 + 64:2]
            go = G2[:, 64 * c2 + 1:64 * c2 + 64:2]
            tt("dve", tb1, ge, bcast(LreB16.unsqueeze(1), shp4), ALU.mult, ["G2", "Lmul"], ["tb1"])
            tt("dve", tb2, rev_axis(ge, 2), bcast(LimS16.unsqueeze(1), shp4), ALU.mult, ["G2", "Lmul"], ["tb2"])
            tt("dve", tb1, tb1, tb2, ALU.add, ["tb1", "tb2"], ["tb1"])
            tt("dve", G4[:, 32 * c2:32 * c2 + 32], tb1, go, ALU.add, ["tb1", "G2"], ["G4"])
        gss = [slice(hh * GH, (hh + 1) * GH) for hh in range(2)]
        for j in range(NJ):
            n = 4 * j
            ci, ni = j % NST, (j + 1) % NST
            cur, nxt = ST[ci], ST[ni]
            if n % 32 == 0 and n > 0:
                for hh in range(2):
                    ts("dve", cur[:, :, gss[hh]], cur[:, :, gss[hh]], keepcol[:, 0:1], None, ALU.mult, None,
                       [("ST", ci, hh), "keepcol"], [("ST", ci, hh)])
            copy_to("act", HIST[0:64, n], cur[0:64], [("ST", ci, 0), ("ST", ci, 1)], ["HIST"])
            copy_to("act", HIST[64:128, NSB - 1 - n], cur[64:128], [("ST", ci, 0), ("ST", ci, 1)], ["HIST"])
            for hh in range(2):
                tt("dve", TA[hh][:, :, 0:GH], cur[:, :, gss[hh]], LreB32[:, :, gss[hh]], ALU.mult,
                   [("ST", ci, hh), "Lmul"], [("TA", hh)])
            for hh in range(2):
                tt("dve", TB[hh][:, :, 0:GH], swap_ri(cur[:, :, gss[hh]]), LimS32[:, :, gss[hh]], ALU.mult,
                   [("ST", ci, hh), "Lmul"], [("TB", hh)])
            for hh in range(2):
                tt("dve", TA[hh][:, :, 0:GH], TA[hh][:, :, 0:GH], TB[hh][:, :, 0:GH], ALU.add,
                   [("TA", hh), ("TB", hh)], [("TA", hh)])
            for hh in range(2):
                tt("dve", nxt[:, :, gss[hh]], TA[hh][:, :, 0:GH], G4[:, j, :, gss[hh]], ALU.add,
                   [("TA", hh), "G4"], [("ST", ni, hh)])
            if (j + 1) % 8 == 0:
                copy_to("act", CAPs[:, (j + 1) // 8 - 1], nxt, [("ST", ni, 0), ("ST", ni, 1)], ["CAPs"])
        hshp = [64, 32, 2, G]

        def bulk(xin, xout, gin, Lr, Li, ps_, gkey):
            tt("dve", tb1[ps_], xin, bcast(Lr[ps_].unsqueeze(1), hshp), ALU.mult, ["HIST", "Lmul"], ["tb1"])
            tt("dve", tb2[ps_], rev_axis(xin, 2), bcast(Li[ps_].unsqueeze(1), hshp), ALU.mult, ["HIST", "Lmul"],
               ["tb2"])
            tt("dve", tb1[ps_], tb1[ps_], tb2[ps_], ALU.add, ["tb1", "tb2"], ["tb1"])
            tt("dve", xout, tb1[ps_], gin, ALU.add, ["tb1", gkey], ["HIST"])
        fw, bw = slice(0, 64), slice(64, 128)
        for c2 in range(2):
            lo = 128 * c2
            bulk(HIST[fw, lo:lo + 128:4], HIST[fw, lo + 2:lo + 128:4], G2[fw, 64 * c2:64 * c2 + 64:2], LreB16, LimS16,
                 fw, "G2")
            kmin = 64 - 64 * c2
            bulk(HIST[bw, lo + 3:lo + 128:4], HIST[bw, lo + 1:lo + 128:4],
                 rev_axis(G2[bw, kmin:kmin + 64:2], 1), LreB16, LimS16, bw, "G2")
        for c4 in range(4):
            lo = 64 * c4
            bulk(HIST[fw, lo:lo + 64:2], HIST[fw, lo + 1:lo + 64:2], Gs[fw, lo:lo + 64:2], LreB, LimS, fw, "Gs")
            nmin = 192 - 64 * c4
            bulk(HIST[bw, lo + 1:lo + 64:2], HIST[bw, lo:lo + 64:2], rev_axis(Gs[bw, nmin:nmin + 64:2], 1), LreB, LimS,
                 bw, "Gs")
        ar.free(Gs, G2, G4, tb1, tb2)
        kb.barrier()
        OUTS = ar.alloc((16, 128), F32)
        for k4 in range(4):
            b = nextbank()
            for j in range(4):
                kk = k4 * 4 + j
                kcap, ri = kk // 2, kk % 2
                tr(banks[b][0:G, j * P:(j + 1) * P], CAPs[:, kcap, ri, :], identf[:], ["CAPs", "identf"], [("ps", b)])
            copy_to(evac_eng(), OUTS[0:G, k4 * 4:(k4 + 1) * 4, :], banks[b][0:G, :].rearrange("p (j n) -> p j n", j=4),
                    [("ps", b)], ["OUTS"])
        for kcap in range(8):
            for ri in range(2):
                for d in range(2):
                    seq = kcap if d == 0 else 7 - kcap
                    kb.dma("sp", news5_d[ri][seq, l, d], OUTS[0:G, kcap * 2 + ri, d * 64:(d + 1) * 64],
                           reads=["OUTS"], writes=[("news5", ri, seq, l, d)])
        ar.free(*ST, *TA, *TB, CAPs, LreB, LimS, LreB16, LimS16, LreB32, LimS32)
        kb.barrier()

        if KS5 < 5:
            return bail5()
        Wgl = ar.alloc((3, 768), BF16)
        load_w(Wgl, w_glu_d[l].rearrange("(k p) n -> p k n", p=P), "Wgl")
        yT = ar.alloc((3, T), BF16)
        Yt = ar.alloc((8, 384), BF16)
        gt = [ar.alloc((512,), F32) for _ in range(4)]
        for half in range(2):
            asl = slice(half * P, (half + 1) * P)
            for g0 in range(0, G, 4):
                b = nextbank()
                for gg in range(4):
                    g = g0 + gg
                    osl = banks[b][:, gg * P:(gg + 1) * P]
                    mm(osl, HIST[:, asl, 0, g], CP[:, g, 0, :], True, False, ["HIST", "CP"],
                       [("ps", b)])
                    mm(osl, HIST[:, asl, 1, g], CP[:, g, 1, :], False, False, ["HIST", "CP"],
                       [("ps", b)])
                    mm(osl, U[:, g, asl], T0[:, g, :], False, True, [("U", half), "T0"], [("ps", b)])
                q = (g0 // 4) % 4
                act(gt[q], banks[b][:], AF.Square, [("ps", b)], [("gt", q)])
                ts("dve", gt[q], gt[q], 0.044715, 1.0, ALU.mult, ALU.add, [("gt", q)], [("gt", q)])
                tt("dve", gt[q], gt[q], banks[b][:], ALU.mult, [("gt", q), ("ps", b)], [("gt", q)])
                act(gt[q], gt[q], AF.Sigmoid, [("gt", q)], [("gt", q)], scale=1.5957691216)
                tt("dve", Yt[:, :, g0 * GC:(g0 + 4) * GC].rearrange("p r (g c) -> p g r c", g=4),
                   gt[q].rearrange("p (g r c) -> p g r c", g=4, r=8), banks[b][:].rearrange("p (g r c) -> p g r c", g=4, r=8),
                   ALU.mult, [("gt", q), ("ps", b)], ["Yt"])
            for cc in range(3):
                bq = nextbank()
                bqv = banks[bq][:].bitcast(BF16)
                for r in range(8):
                    tr(bqv[:, r * P:(r + 1) * P], Yt[:, r, cc * P:(cc + 1) * P], identb[:], ["Yt", "identb"],
                       [("ps", bq)])
                copy_to(evac_eng(), yT[:, cc, half * 1024:(half + 1) * 1024].rearrange("p (a r) -> p r a", r=8),
                        bqv.rearrange("p (r a) -> p r a", r=8), [("ps", bq)], [("yT", 2 * half), ("yT", 2 * half + 1)])
        soT = ar.alloc((3, T), BF16, top=True)
        sg = [ar.alloc((512,), BF16) for _ in range(2)]
        for nb in range(NB):
            sl = slice(nb * 512, (nb + 1) * 512)
            for mi in range(3):
                ba, bb_ = nextbank(), nextbank()
                for k in range(3):
                    mm(banks[ba][:], Wgl[:, k, mi * P:(mi + 1) * P], yT[:, k, sl], k == 0, k == 2, ["Wgl", ("yT", nb)],
                       [("ps", ba)])
                for k in range(3):
                    mm(banks[bb_][:], Wgl[:, k, 384 + mi * P:384 + (mi + 1) * P], yT[:, k, sl], k == 0, k == 2,
                       ["Wgl", ("yT", nb)], [("ps", bb_)])
                q = mi % 2
                act(sg[q], banks[bb_][:], AF.Sigmoid, [("ps", bb_)], [("sg", q)])
                tt("dve", soT[:, mi, sl], banks[ba][:], sg[q], ALU.mult, [("ps", ba), ("sg", q)], [("soT", nb)])
        ar.free(Wgl, yT, Yt, *gt, *sg, U, HIST, CP, T0, OUTS)
        kb.barrier()
        return soT

    def outproj(l, foT, moT, soT):
        Wo = ar.alloc((KC, D), BF16)
        load_w(Wo, w_out_d[l].rearrange("(k p) n -> p k n", p=P), "Wo")
        srcs = []
        if foT is not None:
            srcs += [(0, foT, 0, "foT"), (1, foT, 1, "foT")]
        if moT is not None:
            srcs += [(2 + i, moT, i, "moT") for i in range(3)]
        if soT is not None:
            srcs += [(5 + i, soT, i, "soT") for i in range(3)]
        for nb in range(NB):
            sl = slice(nb * 512, (nb + 1) * 512)
            for mi in range(KC):
                b = nextbank()
                for j, (kc, buf, ci, key) in enumerate(srcs):
                    mm(banks[b][:], Wo[:, kc, mi * P:(mi + 1) * P], buf[:, ci, sl], j == 0, j == len(srcs) - 1,
                       ["Wo", (key, nb)], [("ps", b)])
                resid_evac(l, 2)(b, mi, nb)
        ar.free(Wo)
        for buf in (foT, moT, soT):
            if buf is not None:
                ar.free(buf)
        kb.barrier()

    def ffn(l, XN, extra=None):
        NG = NHC // 2
        Wg = [ar.alloc((KC, 256), BF16) for _ in range(2)]
        Wu = [ar.alloc((KC, 256), BF16) for _ in range(2)]
        Wd = [ar.alloc((2, D), BF16) for _ in range(2)]
        Hh = [ar.alloc((2, T), BF16) for _ in range(2)]
        sg = [ar.alloc((512,), BF16) for _ in range(2)]

        def issue(gi):
            s = gi % 2
            c0 = gi * 256
            load_w(Wg[s], w_gate_d[l, :, c0:c0 + 256].rearrange("(k p) n -> p k n", p=P), ("Wg", s))
            load_w(Wu[s], w_up_d[l, :, c0:c0 + 256].rearrange("(k p) n -> p k n", p=P), ("Wu", s))
            load_w(Wd[s], w_down_d[l, c0:c0 + 256, :].rearrange("(k p) n -> p k n", p=P), ("Wd", s))
        issue(0)
        for gi in range(NG):
            s = gi % 2
            if gi + 1 < NG:
                issue(gi + 1)
            if extra is not None:
                extra(gi)
            for j in range(2):
                for nb in range(NB):
                    sl = slice(nb * 512, (nb + 1) * 512)
                    bg_ = nextbank()
                    for k in range(KC):
                        mm(banks[bg_][:], Wg[s][:, k, j * P:(j + 1) * P], XN[:, k, sl], k == 0, k == KC - 1,
                           reads=[("Wg", s), ("XN", k, nb)], writes=[("ps", bg_)])
                    bu = nextbank()
                    for k in range(KC):
                        mm(banks[bu][:], Wu[s][:, k, j * P:(j + 1) * P], XN[:, k, sl], k == 0, k == KC - 1,
                           reads=[("Wu", s), ("XN", k, nb)], writes=[("ps", bu)])
                    q = (j * NB + nb) % 2
                    act(sg[q], banks[bg_][:], AF.Silu, [("ps", bg_)], [("sg", q)])
                    tt("dve", Hh[s][:, j, sl], banks[bu][:], sg[q], ALU.mult, [("ps", bu), ("sg", q)],
                       [("Hh", s, j, nb)])
            for nb in range(NB):
                for mi in range(KC):
                    b = nextbank()
                    for j in range(2):
                        mm(banks[b][:], Wd[s][:, j, mi * P:(mi + 1) * P], Hh[s][:, j, nb * 512:(nb + 1) * 512], j == 0,
                           j == 1, reads=[("Wd", s), ("Hh", s, j, nb)], writes=[("ps", b)])
                    resid_evac(l, 5)(b, mi, nb)
        ar.free(*Wg, *Wu, *Wd, *Hh, *sg)
        kb.barrier()

    WAs = [ar.alloc((KC, D), BF16) for _ in range(2)]
    stage = [ar.alloc((D,), F32) for _ in range(2)]
    ada_dma(0, 0, WAs[0], bg=True)
    ada_dma(0, 1, WAs[1], bg=True)
    for t in range(NT):
        s = t % 2
        kb.dma("sp", stage[s], x_d[t * P:(t + 1) * P, :], writes=[("stage", s)])
        for half in range(2):
            b = nextbank()
            for j in range(4):
                k = half * 4 + j
                tr(banks[b][:, j * P:(j + 1) * P], stage[s][:, k * P:(k + 1) * P], identf[:],
                   [("stage", s), "identf"], [("ps", b)])
            copy_to(evac_eng(), X[:, half * 4:(half + 1) * 4, t * P:(t + 1) * P],
                    banks[b][:].rearrange("p (j n) -> p j n", j=4), [("ps", b)], [("X", t // 4)])
    ar.free(*stage)
    ada_i = [0]

    def ada_tick():
        if ada_i[0] < 6:
            i_ = ada_i[0]
            ada_compute(0, i_, WAs[i_ % 2])
            if i_ + 2 < 6:
                ada_dma(0, i_ + 2, WAs[i_ % 2], bg=True)
            ada_i[0] += 1
    if flags["s5"]:
        prep_cache[0] = s5_prep(0, tick=ada_tick)
    while ada_i[0] < 6:
        ada_tick()
    ar.free(*WAs)
    kb.barrier()

    for l in range(NL):
        foT = moT = soT = None
        if flags["s5"]:
            soT = s5_all(l)
        if flags["fourier"] or flags["mlstm"]:
            XN = ar.alloc((KC, T), BF16)
            rms_norm(l, 0, XN)
            mb = mlstm_inproj(l, XN) if flags["mlstm"] else None
            zfT = fourier_inproj(l, XN) if flags["fourier"] else None
            ar.free(XN)
            kb.barrier()
            if flags["fourier"]:
                foT = fourier_core(l, zfT)
            if flags["mlstm"]:
                moT = mlstm_core(l, mb)
        if foT is not None or moT is not None or soT is not None:
            outproj(l, foT, moT, soT)
        if flags["ffn"]:
            XN = ar.alloc((KC, T), BF16)
            rms_norm(l, 1, XN)
            if l + 1 < NL:
                WAs = [ar.alloc((KC, D), BF16) for _ in range(2)]

                def extra(gi, l=l, WAs=WAs):
                    if gi < 6:
                        ada_dma(l + 1, gi, WAs[gi % 2])
                    if 1 <= gi < 7:
                        ada_compute(l + 1, gi - 1, WAs[(gi - 1) % 2])
                ffn(l, XN, extra)
                ar.free(*WAs)
            else:
                ffn(l, XN)
            ar.free(XN)
            kb.barrier()
        elif l + 1 < NL:
            WAs = [ar.alloc((KC, D), BF16) for _ in range(2)]
            for i in range(6):
                ada_dma(l + 1, i, WAs[i % 2])
                ada_compute(l + 1, i, WAs[i % 2])
            ar.free(*WAs)
            kb.barrier()

    rms_norm(None, None, None)
    stage = [ar.alloc((D,), F32) for _ in range(2)]
    for t in range(NT):
        s = t % 2
        for half in range(2):
            b = nextbank()
            for j in range(4):
                k = half * 4 + j
                tr(banks[b][:, j * P:(j + 1) * P], X[:, k, t * P:(t + 1) * P], identf[:], [("X", t // 4), "identf"],
                   [("ps", b)])
            copy_to(evac_eng(), stage[s][:, half * 512:(half + 1) * 512], banks[b][:], [("ps", b)], [("stage", s, half)])
        kb.dma("sp", y_d[t * P:(t + 1) * P, :], stage[s], reads=[("stage", s, 0), ("stage", s, 1)],
               writes=[("y", t)])
    nc_done = kb.finish()
    build_program.last_peak = ar.peak * 2
    build_program.ninstr = kb.ninstr
    build_program.counts = dict(kb.cnt)
    return nc_done


def fm(v):
    v = np.asarray(v, np.float32)
    n = v.shape[-1] // P
    r = v.reshape(v.shape[:-1] + (n, P))
    return np.ascontiguousarray(np.moveaxis(r, -1, 0))


def dft_consts():
    i = np.arange(64)
    ang = 2 * np.pi * np.outer(i, i) / 64.0
    C64, S64 = np.cos(ang), np.sin(ang)
    Z = np.zeros((64, 64))
    BDC = np.block([[C64, Z], [Z, C64]])
    BDS = np.block([[S64, Z], [Z, S64]])
    csc = np.concatenate([BDC, BDS], axis=1).astype(ml_dtypes.bfloat16)
    t = np.arange(T)
    pos = t % 256
    same = (t[:, None] // 256) == (t[None, :] // 256)
    angp = 2 * np.pi * np.outer(pos, pos) / 256.0
    nrm = 1.0 / math.sqrt(256 * 64)
    pc = np.where(same, np.cos(angp), 0.0) * nrm
    psn = np.where(same, -np.sin(angp), 0.0) * nrm
    pm_prompt = np.stack([pc, psn], axis=1).reshape(NT, P, 2, T).astype(ml_dtypes.bfloat16)
    r, c = t // 64, t % 64
    angs = 2 * np.pi * (np.outer(r, r) / 32.0 + np.outer(c, c) / 64.0)
    nrm = 1.0 / math.sqrt(32 * 64 * 64)
    pm_sample = np.stack([np.cos(angs) * nrm, -np.sin(angs) * nrm], axis=1).reshape(NT, P, 2, T).astype(ml_dtypes.bfloat16)
    return csc, pm_prompt, pm_sample


def mask_consts():
    i = np.arange(P)
    trile = (i[:, None] <= i[None, :]).astype(np.float32)
    trige = (i[:, None] >= i[None, :]).astype(np.float32)
    r = i // GC
    mF = (r[:, None] <= r[None, :]).astype(np.float32)
    mB = (r[:, None] >= r[None, :]).astype(np.float32)
    J = np.eye(P, dtype=np.float32)[::-1].copy()
    return np.ascontiguousarray(np.stack([trile, trige, mF, mB, J], axis=1))


def rep_dp(a):
    a = np.asarray(a, np.float32)
    return np.ascontiguousarray(a.transpose(1, 3, 0, 2).reshape(P, L, G))


def make_in_maps(inp):
    csc, pm_prompt, pm_sample = dft_consts()
    f32 = lambda a: np.ascontiguousarray(np.asarray(a, np.float32))
    lre, lim = rep_dp(inp["s5_lambda_re"]), rep_dp(inp["s5_lambda_im"])
    lst = np.ascontiguousarray(np.broadcast_to(np.asarray(inp["s5_log_step"], np.float32).transpose(1, 0, 2)[:, None],
                                               (2, 64, L, G)).reshape(P, L, G))
    s5p = np.ascontiguousarray(np.stack([lre, lim, lst], axis=2))

    def rep_b(a):
        a = np.asarray(a, np.float32).transpose(2, 0, 1, 3)
        return np.broadcast_to(a[None], (2, 64, L, G, GC)).reshape(P, L, G, GC)

    def rep_c(a):
        a = np.asarray(a, np.float32).transpose(3, 0, 1, 2)
        return np.broadcast_to(a[None], (2, 64, L, G, GC)).reshape(P, L, G, GC)
    s5bc = np.ascontiguousarray(np.stack([rep_b(inp["s5_b_re"]), rep_b(inp["s5_b_im"]), rep_c(inp["s5_c_re"]),
                                          rep_c(inp["s5_c_im"])], axis=2))
    dcol = np.asarray(inp["s5_d"], np.float32).transpose(2, 0, 1)
    s5dcol = np.ascontiguousarray(np.broadcast_to(dcol[None], (8, GC, L, G)).reshape(P, L, G))
    shared = {
        "w_ada": f32(inp["w_ada"]),
        "b_ada_fm": fm(inp["b_ada"]),
        "n1w": fm(inp["norm1_w"]), "n2w": fm(inp["norm2_w"]), "nfw": fm(inp["norm_f"]),
        "w_in": f32(inp["w_in"]), "w_fourier": f32(inp["w_fourier"]), "w_glu": f32(inp["w_glu"]),
        "w_out": f32(inp["w_out"]), "w_gate": f32(inp["w_gate"]), "w_up": f32(inp["w_up"]),
        "w_down": f32(inp["w_down"]),
        "identf": np.eye(P, dtype=np.float32), "csc": csc, "cmask": mask_consts(),
        "bg_bc": np.ascontiguousarray(np.broadcast_to(np.asarray(inp["b_gates"], np.float32)[None], (P, L, 16))),
        "mnw_bc": np.ascontiguousarray(np.broadcast_to(np.asarray(inp["mlstm_norm_w"], np.float32)[None], (P, L, 384))),
        "s5p": s5p, "s5bc": s5bc, "s5dcol": s5dcol,
    }
    maps = []
    xs = np.asarray(inp["x_sample"], np.float32)
    xp = np.asarray(inp["x_prompt"], np.float32)
    sC = np.asarray(inp["state_mlstm_C"], np.float32)
    sn = np.asarray(inp["state_mlstm_n"], np.float32)
    smm = np.asarray(inp["state_mlstm_m"], np.float32)
    sre = np.asarray(inp["state_s5_re"], np.float32)
    sim = np.asarray(inp["state_s5_im"], np.float32)
    for core in range(8):
        m = dict(shared)
        if core < 4:
            b = core
            m["x"] = np.ascontiguousarray(xs[b])
            m["cvec"] = fm(np.asarray(inp["c"])[b])
            m["posmat"] = pm_sample
            m["keepcol"] = np.ones((P, 1), np.float32)
            m["keeprow"] = np.ones((1, NT), np.float32)
            mC0 = np.concatenate([sC[b].transpose(3, 0, 1, 2, 4), sn[b].transpose(3, 0, 1, 2)[..., None]], axis=-1)
            m["mC0"] = np.ascontiguousarray(mC0)
            m["mM0"] = np.ascontiguousarray(smm[b].reshape(1, L, 8))
            st = np.stack([sre[b], sim[b]], axis=0)
            m["s5st0"] = np.ascontiguousarray(st.transpose(2, 4, 1, 0, 3).reshape(P, L, 2, G))
        else:
            j = core - 4
            m["x"] = np.ascontiguousarray(xp[8 * j:8 * j + 8].reshape(T, D))
            m["cvec"] = fm(inp["c_ctx"])
            m["posmat"] = pm_prompt
            m["keepcol"] = np.zeros((P, 1), np.float32)
            kr = np.zeros((1, NT), np.float32)
            kr[0, 1::2] = 1.0
            m["keeprow"] = kr
            m["mC0"] = np.zeros((DH, L, 2, H, 97), np.float32)
            m["mM0"] = np.zeros((1, L, 8), np.float32)
            m["s5st0"] = np.zeros((P, L, 2, G), np.float32)
        maps.append(m)
    return maps


_CACHE = {}
DEBUG = None
LAST = {}


def kernel(**inputs):
    import os
    maps = make_in_maps(inputs)
    if "nc" not in _CACHE:
        _CACHE["nc"] = build_program(debug=DEBUG)
    nc = _CACHE["nc"]
    dev_cores = os.environ.get("KDEV_CORES")
    if dev_cores:
        ids = [int(c) for c in dev_cores.split(",")]
        res = run_bass_kernel_spmd(nc, [maps[i] for i in ids], core_ids=list(range(len(ids))))
        outs = [None] * 8
        for j, i in enumerate(ids):
            outs[i] = res.results[j]
        for i in range(8):
            if outs[i] is None:
                outs[i] = outs[ids[0] if i < 4 else ids[-1]]
    else:
        res = run_bass_kernel_spmd(nc, maps, core_ids=list(range(8)))
        outs = res.results
    LAST["outs"] = outs
    f = lambda a: np.ascontiguousarray(np.asarray(a, dtype=np.float32))
    y_sample = f(np.stack([outs[c]["y"] for c in range(4)], axis=0))
    y_prompt = f(np.concatenate([np.asarray(outs[c]["y"]).reshape(8, 256, D) for c in range(4, 8)], axis=0))
    cat = lambda k: f(np.concatenate([np.asarray(outs[c][k]) for c in range(4, 8)], axis=0))
    return (y_prompt, y_sample, cat("newC"), cat("newn"), cat("newm"), cat("news5re"), cat("news5im"))
```

```python
import math
from contextlib import ExitStack

import numpy as np
import ml_dtypes

import concourse.bass as bass
import concourse.mybir as mybir
from concourse.ap import AP
from concourse.bass_utils import run_bass_kernel_spmd

F32 = mybir.dt.float32
BF16 = mybir.dt.bfloat16
AF = mybir.ActivationFunctionType
ALU = mybir.AluOpType
AX = mybir.AxisListType

P = 128
T = 2048
D = 1024
KC = 8
NT = 16
NB = 4
L = 2
DFF = 2816
NHC = DFF // 128
PIN = 2192
H = 4
DH = 96
G = 24
GC = 16
SP_ = 64
EPS = 1e-6
NSB = 256
DMA_SCRATCH = 4096
ARENA_BYTES = 116 * 1024

FLAGS = {"fourier": True, "mlstm": True, "s5": True, "ffn": True, "layers": 2}


class KB:
    def __init__(self):
        self.nc = bass.Bass("TRN2", target_bir_lowering=False, dynamic_dma_scratch_size=DMA_SCRATCH)
        self.es = ExitStack()
        nc = self.nc
        self.engs = {"pe": nc.tensor, "dve": nc.vector, "act": nc.scalar, "pool": nc.gpsimd, "sp": nc.sync}
        self.sem = {e: self.es.enter_context(nc.semaphore("s_" + e)) for e in self.engs}
        self.cnt = {e: 0 for e in self.engs}
        self.waited = {}
        self.ND = 32
        self.dsem = [self.es.enter_context(nc.semaphore("d%d" % i)) for i in range(self.ND)]
        self.dcnt = [0] * self.ND
        self.dnext = 0
        self.dnext_sw = 0
        self.res = {}
        self.ninstr = 0
        self.barrier_hooks = []
        self.bg = set()

    def sb(self, name, shape, dtype):
        return self.es.enter_context(self.nc.sbuf_tensor(name, list(shape), dtype))

    def ps(self, name, shape, dtype):
        return self.es.enter_context(self.nc.psum_tensor(name, list(shape), dtype))

    def dram(self, name, shape, dtype, kind):
        return self.nc.dram_tensor(name, list(shape), dtype, kind=kind).ap()

    def _wait(self, e, tok):
        if tok is None:
            return
        kind, src, val = tok
        key = (e, kind, src)
        if self.waited.get(key, 0) >= val:
            return
        self.waited[key] = val
        sem = self.sem[src] if kind == "e" else self.dsem[src]
        self.engs[e].wait_ge(sem, val)

    def _deps(self, e, reads, writes, pe_acc):
        for r in reads:
            st = self.res.get(r)
            if st is not None:
                self._wait(e, st["w"])
        for w in writes:
            st = self.res.get(w)
            if st is not None:
                if not (pe_acc and st["w"] is not None and st["w"][0] == "e" and st["w"][1] == "pe"):
                    self._wait(e, st["w"])
                for t in st["r"]:
                    if not (pe_acc and t[0] == "e" and t[1] == "pe"):
                        self._wait(e, t)

    def _update(self, tok, reads, writes):
        for r in reads:
            st = self.res.setdefault(r, {"w": None, "r": []})
            st["r"].append(tok)
            if len(st["r"]) > 48:
                latest = {}
                for t in st["r"]:
                    latest[(t[0], t[1])] = t
                st["r"] = list(latest.values())
        for w in writes:
            self.res[w] = {"w": tok, "r": []}

    def op(self, e, fn, reads=(), writes=(), pe_acc=False):
        if e != "pe":
            for r in reads:
                if isinstance(r, tuple) and r[0] == "ps":
                    st = self.res.get(("psx", r[1]))
                    if st is not None and st["w"] is not None and st["w"][1] != e:
                        self._wait(e, st["w"])
        self._deps(e, reads, writes, pe_acc)
        ins = fn(self.engs[e])
        ins.then_inc(self.sem[e], 1)
        self.cnt[e] += 1
        tok = ("e", e, self.cnt[e])
        self._update(tok, reads, writes)
        if e != "pe":
            for r in reads:
                if isinstance(r, tuple) and r[0] == "ps":
                    self.res[("psx", r[1])] = {"w": tok, "r": []}
        self.ninstr += 1
        return tok

    def dma(self, q, out, in_, reads=(), writes=(), bg=False, **kw):
        self._deps(q, reads, writes, False)
        half = self.ND // 2
        if q == "pool":
            i = half + self.dnext_sw
            self.dnext_sw = (self.dnext_sw + 1) % half
        else:
            i = self.dnext
            self.dnext = (self.dnext + 1) % half
        if self.dcnt[i] > 0:
            self._wait(q, ("d", i, self.dcnt[i]))
        self.bg.discard(i)
        if bg:
            self.bg.add(i)
        ins = self.engs[q].dma_start(out=out, in_=in_, **kw)
        ins.then_inc(self.dsem[i], 16)
        self.dcnt[i] += 16
        tok = ("d", i, self.dcnt[i])
        self._update(tok, reads, writes)
        self.ninstr += 1
        return tok

    def barrier(self):
        for e in self.engs:
            for e2 in self.engs:
                if self.cnt[e2] > 0:
                    self._wait(e, ("e", e2, self.cnt[e2]))
            for i in range(self.ND):
                if self.dcnt[i] > 0 and i not in self.bg:
                    self._wait(e, ("d", i, self.dcnt[i]))
        self.res = {k: v for k, v in self.res.items() if isinstance(k, tuple) and k[0] == "WA"}
        for h in self.barrier_hooks:
            h()

    def finish(self):
        self.bg = set()
        self.barrier()
        self.es.close()
        return self.nc


class Arena:
    def __init__(self, kb, nbytes):
        self.n = nbytes // 2
        self.t = kb.sb("arena", [P, self.n], BF16)
        self.free_list = [(0, self.n)]
        self.pending = []
        self.live = {}
        self.used = 0
        self.peak = 0
        kb.barrier_hooks.append(self.commit)

    def alloc(self, free_shape, dtype, top=False):
        nel = 1
        for s in free_shape:
            nel *= s
        units = nel * (2 if dtype == F32 else 1)
        units = (units + 31) // 32 * 32
        order = range(len(self.free_list) - 1, -1, -1) if top else range(len(self.free_list))
        for idx in order:
            off, size = self.free_list[idx]
            if size >= units:
                if size == units:
                    self.free_list.pop(idx)
                elif top:
                    self.free_list[idx] = (off, size - units)
                    off = off + size - units
                else:
                    self.free_list[idx] = (off + units, size - units)
                break
        else:
            raise RuntimeError("arena overflow: need %d units, free=%s" % (units, self.free_list))
        self.used += units
        self.peak = max(self.peak, self.used)
        v = self.t[:, off:off + units]
        if dtype == F32:
            v = v.bitcast(F32)[:, 0:nel]
        else:
            v = v[:, 0:nel]
        if len(free_shape) > 1:
            names = " ".join("a%d" % i for i in range(len(free_shape)))
            kw = {"a%d" % i: free_shape[i] for i in range(len(free_shape))}
            v = v.rearrange("p (%s) -> p %s" % (names, names), **kw)
        self.live[id(v)] = (v, off, units)
        return v

    def free(self, *aps):
        for ap in aps:
            v, off, units = self.live.pop(id(ap))
            self.pending.append((off, units))

    def commit(self):
        for off, units in self.pending:
            self.used -= units
            self.free_list.append((off, units))
        self.pending = []
        self.free_list.sort()
        merged = []
        for off, size in self.free_list:
            if merged and merged[-1][0] + merged[-1][1] == off:
                merged[-1] = (merged[-1][0], merged[-1][1] + size)
            else:
                merged.append((off, size))
        self.free_list = merged


def bcast(ap, shape):
    return ap.to_broadcast(list(shape))


def rev_axis(ap, axis):
    a = [list(x) for x in ap.ap]
    st, n = a[axis]
    a[axis] = [-st, n]
    return AP(ap.tensor, ap.offset + st * (n - 1), a)


def swap_ri(ap):
    a = [list(x) for x in ap.ap]
    assert len(a) == 3 and a[1][1] == 2
    st = a[1][0]
    return AP(ap.tensor, ap.offset + st, [a[0], [-st, 2], a[2]])

def build_program(flags=FLAGS, debug=None):
    kb = KB()
    nc = kb.nc
    NL = flags["layers"]

    def din(name, shape, dt=F32):
        return kb.dram(name, shape, dt, "ExternalInput")

    def dout(name, shape, dt=F32):
        return kb.dram(name, shape, dt, "ExternalOutput")

    x_d = din("x", [T, D])
    cvec_d = din("cvec", [P, KC])
    w_ada_d = din("w_ada", [L, D, 6 * D])
    b_ada_d = din("b_ada_fm", [P, L, 48])
    n1w_d = din("n1w", [P, L, KC])
    n2w_d = din("n2w", [P, L, KC])
    nfw_d = din("nfw", [P, KC])
    w_in_d = din("w_in", [L, D, PIN])
    w_fo_d = din("w_fourier", [L, 256, 256])
    w_glu_d = din("w_glu", [L, 384, 768])
    w_out_d = din("w_out", [L, D, D])
    w_gate_d = din("w_gate", [L, D, DFF])
    w_up_d = din("w_up", [L, D, DFF])
    w_down_d = din("w_down", [L, DFF, D])
    identf_d = din("identf", [P, P])
    csc_d = din("csc", [P, 256], BF16)
    posmat_d = din("posmat", [NT, P, 2, T], BF16)
    cmask_d = din("cmask", [P, 5, P])
    keepcol_d = din("keepcol", [P, 1])
    keeprow_d = din("keeprow", [1, NT])
    bg_d = din("bg_bc", [P, L, 16])
    mnw_d = din("mnw_bc", [P, L, 384])
    mC0_d = din("mC0", [DH, L, 2, H, 97])
    mM0_d = din("mM0", [1, L, 8])
    s5p_d = din("s5p", [P, L, 3, G])
    s5bc_d = din("s5bc", [P, L, 4, G, GC])
    s5dcol_d = din("s5dcol", [P, L, G])
    s5st0_d = din("s5st0", [P, L, 2, G])
    y_d = dout("y", [T, D])
    newC_d = dout("newC", [8, L, 2, H, DH, DH])
    newn_d = dout("newn", [8, L, 2, H, DH])
    newm_d = dout("newm", [8, L, 2, H])
    news5_d = [dout("news5re", [8, L, 2, G, SP_]), dout("news5im", [8, L, 2, G, SP_])]

    dbg_d = {}
    if debug:
        for name, (shape, dt) in debug.items():
            dbg_d[name] = dout("dbg_" + name, shape, dt)

    X = kb.sb("X", [P, KC, T], F32)
    identf = kb.sb("identf_sb", [P, P], F32)
    identb = kb.sb("identb", [P, P], BF16)
    onesb = kb.sb("onesb", [P, P], BF16)
    onesf = kb.sb("onesf", [P, P], F32)
    csc = kb.sb("csc_sb", [P, 256], BF16)
    cmask = kb.sb("cmask_sb", [P, 5, P], F32)
    maskb = kb.sb("maskb", [P, 2, P], BF16)
    Jb = kb.sb("Jb", [P, P], BF16)
    keepcol = kb.sb("keepcol_sb", [P, 1], F32)
    keeprow = kb.sb("keeprow_sb", [1, NT], F32)
    cvec = kb.sb("cvec_sb", [P, KC], F32)
    csil = kb.sb("csil", [P, KC], BF16)
    b_ada = kb.sb("b_ada_sb", [P, L, 48], F32)
    n1w = kb.sb("n1w_sb", [P, L, KC], F32)
    n2w = kb.sb("n2w_sb", [P, L, KC], F32)
    nfw = kb.sb("nfw_sb", [P, KC], F32)
    mods = kb.sb("mods", [P, L, 6, KC], F32)
    wn = kb.sb("wn", [P, L, 2, KC], F32)
    banks = [kb.ps("bank%d" % i, [P, 512], F32) for i in range(8)]
    ar = Arena(kb, ARENA_BYTES)
    trile, trige, s5mF, s5mB = cmask[:, 0, :], cmask[:, 1, :], cmask[:, 2, :], cmask[:, 3, :]

    bstate = {"i": 0}

    def nextbank():
        b = bstate["i"]
        bstate["i"] = (b + 1) % 8
        return b

    evq = {"i": 0}

    def evac_eng():
        evq["i"] ^= 1
        return "act" if evq["i"] else "dve"

    def mm(out, lhsT, rhs, start, stop, reads, writes):
        return kb.op("pe", lambda e: e.matmul(out, lhsT, rhs, start=start, stop=stop), reads=reads, writes=writes,
                     pe_acc=True)

    def tr(out, in_, ident, reads, writes):
        return kb.op("pe", lambda e: e.transpose(out, in_, ident), reads=reads, writes=writes, pe_acc=True)

    import os as _os
    PSUB = _os.environ.get("KPOOL", "pool")

    def copy_to(eng, out, in_, reads, writes):
        if eng == "pool":
            eng = PSUB
        if eng == "act":
            return kb.op("act", lambda e: e.activation(out=out, in_=in_, func=AF.Copy), reads=reads, writes=writes)
        return kb.op(eng, lambda e: e.tensor_copy(out=out, in_=in_), reads=reads, writes=writes)

    def tt(eng, out, in0, in1, op, reads, writes):
        if eng == "pool":
            eng = PSUB
        return kb.op(eng, lambda e: e.tensor_tensor(out=out, in0=in0, in1=in1, op=op), reads=reads, writes=writes)

    def ts(eng, out, in0, s1, s2, op0, op1, reads, writes):
        if eng == "pool":
            eng = PSUB
        if op1 is None:
            return kb.op(eng, lambda e: e.tensor_scalar(out=out, in0=in0, scalar1=s1, scalar2=None, op0=op0),
                         reads=reads, writes=writes)
        return kb.op(eng, lambda e: e.tensor_scalar(out=out, in0=in0, scalar1=s1, scalar2=s2, op0=op0, op1=op1),
                     reads=reads, writes=writes)

    def stt(out, in0, scalar, in1, op0, op1, reads, writes):
        return kb.op("dve", lambda e: e.scalar_tensor_tensor(out=out, in0=in0, scalar=scalar, in1=in1, op0=op0,
                                                             op1=op1), reads=reads, writes=writes)

    def act(out, in_, func, reads, writes, scale=1.0, bias=None):
        if bias is None:
            return kb.op("act", lambda e: e.activation(out=out, in_=in_, func=func, scale=scale), reads=reads,
                         writes=writes)
        return kb.op("act", lambda e: e.activation(out=out, in_=in_, func=func, scale=scale, bias=bias),
                     reads=reads, writes=writes)

    def dbg_out(name, ap_sb, keyreads=()):
        if debug and name in dbg_d:
            kb.barrier()
            kb.dma("sp", dbg_d[name], ap_sb, reads=list(keyreads), writes=[("dbg", name)])

    kb.dma("sp", identf[:], identf_d, writes=["identf"])
    kb.dma("sp", csc[:], csc_d, writes=["csc"])
    kb.dma("sp", cvec[:], cvec_d, writes=["cvec"])
    kb.dma("sp", b_ada[:], b_ada_d, writes=["b_ada"])
    kb.dma("sp", n1w[:], n1w_d, writes=["n1w"])
    kb.dma("sp", n2w[:], n2w_d, writes=["n2w"])
    kb.dma("sp", nfw[:], nfw_d, writes=["nfw"])
    kb.dma("sp", cmask[:], cmask_d, writes=["cmask"])
    kb.dma("sp", keepcol[:], keepcol_d, writes=["keepcol"])
    kb.dma("sp", keeprow[:], keeprow_d, writes=["keeprow"])
    kb.op("dve", lambda e: e.memset(onesb[:], 1.0), writes=["onesb"])
    kb.op("dve", lambda e: e.memset(onesf[:], 1.0), writes=["onesf"])
    copy_to("dve", identb[:], identf[:], ["identf"], ["identb"])
    copy_to("dve", maskb[:], cmask[:, 0:2, :], ["cmask"], ["maskb"])
    copy_to("dve", Jb[:], cmask[:, 4, :], ["cmask"], ["Jb"])
    act(csil[:], cvec[:], AF.Silu, ["cvec"], ["csil"])

    def ada_dma(l, i, WA, bg=False):
        src = w_ada_d[l, :, i * D:(i + 1) * D].rearrange("(k p) n -> p k n", p=P)
        kb.dma("pool", WA, src, writes=[("WA", id(WA))], bg=bg)

    def ada_compute(l, i, WA):
        b = nextbank()
        for m in range(KC):
            for k in range(KC):
                mm(banks[b][:, m:m + 1], WA[:, k, m * P:(m + 1) * P], csil[:, k:k + 1], k == 0, k == KC - 1,
                   reads=[("WA", id(WA)), "csil"], writes=[("ps", b)])
        tt("dve", mods[:, l, i, :], banks[b][:, 0:KC], b_ada[:, l, i * KC:(i + 1) * KC], ALU.add,
           [("ps", b), "b_ada"], [("mods", l, i)])
        if i in (1, 4):
            j = 0 if i == 1 else 1
            nw = n1w if i == 1 else n2w
            stt(wn[:, l, j, :], mods[:, l, i, :], 1.0, nw[:, l, :], ALU.add, ALU.mult,
                [("mods", l, i), "n1w", "n2w"], [("wn", l, j)])

    def rms_norm(l, j, XN):
        xsq = ar.alloc((KC, 512), BF16)
        rstd = ar.alloc((512,), F32)
        tmp = [ar.alloc((512,), F32) for _ in range(2)]
        for nb in range(NB):
            sl = slice(nb * 512, (nb + 1) * 512)
            for k in range(KC):
                act(xsq[:, k, :], X[:, k, sl], AF.Square, [("X", nb)], [("xsq", k)])
            b = nextbank()
            for k in range(KC):
                mm(banks[b][:], onesb[:], xsq[:, k, :], k == 0, k == KC - 1, reads=["onesb", ("xsq", k)],
                   writes=[("ps", b)])
            act(rstd, banks[b][:], AF.Sqrt, [("ps", b)], ["rstd"], scale=1.0 / D, bias=EPS)
            kb.op("dve", lambda e: e.reciprocal(out=rstd, in_=rstd), reads=["rstd"], writes=["rstd"])
            for k in range(KC):
                if l is None:
                    stt(X[:, k, sl], X[:, k, sl], nfw[:, k:k + 1], rstd, ALU.mult, ALU.mult,
                        [("X", nb), "nfw", "rstd"], [("X", nb)])
                else:
                    tq = tmp[k % 2]
                    stt(tq, X[:, k, sl], wn[:, l, j, k:k + 1], rstd, ALU.mult, ALU.mult,
                        [("X", nb), ("wn", l, j), "rstd"], [("ntmp", k % 2)])
                    act(XN[:, k, sl], tq, AF.Identity, [("ntmp", k % 2), ("mods", l, 3 * j)], [("XN", k, nb)],
                        bias=mods[:, l, 3 * j, k:k + 1])
        ar.free(xsq, rstd, *tmp)
        kb.barrier()

    def load_w(dst, src, key, q="pool"):
        return kb.dma(q, dst, src, writes=[key])

    def proj_fm(W, wkey, nk, actf, akey, mchunks, evac):
        for nb in range(NB):
            for mi in mchunks:
                b = nextbank()
                for k in range(nk):
                    mm(banks[b][:], W[:, k, mi * P:(mi + 1) * P], actf(k, nb), k == 0, k == nk - 1,
                       reads=[wkey, akey(k, nb)], writes=[("ps", b)])
                evac(b, mi, nb)

    def resid_evac(l, gi):
        def f(b, mi, nb):
            sl = slice(nb * 512, (nb + 1) * 512)
            stt(X[:, mi, sl], banks[b][:], mods[:, l, gi, mi:mi + 1], X[:, mi, sl], ALU.mult, ALU.add,
                [("ps", b), ("mods", l, gi), ("X", nb)], [("X", nb)])
        return f

    def fourier_inproj(l, XN):
        Wf = ar.alloc((KC, 256), BF16)
        load_w(Wf, w_in_d[l, :, 0:256].rearrange("(k p) n -> p k n", p=P), "Wf")
        zfT = ar.alloc((2, T), BF16)

        def ev_zf(b, mi, nb):
            copy_to(evac_eng(), zfT[:, mi, nb * 512:(nb + 1) * 512], banks[b][:], [("ps", b)], [("zfT", nb)])
        proj_fm(Wf, "Wf", KC, lambda k, nb: XN[:, k, nb * 512:(nb + 1) * 512], lambda k, nb: ("XN", k, nb), range(2),
                ev_zf)
        ar.free(Wf)
        return zfT

    def fourier_core(l, zfT):
        foT = ar.alloc((2, T), BF16, top=True)
        Wfo = ar.alloc((2, 256), BF16)
        load_w(Wfo, w_fo_d[l].rearrange("(k p) n -> p k n", p=P), "Wfo")
        ZCS = ar.alloc((NT, 512), BF16)
        NR = 4
        PM = [ar.alloc((2, T // 2), BF16) for _ in range(NR)]
        for t in range(NT):
            b = nextbank()
            for kc in range(2):
                mm(banks[b][:, kc * 256:(kc + 1) * 256], zfT[:, kc, t * P:(t + 1) * P], csc[:], True, True,
                   reads=[("zfT", t // 4), "csc"], writes=[("ps", b)])
            copy_to(evac_eng(), ZCS[:, t, :], banks[b][:], [("ps", b)], [("ZCS", t)])
        kb.barrier()
        yT = zfT
        it = 0
        for hp in range(2):
            for ti in range(NT):
                pm = PM[it % NR]
                kb.dma("sp", pm, posmat_d[ti][:, :, hp * 1024:(hp + 1) * 1024], writes=[("PM", it % NR)])
                for cs in range(2):
                    for fc in range(2):
                        for nbl in range(2):
                            b = hp * 4 + fc * 2 + nbl
                            mm(banks[b][:], ZCS[:, ti, fc * 256 + cs * P: fc * 256 + (cs + 1) * P],
                               pm[:, cs, nbl * 512:(nbl + 1) * 512], ti == 0 and cs == 0, ti == NT - 1 and cs == 1,
                               reads=[("ZCS", ti), ("PM", it % NR)], writes=[("ps", b)])
                it += 1
            for fc in range(2):
                for nbl in range(2):
                    b = hp * 4 + fc * 2 + nbl
                    nb = hp * 2 + nbl
                    copy_to(evac_eng(), yT[:, fc, nb * 512:(nb + 1) * 512], banks[b][:], [("ps", b)], [("zfT", nb)])
        bstate["i"] = 0

        def ev_fo(b, mi, nb):
            copy_to(evac_eng(), foT[:, mi, nb * 512:(nb + 1) * 512], banks[b][:], [("ps", b)], [("foT", nb)])
        proj_fm(Wfo, "Wfo", 2, lambda k, nb: yT[:, k, nb * 512:(nb + 1) * 512], lambda k, nb: ("zfT", nb), range(2),
                ev_fo)
        ar.free(Wfo, ZCS, *PM, zfT)
        kb.barrier()
        return foT

    def mlstm_inproj(l, XN):
        Qt = ar.alloc((NT, 384), BF16)
        Kt = ar.alloc((NT, 384), BF16)
        Vt = ar.alloc((NT, 384), BF16)
        ZO = ar.alloc((NT, 384), BF16)
        GT = ar.alloc((NT, 16), F32)
        bg = ar.alloc((16,), F32)
        mnw = ar.alloc((384,), F32)
        kb.dma("sp", bg, bg_d[:, l, :], writes=["bg"])
        kb.dma("sp", mnw, mnw_d[:, l, :], writes=["mnw"])
        groups = [(256, 640, "q"), (640, 1024, "k"), (1024, 1408, "v"), (1408, 1808, "go")]
        Wp = [ar.alloc((KC, 400), BF16) for _ in range(2)]
        import os
        groups = groups[:int(os.environ.get("KIPG", "4"))]
        for gi, (c0, c1, kind) in enumerate(groups):
            W = Wp[gi % 2]
            n = c1 - c0
            load_w(W[:, :, 0:n], w_in_d[l, :, c0:c1].rearrange("(k p) n -> p k n", p=P), ("Wp", gi % 2))
            for t in range(NT):
                b = nextbank()
                for k in range(KC):
                    mm(banks[b][:, 0:n], XN[:, k, t * P:(t + 1) * P], W[:, k, 0:n], k == 0, k == KC - 1,
                       reads=[("Wp", gi % 2), ("XN", k, t // 4)], writes=[("ps", b)])
                if kind == "q":
                    copy_to(evac_eng(), Qt[:, t, :], banks[b][:, 0:384], [("ps", b)], [("Qt", t)])
                elif kind == "k":
                    act(Kt[:, t, :], banks[b][:, 0:384], AF.Copy, [("ps", b)], [("Kt", t)], scale=float(DH) ** -0.5)
                elif kind == "v":
                    copy_to(evac_eng(), Vt[:, t, :], banks[b][:, 0:384], [("ps", b)], [("Vt", t)])
                else:
                    KGO = int(os.environ.get("KGO", "7"))
                    if KGO & 1:
                        tt("dve", GT[:, t, :], banks[b][:, 0:16], bg, ALU.add, [("ps", b), "bg"], [("GT", t)])
                    if KGO & 2:
                        act(ZO[:, t, :], banks[b][:, 16:400], AF.Sigmoid, [("ps", b)], [("ZO", t)])
                    if KGO & 4:
                        tt("pool", ZO[:, t, :], ZO[:, t, :], mnw, ALU.mult, [("ZO", t), "mnw"], [("ZO", t)])
        ar.free(*Wp, bg, mnw)
        kb.barrier()
        return dict(Qt=Qt, Kt=Kt, Vt=Vt, ZO=ZO, GT=GT)

    def mlstm_core(l, mb):
        Qt, Kt, Vt, ZO, GT = mb["Qt"], mb["Kt"], mb["Vt"], mb["ZO"], mb["GT"]
        import os
        STAGE = int(os.environ.get("KSTAGE", "9"))

        def bail(bufs):
            ar.free(*bufs)
            kb.barrier()
            moT_ = ar.alloc((3, T), BF16)
            kb.op("dve", lambda e: e.memset(moT_, 0.0), writes=[("moT", i) for i in range(NB)])
            kb.barrier()
            return moT_
        if STAGE < 1:
            return bail([Qt, Kt, Vt, ZO, GT])
        allT = [("GT", t) for t in range(NT)]
        SPl = ar.alloc((2, NT, H), F32)
        IG = ar.alloc((2, NT, H), F32)
        A = ar.alloc((128,), F32)
        NBc = ar.alloc((128,), F32)
        Wt = ar.alloc((128,), F32)
        FL = ar.alloc((128,), F32)
        MMbc = ar.alloc((128,), F32)
        INbc = ar.alloc((128,), F32)
        COLS = ar.alloc((2,), F32)
        ROW = ar.alloc((256,), F32)
        MP = ar.alloc((128,), F32)
        MMr = ar.alloc((128,), F32)
        MN = ar.alloc((128,), F32)
        INr = ar.alloc((128,), F32)
        MI = ar.alloc((8,), F32)
        kb.dma("sp", MI[0:1, :], mM0_d[:, l, :], writes=["MI"])
        GTv = GT
        for d in range(2):
            act(SPl[:, d, :, :], GTv[:, :, d * 8 + 4:d * 8 + 8], AF.Exp, allT, [("SPl", d)], scale=-1.0)
            act(SPl[:, d, :, :], SPl[:, d, :, :], AF.Ln, [("SPl", d)], [("SPl", d)], bias=1.0)
            copy_to("dve", IG[:, d, :, :], GTv[:, :, d * 8:d * 8 + 4], allT, [("IG", d)])
        SPf = SPl.rearrange("p d t h -> p (d t h)")
        IGf = IG.rearrange("p d t h -> p (d t h)")
        b = nextbank()
        mm(banks[b][:, 0:64], trile, SPf[:, 0:64], True, True, ["cmask", ("SPl", 0)], [("ps", b)])
        mm(banks[b][:, 64:128], trige, SPf[:, 64:128], True, True, ["cmask", ("SPl", 1)], [("ps", b)])
        copy_to("dve", NBc, banks[b][:, 0:128], [("ps", b)], ["NBc"])
        tt("dve", A, IGf, NBc, ALU.add, [("IG", 0), ("IG", 1), "NBc"], ["A"])
        b = nextbank()
        tr(banks[b][:, 0:128], A, identf[:], ["A", "identf"], [("ps", b)])
        kb.op("dve", lambda e: e.tensor_reduce(out=COLS[:, 0:1], in_=banks[b][:, 0:128], axis=AX.X, op=ALU.max),
              reads=[("ps", b)], writes=[("COLS", 0)])
        b2 = nextbank()
        mm(banks[b2][:, 0:1], SPf, onesf[:, 0:1], True, True, [("SPl", 0), ("SPl", 1), "onesf"], [("ps", b2)])
        ts("dve", COLS[:, 1:2], banks[b2][:, 0:1], -1.0, None, ALU.mult, None, [("ps", b2)], [("COLS", 1)])
        b = nextbank()
        tr(banks[b][0:1, 0:128], COLS[:, 0:1], identf[:], [("COLS", 0), "identf"], [("ps", b)])
        tr(banks[b][0:1, 128:256], COLS[:, 1:2], identf[:], [("COLS", 1), "identf"], [("ps", b)])
        copy_to("dve", ROW[0:1, :], banks[b][0:1, 0:256], [("ps", b)], ["ROW"])
        for d in range(2):
            for n in range(NT):
                t = n if d == 0 else NT - 1 - n
                c = (d * NT + t) * H
                if n == 0:
                    copy_to("dve", MP[0:1, c:c + H], MI[0:1, d * H:(d + 1) * H], ["MI"], [("MP", d)])
                tt("dve", MMr[0:1, c:c + H], MP[0:1, c:c + H], ROW[0:1, c:c + H], ALU.max, [("MP", d), "ROW"],
                   [("MMr", d)])
                tt("dve", MN[0:1, c:c + H], MMr[0:1, c:c + H], ROW[0:1, 128 + c:128 + c + H], ALU.add,
                   [("MMr", d), "ROW"], [("MN", d)])
                if n + 1 < NT:
                    t2 = n + 1 if d == 0 else NT - 2 - n
                    c2 = (d * NT + t2) * H
                    ts("dve", MP[0:1, c2:c2 + H], MN[0:1, c:c + H], keeprow[0:1, n + 1:n + 2], None, ALU.mult, None,
                       [("MN", d), "keeprow"], [("MP", d)])
        chain = [("MP", 0), ("MP", 1), ("MMr", 0), ("MMr", 1)]
        tt("dve", INr[0:1, :], MP[0:1, :], MMr[0:1, :], ALU.subtract, chain, ["INr"])
        act(INr[0:1, :], INr[0:1, :], AF.Exp, ["INr"], ["INr"])
        b = nextbank()
        mm(banks[b][:, 0:128], onesf[0:1, :], MMr[0:1, :], True, True, ["onesf"] + chain, [("ps", b)])
        mm(banks[b][:, 128:256], onesf[0:1, :], INr[0:1, :], True, True, ["onesf", "INr"], [("ps", b)])
        copy_to("dve", MMbc, banks[b][:, 0:128], [("ps", b)], ["MMbc"])
        copy_to("act", INbc, banks[b][:, 128:256], [("ps", b)], ["INbc"])
        tt("dve", Wt, A, MMbc, ALU.subtract, ["A", "MMbc"], ["Wt"])
        act(Wt, Wt, AF.Exp, ["Wt"], ["Wt"])
        tt("dve", FL, NBc, MMbc, ALU.subtract, ["NBc", "MMbc"], ["FL"])
        act(FL, FL, AF.Exp, ["FL"], ["FL"])
        import os
        LVL = int(os.environ.get("KLVL", "9"))
        for d in range(2):
            if LVL < 3:
                break
            src = MN[0:1, d * 64:(d + 1) * 64].rearrange("p (s two h) -> p s two h", two=2, h=H)[:, :, 1 - d, :]
            if d == 0:
                dst = newm_d[:, l, d, :].rearrange("(o s) h -> o s h", o=1)
                kb.dma("sp", dst, src, reads=[("MN", d)], writes=[("newm", l, d)])
            else:
                dst = newm_d[:, l, d, :].rearrange("(o s) h -> o s h", o=1)
                kb.dma("sp", dst, src, reads=[("MN", d)], writes=[("newm", l, d)])

        if STAGE < 2:
            return bail([Qt, Kt, Vt, ZO, GT, SPl, IG, A, NBc, Wt, FL, MMbc, INbc, COLS, ROW, MP, MMr, MN, INr, MI])
        HS = ar.alloc((NT, 384), BF16)
        CST = ar.alloc((2, H, 97), F32)
        kb.dma("sp", CST[0:DH], mC0_d[:, l], writes=[("CST", 0), ("CST", 1)])
        CSb = [ar.alloc((H, 97), BF16) for _ in range(2)]
        QT = [ar.alloc((H, 128), BF16) for _ in range(2)]
        KT = [ar.alloc((H, 128), BF16) for _ in range(2)]
        SM = [ar.alloc((H, 128), BF16) for _ in range(2)]
        VE = [ar.alloc((H, 97), BF16) for _ in range(2)]
        dd = [ar.alloc((H,), F32) for _ in range(2)]
        hp = [ar.alloc((H, DH), F32) for _ in range(2)]
        CAP = [ar.alloc((H, 97), F32) for _ in range(2)]
        bcs = {}

        def stage_a(n, d):
                t = n if d == 0 else NT - 1 - n
                c = (d * NT + t) * H
                bq = nextbank()
                bqv = banks[bq][:].bitcast(BF16)
                for h in range(H):
                    tr(bqv[0:DH, h * P:(h + 1) * P], Qt[:, t, h * DH:(h + 1) * DH], identb[:], [("Qt", t), "identb"],
                       [("ps", bq)])
                    tr(bqv[0:DH, 512 + h * P:512 + (h + 1) * P], Kt[:, t, h * DH:(h + 1) * DH], identb[:],
                       [("Kt", t), "identb"], [("ps", bq)])
                copy_to("act", QT[d][0:DH].rearrange("p h n -> p (h n)"), bqv[0:DH, 0:512], [("ps", bq)], [("QT", d)])
                copy_to("dve", KT[d][0:DH].rearrange("p h n -> p (h n)"), bqv[0:DH, 512:1024], [("ps", bq)],
                        [("KT", d)])
                bs = nextbank()
                for h in range(H):
                    mm(banks[bs][:, h * P:(h + 1) * P], KT[d][0:DH, h, :], QT[d][0:DH, h, :], True, True,
                       [("KT", d), ("QT", d)], [("ps", bs)])
                tt("dve", SM[d], banks[bs][:].rearrange("p (h n) -> p h n", h=H),
                   bcast(maskb[:, d:d + 1, :], [P, H, P]), ALU.mult, [("ps", bs), "maskb"], [("SM", d)])
                tt("pool", VE[d][:, :, 0:DH], Vt[:, t, :].rearrange("p (h e) -> p h e", h=H),
                   bcast(Wt[:, c:c + H].unsqueeze(2), [P, H, DH]), ALU.mult, [("Vt", t), "Wt"], [("VE", d)])
                copy_to("pool", VE[d][:, :, DH:DH + 1], Wt[:, c:c + H].unsqueeze(2), ["Wt", ("VE", d)], [("VE", d)])
                bc = nextbank()
                bcs[d] = bc
                for h in range(H):
                    mm(banks[bc][0:DH, h * 97:(h + 1) * 97], Kt[:, t, h * DH:(h + 1) * DH], VE[d][:, h, :], True, True,
                       [("Kt", t), ("VE", d)], [("ps", bc)])

        def stage_b(n, d):
                t = n if d == 0 else NT - 1 - n
                c = (d * NT + t) * H
                for h in range(H):
                    act(CSb[d][0:DH, h, :], CST[0:DH, d, h, :], AF.Copy, [("CST", d), "INbc"], [("CSb", d)],
                        scale=INbc[0:DH, c + h:c + h + 1])
                bc = bcs[d]
                for h in range(H):
                    stt(CST[0:DH, d, h, :], CST[0:DH, d, h, :], INbc[0:DH, c + h:c + h + 1],
                        banks[bc][0:DH, h * 97:(h + 1) * 97], ALU.mult, ALU.add,
                        [("CST", d), "INbc", ("ps", bc)], [("CST", d)])
                if n % 2 == 1:
                    kcap = n // 2
                    seq = kcap if d == 0 else 7 - kcap
                    copy_to("act", CAP[d][0:DH], CST[0:DH, d], [("CST", d)], [("CAP", d)])
                    if LVL >= 1:
                        kb.dma("sp", newC_d[seq, l, d].rearrange("h k e -> k h e"), CAP[d][0:DH, :, 0:DH],
                               reads=[("CAP", d)], writes=[("newC", seq, l, d)])
                    if LVL >= 2:
                        with nc.allow_non_contiguous_dma(reason="small state vector"):
                            kb.dma("sp", newn_d[seq, l, d].rearrange("h k -> k h"), CAP[d][0:DH, :, DH],
                                   reads=[("CAP", d)], writes=[("newn", seq, l, d)])
                    if n + 1 < NT:
                        ts("dve", CST[0:DH, d], CST[0:DH, d], keepcol[0:DH, 0:1], None, ALU.mult, None,
                           [("CST", d), "keepcol"], [("CST", d)])
                bn = nextbank()
                for h in range(H):
                    mm(banks[bn][:, h * 97:(h + 1) * 97], SM[d][:, h, :], VE[d][:, h, :], True, False,
                       [("SM", d), ("VE", d)], [("ps", bn)])
                    mm(banks[bn][:, h * 97:(h + 1) * 97], QT[d][0:DH, h, :], CSb[d][0:DH, h, :], False, True,
                       [("QT", d), ("CSb", d)], [("ps", bn)])
                ndv = banks[bn][:, 0:H * 97].rearrange("p (h e) -> p h e", h=H)
                act(dd[d], ndv[:, :, DH], AF.Abs, [("ps", bn)], [("dd", d)])
                tt("dve", dd[d], dd[d], FL[:, c:c + H], ALU.max, [("dd", d), "FL"], [("dd", d)])
                kb.op("dve", lambda e: e.reciprocal(out=dd[d], in_=dd[d]), reads=[("dd", d)], writes=[("dd", d)])
                if n < NT // 2:
                    tt("dve", HS[:, t, :].rearrange("p (h e) -> p h e", h=H), ndv[:, :, 0:DH],
                       bcast(dd[d].unsqueeze(2), [P, H, DH]), ALU.mult, [("ps", bn), ("dd", d)], [("HS", t)])
                else:
                    tt("dve", hp[d], ndv[:, :, 0:DH], bcast(dd[d].unsqueeze(2), [P, H, DH]), ALU.mult,
                       [("ps", bn), ("dd", d)], [("hp", d)])
                    tt("pool", HS[:, t, :].rearrange("p (h e) -> p h e", h=H),
                       HS[:, t, :].rearrange("p (h e) -> p h e", h=H), hp[d], ALU.add, [("hp", d), ("HS", t)],
                       [("HS", t)])

        its = [(n, d) for n in range(NT) for d in range(2)]
        stage_a(*its[0])
        for i_ in range(len(its)):
            if i_ + 1 < len(its):
                stage_a(*its[i_ + 1])
            stage_b(*its[i_])
        ar.free(Qt, Kt, Vt, SPl, IG, A, NBc, Wt, FL, MMbc, INbc, COLS, ROW, MP, MMr, MN, INr, MI, CST, *CSb, *QT, *KT,
                *SM, *VE, *dd, *hp, *CAP)
        kb.barrier()
        if STAGE < 3:
            return bail([HS, ZO, GT])
        moT = ar.alloc((3, T), BF16)
        sq = [ar.alloc((384,), F32) for _ in range(2)]
        ss = [ar.alloc((H,), F32) for _ in range(2)]
        mo = [ar.alloc((384,), BF16) for _ in range(2)]
        for t in range(NT):
            q = t % 2
            hs = HS[:, t, :]
            tt("pool", sq[q], hs, hs, ALU.mult, [("HS", t)], [("sq", q)])
            kb.op("dve", lambda e: e.tensor_reduce(out=ss[q], in_=sq[q].rearrange("p (h e) -> p h e", h=H), axis=AX.X,
                                                   op=ALU.add), reads=[("sq", q)], writes=[("ss", q)])
            act(ss[q], ss[q], AF.Sqrt, [("ss", q)], [("ss", q)], scale=1.0 / DH, bias=EPS)
            kb.op("dve", lambda e: e.reciprocal(out=ss[q], in_=ss[q]), reads=[("ss", q)], writes=[("ss", q)])
            tt("dve", sq[q].rearrange("p (h e) -> p h e", h=H), hs.rearrange("p (h e) -> p h e", h=H),
               bcast(ss[q].unsqueeze(2), [P, H, DH]), ALU.mult, [("HS", t), ("ss", q), ("sq", q)], [("sq", q)])
            tt("dve", mo[q], sq[q], ZO[:, t, :], ALU.mult, [("sq", q), ("ZO", t)], [("mo", q)])
            bq = nextbank()
            bqv = banks[bq][:].bitcast(BF16)
            for cc in range(3):
                tr(bqv[:, cc * P:(cc + 1) * P], mo[q][:, cc * P:(cc + 1) * P], identb[:], [("mo", q), "identb"],
                   [("ps", bq)])
            copy_to("act", moT[:, :, t * P:(t + 1) * P], bqv[:, 0:384].rearrange("p (c n) -> p c n", c=3),
                    [("ps", bq)], [("moT", t // 4)])
        ar.free(HS, ZO, GT, *sq, *ss, *mo)
        kb.barrier()
        return moT

    prep_cache = {}

    def s5_prep(l, tick=None):
        HALF_PI = math.pi / 2
        GH = G // 2

        def _tick():
            if tick is not None:
                tick()
        prm = ar.alloc((3, G), F32)
        bc4 = ar.alloc((4, G, GC), F32)
        dcol = ar.alloc((G,), F32)
        kb.dma("sp", prm, s5p_d[:, l], writes=["prm"])
        kb.dma("sp", bc4, s5bc_d[:, l], writes=["bc4"])
        kb.dma("sp", dcol, s5dcol_d[:, l], writes=["dcol"])
        sm = {}

        def sv(name):
            if name not in sm:
                sm[name] = ar.alloc((G,), F32)
            return sm[name]
        K1 = ["s5tmp"]

        def e_tt(out, a, bb, op, eng="dve"):
            tt(eng, out, a, bb, op, K1 + ["prm", "bc4"], K1)

        def e_ts(out, a, s1, op0, s2=None, op1=None):
            ts("dve", out, a, s1, s2, op0, op1, K1 + ["prm"], K1)

        lre, lim, lst = prm[:, 0, :], prm[:, 1, :], prm[:, 2, :]
        step = sv("step")
        act(step, lst, AF.Exp, ["prm"], K1)
        mag = sv("mag")
        e_tt(mag, lre, step, ALU.mult)
        act(mag, mag, AF.Exp, K1, K1)
        ang = sv("ang")
        stt(ang, lim, 1.0 / 16, step, ALU.mult, ALU.mult, K1 + ["prm"], K1)
        cs_, sn_ = sv("c"), sv("s")
        halfpi = sv("halfpi")
        kb.op("dve", lambda e: e.memset(halfpi, HALF_PI), writes=K1)
        act(sn_, ang, AF.Sin, K1, K1)
        act(cs_, ang, AF.Sin, K1, K1, bias=halfpi[:, 0:1])
        t1, t2, t3 = sv("t1"), sv("t2"), sv("t3")
        for _ in range(4):
            e_tt(t1, cs_, cs_, ALU.mult)
            e_tt(t2, sn_, sn_, ALU.mult)
            e_tt(t3, cs_, sn_, ALU.mult)
            e_tt(cs_, t1, t2, ALU.subtract)
            e_ts(sn_, t3, 2.0, ALU.mult)
        PW = ar.alloc((9, 2, G), F32)
        kb.op("dve", lambda e: e.memset(PW[:, 0, 0, :], 1.0), writes=K1)
        kb.op("dve", lambda e: e.memset(PW[:, 0, 1, :], 0.0), writes=K1)
        e_tt(PW[:, 1, 0, :], mag, cs_, ALU.mult)
        e_tt(PW[:, 1, 1, :], mag, sn_, ALU.mult)
        lbre, lbim = PW[:, 1, 0, :], PW[:, 1, 1, :]

        def cmul(ore, oim, are, aim, bre, bim, conj_neg_im=False):
            e_tt(t1x(ore), are, bre, ALU.mult)
            e_tt(t2x(ore), aim, bim, ALU.mult)
            e_tt(ore, t1x(ore), t2x(ore), ALU.subtract)
            e_tt(t1x(ore), are, bim, ALU.mult)
            e_tt(t2x(ore), aim, bre, ALU.mult)
            e_tt(oim, t1x(ore), t2x(ore), ALU.add)

        GQ = 6
        big1 = ar.alloc((GQ, 8, GC), F32)
        big2 = ar.alloc((GQ, 8, GC), F32)

        def t1x(like):
            n = 1
            for s_ in like.shape[1:]:
                n *= s_
            v = big1.rearrange("p a b c -> p (a b c)")[:, 0:n]
            return reshape_like(v, like)

        def t2x(like):
            n = 1
            for s_ in like.shape[1:]:
                n *= s_
            v = big2.rearrange("p a b c -> p (a b c)")[:, 0:n]
            return reshape_like(v, like)

        def reshape_like(v, like):
            sh = like.shape[1:]
            if len(sh) == 1:
                return v
            names = " ".join("a%d" % i for i in range(len(sh)))
            kw = {"a%d" % i: sh[i] for i in range(len(sh))}
            v = v.rearrange("p (%s) -> p %s" % (names, names), **kw)
            if like.shape[0] != P:
                v = v[0:like.shape[0]]
            return v

        for k in range(2, 9):
            cmul(PW[:, k, 0, :], PW[:, k, 1, :], PW[:, k - 1, 0, :], PW[:, k - 1, 1, :], lbre, lbim)
        i8re, i8im, den = sv("i8re"), sv("i8im"), sv("den")
        e_tt(t1, PW[:, 8, 0, :], PW[:, 8, 0, :], ALU.mult)
        e_tt(t2, PW[:, 8, 1, :], PW[:, 8, 1, :], ALU.mult)
        e_tt(den, t1, t2, ALU.add)
        kb.op("dve", lambda e: e.reciprocal(out=den, in_=den), reads=K1, writes=K1)
        e_tt(i8re, PW[:, 8, 0, :], den, ALU.mult)
        e_tt(i8im, PW[:, 8, 1, :], den, ALU.mult)
        e_ts(i8im, i8im, -1.0, ALU.mult)
        cfre, cfim, ar_ = sv("cfre"), sv("cfim"), sv("ar_")
        e_ts(ar_, lbre, -1.0, ALU.add)
        e_tt(t1, lre, lre, ALU.mult)
        e_tt(t2, lim, lim, ALU.mult)
        e_tt(den, t1, t2, ALU.add)
        kb.op("dve", lambda e: e.reciprocal(out=den, in_=den), reads=K1, writes=K1)
        e_tt(t1, ar_, lre, ALU.mult)
        e_tt(t2, lbim, lim, ALU.mult)
        e_tt(cfre, t1, t2, ALU.add)
        e_tt(cfre, cfre, den, ALU.mult)
        e_tt(t1, lbim, lre, ALU.mult)
        e_tt(t2, ar_, lim, ALU.mult)
        e_tt(cfim, t1, t2, ALU.subtract)
        e_tt(cfim, cfim, den, ALU.mult)
        _tick()
        Bb = ar.alloc((2, G, GC), F32)
        Cm2 = ar.alloc((2, G, GC), F32)
        gshape = [P, G, GC]
        cmul(Bb[:, 0], Bb[:, 1], bcast(cfre.unsqueeze(2), gshape), bcast(cfim.unsqueeze(2), gshape), bc4[:, 0], bc4[:, 1])
        cmul(Cm2[:, 0], Cm2[:, 1], bcast(i8re.unsqueeze(2), gshape), bcast(i8im.unsqueeze(2), gshape), bc4[:, 2],
             bc4[:, 3])
        PWe = ar.alloc((8, 2, G), F32)
        PWc = ar.alloc((8, 2, G), F32)
        for r in range(8):
            copy_to("dve", PWe[0:64, r], PW[0:64, 7 - r], K1, K1)
            copy_to("dve", PWe[64:128, r], PW[64:128, r], K1, K1)
            copy_to("dve", PWc[0:64, r], PW[0:64, r + 1], K1, K1)
            copy_to("dve", PWc[64:128, r], PW[64:128, 8 - r], K1, K1)
        ET = ar.alloc((G, 2, 128), BF16)
        CPn = ar.alloc((G, 2, 128), BF16)
        GH = G // 2

        def expand(dst, pw, mat_re, mat_im, neg_im):
            for gh in range(G // GQ):
                gs = slice(gh * GQ, (gh + 1) * GQ)
                shp = [P, GQ, 8, GC]
                pre = bcast(pw[:, :, 0, gs].rearrange("p r g -> p g r").unsqueeze(3), shp)
                pim = bcast(pw[:, :, 1, gs].rearrange("p r g -> p g r").unsqueeze(3), shp)
                mre = bcast(mat_re[:, gs, :].unsqueeze(2), shp)
                mim = bcast(mat_im[:, gs, :].unsqueeze(2), shp)
                dre = dst[:, gs, 0, :].rearrange("p g (r c) -> p g r c", r=8)
                dim_ = dst[:, gs, 1, :].rearrange("p g (r c) -> p g r c", r=8)
                e_tt(big1, pre, mre, ALU.mult)
                e_tt(big2, pim, mim, ALU.mult)
                e_tt(dre, big1, big2, ALU.subtract)
                e_tt(big1, pre, mim, ALU.mult)
                e_tt(big2, pim, mre, ALU.mult)
                if neg_im:
                    e_tt(big1, big1, big2, ALU.add)
                    e_ts(dim_, big1, -1.0, ALU.mult)
                else:
                    e_tt(dim_, big1, big2, ALU.add)
        expand(ET, PWe, Bb[:, 0], Bb[:, 1], False)
        _tick()
        expand(CPn, PWc, Cm2[:, 0], Cm2[:, 1], True)
        _tick()
        Emat = ar.alloc((G, 2, 128), BF16, top=True)
        T0 = ar.alloc((G, 128), BF16, top=True)
        for g in range(G):
            if g % 4 == 3:
                _tick()
            bq = nextbank()
            bqv = banks[bq][:].bitcast(BF16)
            for ri in range(2):
                tr(bqv[:, ri * P:(ri + 1) * P], ET[:, g, ri, :], identb[:], K1 + ["identb"], [("ps", bq)])
            copy_to(evac_eng(), Emat[:, g, :, :], bqv[:, 0:256].rearrange("p (r n) -> p r n", r=2), [("ps", bq)],
                    ["Emat"])
            bts = [nextbank(), nextbank()]
            for dd_ in range(2):
                ps_ = slice(dd_ * 64, (dd_ + 1) * 64)
                for ri in range(2):
                    mm(banks[bts[dd_]][:, 0:P], ET[ps_, g, ri, :], CPn[ps_, g, ri, :], ri == 0, ri == 1,
                       K1, [("ps", bts[dd_])])
            ta, tb = big1.rearrange("p a b c -> p (a b c)")[:, 0:128], big2.rearrange("p a b c -> p (a b c)")[:, 0:128]
            tt("dve", ta, banks[bts[0]][:, 0:128], s5mF, ALU.mult, [("ps", bts[0]), "cmask"] + K1, K1)
            tt("dve", tb, banks[bts[1]][:, 0:128], s5mB, ALU.mult, [("ps", bts[1]), "cmask"] + K1, K1)
            tt("dve", ta, ta, tb, ALU.add, K1, K1)
            stt(T0[:, g, :], identf[:], dcol[:, g:g + 1], ta, ALU.mult, ALU.add, K1 + ["identf", "dcol"], ["T0"])
        ar.free(ET, CPn, Bb, Cm2, PWe)
        kb.barrier()
        CP = ar.alloc((G, 2, 128), BF16, top=True)
        expand(CP, PWc, bc4[:, 2], bc4[:, 3], True)
        LreB = ar.alloc((2, G), F32, top=True)
        LimS = ar.alloc((2, G), F32, top=True)
        copy_to("dve", LreB[:, 0, :], PW[:, 8, 0, :], K1, ["Lmul"])
        copy_to("dve", LreB[:, 1, :], PW[:, 8, 0, :], K1, ["Lmul"])
        ts("dve", LimS[:, 0, :], PW[:, 8, 1, :], -1.0, None, ALU.mult, None, K1, ["Lmul"])
        copy_to("dve", LimS[:, 1, :], PW[:, 8, 1, :], K1, ["Lmul"])
        LreB16 = ar.alloc((2, G), F32, top=True)
        LimS16 = ar.alloc((2, G), F32, top=True)
        e_tt(t1, PW[:, 8, 0, :], PW[:, 8, 0, :], ALU.mult)
        e_tt(t2, PW[:, 8, 1, :], PW[:, 8, 1, :], ALU.mult)
        tt("dve", LreB16[:, 0, :], t1, t2, ALU.subtract, K1, ["Lmul"])
        tt("dve", LreB16[:, 1, :], t1, t2, ALU.subtract, K1, ["Lmul"])
        e_tt(t3, PW[:, 8, 0, :], PW[:, 8, 1, :], ALU.mult)
        ts("dve", LimS16[:, 1, :], t3, 2.0, None, ALU.mult, None, K1, ["Lmul"])
        ts("dve", LimS16[:, 0, :], t3, -2.0, None, ALU.mult, None, K1, ["Lmul"])
        LreB32 = ar.alloc((2, G), F32, top=True)
        LimS32 = ar.alloc((2, G), F32, top=True)
        tt("dve", t1, LreB16[:, 0, :], LreB16[:, 0, :], ALU.mult, K1 + ["Lmul"], K1)
        tt("dve", t2, LimS16[:, 1, :], LimS16[:, 1, :], ALU.mult, K1 + ["Lmul"], K1)
        tt("dve", LreB32[:, 0, :], t1, t2, ALU.subtract, K1, ["Lmul"])
        tt("dve", LreB32[:, 1, :], t1, t2, ALU.subtract, K1, ["Lmul"])
        tt("dve", t3, LreB16[:, 0, :], LimS16[:, 1, :], ALU.mult, K1 + ["Lmul"], K1)
        ts("dve", LimS32[:, 1, :], t3, 2.0, None, ALU.mult, None, K1, ["Lmul"])
        ts("dve", LimS32[:, 0, :], t3, -2.0, None, ALU.mult, None, K1, ["Lmul"])
        ar.free(prm, bc4, dcol, PW, PWc, big1, big2, *sm.values())
        kb.barrier()

        return dict(Emat=Emat, T0=T0, CP=CP, LreB=LreB, LimS=LimS, LreB16=LreB16, LimS16=LimS16, LreB32=LreB32,
                    LimS32=LimS32)

    def s5_all(l):
        HALF_PI = math.pi / 2
        import os
        KS5 = int(os.environ.get("KS5", "9"))
        base_live = set(ar.live.keys())

        def bail5():
            for k_ in list(ar.live.keys()):
                if k_ not in base_live:
                    ar.free(ar.live[k_][0])
            kb.barrier()
            so_ = ar.alloc((3, T), BF16)
            kb.op("dve", lambda e: e.memset(so_, 0.0), writes=[("soT", i) for i in range(NB)])
            kb.barrier()
            return so_
        XN = ar.alloc((KC, T), BF16)
        rms_norm(l, 0, XN)
        Ws = ar.alloc((KC, 384), BF16)
        load_w(Ws, w_in_d[l, :, 1808:2192].rearrange("(k p) n -> p k n", p=P), "Ws")
        U = ar.alloc((G, NSB), BF16, top=True)
        Urev = ar.alloc((G, NSB), BF16, top=True)
        Utok = [ar.alloc((G, 128), BF16) for _ in range(2)]
        for half in range(2):
            ut = Utok[half]
            for r in range(8):
                b = nextbank()
                for k in range(KC):
                    lhsT = XN[:, k, half * 1024 + r:(half + 1) * 1024:8]
                    mm(banks[b][:, 0:384], lhsT, Ws[:, k, :], k == 0, k == KC - 1,
                       reads=["Ws"] + [("XN", k, nb) for nb in (2 * half, 2 * half + 1)], writes=[("ps", b)])
                copy_to(evac_eng(), ut[:, :, r * GC:(r + 1) * GC], banks[b][:, 0:384].rearrange("p (g c) -> p g c", g=G),
                        [("ps", b)], [("Utok", half)])
            for g0 in range(0, G, 4):
                b = nextbank()
                for gg in range(4):
                    mm(banks[b][:, gg * P:(gg + 1) * P], ut[:, g0 + gg, :], identb[:], True, True,
                       [("Utok", half), "identb"], [("ps", b)])
                copy_to(evac_eng(), U[:, g0:g0 + 4, half * P:(half + 1) * P],
                        banks[b][:].rearrange("p (g n) -> p g n", g=4), [("ps", b)], [("U", half)])
                b = nextbank()
                for gg in range(4):
                    mm(banks[b][:, gg * P:(gg + 1) * P], ut[:, g0 + gg, :], Jb[:], True, True,
                       [("Utok", half), "Jb"], [("ps", b)])
                copy_to(evac_eng(), Urev[:, g0:g0 + 4, (1 - half) * P:(2 - half) * P],
                        banks[b][:].rearrange("p (g n) -> p g n", g=4), [("ps", b)], [("Urev", 1 - half)])
        ar.free(XN, Ws, *Utok)
        kb.barrier()

        if KS5 < 2:
            return bail5()
        pp = prep_cache.pop(l, None)
        if pp is None:
            pp = s5_prep(l)
        Emat, T0, CP, LreB, LimS = pp["Emat"], pp["T0"], pp["CP"], pp["LreB"], pp["LimS"]
        LreB16, LimS16, LreB32, LimS32 = pp["LreB16"], pp["LimS16"], pp["LreB32"], pp["LimS32"]
        GH = G // 2

        if KS5 < 3:
            return bail5()
        Gs = ar.alloc((NSB, 2, G), BF16)
        for g in range(G):
            b = nextbank()
            for ri in range(2):
                mm(banks[b][0:64, ri * NSB:(ri + 1) * NSB], Emat[:, g, ri, 0:64], U[:, g, :], True, True,
                   ["Emat", ("U", 0), ("U", 1)], [("ps", b)])
                mm(banks[b][64:128, ri * NSB:(ri + 1) * NSB], Emat[:, g, ri, 64:128], Urev[:, g, :], True, True,
                   ["Emat", ("Urev", 0), ("Urev", 1)], [("ps", b)])
            copy_to(evac_eng(), Gs[:, :, :, g].rearrange("p n r -> p r n"),
                    banks[b][:].rearrange("p (r n) -> p r n", r=2), [("ps", b)], ["Gs"])
        ar.free(Emat, Urev)
        kb.barrier()

        if KS5 < 4:
            return bail5()
        HIST = ar.alloc((NSB, 2, G), BF16)
        NST = 4
        NK = NSB // 2
        ST = [ar.alloc((2, G), F32) for _ in range(NST)]
        TA = [ar.alloc((2, G), F32) for _ in range(2)]
        TB = [ar.alloc((2, G), F32) for _ in range(2)]
        CAPs = ar.alloc((8, 2, G), F32)
        G2 = ar.alloc((NK, 2, G), BF16)
        tb1 = ar.alloc((32, 2, G), F32)
        tb2 = ar.alloc((32, 2, G), F32)
        kb.dma("sp", ST[0], s5st0_d[:, l], writes=[("ST", 0, 0), ("ST", 0, 1)])
        shp4 = [P, 32, 2, G]
        for c4 in range(4):
            ge = Gs[:, 64 * c4:64 * c4 + 64:2]
            go = Gs[:, 64 * c4 + 1:64 * c4 + 64:2]
            tt("dve", tb1, ge, bcast(LreB.unsqueeze(1), shp4), ALU.mult, ["Gs", "Lmul"], ["tb1"])
            tt("dve", tb2, rev_axis(ge, 2), bcast(LimS.unsqueeze(1), shp4), ALU.mult, ["Gs", "Lmul"], ["tb2"])
            tt("dve", tb1, tb1, tb2, ALU.add, ["tb1", "tb2"], ["tb1"])
            tt("dve", G2[:, 32 * c4:32 * c4 + 32], tb1, go, ALU.add, ["tb1", "Gs"], ["G2"])
        NJ = NSB // 4
        G4 = ar.alloc((NJ, 2, G), BF16)
        for c2 in range(2):
            ge = G2[:, 64 * c2:64 * c2 + 64:2]
            go = G2[:, 64 * c2 + 1:64 * c2 + 64:2]
            tt("dve", tb1, ge, bcast(LreB16.unsqueeze(1), shp4), ALU.mult, ["G2", "Lmul"], ["tb1"])
            tt("dve", tb2, rev_axis(ge, 2), bcast(LimS16.unsqueeze(1), shp4), ALU.mult, ["G2", "Lmul"], ["tb2"])
            tt("dve", tb1, tb1, tb2, ALU.add, ["tb1", "tb2"], ["tb1"])
            tt("dve", G4[:, 32 * c2:32 * c2 + 32], tb1, go, ALU.add, ["tb1", "G2"], ["G4"])
        gss = [slice(hh * GH, (hh + 1) * GH) for hh in range(2)]
        for j in range(NJ):
            n = 4 * j
            ci, ni = j % NST, (j + 1) % NST
            cur, nxt = ST[ci], ST[ni]
            if n % 32 == 0 and n > 0:
                for hh in range(2):
                    ts("dve", cur[:, :, gss[hh]], cur[:, :, gss[hh]], keepcol[:, 0:1], None, ALU.mult, None,
                       [("ST", ci, hh), "keepcol"], [("ST", ci, hh)])
            copy_to("act", HIST[0:64, n], cur[0:64], [("ST", ci, 0), ("ST", ci, 1)], ["HIST"])
            copy_to("act", HIST[64:128, NSB - 1 - n], cur[64:128], [("ST", ci, 0), ("ST", ci, 1)], ["HIST"])
            for hh in range(2):
                tt("dve", TA[hh][:, :, 0:GH], cur[:, :, gss[hh]], LreB32[:, :, gss[hh]], ALU.mult,
                   [("ST", ci, hh), "Lmul"], [("TA", hh)])
            for hh in range(2):
                tt("dve", TB[hh][:, :, 0:GH], swap_ri(cur[:, :, gss[hh]]), LimS32[:, :, gss[hh]], ALU.mult,
                   [("ST", ci, hh), "Lmul"], [("TB", hh)])
            for hh in range(2):
                tt("dve", TA[hh][:, :, 0:GH], TA[hh][:, :, 0:GH], TB[hh][:, :, 0:GH], ALU.add,
                   [("TA", hh), ("TB", hh)], [("TA", hh)])
            for hh in range(2):
                tt("dve", nxt[:, :, gss[hh]], TA[hh][:, :, 0:GH], G4[:, j, :, gss[hh]], ALU.add,
                   [("TA", hh), "G4"], [("ST", ni, hh)])
            if (j + 1) % 8 == 0:
                copy_to("act", CAPs[:, (j + 1) // 8 - 1], nxt, [("ST", ni, 0), ("ST", ni, 1)], ["CAPs"])
        hshp = [64, 32, 2, G]

        def bulk(xin, xout, gin, Lr, Li, ps_, gkey):
            tt("dve", tb1[ps_], xin, bcast(Lr[ps_].unsqueeze(1), hshp), ALU.mult, ["HIST", "Lmul"], ["tb1"])
            tt("dve", tb2[ps_], rev_axis(xin, 2), bcast(Li[ps_].unsqueeze(1), hshp), ALU.mult, ["HIST", "Lmul"],
               ["tb2"])
            tt("dve", tb1[ps_], tb1[ps_], tb2[ps_], ALU.add, ["tb1", "tb2"], ["tb1"])
            tt("dve", xout, tb1[ps_], gin, ALU.add, ["tb1", gkey], ["HIST"])
        fw, bw = slice(0, 64), slice(64, 128)
        for c2 in range(2):
            lo = 128 * c2
            bulk(HIST[fw, lo:lo + 128:4], HIST[fw, lo + 2:lo + 128:4], G2[fw, 64 * c2:64 * c2 + 64:2], LreB16, LimS16,
                 fw, "G2")
            kmin = 64 - 64 * c2
            bulk(HIST[bw, lo + 3:lo + 128:4], HIST[bw, lo + 1:lo + 128:4],
                 rev_axis(G2[bw, kmin:kmin + 64:2], 1), LreB16, LimS16, bw, "G2")
        for c4 in range(4):
            lo = 64 * c4
            bulk(HIST[fw, lo:lo + 64:2], HIST[fw, lo + 1:lo + 64:2], Gs[fw, lo:lo + 64:2], LreB, LimS, fw, "Gs")
            nmin = 192 - 64 * c4
            bulk(HIST[bw, lo + 1:lo + 64:2], HIST[bw, lo:lo + 64:2], rev_axis(Gs[bw, nmin:nmin + 64:2], 1), LreB, LimS,
                 bw, "Gs")
        ar.free(Gs, G2, G4, tb1, tb2)
        kb.barrier()
        OUTS = ar.alloc((16, 128), F32)
        for k4 in range(4):
            b = nextbank()
            for j in range(4):
                kk = k4 * 4 + j
                kcap, ri = kk // 2, kk % 2
                tr(banks[b][0:G, j * P:(j + 1) * P], CAPs[:, kcap, ri, :], identf[:], ["CAPs", "identf"], [("ps", b)])
            copy_to(evac_eng(), OUTS[0:G, k4 * 4:(k4 + 1) * 4, :], banks[b][0:G, :].rearrange("p (j n) -> p j n", j=4),
                    [("ps", b)], ["OUTS"])
        for kcap in range(8):
            for ri in range(2):
                for d in range(2):
                    seq = kcap if d == 0 else 7 - kcap
                    kb.dma("sp", news5_d[ri][seq, l, d], OUTS[0:G, kcap * 2 + ri, d * 64:(d + 1) * 64],
                           reads=["OUTS"], writes=[("news5", ri, seq, l, d)])
        ar.free(*ST, *TA, *TB, CAPs, LreB, LimS, LreB16, LimS16, LreB32, LimS32)
        kb.barrier()

        if KS5 < 5:
            return bail5()
        Wgl = ar.alloc((3, 768), BF16)
        load_w(Wgl, w_glu_d[l].rearrange("(k p) n -> p k n", p=P), "Wgl")
        yT = ar.alloc((3, T), BF16)
        Yt = ar.alloc((8, 384), BF16)
        gt = [ar.alloc((512,), F32) for _ in range(2)]
        for half in range(2):
            asl = slice(half * P, (half + 1) * P)
            for g0 in range(0, G, 4):
                b = nextbank()
                for gg in range(4):
                    g = g0 + gg
                    osl = banks[b][:, gg * P:(gg + 1) * P]
                    mm(osl, HIST[:, asl, 0, g], CP[:, g, 0, :], True, False, ["HIST", "CP"],
                       [("ps", b)])
                    mm(osl, HIST[:, asl, 1, g], CP[:, g, 1, :], False, False, ["HIST", "CP"],
                       [("ps", b)])
                    mm(osl, U[:, g, asl], T0[:, g, :], False, True, [("U", half), "T0"], [("ps", b)])
                q = (g0 // 4) % 2
                act(gt[q], banks[b][:], AF.Square, [("ps", b)], [("gt", q)])
                ts("dve", gt[q], gt[q], 0.044715, 1.0, ALU.mult, ALU.add, [("gt", q)], [("gt", q)])
                tt("dve", gt[q], gt[q], banks[b][:], ALU.mult, [("gt", q), ("ps", b)], [("gt", q)])
                act(gt[q], gt[q], AF.Sigmoid, [("gt", q)], [("gt", q)], scale=1.5957691216)
                tt("dve", Yt[:, :, g0 * GC:(g0 + 4) * GC].rearrange("p r (g c) -> p g r c", g=4),
                   gt[q].rearrange("p (g r c) -> p g r c", g=4, r=8), banks[b][:].rearrange("p (g r c) -> p g r c", g=4, r=8),
                   ALU.mult, [("gt", q), ("ps", b)], ["Yt"])
            for cc in range(3):
                bq = nextbank()
                bqv = banks[bq][:].bitcast(BF16)
                for r in range(8):
                    tr(bqv[:, r * P:(r + 1) * P], Yt[:, r, cc * P:(cc + 1) * P], identb[:], ["Yt", "identb"],
                       [("ps", bq)])
                copy_to(evac_eng(), yT[:, cc, half * 1024:(half + 1) * 1024].rearrange("p (a r) -> p r a", r=8),
                        bqv.rearrange("p (r a) -> p r a", r=8), [("ps", bq)], [("yT", 2 * half), ("yT", 2 * half + 1)])
        soT = ar.alloc((3, T), BF16, top=True)
        sg = [ar.alloc((512,), BF16) for _ in range(2)]
        for nb in range(NB):
            sl = slice(nb * 512, (nb + 1) * 512)
            for mi in range(3):
                ba, bb_ = nextbank(), nextbank()
                for k in range(3):
                    mm(banks[ba][:], Wgl[:, k, mi * P:(mi + 1) * P], yT[:, k, sl], k == 0, k == 2, ["Wgl", ("yT", nb)],
                       [("ps", ba)])
                for k in range(3):
                    mm(banks[bb_][:], Wgl[:, k, 384 + mi * P:384 + (mi + 1) * P], yT[:, k, sl], k == 0, k == 2,
                       ["Wgl", ("yT", nb)], [("ps", bb_)])
                q = mi % 2
                act(sg[q], banks[bb_][:], AF.Sigmoid, [("ps", bb_)], [("sg", q)])
                tt("dve", soT[:, mi, sl], banks[ba][:], sg[q], ALU.mult, [("ps", ba), ("sg", q)], [("soT", nb)])
        ar.free(Wgl, yT, Yt, *gt, *sg, U, HIST, CP, T0, OUTS)
        kb.barrier()
        return soT

    def outproj(l, foT, moT, soT):
        Wo = ar.alloc((KC, D), BF16)
        load_w(Wo, w_out_d[l].rearrange("(k p) n -> p k n", p=P), "Wo")
        srcs = []
        if foT is not None:
            srcs += [(0, foT, 0, "foT"), (1, foT, 1, "foT")]
        if moT is not None:
            srcs += [(2 + i, moT, i, "moT") for i in range(3)]
        if soT is not None:
            srcs += [(5 + i, soT, i, "soT") for i in range(3)]
        for nb in range(NB):
            sl = slice(nb * 512, (nb + 1) * 512)
            for mi in range(KC):
                b = nextbank()
                for j, (kc, buf, ci, key) in enumerate(srcs):
                    mm(banks[b][:], Wo[:, kc, mi * P:(mi + 1) * P], buf[:, ci, sl], j == 0, j == len(srcs) - 1,
                       ["Wo", (key, nb)], [("ps", b)])
                resid_evac(l, 2)(b, mi, nb)
        ar.free(Wo)
        for buf in (foT, moT, soT):
            if buf is not None:
                ar.free(buf)
        kb.barrier()

    def ffn(l, XN, extra=None):
        NG = NHC // 2
        Wg = [ar.alloc((KC, 256), BF16) for _ in range(2)]
        Wu = [ar.alloc((KC, 256), BF16) for _ in range(2)]
        Wd = [ar.alloc((2, D), BF16) for _ in range(2)]
        Hh = [ar.alloc((2, T), BF16) for _ in range(2)]
        sg = [ar.alloc((512,), BF16) for _ in range(2)]

        def issue(gi):
            s = gi % 2
            c0 = gi * 256
            load_w(Wg[s], w_gate_d[l, :, c0:c0 + 256].rearrange("(k p) n -> p k n", p=P), ("Wg", s))
            load_w(Wu[s], w_up_d[l, :, c0:c0 + 256].rearrange("(k p) n -> p k n", p=P), ("Wu", s))
            load_w(Wd[s], w_down_d[l, c0:c0 + 256, :].rearrange("(k p) n -> p k n", p=P), ("Wd", s))
        issue(0)
        for gi in range(NG):
            s = gi % 2
            if gi + 1 < NG:
                issue(gi + 1)
            if extra is not None:
                extra(gi)
            for j in range(2):
                for nb in range(NB):
                    sl = slice(nb * 512, (nb + 1) * 512)
                    bg_ = nextbank()
                    for k in range(KC):
                        mm(banks[bg_][:], Wg[s][:, k, j * P:(j + 1) * P], XN[:, k, sl], k == 0, k == KC - 1,
                           reads=[("Wg", s), ("XN", k, nb)], writes=[("ps", bg_)])
                    bu = nextbank()
                    for k in range(KC):
                        mm(banks[bu][:], Wu[s][:, k, j * P:(j + 1) * P], XN[:, k, sl], k == 0, k == KC - 1,
                           reads=[("Wu", s), ("XN", k, nb)], writes=[("ps", bu)])
                    q = (j * NB + nb) % 2
                    act(sg[q], banks[bg_][:], AF.Silu, [("ps", bg_)], [("sg", q)])
                    tt("dve", Hh[s][:, j, sl], banks[bu][:], sg[q], ALU.mult, [("ps", bu), ("sg", q)],
                       [("Hh", s, j, nb)])
            for nb in range(NB):
                for mi in range(KC):
                    b = nextbank()
                    for j in range(2):
                        mm(banks[b][:], Wd[s][:, j, mi * P:(mi + 1) * P], Hh[s][:, j, nb * 512:(nb + 1) * 512], j == 0,
                           j == 1, reads=[("Wd", s), ("Hh", s, j, nb)], writes=[("ps", b)])
                    resid_evac(l, 5)(b, mi, nb)
        ar.free(*Wg, *Wu, *Wd, *Hh, *sg)
        kb.barrier()

    WAs = [ar.alloc((KC, D), BF16) for _ in range(2)]
    stage = [ar.alloc((D,), F32) for _ in range(2)]
    ada_dma(0, 0, WAs[0], bg=True)
    ada_dma(0, 1, WAs[1], bg=True)
    for t in range(NT):
        s = t % 2
        kb.dma("sp", stage[s], x_d[t * P:(t + 1) * P, :], writes=[("stage", s)])
        for half in range(2):
            b = nextbank()
            for j in range(4):
                k = half * 4 + j
                tr(banks[b][:, j * P:(j + 1) * P], stage[s][:, k * P:(k + 1) * P], identf[:],
                   [("stage", s), "identf"], [("ps", b)])
            copy_to(evac_eng(), X[:, half * 4:(half + 1) * 4, t * P:(t + 1) * P],
                    banks[b][:].rearrange("p (j n) -> p j n", j=4), [("ps", b)], [("X", t // 4)])
    ar.free(*stage)
    ada_i = [0]

    def ada_tick():
        if ada_i[0] < 6:
            i_ = ada_i[0]
            ada_compute(0, i_, WAs[i_ % 2])
            if i_ + 2 < 6:
                ada_dma(0, i_ + 2, WAs[i_ % 2], bg=True)
            ada_i[0] += 1
    if flags["s5"]:
        prep_cache[0] = s5_prep(0, tick=ada_tick)
    while ada_i[0] < 6:
        ada_tick()
    ar.free(*WAs)
    kb.barrier()

    for l in range(NL):
        foT = moT = soT = None
        if flags["s5"]:
            soT = s5_all(l)
        if flags["fourier"] or flags["mlstm"]:
            XN = ar.alloc((KC, T), BF16)
            rms_norm(l, 0, XN)
            mb = mlstm_inproj(l, XN) if flags["mlstm"] else None
            zfT = fourier_inproj(l, XN) if flags["fourier"] else None
            ar.free(XN)
            kb.barrier()
            if flags["fourier"]:
                foT = fourier_core(l, zfT)
            if flags["mlstm"]:
                moT = mlstm_core(l, mb)
        if foT is not None or moT is not None or soT is not None:
            outproj(l, foT, moT, soT)
        if flags["ffn"]:
            XN = ar.alloc((KC, T), BF16)
            rms_norm(l, 1, XN)
            if l + 1 < NL:
                WAs = [ar.alloc((KC, D), BF16) for _ in range(2)]

                def extra(gi, l=l, WAs=WAs):
                    if gi < 6:
                        ada_dma(l + 1, gi, WAs[gi % 2])
                    if 1 <= gi < 7:
                        ada_compute(l + 1, gi - 1, WAs[(gi - 1) % 2])
                ffn(l, XN, extra)
                ar.free(*WAs)
            else:
                ffn(l, XN)
            ar.free(XN)
            kb.barrier()
        elif l + 1 < NL:
            WAs = [ar.alloc((KC, D), BF16) for _ in range(2)]
            for i in range(6):
                ada_dma(l + 1, i, WAs[i % 2])
                ada_compute(l + 1, i, WAs[i % 2])
            ar.free(*WAs)
            kb.barrier()

    rms_norm(None, None, None)
    stage = [ar.alloc((D,), F32) for _ in range(2)]
    for t in range(NT):
        s = t % 2
        for half in range(2):
            b = nextbank()
            for j in range(4):
                k = half * 4 + j
                tr(banks[b][:, j * P:(j + 1) * P], X[:, k, t * P:(t + 1) * P], identf[:], [("X", t // 4), "identf"],
                   [("ps", b)])
            copy_to(evac_eng(), stage[s][:, half * 512:(half + 1) * 512], banks[b][:], [("ps", b)], [("stage", s, half)])
        kb.dma("sp", y_d[t * P:(t + 1) * P, :], stage[s], reads=[("stage", s, 0), ("stage", s, 1)],
               writes=[("y", t)])
    nc_done = kb.finish()
    build_program.last_peak = ar.peak * 2
    build_program.ninstr = kb.ninstr
    build_program.counts = dict(kb.cnt)
    return nc_done


def fm(v):
    v = np.asarray(v, np.float32)
    n = v.shape[-1] // P
    r = v.reshape(v.shape[:-1] + (n, P))
    return np.ascontiguousarray(np.moveaxis(r, -1, 0))


def dft_consts():
    i = np.arange(64)
    ang = 2 * np.pi * np.outer(i, i) / 64.0
    C64, S64 = np.cos(ang), np.sin(ang)
    Z = np.zeros((64, 64))
    BDC = np.block([[C64, Z], [Z, C64]])
    BDS = np.block([[S64, Z], [Z, S64]])
    csc = np.concatenate([BDC, BDS], axis=1).astype(ml_dtypes.bfloat16)
    t = np.arange(T)
    pos = t % 256
    same = (t[:, None] // 256) == (t[None, :] // 256)
    angp = 2 * np.pi * np.outer(pos, pos) / 256.0
    nrm = 1.0 / math.sqrt(256 * 64)
    pc = np.where(same, np.cos(angp), 0.0) * nrm
    psn = np.where(same, -np.sin(angp), 0.0) * nrm
    pm_prompt = np.stack([pc, psn], axis=1).reshape(NT, P, 2, T).astype(ml_dtypes.bfloat16)
    r, c = t // 64, t % 64
    angs = 2 * np.pi * (np.outer(r, r) / 32.0 + np.outer(c, c) / 64.0)
    nrm = 1.0 / math.sqrt(32 * 64 * 64)
    pm_sample = np.stack([np.cos(angs) * nrm, -np.sin(angs) * nrm], axis=1).reshape(NT, P, 2, T).astype(ml_dtypes.bfloat16)
    return csc, pm_prompt, pm_sample


def mask_consts():
    i = np.arange(P)
    trile = (i[:, None] <= i[None, :]).astype(np.float32)
    trige = (i[:, None] >= i[None, :]).astype(np.float32)
    r = i // GC
    mF = (r[:, None] <= r[None, :]).astype(np.float32)
    mB = (r[:, None] >= r[None, :]).astype(np.float32)
    J = np.eye(P, dtype=np.float32)[::-1].copy()
    return np.ascontiguousarray(np.stack([trile, trige, mF, mB, J], axis=1))


def rep_dp(a):
    a = np.asarray(a, np.float32)
    return np.ascontiguousarray(a.transpose(1, 3, 0, 2).reshape(P, L, G))


def make_in_maps(inp):
    csc, pm_prompt, pm_sample = dft_consts()
    f32 = lambda a: np.ascontiguousarray(np.asarray(a, np.float32))
    lre, lim = rep_dp(inp["s5_lambda_re"]), rep_dp(inp["s5_lambda_im"])
    lst = np.ascontiguousarray(np.broadcast_to(np.asarray(inp["s5_log_step"], np.float32).transpose(1, 0, 2)[:, None],
                                               (2, 64, L, G)).reshape(P, L, G))
    s5p = np.ascontiguousarray(np.stack([lre, lim, lst], axis=2))

    def rep_b(a):
        a = np.asarray(a, np.float32).transpose(2, 0, 1, 3)
        return np.broadcast_to(a[None], (2, 64, L, G, GC)).reshape(P, L, G, GC)

    def rep_c(a):
        a = np.asarray(a, np.float32).transpose(3, 0, 1, 2)
        return np.broadcast_to(a[None], (2, 64, L, G, GC)).reshape(P, L, G, GC)
    s5bc = np.ascontiguousarray(np.stack([rep_b(inp["s5_b_re"]), rep_b(inp["s5_b_im"]), rep_c(inp["s5_c_re"]),
                                          rep_c(inp["s5_c_im"])], axis=2))
    dcol = np.asarray(inp["s5_d"], np.float32).transpose(2, 0, 1)
    s5dcol = np.ascontiguousarray(np.broadcast_to(dcol[None], (8, GC, L, G)).reshape(P, L, G))
    shared = {
        "w_ada": f32(inp["w_ada"]),
        "b_ada_fm": fm(inp["b_ada"]),
        "n1w": fm(inp["norm1_w"]), "n2w": fm(inp["norm2_w"]), "nfw": fm(inp["norm_f"]),
        "w_in": f32(inp["w_in"]), "w_fourier": f32(inp["w_fourier"]), "w_glu": f32(inp["w_glu"]),
        "w_out": f32(inp["w_out"]), "w_gate": f32(inp["w_gate"]), "w_up": f32(inp["w_up"]),
        "w_down": f32(inp["w_down"]),
        "identf": np.eye(P, dtype=np.float32), "csc": csc, "cmask": mask_consts(),
        "bg_bc": np.ascontiguousarray(np.broadcast_to(np.asarray(inp["b_gates"], np.float32)[None], (P, L, 16))),
        "mnw_bc": np.ascontiguousarray(np.broadcast_to(np.asarray(inp["mlstm_norm_w"], np.float32)[None], (P, L, 384))),
        "s5p": s5p, "s5bc": s5bc, "s5dcol": s5dcol,
    }
    maps = []
    xs = np.asarray(inp["x_sample"], np.float32)
    xp = np.asarray(inp["x_prompt"], np.float32)
    sC = np.asarray(inp["state_mlstm_C"], np.float32)
    sn = np.asarray(inp["state_mlstm_n"], np.float32)
    smm = np.asarray(inp["state_mlstm_m"], np.float32)
    sre = np.asarray(inp["state_s5_re"], np.float32)
    sim = np.asarray(inp["state_s5_im"], np.float32)
    for core in range(8):
        m = dict(shared)
        if core < 4:
            b = core
            m["x"] = np.ascontiguousarray(xs[b])
            m["cvec"] = fm(np.asarray(inp["c"])[b])
            m["posmat"] = pm_sample
            m["keepcol"] = np.ones((P, 1), np.float32)
            m["keeprow"] = np.ones((1, NT), np.float32)
            mC0 = np.concatenate([sC[b].transpose(3, 0, 1, 2, 4), sn[b].transpose(3, 0, 1, 2)[..., None]], axis=-1)
            m["mC0"] = np.ascontiguousarray(mC0)
            m["mM0"] = np.ascontiguousarray(smm[b].reshape(1, L, 8))
            st = np.stack([sre[b], sim[b]], axis=0)
            m["s5st0"] = np.ascontiguousarray(st.transpose(2, 4, 1, 0, 3).reshape(P, L, 2, G))
        else:
            j = core - 4
            m["x"] = np.ascontiguousarray(xp[8 * j:8 * j + 8].reshape(T, D))
            m["cvec"] = fm(inp["c_ctx"])
            m["posmat"] = pm_prompt
            m["keepcol"] = np.zeros((P, 1), np.float32)
            kr = np.zeros((1, NT), np.float32)
            kr[0, 1::2] = 1.0
            m["keeprow"] = kr
            m["mC0"] = np.zeros((DH, L, 2, H, 97), np.float32)
            m["mM0"] = np.zeros((1, L, 8), np.float32)
            m["s5st0"] = np.zeros((P, L, 2, G), np.float32)
        maps.append(m)
    return maps


_CACHE = {}
DEBUG = None
LAST = {}


def kernel(**inputs):
    import os
    maps = make_in_maps(inputs)
    if "nc" not in _CACHE:
        _CACHE["nc"] = build_program(debug=DEBUG)
    nc = _CACHE["nc"]
    dev_cores = os.environ.get("KDEV_CORES")
    if dev_cores:
        ids = [int(c) for c in dev_cores.split(",")]
        res = run_bass_kernel_spmd(nc, [maps[i] for i in ids], core_ids=list(range(len(ids))))
        outs = [None] * 8
        for j, i in enumerate(ids):
            outs[i] = res.results[j]
        for i in range(8):
            if outs[i] is None:
                outs[i] = outs[ids[0] if i < 4 else ids[-1]]
    else:
        res = run_bass_kernel_spmd(nc, maps, core_ids=list(range(8)))
        outs = res.results
    LAST["outs"] = outs
    f = lambda a: np.ascontiguousarray(np.asarray(a, dtype=np.float32))
    y_sample = f(np.stack([outs[c]["y"] for c in range(4)], axis=0))
    y_prompt = f(np.concatenate([np.asarray(outs[c]["y"]).reshape(8, 256, D) for c in range(4, 8)], axis=0))
    cat = lambda k: f(np.concatenate([np.asarray(outs[c][k]) for c in range(4, 8)], axis=0))
    return (y_prompt, y_sample, cat("newC"), cat("newn"), cat("newm"), cat("news5re"), cat("news5im"))
```

```python
import math
from contextlib import ExitStack

import numpy as np
import ml_dtypes

import concourse.bass as bass
import concourse.mybir as mybir
from concourse.ap import AP
from concourse.bass_utils import run_bass_kernel_spmd

F32 = mybir.dt.float32
BF16 = mybir.dt.bfloat16
AF = mybir.ActivationFunctionType
ALU = mybir.AluOpType
AX = mybir.AxisListType

P = 128
T = 2048
D = 1024
KC = 8
NT = 16
NB = 4
L = 2
DFF = 2816
NHC = DFF // 128
PIN = 2192
H = 4
DH = 96
G = 24
GC = 16
SP_ = 64
EPS = 1e-6
NSB = 256
DMA_SCRATCH = 4096
ARENA_BYTES = 116 * 1024

FLAGS = {"fourier": True, "mlstm": True, "s5": True, "ffn": True, "layers": 2}


class KB:
    def __init__(self):
        self.nc = bass.Bass("TRN2", target_bir_lowering=False, dynamic_dma_scratch_size=DMA_SCRATCH)
        self.es = ExitStack()
        nc = self.nc
        self.engs = {"pe": nc.tensor, "dve": nc.vector, "act": nc.scalar, "pool": nc.gpsimd, "sp": nc.sync}
        self.sem = {e: self.es.enter_context(nc.semaphore("s_" + e)) for e in self.engs}
        self.cnt = {e: 0 for e in self.engs}
        self.waited = {}
        self.ND = 32
        self.dsem = [self.es.enter_context(nc.semaphore("d%d" % i)) for i in range(self.ND)]
        self.dcnt = [0] * self.ND
        self.dnext = 0
        self.dnext_sw = 0
        self.res = {}
        self.ninstr = 0
        self.barrier_hooks = []
        self.bg = set()

    def sb(self, name, shape, dtype):
        return self.es.enter_context(self.nc.sbuf_tensor(name, list(shape), dtype))

    def ps(self, name, shape, dtype):
        return self.es.enter_context(self.nc.psum_tensor(name, list(shape), dtype))

    def dram(self, name, shape, dtype, kind):
        return self.nc.dram_tensor(name, list(shape), dtype, kind=kind).ap()

    def _wait(self, e, tok):
        if tok is None:
            return
        kind, src, val = tok
        key = (e, kind, src)
        if self.waited.get(key, 0) >= val:
            return
        self.waited[key] = val
        sem = self.sem[src] if kind == "e" else self.dsem[src]
        self.engs[e].wait_ge(sem, val)

    def _deps(self, e, reads, writes, pe_acc):
        for r in reads:
            st = self.res.get(r)
            if st is not None:
                self._wait(e, st["w"])
        for w in writes:
            st = self.res.get(w)
            if st is not None:
                if not (pe_acc and st["w"] is not None and st["w"][0] == "e" and st["w"][1] == "pe"):
                    self._wait(e, st["w"])
                for t in st["r"]:
                    if not (pe_acc and t[0] == "e" and t[1] == "pe"):
                        self._wait(e, t)

    def _update(self, tok, reads, writes):
        for r in reads:
            st = self.res.setdefault(r, {"w": None, "r": []})
            st["r"].append(tok)
            if len(st["r"]) > 48:
                latest = {}
                for t in st["r"]:
                    latest[(t[0], t[1])] = t
                st["r"] = list(latest.values())
        for w in writes:
            self.res[w] = {"w": tok, "r": []}

    def op(self, e, fn, reads=(), writes=(), pe_acc=False):
        if e != "pe":
            for r in reads:
                if isinstance(r, tuple) and r[0] == "ps":
                    st = self.res.get(("psx", r[1]))
                    if st is not None and st["w"] is not None and st["w"][1] != e:
                        self._wait(e, st["w"])
        self._deps(e, reads, writes, pe_acc)
        ins = fn(self.engs[e])
        ins.then_inc(self.sem[e], 1)
        self.cnt[e] += 1
        tok = ("e", e, self.cnt[e])
        self._update(tok, reads, writes)
        if e != "pe":
            for r in reads:
                if isinstance(r, tuple) and r[0] == "ps":
                    self.res[("psx", r[1])] = {"w": tok, "r": []}
        self.ninstr += 1
        return tok

    def dma(self, q, out, in_, reads=(), writes=(), bg=False, **kw):
        self._deps(q, reads, writes, False)
        half = self.ND // 2
        if q == "pool":
            i = half + self.dnext_sw
            self.dnext_sw = (self.dnext_sw + 1) % half
        else:
            i = self.dnext
            self.dnext = (self.dnext + 1) % half
        if self.dcnt[i] > 0:
            self._wait(q, ("d", i, self.dcnt[i]))
        self.bg.discard(i)
        if bg:
            self.bg.add(i)
        ins = self.engs[q].dma_start(out=out, in_=in_, **kw)
        ins.then_inc(self.dsem[i], 16)
        self.dcnt[i] += 16
        tok = ("d", i, self.dcnt[i])
        self._update(tok, reads, writes)
        self.ninstr += 1
        return tok

    def barrier(self):
        for e in self.engs:
            for e2 in self.engs:
                if self.cnt[e2] > 0:
                    self._wait(e, ("e", e2, self.cnt[e2]))
            for i in range(self.ND):
                if self.dcnt[i] > 0 and i not in self.bg:
                    self._wait(e, ("d", i, self.dcnt[i]))
        self.res = {k: v for k, v in self.res.items() if isinstance(k, tuple) and k[0] == "WA"}
        for h in self.barrier_hooks:
            h()

    def finish(self):
        self.bg = set()
        self.barrier()
        self.es.close()
        return self.nc


class Arena:
    def __init__(self, kb, nbytes):
        self.n = nbytes // 2
        self.t = kb.sb("arena", [P, self.n], BF16)
        self.free_list = [(0, self.n)]
        self.pending = []
        self.live = {}
        self.used = 0
        self.peak = 0
        kb.barrier_hooks.append(self.commit)

    def alloc(self, free_shape, dtype, top=False):
        nel = 1
        for s in free_shape:
            nel *= s
        units = nel * (2 if dtype == F32 else 1)
        units = (units + 31) // 32 * 32
        order = range(len(self.free_list) - 1, -1, -1) if top else range(len(self.free_list))
        for idx in order:
            off, size = self.free_list[idx]
            if size >= units:
                if size == units:
                    self.free_list.pop(idx)
                elif top:
                    self.free_list[idx] = (off, size - units)
                    off = off + size - units
                else:
                    self.free_list[idx] = (off + units, size - units)
                break
        else:
            raise RuntimeError("arena overflow: need %d units, free=%s" % (units, self.free_list))
        self.used += units
        self.peak = max(self.peak, self.used)
        v = self.t[:, off:off + units]
        if dtype == F32:
            v = v.bitcast(F32)[:, 0:nel]
        else:
            v = v[:, 0:nel]
        if len(free_shape) > 1:
            names = " ".join("a%d" % i for i in range(len(free_shape)))
            kw = {"a%d" % i: free_shape[i] for i in range(len(free_shape))}
            v = v.rearrange("p (%s) -> p %s" % (names, names), **kw)
        self.live[id(v)] = (v, off, units)
        return v

    def free(self, *aps):
        for ap in aps:
            v, off, units = self.live.pop(id(ap))
            self.pending.append((off, units))

    def commit(self):
        for off, units in self.pending:
            self.used -= units
            self.free_list.append((off, units))
        self.pending = []
        self.free_list.sort()
        merged = []
        for off, size in self.free_list:
            if merged and merged[-1][0] + merged[-1][1] == off:
                merged[-1] = (merged[-1][0], merged[-1][1] + size)
            else:
                merged.append((off, size))
        self.free_list = merged


def bcast(ap, shape):
    return ap.to_broadcast(list(shape))


def rev_axis(ap, axis):
    a = [list(x) for x in ap.ap]
    st, n = a[axis]
    a[axis] = [-st, n]
    return AP(ap.tensor, ap.offset + st * (n - 1), a)


def swap_ri(ap):
    a = [list(x) for x in ap.ap]
    assert len(a) == 3 and a[1][1] == 2
    st = a[1][0]
    return AP(ap.tensor, ap.offset + st, [a[0], [-st, 2], a[2]])

def build_program(flags=FLAGS, debug=None):
    kb = KB()
    nc = kb.nc
    NL = flags["layers"]

    def din(name, shape, dt=F32):
        return kb.dram(name, shape, dt, "ExternalInput")

    def dout(name, shape, dt=F32):
        return kb.dram(name, shape, dt, "ExternalOutput")

    x_d = din("x", [T, D])
    cvec_d = din("cvec", [P, KC])
    w_ada_d = din("w_ada", [L, D, 6 * D])
    b_ada_d = din("b_ada_fm", [P, L, 48])
    n1w_d = din("n1w", [P, L, KC])
    n2w_d = din("n2w", [P, L, KC])
    nfw_d = din("nfw", [P, KC])
    w_in_d = din("w_in", [L, D, PIN])
    w_fo_d = din("w_fourier", [L, 256, 256])
    w_glu_d = din("w_glu", [L, 384, 768])
    w_out_d = din("w_out", [L, D, D])
    w_gate_d = din("w_gate", [L, D, DFF])
    w_up_d = din("w_up", [L, D, DFF])
    w_down_d = din("w_down", [L, DFF, D])
    identf_d = din("identf", [P, P])
    csc_d = din("csc", [P, 256], BF16)
    posmat_d = din("posmat", [NT, P, 2, T], BF16)
    cmask_d = din("cmask", [P, 5, P])
    keepcol_d = din("keepcol", [P, 1])
    keeprow_d = din("keeprow", [1, NT])
    bg_d = din("bg_bc", [P, L, 16])
    mnw_d = din("mnw_bc", [P, L, 384])
    mC0_d = din("mC0", [DH, L, 2, H, 97])
    mM0_d = din("mM0", [1, L, 8])
    s5p_d = din("s5p", [P, L, 3, G])
    s5bc_d = din("s5bc", [P, L, 4, G, GC])
    s5dcol_d = din("s5dcol", [P, L, G])
    s5st0_d = din("s5st0", [P, L, 2, G])
    y_d = dout("y", [T, D])
    newC_d = dout("newC", [8, L, 2, H, DH, DH])
    newn_d = dout("newn", [8, L, 2, H, DH])
    newm_d = dout("newm", [8, L, 2, H])
    news5_d = [dout("news5re", [8, L, 2, G, SP_]), dout("news5im", [8, L, 2, G, SP_])]

    dbg_d = {}
    if debug:
        for name, (shape, dt) in debug.items():
            dbg_d[name] = dout("dbg_" + name, shape, dt)

    X = kb.sb("X", [P, KC, T], F32)
    identf = kb.sb("identf_sb", [P, P], F32)
    identb = kb.sb("identb", [P, P], BF16)
    onesb = kb.sb("onesb", [P, P], BF16)
    onesf = kb.sb("onesf", [P, P], F32)
    csc = kb.sb("csc_sb", [P, 256], BF16)
    cmask = kb.sb("cmask_sb", [P, 5, P], F32)
    maskb = kb.sb("maskb", [P, 2, P], BF16)
    Jb = kb.sb("Jb", [P, P], BF16)
    keepcol = kb.sb("keepcol_sb", [P, 1], F32)
    keeprow = kb.sb("keeprow_sb", [1, NT], F32)
    cvec = kb.sb("cvec_sb", [P, KC], F32)
    csil = kb.sb("csil", [P, KC], BF16)
    b_ada = kb.sb("b_ada_sb", [P, L, 48], F32)
    n1w = kb.sb("n1w_sb", [P, L, KC], F32)
    n2w = kb.sb("n2w_sb", [P, L, KC], F32)
    nfw = kb.sb("nfw_sb", [P, KC], F32)
    mods = kb.sb("mods", [P, L, 6, KC], F32)
    wn = kb.sb("wn", [P, L, 2, KC], F32)
    banks = [kb.ps("bank%d" % i, [P, 512], F32) for i in range(8)]
    ar = Arena(kb, ARENA_BYTES)
    trile, trige, s5mF, s5mB = cmask[:, 0, :], cmask[:, 1, :], cmask[:, 2, :], cmask[:, 3, :]

    bstate = {"i": 0}

    def nextbank():
        b = bstate["i"]
        bstate["i"] = (b + 1) % 8
        return b

    evq = {"i": 0}

    def evac_eng():
        evq["i"] ^= 1
        return "act" if evq["i"] else "dve"

    def mm(out, lhsT, rhs, start, stop, reads, writes):
        return kb.op("pe", lambda e: e.matmul(out, lhsT, rhs, start=start, stop=stop), reads=reads, writes=writes,
                     pe_acc=True)

    def tr(out, in_, ident, reads, writes):
        return kb.op("pe", lambda e: e.transpose(out, in_, ident), reads=reads, writes=writes, pe_acc=True)

    import os as _os
    PSUB = _os.environ.get("KPOOL", "pool")

    def copy_to(eng, out, in_, reads, writes):
        if eng == "pool":
            eng = PSUB
        if eng == "act":
            return kb.op("act", lambda e: e.activation(out=out, in_=in_, func=AF.Copy), reads=reads, writes=writes)
        return kb.op(eng, lambda e: e.tensor_copy(out=out, in_=in_), reads=reads, writes=writes)

    def tt(eng, out, in0, in1, op, reads, writes):
        if eng == "pool":
            eng = PSUB
        return kb.op(eng, lambda e: e.tensor_tensor(out=out, in0=in0, in1=in1, op=op), reads=reads, writes=writes)

    def ts(eng, out, in0, s1, s2, op0, op1, reads, writes):
        if eng == "pool":
            eng = PSUB
        if op1 is None:
            return kb.op(eng, lambda e: e.tensor_scalar(out=out, in0=in0, scalar1=s1, scalar2=None, op0=op0),
                         reads=reads, writes=writes)
        return kb.op(eng, lambda e: e.tensor_scalar(out=out, in0=in0, scalar1=s1, scalar2=s2, op0=op0, op1=op1),
                     reads=reads, writes=writes)

    def stt(out, in0, scalar, in1, op0, op1, reads, writes):
        return kb.op("dve", lambda e: e.scalar_tensor_tensor(out=out, in0=in0, scalar=scalar, in1=in1, op0=op0,
                                                             op1=op1), reads=reads, writes=writes)

    def act(out, in_, func, reads, writes, scale=1.0, bias=None):
        if bias is None:
            return kb.op("act", lambda e: e.activation(out=out, in_=in_, func=func, scale=scale), reads=reads,
                         writes=writes)
        return kb.op("act", lambda e: e.activation(out=out, in_=in_, func=func, scale=scale, bias=bias),
                     reads=reads, writes=writes)

    def dbg_out(name, ap_sb, keyreads=()):
        if debug and name in dbg_d:
            kb.barrier()
            kb.dma("sp", dbg_d[name], ap_sb, reads=list(keyreads), writes=[("dbg", name)])

    kb.dma("sp", identf[:], identf_d, writes=["identf"])
    kb.dma("sp", csc[:], csc_d, writes=["csc"])
    kb.dma("sp", cvec[:], cvec_d, writes=["cvec"])
    kb.dma("sp", b_ada[:], b_ada_d, writes=["b_ada"])
    kb.dma("sp", n1w[:], n1w_d, writes=["n1w"])
    kb.dma("sp", n2w[:], n2w_d, writes=["n2w"])
    kb.dma("sp", nfw[:], nfw_d, writes=["nfw"])
    kb.dma("sp", cmask[:], cmask_d, writes=["cmask"])
    kb.dma("sp", keepcol[:], keepcol_d, writes=["keepcol"])
    kb.dma("sp", keeprow[:], keeprow_d, writes=["keeprow"])
    kb.op("dve", lambda e: e.memset(onesb[:], 1.0), writes=["onesb"])
    kb.op("dve", lambda e: e.memset(onesf[:], 1.0), writes=["onesf"])
    copy_to("dve", identb[:], identf[:], ["identf"], ["identb"])
    copy_to("dve", maskb[:], cmask[:, 0:2, :], ["cmask"], ["maskb"])
    copy_to("dve", Jb[:], cmask[:, 4, :], ["cmask"], ["Jb"])
    act(csil[:], cvec[:], AF.Silu, ["cvec"], ["csil"])

    def ada_dma(l, i, WA, bg=False):
        src = w_ada_d[l, :, i * D:(i + 1) * D].rearrange("(k p) n -> p k n", p=P)
        kb.dma("pool", WA, src, writes=[("WA", id(WA))], bg=bg)

    def ada_compute(l, i, WA):
        b = nextbank()
        for m in range(KC):
            for k in range(KC):
                mm(banks[b][:, m:m + 1], WA[:, k, m * P:(m + 1) * P], csil[:, k:k + 1], k == 0, k == KC - 1,
                   reads=[("WA", id(WA)), "csil"], writes=[("ps", b)])
        tt("dve", mods[:, l, i, :], banks[b][:, 0:KC], b_ada[:, l, i * KC:(i + 1) * KC], ALU.add,
           [("ps", b), "b_ada"], [("mods", l, i)])
        if i in (1, 4):
            j = 0 if i == 1 else 1
            nw = n1w if i == 1 else n2w
            stt(wn[:, l, j, :], mods[:, l, i, :], 1.0, nw[:, l, :], ALU.add, ALU.mult,
                [("mods", l, i), "n1w", "n2w"], [("wn", l, j)])

    def rms_norm(l, j, XN):
        xsq = ar.alloc((KC, 512), BF16)
        rstd = ar.alloc((512,), F32)
        tmp = [ar.alloc((512,), F32) for _ in range(2)]
        for nb in range(NB):
            sl = slice(nb * 512, (nb + 1) * 512)
            for k in range(KC):
                act(xsq[:, k, :], X[:, k, sl], AF.Square, [("X", nb)], [("xsq", k)])
            b = nextbank()
            for k in range(KC):
                mm(banks[b][:], onesb[:], xsq[:, k, :], k == 0, k == KC - 1, reads=["onesb", ("xsq", k)],
                   writes=[("ps", b)])
            act(rstd, banks[b][:], AF.Sqrt, [("ps", b)], ["rstd"], scale=1.0 / D, bias=EPS)
            kb.op("dve", lambda e: e.reciprocal(out=rstd, in_=rstd), reads=["rstd"], writes=["rstd"])
            for k in range(KC):
                if l is None:
                    stt(X[:, k, sl], X[:, k, sl], nfw[:, k:k + 1], rstd, ALU.mult, ALU.mult,
                        [("X", nb), "nfw", "rstd"], [("X", nb)])
                else:
                    tq = tmp[k % 2]
                    stt(tq, X[:, k, sl], wn[:, l, j, k:k + 1], rstd, ALU.mult, ALU.mult,
                        [("X", nb), ("wn", l, j), "rstd"], [("ntmp", k % 2)])
                    act(XN[:, k, sl], tq, AF.Identity, [("ntmp", k % 2), ("mods", l, 3 * j)], [("XN", k, nb)],
                        bias=mods[:, l, 3 * j, k:k + 1])
        ar.free(xsq, rstd, *tmp)
        kb.barrier()

    def load_w(dst, src, key, q="pool"):
        return kb.dma(q, dst, src, writes=[key])

    def proj_fm(W, wkey, nk, actf, akey, mchunks, evac):
        for nb in range(NB):
            for mi in mchunks:
                b = nextbank()
                for k in range(nk):
                    mm(banks[b][:], W[:, k, mi * P:(mi + 1) * P], actf(k, nb), k == 0, k == nk - 1,
                       reads=[wkey, akey(k, nb)], writes=[("ps", b)])
                evac(b, mi, nb)

    def resid_evac(l, gi):
        def f(b, mi, nb):
            sl = slice(nb * 512, (nb + 1) * 512)
            stt(X[:, mi, sl], banks[b][:], mods[:, l, gi, mi:mi + 1], X[:, mi, sl], ALU.mult, ALU.add,
                [("ps", b), ("mods", l, gi), ("X", nb)], [("X", nb)])
        return f

    def fourier_inproj(l, XN):
        Wf = ar.alloc((KC, 256), BF16)
        load_w(Wf, w_in_d[l, :, 0:256].rearrange("(k p) n -> p k n", p=P), "Wf")
        zfT = ar.alloc((2, T), BF16)

        def ev_zf(b, mi, nb):
            copy_to(evac_eng(), zfT[:, mi, nb * 512:(nb + 1) * 512], banks[b][:], [("ps", b)], [("zfT", nb)])
        proj_fm(Wf, "Wf", KC, lambda k, nb: XN[:, k, nb * 512:(nb + 1) * 512], lambda k, nb: ("XN", k, nb), range(2),
                ev_zf)
        ar.free(Wf)
        return zfT

    def fourier_core(l, zfT):
        foT = ar.alloc((2, T), BF16, top=True)
        Wfo = ar.alloc((2, 256), BF16)
        load_w(Wfo, w_fo_d[l].rearrange("(k p) n -> p k n", p=P), "Wfo")
        ZCS = ar.alloc((NT, 512), BF16)
        NR = 4
        PM = [ar.alloc((2, T // 2), BF16) for _ in range(NR)]
        for t in range(NT):
            b = nextbank()
            for kc in range(2):
                mm(banks[b][:, kc * 256:(kc + 1) * 256], zfT[:, kc, t * P:(t + 1) * P], csc[:], True, True,
                   reads=[("zfT", t // 4), "csc"], writes=[("ps", b)])
            copy_to(evac_eng(), ZCS[:, t, :], banks[b][:], [("ps", b)], [("ZCS", t)])
        kb.barrier()
        yT = zfT
        it = 0
        for hp in range(2):
            for ti in range(NT):
                pm = PM[it % NR]
                kb.dma("sp", pm, posmat_d[ti][:, :, hp * 1024:(hp + 1) * 1024], writes=[("PM", it % NR)])
                for cs in range(2):
                    for fc in range(2):
                        for nbl in range(2):
                            b = hp * 4 + fc * 2 + nbl
                            mm(banks[b][:], ZCS[:, ti, fc * 256 + cs * P: fc * 256 + (cs + 1) * P],
                               pm[:, cs, nbl * 512:(nbl + 1) * 512], ti == 0 and cs == 0, ti == NT - 1 and cs == 1,
                               reads=[("ZCS", ti), ("PM", it % NR)], writes=[("ps", b)])
                it += 1
            for fc in range(2):
                for nbl in range(2):
                    b = hp * 4 + fc * 2 + nbl
                    nb = hp * 2 + nbl
                    copy_to(evac_eng(), yT[:, fc, nb * 512:(nb + 1) * 512], banks[b][:], [("ps", b)], [("zfT", nb)])
        bstate["i"] = 0

        def ev_fo(b, mi, nb):
            copy_to(evac_eng(), foT[:, mi, nb * 512:(nb + 1) * 512], banks[b][:], [("ps", b)], [("foT", nb)])
        proj_fm(Wfo, "Wfo", 2, lambda k, nb: yT[:, k, nb * 512:(nb + 1) * 512], lambda k, nb: ("zfT", nb), range(2),
                ev_fo)
        ar.free(Wfo, ZCS, *PM, zfT)
        kb.barrier()
        return foT

    def mlstm_inproj(l, XN):
        Qt = ar.alloc((NT, 384), BF16)
        Kt = ar.alloc((NT, 384), BF16)
        Vt = ar.alloc((NT, 384), BF16)
        ZO = ar.alloc((NT, 384), BF16)
        GT = ar.alloc((NT, 16), F32)
        bg = ar.alloc((16,), F32)
        mnw = ar.alloc((384,), F32)
        kb.dma("sp", bg, bg_d[:, l, :], writes=["bg"])
        kb.dma("sp", mnw, mnw_d[:, l, :], writes=["mnw"])
        groups = [(256, 640, "q"), (640, 1024, "k"), (1024, 1408, "v"), (1408, 1808, "go")]
        Wp = [ar.alloc((KC, 400), BF16) for _ in range(2)]
        import os
        groups = groups[:int(os.environ.get("KIPG", "4"))]
        for gi, (c0, c1, kind) in enumerate(groups):
            W = Wp[gi % 2]
            n = c1 - c0
            load_w(W[:, :, 0:n], w_in_d[l, :, c0:c1].rearrange("(k p) n -> p k n", p=P), ("Wp", gi % 2))
            for t in range(NT):
                b = nextbank()
                for k in range(KC):
                    mm(banks[b][:, 0:n], XN[:, k, t * P:(t + 1) * P], W[:, k, 0:n], k == 0, k == KC - 1,
                       reads=[("Wp", gi % 2), ("XN", k, t // 4)], writes=[("ps", b)])
                if kind == "q":
                    copy_to(evac_eng(), Qt[:, t, :], banks[b][:, 0:384], [("ps", b)], [("Qt", t)])
                elif kind == "k":
                    act(Kt[:, t, :], banks[b][:, 0:384], AF.Copy, [("ps", b)], [("Kt", t)], scale=float(DH) ** -0.5)
                elif kind == "v":
                    copy_to(evac_eng(), Vt[:, t, :], banks[b][:, 0:384], [("ps", b)], [("Vt", t)])
                else:
                    KGO = int(os.environ.get("KGO", "7"))
                    if KGO & 1:
                        tt("dve", GT[:, t, :], banks[b][:, 0:16], bg, ALU.add, [("ps", b), "bg"], [("GT", t)])
                    if KGO & 2:
                        act(ZO[:, t, :], banks[b][:, 16:400], AF.Sigmoid, [("ps", b)], [("ZO", t)])
                    if KGO & 4:
                        tt("pool", ZO[:, t, :], ZO[:, t, :], mnw, ALU.mult, [("ZO", t), "mnw"], [("ZO", t)])
        ar.free(*Wp, bg, mnw)
        kb.barrier()
        return dict(Qt=Qt, Kt=Kt, Vt=Vt, ZO=ZO, GT=GT)

    def mlstm_core(l, mb):
        Qt, Kt, Vt, ZO, GT = mb["Qt"], mb["Kt"], mb["Vt"], mb["ZO"], mb["GT"]
        import os
        STAGE = int(os.environ.get("KSTAGE", "9"))

        def bail(bufs):
            ar.free(*bufs)
            kb.barrier()
            moT_ = ar.alloc((3, T), BF16)
            kb.op("dve", lambda e: e.memset(moT_, 0.0), writes=[("moT", i) for i in range(NB)])
            kb.barrier()
            return moT_
        if STAGE < 1:
            return bail([Qt, Kt, Vt, ZO, GT])
        allT = [("GT", t) for t in range(NT)]
        SPl = ar.alloc((2, NT, H), F32)
        IG = ar.alloc((2, NT, H), F32)
        A = ar.alloc((128,), F32)
        NBc = ar.alloc((128,), F32)
        Wt = ar.alloc((128,), F32)
        FL = ar.alloc((128,), F32)
        MMbc = ar.alloc((128,), F32)
        INbc = ar.alloc((128,), F32)
        COLS = ar.alloc((2,), F32)
        ROW = ar.alloc((256,), F32)
        MP = ar.alloc((128,), F32)
        MMr = ar.alloc((128,), F32)
        MN = ar.alloc((128,), F32)
        INr = ar.alloc((128,), F32)
        MI = ar.alloc((8,), F32)
        kb.dma("sp", MI[0:1, :], mM0_d[:, l, :], writes=["MI"])
        GTv = GT
        for d in range(2):
            act(SPl[:, d, :, :], GTv[:, :, d * 8 + 4:d * 8 + 8], AF.Exp, allT, [("SPl", d)], scale=-1.0)
            act(SPl[:, d, :, :], SPl[:, d, :, :], AF.Ln, [("SPl", d)], [("SPl", d)], bias=1.0)
            copy_to("dve", IG[:, d, :, :], GTv[:, :, d * 8:d * 8 + 4], allT, [("IG", d)])
        SPf = SPl.rearrange("p d t h -> p (d t h)")
        IGf = IG.rearrange("p d t h -> p (d t h)")
        b = nextbank()
        mm(banks[b][:, 0:64], trile, SPf[:, 0:64], True, True, ["cmask", ("SPl", 0)], [("ps", b)])
        mm(banks[b][:, 64:128], trige, SPf[:, 64:128], True, True, ["cmask", ("SPl", 1)], [("ps", b)])
        copy_to("dve", NBc, banks[b][:, 0:128], [("ps", b)], ["NBc"])
        tt("dve", A, IGf, NBc, ALU.add, [("IG", 0), ("IG", 1), "NBc"], ["A"])
        b = nextbank()
        tr(banks[b][:, 0:128], A, identf[:], ["A", "identf"], [("ps", b)])
        kb.op("dve", lambda e: e.tensor_reduce(out=COLS[:, 0:1], in_=banks[b][:, 0:128], axis=AX.X, op=ALU.max),
              reads=[("ps", b)], writes=[("COLS", 0)])
        b2 = nextbank()
        mm(banks[b2][:, 0:1], SPf, onesf[:, 0:1], True, True, [("SPl", 0), ("SPl", 1), "onesf"], [("ps", b2)])
        ts("dve", COLS[:, 1:2], banks[b2][:, 0:1], -1.0, None, ALU.mult, None, [("ps", b2)], [("COLS", 1)])
        b = nextbank()
        tr(banks[b][0:1, 0:128], COLS[:, 0:1], identf[:], [("COLS", 0), "identf"], [("ps", b)])
        tr(banks[b][0:1, 128:256], COLS[:, 1:2], identf[:], [("COLS", 1), "identf"], [("ps", b)])
        copy_to("dve", ROW[0:1, :], banks[b][0:1, 0:256], [("ps", b)], ["ROW"])
        for d in range(2):
            for n in range(NT):
                t = n if d == 0 else NT - 1 - n
                c = (d * NT + t) * H
                if n == 0:
                    copy_to("dve", MP[0:1, c:c + H], MI[0:1, d * H:(d + 1) * H], ["MI"], [("MP", d)])
                tt("dve", MMr[0:1, c:c + H], MP[0:1, c:c + H], ROW[0:1, c:c + H], ALU.max, [("MP", d), "ROW"],
                   [("MMr", d)])
                tt("dve", MN[0:1, c:c + H], MMr[0:1, c:c + H], ROW[0:1, 128 + c:128 + c + H], ALU.add,
                   [("MMr", d), "ROW"], [("MN", d)])
                if n + 1 < NT:
                    t2 = n + 1 if d == 0 else NT - 2 - n
                    c2 = (d * NT + t2) * H
                    ts("dve", MP[0:1, c2:c2 + H], MN[0:1, c:c + H], keeprow[0:1, n + 1:n + 2], None, ALU.mult, None,
                       [("MN", d), "keeprow"], [("MP", d)])
        chain = [("MP", 0), ("MP", 1), ("MMr", 0), ("MMr", 1)]
        tt("dve", INr[0:1, :], MP[0:1, :], MMr[0:1, :], ALU.subtract, chain, ["INr"])
        act(INr[0:1, :], INr[0:1, :], AF.Exp, ["INr"], ["INr"])
        b = nextbank()
        mm(banks[b][:, 0:128], onesf[0:1, :], MMr[0:1, :], True, True, ["onesf"] + chain, [("ps", b)])
        mm(banks[b][:, 128:256], onesf[0:1, :], INr[0:1, :], True, True, ["onesf", "INr"], [("ps", b)])
        copy_to("dve", MMbc, banks[b][:, 0:128], [("ps", b)], ["MMbc"])
        copy_to("act", INbc, banks[b][:, 128:256], [("ps", b)], ["INbc"])
        tt("dve", Wt, A, MMbc, ALU.subtract, ["A", "MMbc"], ["Wt"])
        act(Wt, Wt, AF.Exp, ["Wt"], ["Wt"])
        tt("dve", FL, NBc, MMbc, ALU.subtract, ["NBc", "MMbc"], ["FL"])
        act(FL, FL, AF.Exp, ["FL"], ["FL"])
        import os
        LVL = int(os.environ.get("KLVL", "9"))
        for d in range(2):
            if LVL < 3:
                break
            src = MN[0:1, d * 64:(d + 1) * 64].rearrange("p (s two h) -> p s two h", two=2, h=H)[:, :, 1 - d, :]
            if d == 0:
                dst = newm_d[:, l, d, :].rearrange("(o s) h -> o s h", o=1)
                kb.dma("sp", dst, src, reads=[("MN", d)], writes=[("newm", l, d)])
            else:
                dst = newm_d[:, l, d, :].rearrange("(o s) h -> o s h", o=1)
                kb.dma("sp", dst, src, reads=[("MN", d)], writes=[("newm", l, d)])

        if STAGE < 2:
            return bail([Qt, Kt, Vt, ZO, GT, SPl, IG, A, NBc, Wt, FL, MMbc, INbc, COLS, ROW, MP, MMr, MN, INr, MI])
        HS = ar.alloc((NT, 384), BF16)
        CST = ar.alloc((2, H, 97), F32)
        kb.dma("sp", CST[0:DH], mC0_d[:, l], writes=[("CST", 0), ("CST", 1)])
        CSb = [ar.alloc((H, 97), BF16) for _ in range(2)]
        QT = [ar.alloc((H, 128), BF16) for _ in range(2)]
        KT = [ar.alloc((H, 128), BF16) for _ in range(2)]
        SM = [ar.alloc((H, 128), BF16) for _ in range(2)]
        VE = [ar.alloc((H, 97), BF16) for _ in range(2)]
        dd = [ar.alloc((H,), F32) for _ in range(2)]
        hp = [ar.alloc((H, DH), F32) for _ in range(2)]
        CAP = [ar.alloc((H, 97), F32) for _ in range(2)]
        bcs = {}

        def stage_a(n, d):
                t = n if d == 0 else NT - 1 - n
                c = (d * NT + t) * H
                bq = nextbank()
                bqv = banks[bq][:].bitcast(BF16)
                for h in range(H):
                    tr(bqv[0:DH, h * P:(h + 1) * P], Qt[:, t, h * DH:(h + 1) * DH], identb[:], [("Qt", t), "identb"],
                       [("ps", bq)])
                    tr(bqv[0:DH, 512 + h * P:512 + (h + 1) * P], Kt[:, t, h * DH:(h + 1) * DH], identb[:],
                       [("Kt", t), "identb"], [("ps", bq)])
                copy_to("act", QT[d][0:DH].rearrange("p h n -> p (h n)"), bqv[0:DH, 0:512], [("ps", bq)], [("QT", d)])
                copy_to("dve", KT[d][0:DH].rearrange("p h n -> p (h n)"), bqv[0:DH, 512:1024], [("ps", bq)],
                        [("KT", d)])
                bs = nextbank()
                for h in range(H):
                    mm(banks[bs][:, h * P:(h + 1) * P], KT[d][0:DH, h, :], QT[d][0:DH, h, :], True, True,
                       [("KT", d), ("QT", d)], [("ps", bs)])
                tt("dve", SM[d], banks[bs][:].rearrange("p (h n) -> p h n", h=H),
                   bcast(maskb[:, d:d + 1, :], [P, H, P]), ALU.mult, [("ps", bs), "maskb"], [("SM", d)])
                tt("pool", VE[d][:, :, 0:DH], Vt[:, t, :].rearrange("p (h e) -> p h e", h=H),
                   bcast(Wt[:, c:c + H].unsqueeze(2), [P, H, DH]), ALU.mult, [("Vt", t), "Wt"], [("VE", d)])
                copy_to("pool", VE[d][:, :, DH:DH + 1], Wt[:, c:c + H].unsqueeze(2), ["Wt", ("VE", d)], [("VE", d)])
                bc = nextbank()
                bcs[d] = bc
                for h in range(H):
                    mm(banks[bc][0:DH, h * 97:(h + 1) * 97], Kt[:, t, h * DH:(h + 1) * DH], VE[d][:, h, :], True, True,
                       [("Kt", t), ("VE", d)], [("ps", bc)])

        def stage_b(n, d):
                t = n if d == 0 else NT - 1 - n
                c = (d * NT + t) * H
                for h in range(H):
                    act(CSb[d][0:DH, h, :], CST[0:DH, d, h, :], AF.Copy, [("CST", d), "INbc"], [("CSb", d)],
                        scale=INbc[0:DH, c + h:c + h + 1])
                bc = bcs[d]
                for h in range(H):
                    stt(CST[0:DH, d, h, :], CST[0:DH, d, h, :], INbc[0:DH, c + h:c + h + 1],
                        banks[bc][0:DH, h * 97:(h + 1) * 97], ALU.mult, ALU.add,
                        [("CST", d), "INbc", ("ps", bc)], [("CST", d)])
                if n % 2 == 1:
                    kcap = n // 2
                    seq = kcap if d == 0 else 7 - kcap
                    copy_to("act", CAP[d][0:DH], CST[0:DH, d], [("CST", d)], [("CAP", d)])
                    if LVL >= 1:
                        kb.dma("sp", newC_d[seq, l, d].rearrange("h k e -> k h e"), CAP[d][0:DH, :, 0:DH],
                               reads=[("CAP", d)], writes=[("newC", seq, l, d)])
                    if LVL >= 2:
                        with nc.allow_non_contiguous_dma(reason="small state vector"):
                            kb.dma("sp", newn_d[seq, l, d].rearrange("h k -> k h"), CAP[d][0:DH, :, DH],
                                   reads=[("CAP", d)], writes=[("newn", seq, l, d)])
                    if n + 1 < NT:
                        ts("dve", CST[0:DH, d], CST[0:DH, d], keepcol[0:DH, 0:1], None, ALU.mult, None,
                           [("CST", d), "keepcol"], [("CST", d)])
                bn = nextbank()
                for h in range(H):
                    mm(banks[bn][:, h * 97:(h + 1) * 97], SM[d][:, h, :], VE[d][:, h, :], True, False,
                       [("SM", d), ("VE", d)], [("ps", bn)])
                    mm(banks[bn][:, h * 97:(h + 1) * 97], QT[d][0:DH, h, :], CSb[d][0:DH, h, :], False, True,
                       [("QT", d), ("CSb", d)], [("ps", bn)])
                ndv = banks[bn][:, 0:H * 97].rearrange("p (h e) -> p h e", h=H)
                act(dd[d], ndv[:, :, DH], AF.Abs, [("ps", bn)], [("dd", d)])
                tt("dve", dd[d], dd[d], FL[:, c:c + H], ALU.max, [("dd", d), "FL"], [("dd", d)])
                kb.op("dve", lambda e: e.reciprocal(out=dd[d], in_=dd[d]), reads=[("dd", d)], writes=[("dd", d)])
                if n < NT // 2:
                    tt("dve", HS[:, t, :].rearrange("p (h e) -> p h e", h=H), ndv[:, :, 0:DH],
                       bcast(dd[d].unsqueeze(2), [P, H, DH]), ALU.mult, [("ps", bn), ("dd", d)], [("HS", t)])
                else:
                    tt("dve", hp[d], ndv[:, :, 0:DH], bcast(dd[d].unsqueeze(2), [P, H, DH]), ALU.mult,
                       [("ps", bn), ("dd", d)], [("hp", d)])
                    tt("pool", HS[:, t, :].rearrange("p (h e) -> p h e", h=H),
                       HS[:, t, :].rearrange("p (h e) -> p h e", h=H), hp[d], ALU.add, [("hp", d), ("HS", t)],
                       [("HS", t)])

        its = [(n, d) for n in range(NT) for d in range(2)]
        stage_a(*its[0])
        for i_ in range(len(its)):
            if i_ + 1 < len(its):
                stage_a(*its[i_ + 1])
            stage_b(*its[i_])
        ar.free(Qt, Kt, Vt, SPl, IG, A, NBc, Wt, FL, MMbc, INbc, COLS, ROW, MP, MMr, MN, INr, MI, CST, *CSb, *QT, *KT,
                *SM, *VE, *dd, *hp, *CAP)
        kb.barrier()
        if STAGE < 3:
            return bail([HS, ZO, GT])
        moT = ar.alloc((3, T), BF16)
        sq = [ar.alloc((384,), F32) for _ in range(2)]
        ss = [ar.alloc((H,), F32) for _ in range(2)]
        mo = [ar.alloc((384,), BF16) for _ in range(2)]
        for t in range(NT):
            q = t % 2
            hs = HS[:, t, :]
            tt("pool", sq[q], hs, hs, ALU.mult, [("HS", t)], [("sq", q)])
            kb.op("dve", lambda e: e.tensor_reduce(out=ss[q], in_=sq[q].rearrange("p (h e) -> p h e", h=H), axis=AX.X,
                                                   op=ALU.add), reads=[("sq", q)], writes=[("ss", q)])
            act(ss[q], ss[q], AF.Sqrt, [("ss", q)], [("ss", q)], scale=1.0 / DH, bias=EPS)
            kb.op("dve", lambda e: e.reciprocal(out=ss[q], in_=ss[q]), reads=[("ss", q)], writes=[("ss", q)])
            tt("dve", sq[q].rearrange("p (h e) -> p h e", h=H), hs.rearrange("p (h e) -> p h e", h=H),
               bcast(ss[q].unsqueeze(2), [P, H, DH]), ALU.mult, [("HS", t), ("ss", q), ("sq", q)], [("sq", q)])
            tt("dve", mo[q], sq[q], ZO[:, t, :], ALU.mult, [("sq", q), ("ZO", t)], [("mo", q)])
            bq = nextbank()
            bqv = banks[bq][:].bitcast(BF16)
            for cc in range(3):
                tr(bqv[:, cc * P:(cc + 1) * P], mo[q][:, cc * P:(cc + 1) * P], identb[:], [("mo", q), "identb"],
                   [("ps", bq)])
            copy_to("act", moT[:, :, t * P:(t + 1) * P], bqv[:, 0:384].rearrange("p (c n) -> p c n", c=3),
                    [("ps", bq)], [("moT", t // 4)])
        ar.free(HS, ZO, GT, *sq, *ss, *mo)
        kb.barrier()
        return moT

    prep_cache = {}

    def s5_prep(l, tick=None):
        HALF_PI = math.pi / 2
        GH = G // 2

        def _tick():
            if tick is not None:
                tick()
        prm = ar.alloc((3, G), F32)
        bc4 = ar.alloc((4, G, GC), F32)
        dcol = ar.alloc((G,), F32)
        kb.dma("sp", prm, s5p_d[:, l], writes=["prm"])
        kb.dma("sp", bc4, s5bc_d[:, l], writes=["bc4"])
        kb.dma("sp", dcol, s5dcol_d[:, l], writes=["dcol"])
        sm = {}

        def sv(name):
            if name not in sm:
                sm[name] = ar.alloc((G,), F32)
            return sm[name]
        K1 = ["s5tmp"]

        def e_tt(out, a, bb, op, eng="dve"):
            tt(eng, out, a, bb, op, K1 + ["prm", "bc4"], K1)

        def e_ts(out, a, s1, op0, s2=None, op1=None):
            ts("dve", out, a, s1, s2, op0, op1, K1 + ["prm"], K1)

        lre, lim, lst = prm[:, 0, :], prm[:, 1, :], prm[:, 2, :]
        step = sv("step")
        act(step, lst, AF.Exp, ["prm"], K1)
        mag = sv("mag")
        e_tt(mag, lre, step, ALU.mult)
        act(mag, mag, AF.Exp, K1, K1)
        ang = sv("ang")
        stt(ang, lim, 1.0 / 16, step, ALU.mult, ALU.mult, K1 + ["prm"], K1)
        cs_, sn_ = sv("c"), sv("s")
        halfpi = sv("halfpi")
        kb.op("dve", lambda e: e.memset(halfpi, HALF_PI), writes=K1)
        act(sn_, ang, AF.Sin, K1, K1)
        act(cs_, ang, AF.Sin, K1, K1, bias=halfpi[:, 0:1])
        t1, t2, t3 = sv("t1"), sv("t2"), sv("t3")
        for _ in range(4):
            e_tt(t1, cs_, cs_, ALU.mult)
            e_tt(t2, sn_, sn_, ALU.mult)
            e_tt(t3, cs_, sn_, ALU.mult)
            e_tt(cs_, t1, t2, ALU.subtract)
            e_ts(sn_, t3, 2.0, ALU.mult)
        PW = ar.alloc((9, 2, G), F32)
        kb.op("dve", lambda e: e.memset(PW[:, 0, 0, :], 1.0), writes=K1)
        kb.op("dve", lambda e: e.memset(PW[:, 0, 1, :], 0.0), writes=K1)
        e_tt(PW[:, 1, 0, :], mag, cs_, ALU.mult)
        e_tt(PW[:, 1, 1, :], mag, sn_, ALU.mult)
        lbre, lbim = PW[:, 1, 0, :], PW[:, 1, 1, :]

        def cmul(ore, oim, are, aim, bre, bim, conj_neg_im=False):
            e_tt(t1x(ore), are, bre, ALU.mult)
            e_tt(t2x(ore), aim, bim, ALU.mult)
            e_tt(ore, t1x(ore), t2x(ore), ALU.subtract)
            e_tt(t1x(ore), are, bim, ALU.mult)
            e_tt(t2x(ore), aim, bre, ALU.mult)
            e_tt(oim, t1x(ore), t2x(ore), ALU.add)

        GQ = 6
        big1 = ar.alloc((GQ, 8, GC), F32)
        big2 = ar.alloc((GQ, 8, GC), F32)

        def t1x(like):
            n = 1
            for s_ in like.shape[1:]:
                n *= s_
            v = big1.rearrange("p a b c -> p (a b c)")[:, 0:n]
            return reshape_like(v, like)

        def t2x(like):
            n = 1
            for s_ in like.shape[1:]:
                n *= s_
            v = big2.rearrange("p a b c -> p (a b c)")[:, 0:n]
            return reshape_like(v, like)

        def reshape_like(v, like):
            sh = like.shape[1:]
            if len(sh) == 1:
                return v
            names = " ".join("a%d" % i for i in range(len(sh)))
            kw = {"a%d" % i: sh[i] for i in range(len(sh))}
            v = v.rearrange("p (%s) -> p %s" % (names, names), **kw)
            if like.shape[0] != P:
                v = v[0:like.shape[0]]
            return v

        for k in range(2, 9):
            cmul(PW[:, k, 0, :], PW[:, k, 1, :], PW[:, k - 1, 0, :], PW[:, k - 1, 1, :], lbre, lbim)
        i8re, i8im, den = sv("i8re"), sv("i8im"), sv("den")
        e_tt(t1, PW[:, 8, 0, :], PW[:, 8, 0, :], ALU.mult)
        e_tt(t2, PW[:, 8, 1, :], PW[:, 8, 1, :], ALU.mult)
        e_tt(den, t1, t2, ALU.add)
        kb.op("dve", lambda e: e.reciprocal(out=den, in_=den), reads=K1, writes=K1)
        e_tt(i8re, PW[:, 8, 0, :], den, ALU.mult)
        e_tt(i8im, PW[:, 8, 1, :], den, ALU.mult)
        e_ts(i8im, i8im, -1.0, ALU.mult)
        cfre, cfim, ar_ = sv("cfre"), sv("cfim"), sv("ar_")
        e_ts(ar_, lbre, -1.0, ALU.add)
        e_tt(t1, lre, lre, ALU.mult)
        e_tt(t2, lim, lim, ALU.mult)
        e_tt(den, t1, t2, ALU.add)
        kb.op("dve", lambda e: e.reciprocal(out=den, in_=den), reads=K1, writes=K1)
        e_tt(t1, ar_, lre, ALU.mult)
        e_tt(t2, lbim, lim, ALU.mult)
        e_tt(cfre, t1, t2, ALU.add)
        e_tt(cfre, cfre, den, ALU.mult)
        e_tt(t1, lbim, lre, ALU.mult)
        e_tt(t2, ar_, lim, ALU.mult)
        e_tt(cfim, t1, t2, ALU.subtract)
        e_tt(cfim, cfim, den, ALU.mult)
        _tick()
        Bb = ar.alloc((2, G, GC), F32)
        Cm2 = ar.alloc((2, G, GC), F32)
        gshape = [P, G, GC]
        cmul(Bb[:, 0], Bb[:, 1], bcast(cfre.unsqueeze(2), gshape), bcast(cfim.unsqueeze(2), gshape), bc4[:, 0], bc4[:, 1])
        cmul(Cm2[:, 0], Cm2[:, 1], bcast(i8re.unsqueeze(2), gshape), bcast(i8im.unsqueeze(2), gshape), bc4[:, 2],
             bc4[:, 3])
        PWe = ar.alloc((8, 2, G), F32)
        PWc = ar.alloc((8, 2, G), F32)
        for r in range(8):
            copy_to("dve", PWe[0:64, r], PW[0:64, 7 - r], K1, K1)
            copy_to("dve", PWe[64:128, r], PW[64:128, r], K1, K1)
            copy_to("dve", PWc[0:64, r], PW[0:64, r + 1], K1, K1)
            copy_to("dve", PWc[64:128, r], PW[64:128, 8 - r], K1, K1)
        ET = ar.alloc((G, 2, 128), BF16)
        CPn = ar.alloc((G, 2, 128), BF16)
        GH = G // 2

        def expand(dst, pw, mat_re, mat_im, neg_im):
            for gh in range(G // GQ):
                gs = slice(gh * GQ, (gh + 1) * GQ)
                shp = [P, GQ, 8, GC]
                pre = bcast(pw[:, :, 0, gs].rearrange("p r g -> p g r").unsqueeze(3), shp)
                pim = bcast(pw[:, :, 1, gs].rearrange("p r g -> p g r").unsqueeze(3), shp)
                mre = bcast(mat_re[:, gs, :].unsqueeze(2), shp)
                mim = bcast(mat_im[:, gs, :].unsqueeze(2), shp)
                dre = dst[:, gs, 0, :].rearrange("p g (r c) -> p g r c", r=8)
                dim_ = dst[:, gs, 1, :].rearrange("p g (r c) -> p g r c", r=8)
                e_tt(big1, pre, mre, ALU.mult)
                e_tt(big2, pim, mim, ALU.mult)
                e_tt(dre, big1, big2, ALU.subtract)
                e_tt(big1, pre, mim, ALU.mult)
                e_tt(big2, pim, mre, ALU.mult)
                if neg_im:
                    e_tt(big1, big1, big2, ALU.add)
                    e_ts(dim_, big1, -1.0, ALU.mult)
                else:
                    e_tt(dim_, big1, big2, ALU.add)
        expand(ET, PWe, Bb[:, 0], Bb[:, 1], False)
        _tick()
        expand(CPn, PWc, Cm2[:, 0], Cm2[:, 1], True)
        _tick()
        Emat = ar.alloc((G, 2, 128), BF16, top=True)
        T0 = ar.alloc((G, 128), BF16, top=True)
        for g in range(G):
            if g % 4 == 3:
                _tick()
            bq = nextbank()
            bqv = banks[bq][:].bitcast(BF16)
            for ri in range(2):
                tr(bqv[:, ri * P:(ri + 1) * P], ET[:, g, ri, :], identb[:], K1 + ["identb"], [("ps", bq)])
            copy_to(evac_eng(), Emat[:, g, :, :], bqv[:, 0:256].rearrange("p (r n) -> p r n", r=2), [("ps", bq)],
                    ["Emat"])
            bts = [nextbank(), nextbank()]
            for dd_ in range(2):
                ps_ = slice(dd_ * 64, (dd_ + 1) * 64)
                for ri in range(2):
                    mm(banks[bts[dd_]][:, 0:P], ET[ps_, g, ri, :], CPn[ps_, g, ri, :], ri == 0, ri == 1,
                       K1, [("ps", bts[dd_])])
            ta, tb = big1.rearrange("p a b c -> p (a b c)")[:, 0:128], big2.rearrange("p a b c -> p (a b c)")[:, 0:128]
            tt("dve", ta, banks[bts[0]][:, 0:128], s5mF, ALU.mult, [("ps", bts[0]), "cmask"] + K1, K1)
            tt("dve", tb, banks[bts[1]][:, 0:128], s5mB, ALU.mult, [("ps", bts[1]), "cmask"] + K1, K1)
            tt("dve", ta, ta, tb, ALU.add, K1, K1)
            stt(T0[:, g, :], identf[:], dcol[:, g:g + 1], ta, ALU.mult, ALU.add, K1 + ["identf", "dcol"], ["T0"])
        ar.free(ET, CPn, Bb, Cm2, PWe)
        kb.barrier()
        CP = ar.alloc((G, 2, 128), BF16, top=True)
        expand(CP, PWc, bc4[:, 2], bc4[:, 3], True)
        LreB = ar.alloc((2, G), F32, top=True)
        LimS = ar.alloc((2, G), F32, top=True)
        copy_to("dve", LreB[:, 0, :], PW[:, 8, 0, :], K1, ["Lmul"])
        copy_to("dve", LreB[:, 1, :], PW[:, 8, 0, :], K1, ["Lmul"])
        ts("dve", LimS[:, 0, :], PW[:, 8, 1, :], -1.0, None, ALU.mult, None, K1, ["Lmul"])
        copy_to("dve", LimS[:, 1, :], PW[:, 8, 1, :], K1, ["Lmul"])
        LreB16 = ar.alloc((2, G), F32, top=True)
        LimS16 = ar.alloc((2, G), F32, top=True)
        e_tt(t1, PW[:, 8, 0, :], PW[:, 8, 0, :], ALU.mult)
        e_tt(t2, PW[:, 8, 1, :], PW[:, 8, 1, :], ALU.mult)
        tt("dve", LreB16[:, 0, :], t1, t2, ALU.subtract, K1, ["Lmul"])
        tt("dve", LreB16[:, 1, :], t1, t2, ALU.subtract, K1, ["Lmul"])
        e_tt(t3, PW[:, 8, 0, :], PW[:, 8, 1, :], ALU.mult)
        ts("dve", LimS16[:, 1, :], t3, 2.0, None, ALU.mult, None, K1, ["Lmul"])
        ts("dve", LimS16[:, 0, :], t3, -2.0, None, ALU.mult, None, K1, ["Lmul"])
        LreB32 = ar.alloc((2, G), F32, top=True)
        LimS32 = ar.alloc((2, G), F32, top=True)
        tt("dve", t1, LreB16[:, 0, :], LreB16[:, 0, :], ALU.mult, K1 + ["Lmul"], K1)
        tt("dve", t2, LimS16[:, 1, :], LimS16[:, 1, :], ALU.mult, K1 + ["Lmul"], K1)
        tt("dve", LreB32[:, 0, :], t1, t2, ALU.subtract, K1, ["Lmul"])
        tt("dve", LreB32[:, 1, :], t1, t2, ALU.subtract, K1, ["Lmul"])
        tt("dve", t3, LreB16[:, 0, :], LimS16[:, 1, :], ALU.mult, K1 + ["Lmul"], K1)
        ts("dve", LimS32[:, 1, :], t3, 2.0, None, ALU.mult, None, K1, ["Lmul"])
        ts("dve", LimS32[:, 0, :], t3, -2.0, None, ALU.mult, None, K1, ["Lmul"])
        ar.free(prm, bc4, dcol, PW, PWc, big1, big2, *sm.values())
        kb.barrier()

        return dict(Emat=Emat, T0=T0, CP=CP, LreB=LreB, LimS=LimS, LreB16=LreB16, LimS16=LimS16, LreB32=LreB32,
                    LimS32=LimS32)

    def s5_all(l):
        HALF_PI = math.pi / 2
        import os
        KS5 = int(os.environ.get("KS5", "9"))
        base_live = set(ar.live.keys())

        def bail5():
            for k_ in list(ar.live.keys()):
                if k_ not in base_live:
                    ar.free(ar.live[k_][0])
            kb.barrier()
            so_ = ar.alloc((3, T), BF16)
            kb.op("dve", lambda e: e.memset(so_, 0.0), writes=[("soT", i) for i in range(NB)])
            kb.barrier()
            return so_
        XN = ar.alloc((KC, T), BF16)
        rms_norm(l, 0, XN)
        Ws = ar.alloc((KC, 384), BF16)
        load_w(Ws, w_in_d[l, :, 1808:2192].rearrange("(k p) n -> p k n", p=P), "Ws")
        U = ar.alloc((G, NSB), BF16, top=True)
        Urev = ar.alloc((G, NSB), BF16, top=True)
        Utok = [ar.alloc((G, 128), BF16) for _ in range(2)]
        for half in range(2):
            ut = Utok[half]
            for r in range(8):
                b = nextbank()
                for k in range(KC):
                    lhsT = XN[:, k, half * 1024 + r:(half + 1) * 1024:8]
                    mm(banks[b][:, 0:384], lhsT, Ws[:, k, :], k == 0, k == KC - 1,
                       reads=["Ws"] + [("XN", k, nb) for nb in (2 * half, 2 * half + 1)], writes=[("ps", b)])
                copy_to(evac_eng(), ut[:, :, r * GC:(r + 1) * GC], banks[b][:, 0:384].rearrange("p (g c) -> p g c", g=G),
                        [("ps", b)], [("Utok", half)])
            for g0 in range(0, G, 4):
                b = nextbank()
                for gg in range(4):
                    mm(banks[b][:, gg * P:(gg + 1) * P], ut[:, g0 + gg, :], identb[:], True, True,
                       [("Utok", half), "identb"], [("ps", b)])
                copy_to(evac_eng(), U[:, g0:g0 + 4, half * P:(half + 1) * P],
                        banks[b][:].rearrange("p (g n) -> p g n", g=4), [("ps", b)], [("U", half)])
                b = nextbank()
                for gg in range(4):
                    mm(banks[b][:, gg * P:(gg + 1) * P], ut[:, g0 + gg, :], Jb[:], True, True,
                       [("Utok", half), "Jb"], [("ps", b)])
                copy_to(evac_eng(), Urev[:, g0:g0 + 4, (1 - half) * P:(2 - half) * P],
                        banks[b][:].rearrange("p (g n) -> p g n", g=4), [("ps", b)], [("Urev", 1 - half)])
        ar.free(XN, Ws, *Utok)
        kb.barrier()

        if KS5 < 2:
            return bail5()
        pp = prep_cache.pop(l, None)
        if pp is None:
            pp = s5_prep(l)
        Emat, T0, CP, LreB, LimS = pp["Emat"], pp["T0"], pp["CP"], pp["LreB"], pp["LimS"]
        LreB16, LimS16, LreB32, LimS32 = pp["LreB16"], pp["LimS16"], pp["LreB32"], pp["LimS32"]
        GH = G // 2

        if KS5 < 3:
            return bail5()
        Gs = ar.alloc((NSB, 2, G), BF16)
        for g in range(G):
            b = nextbank()
            for ri in range(2):
                mm(banks[b][0:64, ri * NSB:(ri + 1) * NSB], Emat[:, g, ri, 0:64], U[:, g, :], True, True,
                   ["Emat", ("U", 0), ("U", 1)], [("ps", b)])
                mm(banks[b][64:128, ri * NSB:(ri + 1) * NSB], Emat[:, g, ri, 64:128], Urev[:, g, :], True, True,
                   ["Emat", ("Urev", 0), ("Urev", 1)], [("ps", b)])
            copy_to(evac_eng(), Gs[:, :, :, g].rearrange("p n r -> p r n"),
                    banks[b][:].rearrange("p (r n) -> p r n", r=2), [("ps", b)], ["Gs"])
        ar.free(Emat, Urev)
        kb.barrier()

        if KS5 < 4:
            return bail5()
        HIST = ar.alloc((NSB, 2, G), BF16)
        NST = 4
        NK = NSB // 2
        ST = [ar.alloc((2, G), F32) for _ in range(NST)]
        TA = [ar.alloc((2, G), F32) for _ in range(2)]
        TB = [ar.alloc((2, G), F32) for _ in range(2)]
        CAPs = ar.alloc((8, 2, G), F32)
        G2 = ar.alloc((NK, 2, G), BF16)
        tb1 = ar.alloc((32, 2, G), F32)
        tb2 = ar.alloc((32, 2, G), F32)
        kb.dma("sp", ST[0], s5st0_d[:, l], writes=[("ST", 0, 0), ("ST", 0, 1)])
        shp4 = [P, 32, 2, G]
        for c4 in range(4):
            ge = Gs[:, 64 * c4:64 * c4 + 64:2]
            go = Gs[:, 64 * c4 + 1:64 * c4 + 64:2]
            tt("dve", tb1, ge, bcast(LreB.unsqueeze(1), shp4), ALU.mult, ["Gs", "Lmul"], ["tb1"])
            tt("dve", tb2, rev_axis(ge, 2), bcast(LimS.unsqueeze(1), shp4), ALU.mult, ["Gs", "Lmul"], ["tb2"])
            tt("dve", tb1, tb1, tb2, ALU.add, ["tb1", "tb2"], ["tb1"])
            tt("dve", G2[:, 32 * c4:32 * c4 + 32], tb1, go, ALU.add, ["tb1", "Gs"], ["G2"])
        NJ = NSB // 4
        G4 = ar.alloc((NJ, 2, G), BF16)
        for c2 in range(2):
            ge = G2[:, 64 * c2:64 * c2 + 64:2]
            go = G2[:, 64 * c2 + 1:64 * c2 + 64:2]
            tt("dve", tb1, ge, bcast(LreB16.unsqueeze(1), shp4), ALU.mult, ["G2", "Lmul"], ["tb1"])
            tt("dve", tb2, rev_axis(ge, 2), bcast(LimS16.unsqueeze(1), shp4), ALU.mult, ["G2", "Lmul"], ["tb2"])
            tt("dve", tb1, tb1, tb2, ALU.add, ["tb1", "tb2"], ["tb1"])
            tt("dve", G4[:, 32 * c2:32 * c2 + 32], tb1, go, ALU.add, ["tb1", "G2"], ["G4"])
        gss = [slice(hh * GH, (hh + 1) * GH) for hh in range(2)]
        for j in range(NJ):
            n = 4 * j
            ci, ni = j % NST, (j + 1) % NST
            cur, nxt = ST[ci], ST[ni]
            if n % 32 == 0 and n > 0:
                for hh in range(2):
                    ts("dve", cur[:, :, gss[hh]], cur[:, :, gss[hh]], keepcol[:, 0:1], None, ALU.mult, None,
                       [("ST", ci, hh), "keepcol"], [("ST", ci, hh)])
            copy_to("act", HIST[0:64, n], cur[0:64], [("ST", ci, 0), ("ST", ci, 1)], ["HIST"])
            copy_to("act", HIST[64:128, NSB - 1 - n], cur[64:128], [("ST", ci, 0), ("ST", ci, 1)], ["HIST"])
            for hh in range(2):
                tt("dve", TA[hh][:, :, 0:GH], cur[:, :, gss[hh]], LreB32[:, :, gss[hh]], ALU.mult,
                   [("ST", ci, hh), "Lmul"], [("TA", hh)])
            for hh in range(2):
                tt("dve", TB[hh][:, :, 0:GH], swap_ri(cur[:, :, gss[hh]]), LimS32[:, :, gss[hh]], ALU.mult,
                   [("ST", ci, hh), "Lmul"], [("TB", hh)])
            for hh in range(2):
                tt("dve", TA[hh][:, :, 0:GH], TA[hh][:, :, 0:GH], TB[hh][:, :, 0:GH], ALU.add,
                   [("TA", hh), ("TB", hh)], [("TA", hh)])
            for hh in range(2):
                tt("dve", nxt[:, :, gss[hh]], TA[hh][:, :, 0:GH], G4[:, j, :, gss[hh]], ALU.add,
                   [("TA", hh), "G4"], [("ST", ni, hh)])
            if (j + 1) % 8 == 0:
                copy_to("act", CAPs[:, (j + 1) // 8 - 1], nxt, [("ST", ni, 0), ("ST", ni, 1)], ["CAPs"])
        hshp = [64, 32, 2, G]

        def bulk(xin, xout, gin, Lr, Li, ps_, gkey):
            tt("dve", tb1[ps_], xin, bcast(Lr[ps_].unsqueeze(1), hshp), ALU.mult, ["HIST", "Lmul"], ["tb1"])
            tt("dve", tb2[ps_], rev_axis(xin, 2), bcast(Li[ps_].unsqueeze(1), hshp), ALU.mult, ["HIST", "Lmul"],
               ["tb2"])
            tt("dve", tb1[ps_], tb1[ps_], tb2[ps_], ALU.add, ["tb1", "tb2"], ["tb1"])
            tt("dve", xout, tb1[ps_], gin, ALU.add, ["tb1", gkey], ["HIST"])
        fw, bw = slice(0, 64), slice(64, 128)
        for c2 in range(2):
            lo = 128 * c2
            bulk(HIST[fw, lo:lo + 128:4], HIST[fw, lo + 2:lo + 128:4], G2[fw, 64 * c2:64 * c2 + 64:2], LreB16, LimS16,
                 fw, "G2")
            kmin = 64 - 64 * c2
            bulk(HIST[bw, lo + 3:lo + 128:4], HIST[bw, lo + 1:lo + 128:4],
                 rev_axis(G2[bw, kmin:kmin + 64:2], 1), LreB16, LimS16, bw, "G2")
        for c4 in range(4):
            lo = 64 * c4
            bulk(HIST[fw, lo:lo + 64:2], HIST[fw, lo + 1:lo + 64:2], Gs[fw, lo:lo + 64:2], LreB, LimS, fw, "Gs")
            nmin = 192 - 64 * c4
            bulk(HIST[bw, lo + 1:lo + 64:2], HIST[bw, lo:lo + 64:2], rev_axis(Gs[bw, nmin:nmin + 64:2], 1), LreB, LimS,
                 bw, "Gs")
        ar.free(Gs, G2, G4, tb1, tb2)
        kb.barrier()
        OUTS = ar.alloc((16, 128), F32)
        for k4 in range(4):
            b = nextbank()
            for j in range(4):
                kk = k4 * 4 + j
                kcap, ri = kk // 2, kk % 2
                tr(banks[b][0:G, j * P:(j + 1) * P], CAPs[:, kcap, ri, :], identf[:], ["CAPs", "identf"], [("ps", b)])
            copy_to(evac_eng(), OUTS[0:G, k4 * 4:(k4 + 1) * 4, :], banks[b][0:G, :].rearrange("p (j n) -> p j n", j=4),
                    [("ps", b)], ["OUTS"])
        for kcap in range(8):
            for ri in range(2):
                for d in range(2):
                    seq = kcap if d == 0 else 7 - kcap
                    kb.dma("sp", news5_d[ri][seq, l, d], OUTS[0:G, kcap * 2 + ri, d * 64:(d + 1) * 64],
                           reads=["OUTS"], writes=[("news5", ri, seq, l, d)])
        ar.free(*ST, *TA, *TB, CAPs, LreB, LimS, LreB16, LimS16, LreB32, LimS32)
        kb.barrier()

        if KS5 < 5:
            return bail5()
        Wgl = ar.alloc((3, 768), BF16)
        load_w(Wgl, w_glu_d[l].rearrange("(k p) n -> p k n", p=P), "Wgl")
        yT = ar.alloc((3, T), BF16)
        Yt = ar.alloc((8, 384), BF16)
        gt = [ar.alloc((512,), F32) for _ in range(2)]
        for half in range(2):
            asl = slice(half * P, (half + 1) * P)
            for g0 in range(0, G, 4):
                b = nextbank()
                for gg in range(4):
                    g = g0 + gg
                    osl = banks[b][:, gg * P:(gg + 1) * P]
                    mm(osl, HIST[:, asl, 0, g], CP[:, g, 0, :], True, False, ["HIST", "CP"],
                       [("ps", b)])
                    mm(osl, HIST[:, asl, 1, g], CP[:, g, 1, :], False, False, ["HIST", "CP"],
                       [("ps", b)])
                    mm(osl, U[:, g, asl], T0[:, g, :], False, True, [("U", half), "T0"], [("ps", b)])
                q = (g0 // 4) % 2
                act(gt[q], banks[b][:], AF.Square, [("ps", b)], [("gt", q)])
                ts("dve", gt[q], gt[q], 0.044715, 1.0, ALU.mult, ALU.add, [("gt", q)], [("gt", q)])
                tt("dve", gt[q], gt[q], banks[b][:], ALU.mult, [("gt", q), ("ps", b)], [("gt", q)])
                act(gt[q], gt[q], AF.Sigmoid, [("gt", q)], [("gt", q)], scale=1.5957691216)
                tt("dve", Yt[:, :, g0 * GC:(g0 + 4) * GC].rearrange("p r (g c) -> p g r c", g=4),
                   gt[q].rearrange("p (g r c) -> p g r c", g=4, r=8), banks[b][:].rearrange("p (g r c) -> p g r c", g=4, r=8),
                   ALU.mult, [("gt", q), ("ps", b)], ["Yt"])
            for cc in range(3):
                bq = nextbank()
                bqv = banks[bq][:].bitcast(BF16)
                for r in range(8):
                    tr(bqv[:, r * P:(r + 1) * P], Yt[:, r, cc * P:(cc + 1) * P], identb[:], ["Yt", "identb"],
                       [("ps", bq)])
                copy_to(evac_eng(), yT[:, cc, half * 1024:(half + 1) * 1024].rearrange("p (a r) -> p r a", r=8),
                        bqv.rearrange("p (r a) -> p r a", r=8), [("ps", bq)], [("yT", 2 * half), ("yT", 2 * half + 1)])
        soT = ar.alloc((3, T), BF16, top=True)
        sg = [ar.alloc((512,), BF16) for _ in range(2)]
        for nb in range(NB):
            sl = slice(nb * 512, (nb + 1) * 512)
            for mi in range(3):
                ba, bb_ = nextbank(), nextbank()
                for k in range(3):
                    mm(banks[ba][:], Wgl[:, k, mi * P:(mi + 1) * P], yT[:, k, sl], k == 0, k == 2, ["Wgl", ("yT", nb)],
                       [("ps", ba)])
                for k in range(3):
                    mm(banks[bb_][:], Wgl[:, k, 384 + mi * P:384 + (mi + 1) * P], yT[:, k, sl], k == 0, k == 2,
                       ["Wgl", ("yT", nb)], [("ps", bb_)])
                q = mi % 2
                act(sg[q], banks[bb_][:], AF.Sigmoid, [("ps", bb_)], [("sg", q)])
                tt("dve", soT[:, mi, sl], banks[ba][:], sg[q], ALU.mult, [("ps", ba), ("sg", q)], [("soT", nb)])
        ar.free(Wgl, yT, Yt, *gt, *sg, U, HIST, CP, T0, OUTS)
        kb.barrier()
        return soT

    def outproj(l, foT, moT, soT):
        Wo = ar.alloc((KC, D), BF16)
        load_w(Wo, w_out_d[l].rearrange("(k p) n -> p k n", p=P), "Wo")
        srcs = []
        if foT is not None:
            srcs += [(0, foT, 0, "foT"), (1, foT, 1, "foT")]
        if moT is not None:
            srcs += [(2 + i, moT, i, "moT") for i in range(3)]
        if soT is not None:
            srcs += [(5 + i, soT, i, "soT") for i in range(3)]
        for nb in range(NB):
            sl = slice(nb * 512, (nb + 1) * 512)
            for mi in range(KC):
                b = nextbank()
                for j, (kc, buf, ci, key) in enumerate(srcs):
                    mm(banks[b][:], Wo[:, kc, mi * P:(mi + 1) * P], buf[:, ci, sl], j == 0, j == len(srcs) - 1,
                       ["Wo", (key, nb)], [("ps", b)])
                resid_evac(l, 2)(b, mi, nb)
        ar.free(Wo)
        for buf in (foT, moT, soT):
            if buf is not None:
                ar.free(buf)
        kb.barrier()

    def ffn(l, XN, extra=None):
        NG = NHC // 2
        Wg = [ar.alloc((KC, 256), BF16) for _ in range(2)]
        Wu = [ar.alloc((KC, 256), BF16) for _ in range(2)]
        Wd = [ar.alloc((2, D), BF16) for _ in range(2)]
        Hh = [ar.alloc((2, T), BF16) for _ in range(2)]
        sg = [ar.alloc((512,), BF16) for _ in range(2)]

        def issue(gi):
            s = gi % 2
            c0 = gi * 256
            load_w(Wg[s], w_gate_d[l, :, c0:c0 + 256].rearrange("(k p) n -> p k n", p=P), ("Wg", s))
            load_w(Wu[s], w_up_d[l, :, c0:c0 + 256].rearrange("(k p) n -> p k n", p=P), ("Wu", s))
            load_w(Wd[s], w_down_d[l, c0:c0 + 256, :].rearrange("(k p) n -> p k n", p=P), ("Wd", s))
        issue(0)
        for gi in range(NG):
            s = gi % 2
            if gi + 1 < NG:
                issue(gi + 1)
            if extra is not None:
                extra(gi)
            for j in range(2):
                for nb in range(NB):
                    sl = slice(nb * 512, (nb + 1) * 512)
                    bg_ = nextbank()
                    for k in range(KC):
                        mm(banks[bg_][:], Wg[s][:, k, j * P:(j + 1) * P], XN[:, k, sl], k == 0, k == KC - 1,
                           reads=[("Wg", s), ("XN", k, nb)], writes=[("ps", bg_)])
                    bu = nextbank()
                    for k in range(KC):
                        mm(banks[bu][:], Wu[s][:, k, j * P:(j + 1) * P], XN[:, k, sl], k == 0, k == KC - 1,
                           reads=[("Wu", s), ("XN", k, nb)], writes=[("ps", bu)])
                    q = (j * NB + nb) % 2
                    act(sg[q], banks[bg_][:], AF.Silu, [("ps", bg_)], [("sg", q)])
                    tt("dve", Hh[s][:, j, sl], banks[bu][:], sg[q], ALU.mult, [("ps", bu), ("sg", q)],
                       [("Hh", s, j, nb)])
            for nb in range(NB):
                for mi in range(KC):
                    b = nextbank()
                    for j in range(2):
                        mm(banks[b][:], Wd[s][:, j, mi * P:(mi + 1) * P], Hh[s][:, j, nb * 512:(nb + 1) * 512], j == 0,
                           j == 1, reads=[("Wd", s), ("Hh", s, j, nb)], writes=[("ps", b)])
                    resid_evac(l, 5)(b, mi, nb)
        ar.free(*Wg, *Wu, *Wd, *Hh, *sg)
        kb.barrier()

    WAs = [ar.alloc((KC, D), BF16) for _ in range(2)]
    stage = [ar.alloc((D,), F32) for _ in range(2)]
    ada_dma(0, 0, WAs[0], bg=True)
    ada_dma(0, 1, WAs[1], bg=True)
    for t in range(NT):
        s = t % 2
        kb.dma("sp", stage[s], x_d[t * P:(t + 1) * P, :], writes=[("stage", s)])
        for half in range(2):
            b = nextbank()
            for j in range(4):
                k = half * 4 + j
                tr(banks[b][:, j * P:(j + 1) * P], stage[s][:, k * P:(k + 1) * P], identf[:],
                   [("stage", s), "identf"], [("ps", b)])
            copy_to(evac_eng(), X[:, half * 4:(half + 1) * 4, t * P:(t + 1) * P],
                    banks[b][:].rearrange("p (j n) -> p j n", j=4), [("ps", b)], [("X", t // 4)])
    ar.free(*stage)
    ada_i = [0]

    def ada_tick():
        if ada_i[0] < 6:
            i_ = ada_i[0]
            ada_compute(0, i_, WAs[i_ % 2])
            if i_ + 2 < 6:
                ada_dma(0, i_ + 2, WAs[i_ % 2], bg=True)
            ada_i[0] += 1
    if flags["s5"]:
        prep_cache[0] = s5_prep(0, tick=ada_tick)
    while ada_i[0] < 6:
        ada_tick()
    ar.free(*WAs)
    kb.barrier()

    for l in range(NL):
        foT = moT = soT = None
        if flags["s5"]:
            soT = s5_all(l)
        if flags["fourier"] or flags["mlstm"]:
            XN = ar.alloc((KC, T), BF16)
            rms_norm(l, 0, XN)
            mb = mlstm_inproj(l, XN) if flags["mlstm"] else None
            zfT = fourier_inproj(l, XN) if flags["fourier"] else None
            ar.free(XN)
            kb.barrier()
            if flags["fourier"]:
                foT = fourier_core(l, zfT)
            if flags["mlstm"]:
                moT = mlstm_core(l, mb)
        if foT is not None or moT is not None or soT is not None:
            outproj(l, foT, moT, soT)
        if flags["ffn"]:
            XN = ar.alloc((KC, T), BF16)
            rms_norm(l, 1, XN)
            if l + 1 < NL:
                WAs = [ar.alloc((KC, D), BF16) for _ in range(2)]

                def extra(gi, l=l, WAs=WAs):
                    if gi < 6:
                        ada_dma(l + 1, gi, WAs[gi % 2])
                    if 1 <= gi < 7:
                        ada_compute(l + 1, gi - 1, WAs[(gi - 1) % 2])
                ffn(l, XN, extra)
                ar.free(*WAs)
            else:
                ffn(l, XN)
            ar.free(XN)
            kb.barrier()
        elif l + 1 < NL:
            WAs = [ar.alloc((KC, D), BF16) for _ in range(2)]
            for i in range(6):
                ada_dma(l + 1, i, WAs[i % 2])
                ada_compute(l + 1, i, WAs[i % 2])
            ar.free(*WAs)
            kb.barrier()

    rms_norm(None, None, None)
    stage = [ar.alloc((D,), F32) for _ in range(4)]
    for t in range(NT):
        s = t % 4
        for half in range(2):
            b = nextbank()
            for j in range(4):
                k = half * 4 + j
                tr(banks[b][:, j * P:(j + 1) * P], X[:, k, t * P:(t + 1) * P], identf[:], [("X", t // 4), "identf"],
                   [("ps", b)])
            copy_to(evac_eng(), stage[s][:, half * 512:(half + 1) * 512], banks[b][:], [("ps", b)], [("stage", s, half)])
        kb.dma("sp", y_d[t * P:(t + 1) * P, :], stage[s], reads=[("stage", s, 0), ("stage", s, 1)],
               writes=[("y", t)])
    nc_done = kb.finish()
    build_program.last_peak = ar.peak * 2
    build_program.ninstr = kb.ninstr
    build_program.counts = dict(kb.cnt)
    return nc_done


def fm(v):
    v = np.asarray(v, np.float32)
    n = v.shape[-1] // P
    r = v.reshape(v.shape[:-1] + (n, P))
    return np.ascontiguousarray(np.moveaxis(r, -1, 0))


def dft_consts():
    i = np.arange(64)
    ang = 2 * np.pi * np.outer(i, i) / 64.0
    C64, S64 = np.cos(ang), np.sin(ang)
    Z = np.zeros((64, 64))
    BDC = np.block([[C64, Z], [Z, C64]])
    BDS = np.block([[S64, Z], [Z, S64]])
    csc = np.concatenate([BDC, BDS], axis=1).astype(ml_dtypes.bfloat16)
    t = np.arange(T)
    pos = t % 256
    same = (t[:, None] // 256) == (t[None, :] // 256)
    angp = 2 * np.pi * np.outer(pos, pos) / 256.0
    nrm = 1.0 / math.sqrt(256 * 64)
    pc = np.where(same, np.cos(angp), 0.0) * nrm
    psn = np.where(same, -np.sin(angp), 0.0) * nrm
    pm_prompt = np.stack([pc, psn], axis=1).reshape(NT, P, 2, T).astype(ml_dtypes.bfloat16)
    r, c = t // 64, t % 64
    angs = 2 * np.pi * (np.outer(r, r) / 32.0 + np.outer(c, c) / 64.0)
    nrm = 1.0 / math.sqrt(32 * 64 * 64)
    pm_sample = np.stack([np.cos(angs) * nrm, -np.sin(angs) * nrm], axis=1).reshape(NT, P, 2, T).astype(ml_dtypes.bfloat16)
    return csc, pm_prompt, pm_sample


def mask_consts():
    i = np.arange(P)
    trile = (i[:, None] <= i[None, :]).astype(np.float32)
    trige = (i[:, None] >= i[None, :]).astype(np.float32)
    r = i // GC
    mF = (r[:, None] <= r[None, :]).astype(np.float32)
    mB = (r[:, None] >= r[None, :]).astype(np.float32)
    J = np.eye(P, dtype=np.float32)[::-1].copy()
    return np.ascontiguousarray(np.stack([trile, trige, mF, mB, J], axis=1))


def rep_dp(a):
    a = np.asarray(a, np.float32)
    return np.ascontiguousarray(a.transpose(1, 3, 0, 2).reshape(P, L, G))


def make_in_maps(inp):
    csc, pm_prompt, pm_sample = dft_consts()
    f32 = lambda a: np.ascontiguousarray(np.asarray(a, np.float32))
    lre, lim = rep_dp(inp["s5_lambda_re"]), rep_dp(inp["s5_lambda_im"])
    lst = np.ascontiguousarray(np.broadcast_to(np.asarray(inp["s5_log_step"], np.float32).transpose(1, 0, 2)[:, None],
                                               (2, 64, L, G)).reshape(P, L, G))
    s5p = np.ascontiguousarray(np.stack([lre, lim, lst], axis=2))

    def rep_b(a):
        a = np.asarray(a, np.float32).transpose(2, 0, 1, 3)
        return np.broadcast_to(a[None], (2, 64, L, G, GC)).reshape(P, L, G, GC)

    def rep_c(a):
        a = np.asarray(a, np.float32).transpose(3, 0, 1, 2)
        return np.broadcast_to(a[None], (2, 64, L, G, GC)).reshape(P, L, G, GC)
    s5bc = np.ascontiguousarray(np.stack([rep_b(inp["s5_b_re"]), rep_b(inp["s5_b_im"]), rep_c(inp["s5_c_re"]),
                                          rep_c(inp["s5_c_im"])], axis=2))
    dcol = np.asarray(inp["s5_d"], np.float32).transpose(2, 0, 1)
    s5dcol = np.ascontiguousarray(np.broadcast_to(dcol[None], (8, GC, L, G)).reshape(P, L, G))
    shared = {
        "w_ada": f32(inp["w_ada"]),
        "b_ada_fm": fm(inp["b_ada"]),
        "n1w": fm(inp["norm1_w"]), "n2w": fm(inp["norm2_w"]), "nfw": fm(inp["norm_f"]),
        "w_in": f32(inp["w_in"]), "w_fourier": f32(inp["w_fourier"]), "w_glu": f32(inp["w_glu"]),
        "w_out": f32(inp["w_out"]), "w_gate": f32(inp["w_gate"]), "w_up": f32(inp["w_up"]),
        "w_down": f32(inp["w_down"]),
        "identf": np.eye(P, dtype=np.float32), "csc": csc, "cmask": mask_consts(),
        "bg_bc": np.ascontiguousarray(np.broadcast_to(np.asarray(inp["b_gates"], np.float32)[None], (P, L, 16))),
        "mnw_bc": np.ascontiguousarray(np.broadcast_to(np.asarray(inp["mlstm_norm_w"], np.float32)[None], (P, L, 384))),
        "s5p": s5p, "s5bc": s5bc, "s5dcol": s5dcol,
    }
    maps = []
    xs = np.asarray(inp["x_sample"], np.float32)
    xp = np.asarray(inp["x_prompt"], np.float32)
    sC = np.asarray(inp["state_mlstm_C"], np.float32)
    sn = np.asarray(inp["state_mlstm_n"], np.float32)
    smm = np.asarray(inp["state_mlstm_m"], np.float32)
    sre = np.asarray(inp["state_s5_re"], np.float32)
    sim = np.asarray(inp["state_s5_im"], np.float32)
    for core in range(8):
        m = dict(shared)
        if core < 4:
            b = core
            m["x"] = np.ascontiguousarray(xs[b])
            m["cvec"] = fm(np.asarray(inp["c"])[b])
            m["posmat"] = pm_sample
            m["keepcol"] = np.ones((P, 1), np.float32)
            m["keeprow"] = np.ones((1, NT), np.float32)
            mC0 = np.concatenate([sC[b].transpose(3, 0, 1, 2, 4), sn[b].transpose(3, 0, 1, 2)[..., None]], axis=-1)
            m["mC0"] = np.ascontiguousarray(mC0)
            m["mM0"] = np.ascontiguousarray(smm[b].reshape(1, L, 8))
            st = np.stack([sre[b], sim[b]], axis=0)
            m["s5st0"] = np.ascontiguousarray(st.transpose(2, 4, 1, 0, 3).reshape(P, L, 2, G))
        else:
            j = core - 4
            m["x"] = np.ascontiguousarray(xp[8 * j:8 * j + 8].reshape(T, D))
            m["cvec"] = fm(inp["c_ctx"])
            m["posmat"] = pm_prompt
            m["keepcol"] = np.zeros((P, 1), np.float32)
            kr = np.zeros((1, NT), np.float32)
            kr[0, 1::2] = 1.0
            m["keeprow"] = kr
            m["mC0"] = np.zeros((DH, L, 2, H, 97), np.float32)
            m["mM0"] = np.zeros((1, L, 8), np.float32)
            m["s5st0"] = np.zeros((P, L, 2, G), np.float32)
        maps.append(m)
    return maps


_CACHE = {}
DEBUG = None
LAST = {}


def kernel(**inputs):
    import os
    maps = make_in_maps(inputs)
    if "nc" not in _CACHE:
        _CACHE["nc"] = build_program(debug=DEBUG)
    nc = _CACHE["nc"]
    dev_cores = os.environ.get("KDEV_CORES")
    if dev_cores:
        ids = [int(c) for c in dev_cores.split(",")]
        res = run_bass_kernel_spmd(nc, [maps[i] for i in ids], core_ids=list(range(len(ids))))
        outs = [None] * 8
        for j, i in enumerate(ids):
            outs[i] = res.results[j]
        for i in range(8):
            if outs[i] is None:
                outs[i] = outs[ids[0] if i < 4 else ids[-1]]
    else:
        res = run_bass_kernel_spmd(nc, maps, core_ids=list(range(8)))
        outs = res.results
    LAST["outs"] = outs
    f = lambda a: np.ascontiguousarray(np.asarray(a, dtype=np.float32))
    y_sample = f(np.stack([outs[c]["y"] for c in range(4)], axis=0))
    y_prompt = f(np.concatenate([np.asarray(outs[c]["y"]).reshape(8, 256, D) for c in range(4, 8)], axis=0))
    cat = lambda k: f(np.concatenate([np.asarray(outs[c][k]) for c in range(4, 8)], axis=0))
    return (y_prompt, y_sample, cat("newC"), cat("newn"), cat("newm"), cat("news5re"), cat("news5im"))
```

```python
import math
from contextlib import ExitStack

import numpy as np
import ml_dtypes

import concourse.bass as bass
import concourse.mybir as mybir
from concourse.ap import AP
from concourse.bass_utils import run_bass_kernel_spmd

F32 = mybir.dt.float32
BF16 = mybir.dt.bfloat16
AF = mybir.ActivationFunctionType
ALU = mybir.AluOpType
AX = mybir.AxisListType

P = 128
T = 2048
D = 1024
KC = 8
NT = 16
NB = 4
L = 2
DFF = 2816
NHC = DFF // 128
PIN = 2192
H = 4
DH = 96
G = 24
GC = 16
SP_ = 64
EPS = 1e-6
NSB = 256
DMA_SCRATCH = 4096
ARENA_BYTES = 116 * 1024

FLAGS = {"fourier": True, "mlstm": True, "s5": True, "ffn": True, "layers": 2}


class KB:
    def __init__(self):
        self.nc = bass.Bass("TRN2", target_bir_lowering=False, dynamic_dma_scratch_size=DMA_SCRATCH)
        self.es = ExitStack()
        nc = self.nc
        self.engs = {"pe": nc.tensor, "dve": nc.vector, "act": nc.scalar, "pool": nc.gpsimd, "sp": nc.sync}
        self.sem = {e: self.es.enter_context(nc.semaphore("s_" + e)) for e in self.engs}
        self.cnt = {e: 0 for e in self.engs}
        self.waited = {}
        self.ND = 32
        self.dsem = [self.es.enter_context(nc.semaphore("d%d" % i)) for i in range(self.ND)]
        self.dcnt = [0] * self.ND
        self.dnext = 0
        self.dnext_sw = 0
        self.res = {}
        self.ninstr = 0
        self.barrier_hooks = []
        self.bg = set()

    def sb(self, name, shape, dtype):
        return self.es.enter_context(self.nc.sbuf_tensor(name, list(shape), dtype))

    def ps(self, name, shape, dtype):
        return self.es.enter_context(self.nc.psum_tensor(name, list(shape), dtype))

    def dram(self, name, shape, dtype, kind):
        return self.nc.dram_tensor(name, list(shape), dtype, kind=kind).ap()

    def _wait(self, e, tok):
        if tok is None:
            return
        kind, src, val = tok
        key = (e, kind, src)
        if self.waited.get(key, 0) >= val:
            return
        self.waited[key] = val
        sem = self.sem[src] if kind == "e" else self.dsem[src]
        self.engs[e].wait_ge(sem, val)

    def _deps(self, e, reads, writes, pe_acc):
        for r in reads:
            st = self.res.get(r)
            if st is not None:
                self._wait(e, st["w"])
        for w in writes:
            st = self.res.get(w)
            if st is not None:
                if not (pe_acc and st["w"] is not None and st["w"][0] == "e" and st["w"][1] == "pe"):
                    self._wait(e, st["w"])
                for t in st["r"]:
                    if not (pe_acc and t[0] == "e" and t[1] == "pe"):
                        self._wait(e, t)

    def _update(self, tok, reads, writes):
        for r in reads:
            st = self.res.setdefault(r, {"w": None, "r": []})
            st["r"].append(tok)
            if len(st["r"]) > 48:
                latest = {}
                for t in st["r"]:
                    latest[(t[0], t[1])] = t
                st["r"] = list(latest.values())
        for w in writes:
            self.res[w] = {"w": tok, "r": []}

    def op(self, e, fn, reads=(), writes=(), pe_acc=False):
        if e != "pe":
            for r in reads:
                if isinstance(r, tuple) and r[0] == "ps":
                    st = self.res.get(("psx", r[1]))
                    if st is not None and st["w"] is not None and st["w"][1] != e:
                        self._wait(e, st["w"])
        self._deps(e, reads, writes, pe_acc)
        ins = fn(self.engs[e])
        ins.then_inc(self.sem[e], 1)
        self.cnt[e] += 1
        tok = ("e", e, self.cnt[e])
        self._update(tok, reads, writes)
        if e != "pe":
            for r in reads:
                if isinstance(r, tuple) and r[0] == "ps":
                    self.res[("psx", r[1])] = {"w": tok, "r": []}
        self.ninstr += 1
        return tok

    def dma(self, q, out, in_, reads=(), writes=(), bg=False, **kw):
        self._deps(q, reads, writes, False)
        half = self.ND // 2
        if q == "pool":
            i = half + self.dnext_sw
            self.dnext_sw = (self.dnext_sw + 1) % half
        else:
            i = self.dnext
            self.dnext = (self.dnext + 1) % half
        if self.dcnt[i] > 0:
            self._wait(q, ("d", i, self.dcnt[i]))
        self.bg.discard(i)
        if bg:
            self.bg.add(i)
        ins = self.engs[q].dma_start(out=out, in_=in_, **kw)
        ins.then_inc(self.dsem[i], 16)
        self.dcnt[i] += 16
        tok = ("d", i, self.dcnt[i])
        self._update(tok, reads, writes)
        self.ninstr += 1
        return tok

    def barrier(self):
        for e in self.engs:
            for e2 in self.engs:
                if self.cnt[e2] > 0:
                    self._wait(e, ("e", e2, self.cnt[e2]))
            for i in range(self.ND):
                if self.dcnt[i] > 0 and i not in self.bg:
                    self._wait(e, ("d", i, self.dcnt[i]))
        self.res = {k: v for k, v in self.res.items() if isinstance(k, tuple) and k[0] == "WA"}
        for h in self.barrier_hooks:
            h()

    def finish(self):
        self.bg = set()
        self.barrier()
        self.es.close()
        return self.nc


class Arena:
    def __init__(self, kb, nbytes):
        self.n = nbytes // 2
        self.t = kb.sb("arena", [P, self.n], BF16)
        self.free_list = [(0, self.n)]
        self.pending = []
        self.live = {}
        self.used = 0
        self.peak = 0
        kb.barrier_hooks.append(self.commit)

    def alloc(self, free_shape, dtype, top=False):
        nel = 1
        for s in free_shape:
            nel *= s
        units = nel * (2 if dtype == F32 else 1)
        units = (units + 31) // 32 * 32
        order = range(len(self.free_list) - 1, -1, -1) if top else range(len(self.free_list))
        for idx in order:
            off, size = self.free_list[idx]
            if size >= units:
                if size == units:
                    self.free_list.pop(idx)
                elif top:
                    self.free_list[idx] = (off, size - units)
                    off = off + size - units
                else:
                    self.free_list[idx] = (off + units, size - units)
                break
        else:
            raise RuntimeError("arena overflow: need %d units, free=%s" % (units, self.free_list))
        self.used += units
        self.peak = max(self.peak, self.used)
        v = self.t[:, off:off + units]
        if dtype == F32:
            v = v.bitcast(F32)[:, 0:nel]
        else:
            v = v[:, 0:nel]
        if len(free_shape) > 1:
            names = " ".join("a%d" % i for i in range(len(free_shape)))
            kw = {"a%d" % i: free_shape[i] for i in range(len(free_shape))}
            v = v.rearrange("p (%s) -> p %s" % (names, names), **kw)
        self.live[id(v)] = (v, off, units)
        return v

    def free(self, *aps):
        for ap in aps:
            v, off, units = self.live.pop(id(ap))
            self.pending.append((off, units))

    def commit(self):
        for off, units in self.pending:
            self.used -= units
            self.free_list.append((off, units))
        self.pending = []
        self.free_list.sort()
        merged = []
        for off, size in self.free_list:
            if merged and merged[-1][0] + merged[-1][1] == off:
                merged[-1] = (merged[-1][0], merged[-1][1] + size)
            else:
                merged.append((off, size))
        self.free_list = merged


def bcast(ap, shape):
    return ap.to_broadcast(list(shape))


def rev_axis(ap, axis):
    a = [list(x) for x in ap.ap]
    st, n = a[axis]
    a[axis] = [-st, n]
    return AP(ap.tensor, ap.offset + st * (n - 1), a)


def swap_ri(ap):
    a = [list(x) for x in ap.ap]
    assert len(a) == 3 and a[1][1] == 2
    st = a[1][0]
    return AP(ap.tensor, ap.offset + st, [a[0], [-st, 2], a[2]])

def build_program(flags=FLAGS, debug=None):
    kb = KB()
    nc = kb.nc
    NL = flags["layers"]

    def din(name, shape, dt=F32):
        return kb.dram(name, shape, dt, "ExternalInput")

    def dout(name, shape, dt=F32):
        return kb.dram(name, shape, dt, "ExternalOutput")

    x_d = din("x", [T, D])
    cvec_d = din("cvec", [P, KC])
    w_ada_d = din("w_ada", [L, D, 6 * D])
    b_ada_d = din("b_ada_fm", [P, L, 48])
    n1w_d = din("n1w", [P, L, KC])
    n2w_d = din("n2w", [P, L, KC])
    nfw_d = din("nfw", [P, KC])
    w_in_d = din("w_in", [L, D, PIN])
    w_fo_d = din("w_fourier", [L, 256, 256])
    w_glu_d = din("w_glu", [L, 384, 768])
    w_out_d = din("w_out", [L, D, D])
    w_gate_d = din("w_gate", [L, D, DFF])
    w_up_d = din("w_up", [L, D, DFF])
    w_down_d = din("w_down", [L, DFF, D])
    identf_d = din("identf", [P, P])
    csc_d = din("csc", [P, 256], BF16)
    posmat_d = din("posmat", [NT, P, 2, T], BF16)
    cmask_d = din("cmask", [P, 5, P])
    keepcol_d = din("keepcol", [P, 1])
    keeprow_d = din("keeprow", [1, NT])
    bg_d = din("bg_bc", [P, L, 16])
    mnw_d = din("mnw_bc", [P, L, 384])
    mC0_d = din("mC0", [DH, L, 2, H, 97])
    mM0_d = din("mM0", [1, L, 8])
    s5p_d = din("s5p", [P, L, 3, G])
    s5bc_d = din("s5bc", [P, L, 4, G, GC])
    s5dcol_d = din("s5dcol", [P, L, G])
    s5st0_d = din("s5st0", [P, L, 2, G])
    y_d = dout("y", [T, D])
    newC_d = dout("newC", [8, L, 2, H, DH, DH])
    newn_d = dout("newn", [8, L, 2, H, DH])
    newm_d = dout("newm", [8, L, 2, H])
    news5_d = [dout("news5re", [8, L, 2, G, SP_]), dout("news5im", [8, L, 2, G, SP_])]

    dbg_d = {}
    if debug:
        for name, (shape, dt) in debug.items():
            dbg_d[name] = dout("dbg_" + name, shape, dt)

    X = kb.sb("X", [P, KC, T], F32)
    identf = kb.sb("identf_sb", [P, P], F32)
    identb = kb.sb("identb", [P, P], BF16)
    onesb = kb.sb("onesb", [P, P], BF16)
    onesf = kb.sb("onesf", [P, P], F32)
    csc = kb.sb("csc_sb", [P, 256], BF16)
    cmask = kb.sb("cmask_sb", [P, 5, P], F32)
    maskb = kb.sb("maskb", [P, 2, P], BF16)
    Jb = kb.sb("Jb", [P, P], BF16)
    keepcol = kb.sb("keepcol_sb", [P, 1], F32)
    keeprow = kb.sb("keeprow_sb", [1, NT], F32)
    cvec = kb.sb("cvec_sb", [P, KC], F32)
    csil = kb.sb("csil", [P, KC], BF16)
    b_ada = kb.sb("b_ada_sb", [P, L, 48], F32)
    n1w = kb.sb("n1w_sb", [P, L, KC], F32)
    n2w = kb.sb("n2w_sb", [P, L, KC], F32)
    nfw = kb.sb("nfw_sb", [P, KC], F32)
    mods = kb.sb("mods", [P, L, 6, KC], F32)
    wn = kb.sb("wn", [P, L, 2, KC], F32)
    banks = [kb.ps("bank%d" % i, [P, 512], F32) for i in range(8)]
    ar = Arena(kb, ARENA_BYTES)
    trile, trige, s5mF, s5mB = cmask[:, 0, :], cmask[:, 1, :], cmask[:, 2, :], cmask[:, 3, :]

    bstate = {"i": 0}

    def nextbank():
        b = bstate["i"]
        bstate["i"] = (b + 1) % 8
        return b

    evq = {"i": 0}

    def evac_eng():
        evq["i"] ^= 1
        return "act" if evq["i"] else "dve"

    def mm(out, lhsT, rhs, start, stop, reads, writes):
        return kb.op("pe", lambda e: e.matmul(out, lhsT, rhs, start=start, stop=stop), reads=reads, writes=writes,
                     pe_acc=True)

    def tr(out, in_, ident, reads, writes):
        return kb.op("pe", lambda e: e.transpose(out, in_, ident), reads=reads, writes=writes, pe_acc=True)

    import os as _os
    PSUB = _os.environ.get("KPOOL", "pool")

    def copy_to(eng, out, in_, reads, writes):
        if eng == "pool":
            eng = PSUB
        if eng == "act":
            return kb.op("act", lambda e: e.activation(out=out, in_=in_, func=AF.Copy), reads=reads, writes=writes)
        return kb.op(eng, lambda e: e.tensor_copy(out=out, in_=in_), reads=reads, writes=writes)

    def tt(eng, out, in0, in1, op, reads, writes):
        if eng == "pool":
            eng = PSUB
        return kb.op(eng, lambda e: e.tensor_tensor(out=out, in0=in0, in1=in1, op=op), reads=reads, writes=writes)

    def ts(eng, out, in0, s1, s2, op0, op1, reads, writes):
        if eng == "pool":
            eng = PSUB
        if op1 is None:
            return kb.op(eng, lambda e: e.tensor_scalar(out=out, in0=in0, scalar1=s1, scalar2=None, op0=op0),
                         reads=reads, writes=writes)
        return kb.op(eng, lambda e: e.tensor_scalar(out=out, in0=in0, scalar1=s1, scalar2=s2, op0=op0, op1=op1),
                     reads=reads, writes=writes)

    def stt(out, in0, scalar, in1, op0, op1, reads, writes):
        return kb.op("dve", lambda e: e.scalar_tensor_tensor(out=out, in0=in0, scalar=scalar, in1=in1, op0=op0,
                                                             op1=op1), reads=reads, writes=writes)

    def act(out, in_, func, reads, writes, scale=1.0, bias=None):
        if bias is None:
            return kb.op("act", lambda e: e.activation(out=out, in_=in_, func=func, scale=scale), reads=reads,
                         writes=writes)
        return kb.op("act", lambda e: e.activation(out=out, in_=in_, func=func, scale=scale, bias=bias),
                     reads=reads, writes=writes)

    def dbg_out(name, ap_sb, keyreads=()):
        if debug and name in dbg_d:
            kb.barrier()
            kb.dma("sp", dbg_d[name], ap_sb, reads=list(keyreads), writes=[("dbg", name)])

    kb.dma("sp", identf[:], identf_d, writes=["identf"])
    kb.dma("sp", csc[:], csc_d, writes=["csc"])
    kb.dma("sp", cvec[:], cvec_d, writes=["cvec"])
    kb.dma("sp", b_ada[:], b_ada_d, writes=["b_ada"])
    kb.dma("sp", n1w[:], n1w_d, writes=["n1w"])
    kb.dma("sp", n2w[:], n2w_d, writes=["n2w"])
    kb.dma("sp", nfw[:], nfw_d, writes=["nfw"])
    kb.dma("sp", cmask[:], cmask_d, writes=["cmask"])
    kb.dma("sp", keepcol[:], keepcol_d, writes=["keepcol"])
    kb.dma("sp", keeprow[:], keeprow_d, writes=["keeprow"])
    kb.op("dve", lambda e: e.memset(onesb[:], 1.0), writes=["onesb"])
    kb.op("dve", lambda e: e.memset(onesf[:], 1.0), writes=["onesf"])
    copy_to("dve", identb[:], identf[:], ["identf"], ["identb"])
    copy_to("dve", maskb[:], cmask[:, 0:2, :], ["cmask"], ["maskb"])
    copy_to("dve", Jb[:], cmask[:, 4, :], ["cmask"], ["Jb"])
    act(csil[:], cvec[:], AF.Silu, ["cvec"], ["csil"])

    def ada_dma(l, i, WA, bg=False):
        src = w_ada_d[l, :, i * D:(i + 1) * D].rearrange("(k p) n -> p k n", p=P)
        kb.dma("pool", WA, src, writes=[("WA", id(WA))], bg=bg)

    def ada_compute(l, i, WA):
        b = nextbank()
        for m in range(KC):
            for k in range(KC):
                mm(banks[b][:, m:m + 1], WA[:, k, m * P:(m + 1) * P], csil[:, k:k + 1], k == 0, k == KC - 1,
                   reads=[("WA", id(WA)), "csil"], writes=[("ps", b)])
        tt("dve", mods[:, l, i, :], banks[b][:, 0:KC], b_ada[:, l, i * KC:(i + 1) * KC], ALU.add,
           [("ps", b), "b_ada"], [("mods", l, i)])
        if i in (1, 4):
            j = 0 if i == 1 else 1
            nw = n1w if i == 1 else n2w
            stt(wn[:, l, j, :], mods[:, l, i, :], 1.0, nw[:, l, :], ALU.add, ALU.mult,
                [("mods", l, i), "n1w", "n2w"], [("wn", l, j)])

    def rms_norm(l, j, XN):
        xsq = ar.alloc((KC, 512), BF16)
        rstd = ar.alloc((512,), F32)
        tmp = [ar.alloc((512,), F32) for _ in range(2)]
        for nb in range(NB):
            sl = slice(nb * 512, (nb + 1) * 512)
            for k in range(KC):
                act(xsq[:, k, :], X[:, k, sl], AF.Square, [("X", nb)], [("xsq", k)])
            b = nextbank()
            for k in range(KC):
                mm(banks[b][:], onesb[:], xsq[:, k, :], k == 0, k == KC - 1, reads=["onesb", ("xsq", k)],
                   writes=[("ps", b)])
            act(rstd, banks[b][:], AF.Sqrt, [("ps", b)], ["rstd"], scale=1.0 / D, bias=EPS)
            kb.op("dve", lambda e: e.reciprocal(out=rstd, in_=rstd), reads=["rstd"], writes=["rstd"])
            for k in range(KC):
                if l is None:
                    stt(X[:, k, sl], X[:, k, sl], nfw[:, k:k + 1], rstd, ALU.mult, ALU.mult,
                        [("X", nb), "nfw", "rstd"], [("X", nb)])
                else:
                    tq = tmp[k % 2]
                    stt(tq, X[:, k, sl], wn[:, l, j, k:k + 1], rstd, ALU.mult, ALU.mult,
                        [("X", nb), ("wn", l, j), "rstd"], [("ntmp", k % 2)])
                    act(XN[:, k, sl], tq, AF.Identity, [("ntmp", k % 2), ("mods", l, 3 * j)], [("XN", k, nb)],
                        bias=mods[:, l, 3 * j, k:k + 1])
        ar.free(xsq, rstd, *tmp)
        kb.barrier()

    def load_w(dst, src, key, q="pool"):
        return kb.dma(q, dst, src, writes=[key])

    def proj_fm(W, wkey, nk, actf, akey, mchunks, evac):
        for nb in range(NB):
            for mi in mchunks:
                b = nextbank()
                for k in range(nk):
                    mm(banks[b][:], W[:, k, mi * P:(mi + 1) * P], actf(k, nb), k == 0, k == nk - 1,
                       reads=[wkey, akey(k, nb)], writes=[("ps", b)])
                evac(b, mi, nb)

    def resid_evac(l, gi):
        def f(b, mi, nb):
            sl = slice(nb * 512, (nb + 1) * 512)
            stt(X[:, mi, sl], banks[b][:], mods[:, l, gi, mi:mi + 1], X[:, mi, sl], ALU.mult, ALU.add,
                [("ps", b), ("mods", l, gi), ("X", nb)], [("X", nb)])
        return f

    def fourier_inproj(l, XN):
        Wf = ar.alloc((KC, 256), BF16)
        load_w(Wf, w_in_d[l, :, 0:256].rearrange("(k p) n -> p k n", p=P), "Wf")
        zfT = ar.alloc((2, T), BF16)

        def ev_zf(b, mi, nb):
            copy_to(evac_eng(), zfT[:, mi, nb * 512:(nb + 1) * 512], banks[b][:], [("ps", b)], [("zfT", nb)])
        proj_fm(Wf, "Wf", KC, lambda k, nb: XN[:, k, nb * 512:(nb + 1) * 512], lambda k, nb: ("XN", k, nb), range(2),
                ev_zf)
        ar.free(Wf)
        return zfT

    def fourier_core(l, zfT):
        foT = ar.alloc((2, T), BF16, top=True)
        Wfo = ar.alloc((2, 256), BF16)
        load_w(Wfo, w_fo_d[l].rearrange("(k p) n -> p k n", p=P), "Wfo")
        ZCS = ar.alloc((NT, 512), BF16)
        NR = 4
        PM = [ar.alloc((2, T // 2), BF16) for _ in range(NR)]
        for t in range(NT):
            b = nextbank()
            for kc in range(2):
                mm(banks[b][:, kc * 256:(kc + 1) * 256], zfT[:, kc, t * P:(t + 1) * P], csc[:], True, True,
                   reads=[("zfT", t // 4), "csc"], writes=[("ps", b)])
            copy_to(evac_eng(), ZCS[:, t, :], banks[b][:], [("ps", b)], [("ZCS", t)])
        kb.barrier()
        yT = zfT
        it = 0
        for hp in range(2):
            for ti in range(NT):
                pm = PM[it % NR]
                kb.dma("sp", pm, posmat_d[ti][:, :, hp * 1024:(hp + 1) * 1024], writes=[("PM", it % NR)])
                for cs in range(2):
                    for fc in range(2):
                        for nbl in range(2):
                            b = hp * 4 + fc * 2 + nbl
                            mm(banks[b][:], ZCS[:, ti, fc * 256 + cs * P: fc * 256 + (cs + 1) * P],
                               pm[:, cs, nbl * 512:(nbl + 1) * 512], ti == 0 and cs == 0, ti == NT - 1 and cs == 1,
                               reads=[("ZCS", ti), ("PM", it % NR)], writes=[("ps", b)])
                it += 1
            for fc in range(2):
                for nbl in range(2):
                    b = hp * 4 + fc * 2 + nbl
                    nb = hp * 2 + nbl
                    copy_to(evac_eng(), yT[:, fc, nb * 512:(nb + 1) * 512], banks[b][:], [("ps", b)], [("zfT", nb)])
        bstate["i"] = 0

        def ev_fo(b, mi, nb):
            copy_to(evac_eng(), foT[:, mi, nb * 512:(nb + 1) * 512], banks[b][:], [("ps", b)], [("foT", nb)])
        proj_fm(Wfo, "Wfo", 2, lambda k, nb: yT[:, k, nb * 512:(nb + 1) * 512], lambda k, nb: ("zfT", nb), range(2),
                ev_fo)
        ar.free(Wfo, ZCS, *PM, zfT)
        kb.barrier()
        return foT

    def mlstm_inproj(l, XN):
        Qt = ar.alloc((NT, 384), BF16)
        Kt = ar.alloc((NT, 384), BF16)
        Vt = ar.alloc((NT, 384), BF16)
        ZO = ar.alloc((NT, 384), BF16)
        GT = ar.alloc((NT, 16), F32)
        bg = ar.alloc((16,), F32)
        mnw = ar.alloc((384,), F32)
        kb.dma("sp", bg, bg_d[:, l, :], writes=["bg"])
        kb.dma("sp", mnw, mnw_d[:, l, :], writes=["mnw"])
        groups = [(256, 640, "q"), (640, 1024, "k"), (1024, 1408, "v"), (1408, 1808, "go")]
        Wp = [ar.alloc((KC, 400), BF16) for _ in range(2)]
        import os
        groups = groups[:int(os.environ.get("KIPG", "4"))]
        for gi, (c0, c1, kind) in enumerate(groups):
            W = Wp[gi % 2]
            n = c1 - c0
            load_w(W[:, :, 0:n], w_in_d[l, :, c0:c1].rearrange("(k p) n -> p k n", p=P), ("Wp", gi % 2))
            for t in range(NT):
                b = nextbank()
                for k in range(KC):
                    mm(banks[b][:, 0:n], XN[:, k, t * P:(t + 1) * P], W[:, k, 0:n], k == 0, k == KC - 1,
                       reads=[("Wp", gi % 2), ("XN", k, t // 4)], writes=[("ps", b)])
                if kind == "q":
                    copy_to(evac_eng(), Qt[:, t, :], banks[b][:, 0:384], [("ps", b)], [("Qt", t)])
                elif kind == "k":
                    act(Kt[:, t, :], banks[b][:, 0:384], AF.Copy, [("ps", b)], [("Kt", t)], scale=float(DH) ** -0.5)
                elif kind == "v":
                    copy_to(evac_eng(), Vt[:, t, :], banks[b][:, 0:384], [("ps", b)], [("Vt", t)])
                else:
                    KGO = int(os.environ.get("KGO", "7"))
                    if KGO & 1:
                        tt("dve", GT[:, t, :], banks[b][:, 0:16], bg, ALU.add, [("ps", b), "bg"], [("GT", t)])
                    if KGO & 2:
                        act(ZO[:, t, :], banks[b][:, 16:400], AF.Sigmoid, [("ps", b)], [("ZO", t)])
                    if KGO & 4:
                        tt("pool", ZO[:, t, :], ZO[:, t, :], mnw, ALU.mult, [("ZO", t), "mnw"], [("ZO", t)])
        ar.free(*Wp, bg, mnw)
        kb.barrier()
        return dict(Qt=Qt, Kt=Kt, Vt=Vt, ZO=ZO, GT=GT)

    def mlstm_core(l, mb):
        Qt, Kt, Vt, ZO, GT = mb["Qt"], mb["Kt"], mb["Vt"], mb["ZO"], mb["GT"]
        import os
        STAGE = int(os.environ.get("KSTAGE", "9"))

        def bail(bufs):
            ar.free(*bufs)
            kb.barrier()
            moT_ = ar.alloc((3, T), BF16)
            kb.op("dve", lambda e: e.memset(moT_, 0.0), writes=[("moT", i) for i in range(NB)])
            kb.barrier()
            return moT_
        if STAGE < 1:
            return bail([Qt, Kt, Vt, ZO, GT])
        allT = [("GT", t) for t in range(NT)]
        SPl = ar.alloc((2, NT, H), F32)
        IG = ar.alloc((2, NT, H), F32)
        A = ar.alloc((128,), F32)
        NBc = ar.alloc((128,), F32)
        Wt = ar.alloc((128,), F32)
        FL = ar.alloc((128,), F32)
        MMbc = ar.alloc((128,), F32)
        INbc = ar.alloc((128,), F32)
        COLS = ar.alloc((2,), F32)
        ROW = ar.alloc((256,), F32)
        MP = ar.alloc((128,), F32)
        MMr = ar.alloc((128,), F32)
        MN = ar.alloc((128,), F32)
        INr = ar.alloc((128,), F32)
        MI = ar.alloc((8,), F32)
        kb.dma("sp", MI[0:1, :], mM0_d[:, l, :], writes=["MI"])
        GTv = GT
        for d in range(2):
            act(SPl[:, d, :, :], GTv[:, :, d * 8 + 4:d * 8 + 8], AF.Exp, allT, [("SPl", d)], scale=-1.0)
            act(SPl[:, d, :, :], SPl[:, d, :, :], AF.Ln, [("SPl", d)], [("SPl", d)], bias=1.0)
            copy_to("dve", IG[:, d, :, :], GTv[:, :, d * 8:d * 8 + 4], allT, [("IG", d)])
        SPf = SPl.rearrange("p d t h -> p (d t h)")
        IGf = IG.rearrange("p d t h -> p (d t h)")
        b = nextbank()
        mm(banks[b][:, 0:64], trile, SPf[:, 0:64], True, True, ["cmask", ("SPl", 0)], [("ps", b)])
        mm(banks[b][:, 64:128], trige, SPf[:, 64:128], True, True, ["cmask", ("SPl", 1)], [("ps", b)])
        copy_to("dve", NBc, banks[b][:, 0:128], [("ps", b)], ["NBc"])
        tt("dve", A, IGf, NBc, ALU.add, [("IG", 0), ("IG", 1), "NBc"], ["A"])
        b = nextbank()
        tr(banks[b][:, 0:128], A, identf[:], ["A", "identf"], [("ps", b)])
        kb.op("dve", lambda e: e.tensor_reduce(out=COLS[:, 0:1], in_=banks[b][:, 0:128], axis=AX.X, op=ALU.max),
              reads=[("ps", b)], writes=[("COLS", 0)])
        b2 = nextbank()
        mm(banks[b2][:, 0:1], SPf, onesf[:, 0:1], True, True, [("SPl", 0), ("SPl", 1), "onesf"], [("ps", b2)])
        ts("dve", COLS[:, 1:2], banks[b2][:, 0:1], -1.0, None, ALU.mult, None, [("ps", b2)], [("COLS", 1)])
        b = nextbank()
        tr(banks[b][0:1, 0:128], COLS[:, 0:1], identf[:], [("COLS", 0), "identf"], [("ps", b)])
        tr(banks[b][0:1, 128:256], COLS[:, 1:2], identf[:], [("COLS", 1), "identf"], [("ps", b)])
        copy_to("dve", ROW[0:1, :], banks[b][0:1, 0:256], [("ps", b)], ["ROW"])
        for d in range(2):
            for n in range(NT):
                t = n if d == 0 else NT - 1 - n
                c = (d * NT + t) * H
                if n == 0:
                    copy_to("dve", MP[0:1, c:c + H], MI[0:1, d * H:(d + 1) * H], ["MI"], [("MP", d)])
                tt("dve", MMr[0:1, c:c + H], MP[0:1, c:c + H], ROW[0:1, c:c + H], ALU.max, [("MP", d), "ROW"],
                   [("MMr", d)])
                tt("dve", MN[0:1, c:c + H], MMr[0:1, c:c + H], ROW[0:1, 128 + c:128 + c + H], ALU.add,
                   [("MMr", d), "ROW"], [("MN", d)])
                if n + 1 < NT:
                    t2 = n + 1 if d == 0 else NT - 2 - n
                    c2 = (d * NT + t2) * H
                    ts("dve", MP[0:1, c2:c2 + H], MN[0:1, c:c + H], keeprow[0:1, n + 1:n + 2], None, ALU.mult, None,
                       [("MN", d), "keeprow"], [("MP", d)])
        chain = [("MP", 0), ("MP", 1), ("MMr", 0), ("MMr", 1)]
        tt("dve", INr[0:1, :], MP[0:1, :], MMr[0:1, :], ALU.subtract, chain, ["INr"])
        act(INr[0:1, :], INr[0:1, :], AF.Exp, ["INr"], ["INr"])
        b = nextbank()
        mm(banks[b][:, 0:128], onesf[0:1, :], MMr[0:1, :], True, True, ["onesf"] + chain, [("ps", b)])
        mm(banks[b][:, 128:256], onesf[0:1, :], INr[0:1, :], True, True, ["onesf", "INr"], [("ps", b)])
        copy_to("dve", MMbc, banks[b][:, 0:128], [("ps", b)], ["MMbc"])
        copy_to("act", INbc, banks[b][:, 128:256], [("ps", b)], ["INbc"])
        tt("dve", Wt, A, MMbc, ALU.subtract, ["A", "MMbc"], ["Wt"])
        act(Wt, Wt, AF.Exp, ["Wt"], ["Wt"])
        tt("dve", FL, NBc, MMbc, ALU.subtract, ["NBc", "MMbc"], ["FL"])
        act(FL, FL, AF.Exp, ["FL"], ["FL"])
        import os
        LVL = int(os.environ.get("KLVL", "9"))
        for d in range(2):
            if LVL < 3:
                break
            src = MN[0:1, d * 64:(d + 1) * 64].rearrange("p (s two h) -> p s two h", two=2, h=H)[:, :, 1 - d, :]
            if d == 0:
                dst = newm_d[:, l, d, :].rearrange("(o s) h -> o s h", o=1)
                kb.dma("sp", dst, src, reads=[("MN", d)], writes=[("newm", l, d)])
            else:
                dst = newm_d[:, l, d, :].rearrange("(o s) h -> o s h", o=1)
                kb.dma("sp", dst, src, reads=[("MN", d)], writes=[("newm", l, d)])

        if STAGE < 2:
            return bail([Qt, Kt, Vt, ZO, GT, SPl, IG, A, NBc, Wt, FL, MMbc, INbc, COLS, ROW, MP, MMr, MN, INr, MI])
        HS = ar.alloc((NT, 384), BF16)
        CST = ar.alloc((2, H, 97), F32)
        kb.dma("sp", CST[0:DH], mC0_d[:, l], writes=[("CST", 0), ("CST", 1)])
        CSb = [ar.alloc((H, 97), BF16) for _ in range(2)]
        QT = [ar.alloc((H, 128), BF16) for _ in range(2)]
        KT = [ar.alloc((H, 128), BF16) for _ in range(2)]
        SM = [ar.alloc((H, 128), BF16) for _ in range(2)]
        VE = [ar.alloc((H, 97), BF16) for _ in range(2)]
        dd = [ar.alloc((H,), F32) for _ in range(2)]
        hp = [ar.alloc((H, DH), F32) for _ in range(2)]
        CAP = [ar.alloc((H, 97), F32) for _ in range(2)]
        bcs = {}

        def stage_a(n, d):
                t = n if d == 0 else NT - 1 - n
                c = (d * NT + t) * H
                bq = nextbank()
                bqv = banks[bq][:].bitcast(BF16)
                for h in range(H):
                    tr(bqv[0:DH, h * P:(h + 1) * P], Qt[:, t, h * DH:(h + 1) * DH], identb[:], [("Qt", t), "identb"],
                       [("ps", bq)])
                    tr(bqv[0:DH, 512 + h * P:512 + (h + 1) * P], Kt[:, t, h * DH:(h + 1) * DH], identb[:],
                       [("Kt", t), "identb"], [("ps", bq)])
                copy_to("act", QT[d][0:DH].rearrange("p h n -> p (h n)"), bqv[0:DH, 0:512], [("ps", bq)], [("QT", d)])
                copy_to("dve", KT[d][0:DH].rearrange("p h n -> p (h n)"), bqv[0:DH, 512:1024], [("ps", bq)],
                        [("KT", d)])
                bs = nextbank()
                for h in range(H):
                    mm(banks[bs][:, h * P:(h + 1) * P], KT[d][0:DH, h, :], QT[d][0:DH, h, :], True, True,
                       [("KT", d), ("QT", d)], [("ps", bs)])
                tt("dve", SM[d], banks[bs][:].rearrange("p (h n) -> p h n", h=H),
                   bcast(maskb[:, d:d + 1, :], [P, H, P]), ALU.mult, [("ps", bs), "maskb"], [("SM", d)])
                tt("pool", VE[d][:, :, 0:DH], Vt[:, t, :].rearrange("p (h e) -> p h e", h=H),
                   bcast(Wt[:, c:c + H].unsqueeze(2), [P, H, DH]), ALU.mult, [("Vt", t), "Wt"], [("VE", d)])
                copy_to("pool", VE[d][:, :, DH:DH + 1], Wt[:, c:c + H].unsqueeze(2), ["Wt", ("VE", d)], [("VE", d)])
                bc = nextbank()
                bcs[d] = bc
                for h in range(H):
                    mm(banks[bc][0:DH, h * 97:(h + 1) * 97], Kt[:, t, h * DH:(h + 1) * DH], VE[d][:, h, :], True, True,
                       [("Kt", t), ("VE", d)], [("ps", bc)])

        def stage_b(n, d):
                t = n if d == 0 else NT - 1 - n
                c = (d * NT + t) * H
                for h in range(H):
                    act(CSb[d][0:DH, h, :], CST[0:DH, d, h, :], AF.Copy, [("CST", d), "INbc"], [("CSb", d)],
                        scale=INbc[0:DH, c + h:c + h + 1])
                bc = bcs[d]
                for h in range(H):
                    stt(CST[0:DH, d, h, :], CST[0:DH, d, h, :], INbc[0:DH, c + h:c + h + 1],
                        banks[bc][0:DH, h * 97:(h + 1) * 97], ALU.mult, ALU.add,
                        [("CST", d), "INbc", ("ps", bc)], [("CST", d)])
                if n % 2 == 1:
                    kcap = n // 2
                    seq = kcap if d == 0 else 7 - kcap
                    copy_to("act", CAP[d][0:DH], CST[0:DH, d], [("CST", d)], [("CAP", d)])
                    if LVL >= 1:
                        kb.dma("sp", newC_d[seq, l, d].rearrange("h k e -> k h e"), CAP[d][0:DH, :, 0:DH],
                               reads=[("CAP", d)], writes=[("newC", seq, l, d)])
                    if LVL >= 2:
                        with nc.allow_non_contiguous_dma(reason="small state vector"):
                            kb.dma("sp", newn_d[seq, l, d].rearrange("h k -> k h"), CAP[d][0:DH, :, DH],
                                   reads=[("CAP", d)], writes=[("newn", seq, l, d)])
                    if n + 1 < NT:
                        ts("dve", CST[0:DH, d], CST[0:DH, d], keepcol[0:DH, 0:1], None, ALU.mult, None,
                           [("CST", d), "keepcol"], [("CST", d)])
                bn = nextbank()
                for h in range(H):
                    mm(banks[bn][:, h * 97:(h + 1) * 97], SM[d][:, h, :], VE[d][:, h, :], True, False,
                       [("SM", d), ("VE", d)], [("ps", bn)])
                    mm(banks[bn][:, h * 97:(h + 1) * 97], QT[d][0:DH, h, :], CSb[d][0:DH, h, :], False, True,
                       [("QT", d), ("CSb", d)], [("ps", bn)])
                ndv = banks[bn][:, 0:H * 97].rearrange("p (h e) -> p h e", h=H)
                act(dd[d], ndv[:, :, DH], AF.Abs, [("ps", bn)], [("dd", d)])
                tt("dve", dd[d], dd[d], FL[:, c:c + H], ALU.max, [("dd", d), "FL"], [("dd", d)])
                kb.op("dve", lambda e: e.reciprocal(out=dd[d], in_=dd[d]), reads=[("dd", d)], writes=[("dd", d)])
                if n < NT // 2:
                    tt("dve", HS[:, t, :].rearrange("p (h e) -> p h e", h=H), ndv[:, :, 0:DH],
                       bcast(dd[d].unsqueeze(2), [P, H, DH]), ALU.mult, [("ps", bn), ("dd", d)], [("HS", t)])
                else:
                    tt("dve", hp[d], ndv[:, :, 0:DH], bcast(dd[d].unsqueeze(2), [P, H, DH]), ALU.mult,
                       [("ps", bn), ("dd", d)], [("hp", d)])
                    tt("pool", HS[:, t, :].rearrange("p (h e) -> p h e", h=H),
                       HS[:, t, :].rearrange("p (h e) -> p h e", h=H), hp[d], ALU.add, [("hp", d), ("HS", t)],
                       [("HS", t)])

        its = [(n, d) for n in range(NT) for d in range(2)]
        stage_a(*its[0])
        for i_ in range(len(its)):
            if i_ + 1 < len(its):
                stage_a(*its[i_ + 1])
            stage_b(*its[i_])
        ar.free(Qt, Kt, Vt, SPl, IG, A, NBc, Wt, FL, MMbc, INbc, COLS, ROW, MP, MMr, MN, INr, MI, CST, *CSb, *QT, *KT,
                *SM, *VE, *dd, *hp, *CAP)
        kb.barrier()
        if STAGE < 3:
            return bail([HS, ZO, GT])
        moT = ar.alloc((3, T), BF16)
        sq = [ar.alloc((384,), F32) for _ in range(2)]
        ss = [ar.alloc((H,), F32) for _ in range(2)]
        mo = [ar.alloc((384,), BF16) for _ in range(2)]
        for t in range(NT):
            q = t % 2
            hs = HS[:, t, :]
            tt("pool", sq[q], hs, hs, ALU.mult, [("HS", t)], [("sq", q)])
            kb.op("dve", lambda e: e.tensor_reduce(out=ss[q], in_=sq[q].rearrange("p (h e) -> p h e", h=H), axis=AX.X,
                                                   op=ALU.add), reads=[("sq", q)], writes=[("ss", q)])
            act(ss[q], ss[q], AF.Sqrt, [("ss", q)], [("ss", q)], scale=1.0 / DH, bias=EPS)
            kb.op("dve", lambda e: e.reciprocal(out=ss[q], in_=ss[q]), reads=[("ss", q)], writes=[("ss", q)])
            tt("dve", sq[q].rearrange("p (h e) -> p h e", h=H), hs.rearrange("p (h e) -> p h e", h=H),
               bcast(ss[q].unsqueeze(2), [P, H, DH]), ALU.mult, [("HS", t), ("ss", q), ("sq", q)], [("sq", q)])
            tt("dve", mo[q], sq[q], ZO[:, t, :], ALU.mult, [("sq", q), ("ZO", t)], [("mo", q)])
            bq = nextbank()
            bqv = banks[bq][:].bitcast(BF16)
            for cc in range(3):
                tr(bqv[:, cc * P:(cc + 1) * P], mo[q][:, cc * P:(cc + 1) * P], identb[:], [("mo", q), "identb"],
                   [("ps", bq)])
            copy_to("act", moT[:, :, t * P:(t + 1) * P], bqv[:, 0:384].rearrange("p (c n) -> p c n", c=3),
                    [("ps", bq)], [("moT", t // 4)])
        ar.free(HS, ZO, GT, *sq, *ss, *mo)
        kb.barrier()
        return moT

    prep_cache = {}

    def s5_prep(l, tick=None):
        HALF_PI = math.pi / 2
        GH = G // 2

        def _tick():
            if tick is not None:
                tick()
        prm = ar.alloc((3, G), F32)
        bc4 = ar.alloc((4, G, GC), F32)
        dcol = ar.alloc((G,), F32)
        kb.dma("sp", prm, s5p_d[:, l], writes=["prm"])
        kb.dma("sp", bc4, s5bc_d[:, l], writes=["bc4"])
        kb.dma("sp", dcol, s5dcol_d[:, l], writes=["dcol"])
        sm = {}

        def sv(name):
            if name not in sm:
                sm[name] = ar.alloc((G,), F32)
            return sm[name]
        K1 = ["s5tmp"]

        def e_tt(out, a, bb, op, eng="dve"):
            tt(eng, out, a, bb, op, K1 + ["prm", "bc4"], K1)

        def e_ts(out, a, s1, op0, s2=None, op1=None):
            ts("dve", out, a, s1, s2, op0, op1, K1 + ["prm"], K1)

        lre, lim, lst = prm[:, 0, :], prm[:, 1, :], prm[:, 2, :]
        step = sv("step")
        act(step, lst, AF.Exp, ["prm"], K1)
        mag = sv("mag")
        e_tt(mag, lre, step, ALU.mult)
        act(mag, mag, AF.Exp, K1, K1)
        ang = sv("ang")
        stt(ang, lim, 1.0 / 16, step, ALU.mult, ALU.mult, K1 + ["prm"], K1)
        cs_, sn_ = sv("c"), sv("s")
        halfpi = sv("halfpi")
        kb.op("dve", lambda e: e.memset(halfpi, HALF_PI), writes=K1)
        act(sn_, ang, AF.Sin, K1, K1)
        act(cs_, ang, AF.Sin, K1, K1, bias=halfpi[:, 0:1])
        t1, t2, t3 = sv("t1"), sv("t2"), sv("t3")
        for _ in range(4):
            e_tt(t1, cs_, cs_, ALU.mult)
            e_tt(t2, sn_, sn_, ALU.mult)
            e_tt(t3, cs_, sn_, ALU.mult)
            e_tt(cs_, t1, t2, ALU.subtract)
            e_ts(sn_, t3, 2.0, ALU.mult)
        PW = ar.alloc((9, 2, G), F32)
        kb.op("dve", lambda e: e.memset(PW[:, 0, 0, :], 1.0), writes=K1)
        kb.op("dve", lambda e: e.memset(PW[:, 0, 1, :], 0.0), writes=K1)
        e_tt(PW[:, 1, 0, :], mag, cs_, ALU.mult)
        e_tt(PW[:, 1, 1, :], mag, sn_, ALU.mult)
        lbre, lbim = PW[:, 1, 0, :], PW[:, 1, 1, :]

        def cmul(ore, oim, are, aim, bre, bim, conj_neg_im=False):
            e_tt(t1x(ore), are, bre, ALU.mult)
            e_tt(t2x(ore), aim, bim, ALU.mult)
            e_tt(ore, t1x(ore), t2x(ore), ALU.subtract)
            e_tt(t1x(ore), are, bim, ALU.mult)
            e_tt(t2x(ore), aim, bre, ALU.mult)
            e_tt(oim, t1x(ore), t2x(ore), ALU.add)

        GQ = 6
        big1 = ar.alloc((GQ, 8, GC), F32)
        big2 = ar.alloc((GQ, 8, GC), F32)

        def t1x(like):
            n = 1
            for s_ in like.shape[1:]:
                n *= s_
            v = big1.rearrange("p a b c -> p (a b c)")[:, 0:n]
            return reshape_like(v, like)

        def t2x(like):
            n = 1
            for s_ in like.shape[1:]:
                n *= s_
            v = big2.rearrange("p a b c -> p (a b c)")[:, 0:n]
            return reshape_like(v, like)

        def reshape_like(v, like):
            sh = like.shape[1:]
            if len(sh) == 1:
                return v
            names = " ".join("a%d" % i for i in range(len(sh)))
            kw = {"a%d" % i: sh[i] for i in range(len(sh))}
            v = v.rearrange("p (%s) -> p %s" % (names, names), **kw)
            if like.shape[0] != P:
                v = v[0:like.shape[0]]
            return v

        for k in range(2, 9):
            cmul(PW[:, k, 0, :], PW[:, k, 1, :], PW[:, k - 1, 0, :], PW[:, k - 1, 1, :], lbre, lbim)
        i8re, i8im, den = sv("i8re"), sv("i8im"), sv("den")
        e_tt(t1, PW[:, 8, 0, :], PW[:, 8, 0, :], ALU.mult)
        e_tt(t2, PW[:, 8, 1, :], PW[:, 8, 1, :], ALU.mult)
        e_tt(den, t1, t2, ALU.add)
        kb.op("dve", lambda e: e.reciprocal(out=den, in_=den), reads=K1, writes=K1)
        e_tt(i8re, PW[:, 8, 0, :], den, ALU.mult)
        e_tt(i8im, PW[:, 8, 1, :], den, ALU.mult)
        e_ts(i8im, i8im, -1.0, ALU.mult)
        cfre, cfim, ar_ = sv("cfre"), sv("cfim"), sv("ar_")
        e_ts(ar_, lbre, -1.0, ALU.add)
        e_tt(t1, lre, lre, ALU.mult)
        e_tt(t2, lim, lim, ALU.mult)
        e_tt(den, t1, t2, ALU.add)
        kb.op("dve", lambda e: e.reciprocal(out=den, in_=den), reads=K1, writes=K1)
        e_tt(t1, ar_, lre, ALU.mult)
        e_tt(t2, lbim, lim, ALU.mult)
        e_tt(cfre, t1, t2, ALU.add)
        e_tt(cfre, cfre, den, ALU.mult)
        e_tt(t1, lbim, lre, ALU.mult)
        e_tt(t2, ar_, lim, ALU.mult)
        e_tt(cfim, t1, t2, ALU.subtract)
        e_tt(cfim, cfim, den, ALU.mult)
        _tick()
        Bb = ar.alloc((2, G, GC), F32)
        Cm2 = ar.alloc((2, G, GC), F32)
        gshape = [P, G, GC]
        cmul(Bb[:, 0], Bb[:, 1], bcast(cfre.unsqueeze(2), gshape), bcast(cfim.unsqueeze(2), gshape), bc4[:, 0], bc4[:, 1])
        cmul(Cm2[:, 0], Cm2[:, 1], bcast(i8re.unsqueeze(2), gshape), bcast(i8im.unsqueeze(2), gshape), bc4[:, 2],
             bc4[:, 3])
        PWe = ar.alloc((8, 2, G), F32)
        PWc = ar.alloc((8, 2, G), F32)
        for r in range(8):
            copy_to("dve", PWe[0:64, r], PW[0:64, 7 - r], K1, K1)
            copy_to("dve", PWe[64:128, r], PW[64:128, r], K1, K1)
            copy_to("dve", PWc[0:64, r], PW[0:64, r + 1], K1, K1)
            copy_to("dve", PWc[64:128, r], PW[64:128, 8 - r], K1, K1)
        ET = ar.alloc((G, 2, 128), BF16)
        CPn = ar.alloc((G, 2, 128), BF16)
        GH = G // 2

        def expand(dst, pw, mat_re, mat_im, neg_im):
            for gh in range(G // GQ):
                gs = slice(gh * GQ, (gh + 1) * GQ)
                shp = [P, GQ, 8, GC]
                pre = bcast(pw[:, :, 0, gs].rearrange("p r g -> p g r").unsqueeze(3), shp)
                pim = bcast(pw[:, :, 1, gs].rearrange("p r g -> p g r").unsqueeze(3), shp)
                mre = bcast(mat_re[:, gs, :].unsqueeze(2), shp)
                mim = bcast(mat_im[:, gs, :].unsqueeze(2), shp)
                dre = dst[:, gs, 0, :].rearrange("p g (r c) -> p g r c", r=8)
                dim_ = dst[:, gs, 1, :].rearrange("p g (r c) -> p g r c", r=8)
                e_tt(big1, pre, mre, ALU.mult)
                e_tt(big2, pim, mim, ALU.mult)
                e_tt(dre, big1, big2, ALU.subtract)
                e_tt(big1, pre, mim, ALU.mult)
                e_tt(big2, pim, mre, ALU.mult)
                if neg_im:
                    e_tt(big1, big1, big2, ALU.add)
                    e_ts(dim_, big1, -1.0, ALU.mult)
                else:
                    e_tt(dim_, big1, big2, ALU.add)
        expand(ET, PWe, Bb[:, 0], Bb[:, 1], False)
        _tick()
        expand(CPn, PWc, Cm2[:, 0], Cm2[:, 1], True)
        _tick()
        Emat = ar.alloc((G, 2, 128), BF16, top=True)
        T0 = ar.alloc((G, 128), BF16, top=True)
        for g in range(G):
            if g % 4 == 3:
                _tick()
            bq = nextbank()
            bqv = banks[bq][:].bitcast(BF16)
            for ri in range(2):
                tr(bqv[:, ri * P:(ri + 1) * P], ET[:, g, ri, :], identb[:], K1 + ["identb"], [("ps", bq)])
            copy_to(evac_eng(), Emat[:, g, :, :], bqv[:, 0:256].rearrange("p (r n) -> p r n", r=2), [("ps", bq)],
                    ["Emat"])
            bts = [nextbank(), nextbank()]
            for dd_ in range(2):
                ps_ = slice(dd_ * 64, (dd_ + 1) * 64)
                for ri in range(2):
                    mm(banks[bts[dd_]][:, 0:P], ET[ps_, g, ri, :], CPn[ps_, g, ri, :], ri == 0, ri == 1,
                       K1, [("ps", bts[dd_])])
            ta, tb = big1.rearrange("p a b c -> p (a b c)")[:, 0:128], big2.rearrange("p a b c -> p (a b c)")[:, 0:128]
            tt("dve", ta, banks[bts[0]][:, 0:128], s5mF, ALU.mult, [("ps", bts[0]), "cmask"] + K1, K1)
            tt("dve", tb, banks[bts[1]][:, 0:128], s5mB, ALU.mult, [("ps", bts[1]), "cmask"] + K1, K1)
            tt("dve", ta, ta, tb, ALU.add, K1, K1)
            stt(T0[:, g, :], identf[:], dcol[:, g:g + 1], ta, ALU.mult, ALU.add, K1 + ["identf", "dcol"], ["T0"])
        ar.free(ET, CPn, Bb, Cm2, PWe)
        kb.barrier()
        CP = ar.alloc((G, 2, 128), BF16, top=True)
        expand(CP, PWc, bc4[:, 2], bc4[:, 3], True)
        LreB = ar.alloc((2, G), F32, top=True)
        LimS = ar.alloc((2, G), F32, top=True)
        copy_to("dve", LreB[:, 0, :], PW[:, 8, 0, :], K1, ["Lmul"])
        copy_to("dve", LreB[:, 1, :], PW[:, 8, 0, :], K1, ["Lmul"])
        ts("dve", LimS[:, 0, :], PW[:, 8, 1, :], -1.0, None, ALU.mult, None, K1, ["Lmul"])
        copy_to("dve", LimS[:, 1, :], PW[:, 8, 1, :], K1, ["Lmul"])
        LreB16 = ar.alloc((2, G), F32, top=True)
        LimS16 = ar.alloc((2, G), F32, top=True)
        e_tt(t1, PW[:, 8, 0, :], PW[:, 8, 0, :], ALU.mult)
        e_tt(t2, PW[:, 8, 1, :], PW[:, 8, 1, :], ALU.mult)
        tt("dve", LreB16[:, 0, :], t1, t2, ALU.subtract, K1, ["Lmul"])
        tt("dve", LreB16[:, 1, :], t1, t2, ALU.subtract, K1, ["Lmul"])
        e_tt(t3, PW[:, 8, 0, :], PW[:, 8, 1, :], ALU.mult)
        ts("dve", LimS16[:, 1, :], t3, 2.0, None, ALU.mult, None, K1, ["Lmul"])
        ts("dve", LimS16[:, 0, :], t3, -2.0, None, ALU.mult, None, K1, ["Lmul"])
        LreB32 = ar.alloc((2, G), F32, top=True)
        LimS32 = ar.alloc((2, G), F32, top=True)
        tt("dve", t1, LreB16[:, 0, :], LreB16[:, 0, :], ALU.mult, K1 + ["Lmul"], K1)
        tt("dve", t2, LimS16[:, 1, :], LimS16[:, 1, :], ALU.mult, K1 + ["Lmul"], K1)
        tt("dve", LreB32[:, 0, :], t1, t2, ALU.subtract, K1, ["Lmul"])
        tt("dve", LreB32[:, 1, :], t1, t2, ALU.subtract, K1, ["Lmul"])
        tt("dve", t3, LreB16[:, 0, :], LimS16[:, 1, :], ALU.mult, K1 + ["Lmul"], K1)
        ts("dve", LimS32[:, 1, :], t3, 2.0, None, ALU.mult, None, K1, ["Lmul"])
        ts("dve", LimS32[:, 0, :], t3, -2.0, None, ALU.mult, None, K1, ["Lmul"])
        ar.free(prm, bc4, dcol, PW, PWc, big1, big2, *sm.values())
        kb.barrier()

        return dict(Emat=Emat, T0=T0, CP=CP, LreB=LreB, LimS=LimS, LreB16=LreB16, LimS16=LimS16, LreB32=LreB32,
                    LimS32=LimS32)

    def s5_all(l):
        HALF_PI = math.pi / 2
        import os
        KS5 = int(os.environ.get("KS5", "9"))
        base_live = set(ar.live.keys())

        def bail5():
            for k_ in list(ar.live.keys()):
                if k_ not in base_live:
                    ar.free(ar.live[k_][0])
            kb.barrier()
            so_ = ar.alloc((3, T), BF16)
            kb.op("dve", lambda e: e.memset(so_, 0.0), writes=[("soT", i) for i in range(NB)])
            kb.barrier()
            return so_
        XN = ar.alloc((KC, T), BF16)
        rms_norm(l, 0, XN)
        Ws = ar.alloc((KC, 384), BF16)
        load_w(Ws, w_in_d[l, :, 1808:2192].rearrange("(k p) n -> p k n", p=P), "Ws")
        U = ar.alloc((G, NSB), BF16, top=True)
        Urev = ar.alloc((G, NSB), BF16, top=True)
        Utok = [ar.alloc((G, 128), BF16) for _ in range(2)]
        for half in range(2):
            ut = Utok[half]
            for r in range(8):
                b = nextbank()
                for k in range(KC):
                    lhsT = XN[:, k, half * 1024 + r:(half + 1) * 1024:8]
                    mm(banks[b][:, 0:384], lhsT, Ws[:, k, :], k == 0, k == KC - 1,
                       reads=["Ws"] + [("XN", k, nb) for nb in (2 * half, 2 * half + 1)], writes=[("ps", b)])
                copy_to(evac_eng(), ut[:, :, r * GC:(r + 1) * GC], banks[b][:, 0:384].rearrange("p (g c) -> p g c", g=G),
                        [("ps", b)], [("Utok", half)])
            for g0 in range(0, G, 4):
                b = nextbank()
                for gg in range(4):
                    mm(banks[b][:, gg * P:(gg + 1) * P], ut[:, g0 + gg, :], identb[:], True, True,
                       [("Utok", half), "identb"], [("ps", b)])
                copy_to(evac_eng(), U[:, g0:g0 + 4, half * P:(half + 1) * P],
                        banks[b][:].rearrange("p (g n) -> p g n", g=4), [("ps", b)], [("U", half)])
                b = nextbank()
                for gg in range(4):
                    mm(banks[b][:, gg * P:(gg + 1) * P], ut[:, g0 + gg, :], Jb[:], True, True,
                       [("Utok", half), "Jb"], [("ps", b)])
                copy_to(evac_eng(), Urev[:, g0:g0 + 4, (1 - half) * P:(2 - half) * P],
                        banks[b][:].rearrange("p (g n) -> p g n", g=4), [("ps", b)], [("Urev", 1 - half)])
        ar.free(XN, Ws, *Utok)
        kb.barrier()

        if KS5 < 2:
            return bail5()
        pp = prep_cache.pop(l, None)
        if pp is None:
            pp = s5_prep(l)
        Emat, T0, CP, LreB, LimS = pp["Emat"], pp["T0"], pp["CP"], pp["LreB"], pp["LimS"]
        LreB16, LimS16, LreB32, LimS32 = pp["LreB16"], pp["LimS16"], pp["LreB32"], pp["LimS32"]
        GH = G // 2

        if KS5 < 3:
            return bail5()
        Gs = ar.alloc((NSB, 2, G), BF16)
        for g in range(G):
            b = nextbank()
            for ri in range(2):
                mm(banks[b][0:64, ri * NSB:(ri + 1) * NSB], Emat[:, g, ri, 0:64], U[:, g, :], True, True,
                   ["Emat", ("U", 0), ("U", 1)], [("ps", b)])
                mm(banks[b][64:128, ri * NSB:(ri + 1) * NSB], Emat[:, g, ri, 64:128], Urev[:, g, :], True, True,
                   ["Emat", ("Urev", 0), ("Urev", 1)], [("ps", b)])
            copy_to(evac_eng(), Gs[:, :, :, g].rearrange("p n r -> p r n"),
                    banks[b][:].rearrange("p (r n) -> p r n", r=2), [("ps", b)], ["Gs"])
        ar.free(Emat, Urev)
        kb.barrier()

        if KS5 < 4:
            return bail5()
        HIST = ar.alloc((NSB, 2, G), BF16)
        NST = 4
        NK = NSB // 2
        ST = [ar.alloc((2, G), F32) for _ in range(NST)]
        TA = [ar.alloc((2, G), F32) for _ in range(2)]
        TB = [ar.alloc((2, G), F32) for _ in range(2)]
        CAPs = ar.alloc((8, 2, G), F32)
        G2 = ar.alloc((NK, 2, G), BF16)
        tb1 = ar.alloc((32, 2, G), F32)
        tb2 = ar.alloc((32, 2, G), F32)
        kb.dma("sp", ST[0], s5st0_d[:, l], writes=[("ST", 0, 0), ("ST", 0, 1)])
        shp4 = [P, 32, 2, G]
        for c4 in range(4):
            ge = Gs[:, 64 * c4:64 * c4 + 64:2]
            go = Gs[:, 64 * c4 + 1:64 * c4 + 64:2]
            tt("dve", tb1, ge, bcast(LreB.unsqueeze(1), shp4), ALU.mult, ["Gs", "Lmul"], ["tb1"])
            tt("dve", tb2, rev_axis(ge, 2), bcast(LimS.unsqueeze(1), shp4), ALU.mult, ["Gs", "Lmul"], ["tb2"])
            tt("dve", tb1, tb1, tb2, ALU.add, ["tb1", "tb2"], ["tb1"])
            tt("dve", G2[:, 32 * c4:32 * c4 + 32], tb1, go, ALU.add, ["tb1", "Gs"], ["G2"])
        NJ = NSB // 4
        G4 = ar.alloc((NJ, 2, G), BF16)
        for c2 in range(2):
            ge = G2[:, 64 * c2:64 * c2 + 64:2]
            go = G2[:, 64 * c2 + 1:64 * c2 + 64:2]
            tt("dve", tb1, ge, bcast(LreB16.unsqueeze(1), shp4), ALU.mult, ["G2", "Lmul"], ["tb1"])
            tt("dve", tb2, rev_axis(ge, 2), bcast(LimS16.unsqueeze(1), shp4), ALU.mult, ["G2", "Lmul"], ["tb2"])
            tt("dve", tb1, tb1, tb2, ALU.add, ["tb1", "tb2"], ["tb1"])
            tt("dve", G4[:, 32 * c2:32 * c2 + 32], tb1, go, ALU.add, ["tb1", "G2"], ["G4"])
        gss = [slice(hh * GH, (hh + 1) * GH) for hh in range(2)]
        for j in range(NJ):
            n = 4 * j
            ci, ni = j % NST, (j + 1) % NST
            cur, nxt = ST[ci], ST[ni]
            if n % 32 == 0 and n > 0:
                for hh in range(2):
                    ts("dve", cur[:, :, gss[hh]], cur[:, :, gss[hh]], keepcol[:, 0:1], None, ALU.mult, None,
                       [("ST", ci, hh), "keepcol"], [("ST", ci, hh)])
            copy_to("act", HIST[0:64, n], cur[0:64], [("ST", ci, 0), ("ST", ci, 1)], ["HIST"])
            copy_to("act", HIST[64:128, NSB - 1 - n], cur[64:128], [("ST", ci, 0), ("ST", ci, 1)], ["HIST"])
            for hh in range(2):
                tt("dve", TA[hh][:, :, 0:GH], cur[:, :, gss[hh]], LreB32[:, :, gss[hh]], ALU.mult,
                   [("ST", ci, hh), "Lmul"], [("TA", hh)])
            for hh in range(2):
                tt("dve", TB[hh][:, :, 0:GH], swap_ri(cur[:, :, gss[hh]]), LimS32[:, :, gss[hh]], ALU.mult,
                   [("ST", ci, hh), "Lmul"], [("TB", hh)])
            for hh in range(2):
                tt("dve", TA[hh][:, :, 0:GH], TA[hh][:, :, 0:GH], TB[hh][:, :, 0:GH], ALU.add,
                   [("TA", hh), ("TB", hh)], [("TA", hh)])
            for hh in range(2):
                tt("dve", nxt[:, :, gss[hh]], TA[hh][:, :, 0:GH], G4[:, j, :, gss[hh]], ALU.add,
                   [("TA", hh), "G4"], [("ST", ni, hh)])
            if (j + 1) % 8 == 0:
                copy_to("act", CAPs[:, (j + 1) // 8 - 1], nxt, [("ST", ni, 0), ("ST", ni, 1)], ["CAPs"])
        hshp = [64, 32, 2, G]

        def bulk(xin, xout, gin, Lr, Li, ps_, gkey):
            tt("dve", tb1[ps_], xin, bcast(Lr[ps_].unsqueeze(1), hshp), ALU.mult, ["HIST", "Lmul"], ["tb1"])
            tt("dve", tb2[ps_], rev_axis(xin, 2), bcast(Li[ps_].unsqueeze(1), hshp), ALU.mult, ["HIST", "Lmul"],
               ["tb2"])
            tt("dve", tb1[ps_], tb1[ps_], tb2[ps_], ALU.add, ["tb1", "tb2"], ["tb1"])
            tt("dve", xout, tb1[ps_], gin, ALU.add, ["tb1", gkey], ["HIST"])
        fw, bw = slice(0, 64), slice(64, 128)
        for c2 in range(2):
            lo = 128 * c2
            bulk(HIST[fw, lo:lo + 128:4], HIST[fw, lo + 2:lo + 128:4], G2[fw, 64 * c2:64 * c2 + 64:2], LreB16, LimS16,
                 fw, "G2")
            kmin = 64 - 64 * c2
            bulk(HIST[bw, lo + 3:lo + 128:4], HIST[bw, lo + 1:lo + 128:4],
                 rev_axis(G2[bw, kmin:kmin + 64:2], 1), LreB16, LimS16, bw, "G2")
        for c4 in range(4):
            lo = 64 * c4
            bulk(HIST[fw, lo:lo + 64:2], HIST[fw, lo + 1:lo + 64:2], Gs[fw, lo:lo + 64:2], LreB, LimS, fw, "Gs")
            nmin = 192 - 64 * c4
            bulk(HIST[bw, lo + 1:lo + 64:2], HIST[bw, lo:lo + 64:2], rev_axis(Gs[bw, nmin:nmin + 64:2], 1), LreB, LimS,
                 bw, "Gs")
        ar.free(Gs, G2, G4, tb1, tb2)
        kb.barrier()
        OUTS = ar.alloc((16, 128), F32)
        for k4 in range(4):
            b = nextbank()
            for j in range(4):
                kk = k4 * 4 + j
                kcap, ri = kk // 2, kk % 2
                tr(banks[b][0:G, j * P:(j + 1) * P], CAPs[:, kcap, ri, :], identf[:], ["CAPs", "identf"], [("ps", b)])
            copy_to(evac_eng(), OUTS[0:G, k4 * 4:(k4 + 1) * 4, :], banks[b][0:G, :].rearrange("p (j n) -> p j n", j=4),
                    [("ps", b)], ["OUTS"])
        for kcap in range(8):
            for ri in range(2):
                for d in range(2):
                    seq = kcap if d == 0 else 7 - kcap
                    kb.dma("sp", news5_d[ri][seq, l, d], OUTS[0:G, kcap * 2 + ri, d * 64:(d + 1) * 64],
                           reads=["OUTS"], writes=[("news5", ri, seq, l, d)])
        ar.free(*ST, *TA, *TB, CAPs, LreB, LimS, LreB16, LimS16, LreB32, LimS32)
        kb.barrier()

        if KS5 < 5:
            return bail5()
        Wgl = ar.alloc((3, 768), BF16)
        load_w(Wgl, w_glu_d[l].rearrange("(k p) n -> p k n", p=P), "Wgl")
        yT = ar.alloc((3, T), BF16)
        Yt = ar.alloc((8, 384), BF16)
        gt = [ar.alloc((512,), F32) for _ in range(2)]
        for half in range(2):
            asl = slice(half * P, (half + 1) * P)
            for g0 in range(0, G, 4):
                b = nextbank()
                for gg in range(4):
                    g = g0 + gg
                    osl = banks[b][:, gg * P:(gg + 1) * P]
                    mm(osl, HIST[:, asl, 0, g], CP[:, g, 0, :], True, False, ["HIST", "CP"],
                       [("ps", b)])
                    mm(osl, HIST[:, asl, 1, g], CP[:, g, 1, :], False, False, ["HIST", "CP"],
                       [("ps", b)])
                    mm(osl, U[:, g, asl], T0[:, g, :], False, True, [("U", half), "T0"], [("ps", b)])
                q = (g0 // 4) % 2
                act(gt[q], banks[b][:], AF.Square, [("ps", b)], [("gt", q)])
                ts("dve", gt[q], gt[q], 0.044715, 1.0, ALU.mult, ALU.add, [("gt", q)], [("gt", q)])
                tt("dve", gt[q], gt[q], banks[b][:], ALU.mult, [("gt", q), ("ps", b)], [("gt", q)])
                act(gt[q], gt[q], AF.Sigmoid, [("gt", q)], [("gt", q)], scale=1.5957691216)
                tt("dve", Yt[:, :, g0 * GC:(g0 + 4) * GC].rearrange("p r (g c) -> p g r c", g=4),
                   gt[q].rearrange("p (g r c) -> p g r c", g=4, r=8), banks[b][:].rearrange("p (g r c) -> p g r c", g=4, r=8),
                   ALU.mult, [("gt", q), ("ps", b)], ["Yt"])
            for cc in range(3):
                bq = nextbank()
                bqv = banks[bq][:].bitcast(BF16)
                for r in range(8):
                    tr(bqv[:, r * P:(r + 1) * P], Yt[:, r, cc * P:(cc + 1) * P], identb[:], ["Yt", "identb"],
                       [("ps", bq)])
                copy_to(evac_eng(), yT[:, cc, half * 1024:(half + 1) * 1024].rearrange("p (a r) -> p r a", r=8),
                        bqv.rearrange("p (r a) -> p r a", r=8), [("ps", bq)], [("yT", 2 * half), ("yT", 2 * half + 1)])
        soT = ar.alloc((3, T), BF16, top=True)
        sg = [ar.alloc((512,), BF16) for _ in range(2)]
        for nb in range(NB):
            sl = slice(nb * 512, (nb + 1) * 512)
            for mi in range(3):
                ba, bb_ = nextbank(), nextbank()
                for k in range(3):
                    mm(banks[ba][:], Wgl[:, k, mi * P:(mi + 1) * P], yT[:, k, sl], k == 0, k == 2, ["Wgl", ("yT", nb)],
                       [("ps", ba)])
                for k in range(3):
                    mm(banks[bb_][:], Wgl[:, k, 384 + mi * P:384 + (mi + 1) * P], yT[:, k, sl], k == 0, k == 2,
                       ["Wgl", ("yT", nb)], [("ps", bb_)])
                q = mi % 2
                act(sg[q], banks[bb_][:], AF.Sigmoid, [("ps", bb_)], [("sg", q)])
                tt("dve", soT[:, mi, sl], banks[ba][:], sg[q], ALU.mult, [("ps", ba), ("sg", q)], [("soT", nb)])
        ar.free(Wgl, yT, Yt, *gt, *sg, U, HIST, CP, T0, OUTS)
        kb.barrier()
        return soT

    def outproj(l, foT, moT, soT):
        Wo = ar.alloc((KC, D), BF16)
        load_w(Wo, w_out_d[l].rearrange("(k p) n -> p k n", p=P), "Wo")
        srcs = []
        if foT is not None:
            srcs += [(0, foT, 0, "foT"), (1, foT, 1, "foT")]
        if moT is not None:
            srcs += [(2 + i, moT, i, "moT") for i in range(3)]
        if soT is not None:
            srcs += [(5 + i, soT, i, "soT") for i in range(3)]
        for nb in range(NB):
            sl = slice(nb * 512, (nb + 1) * 512)
            for mi in range(KC):
                b = nextbank()
                for j, (kc, buf, ci, key) in enumerate(srcs):
                    mm(banks[b][:], Wo[:, kc, mi * P:(mi + 1) * P], buf[:, ci, sl], j == 0, j == len(srcs) - 1,
                       ["Wo", (key, nb)], [("ps", b)])
                resid_evac(l, 2)(b, mi, nb)
        ar.free(Wo)
        for buf in (foT, moT, soT):
            if buf is not None:
                ar.free(buf)
        kb.barrier()

    def ffn(l, XN, extra=None):
        NG = NHC // 2
        Wg = [ar.alloc((KC, 256), BF16) for _ in range(2)]
        Wu = [ar.alloc((KC, 256), BF16) for _ in range(2)]
        Wd = [ar.alloc((2, D), BF16) for _ in range(2)]
        Hh = [ar.alloc((2, T), BF16) for _ in range(2)]
        sg = [ar.alloc((512,), BF16) for _ in range(2)]

        def issue(gi):
            s = gi % 2
            c0 = gi * 256
            load_w(Wg[s], w_gate_d[l, :, c0:c0 + 256].rearrange("(k p) n -> p k n", p=P), ("Wg", s))
            load_w(Wu[s], w_up_d[l, :, c0:c0 + 256].rearrange("(k p) n -> p k n", p=P), ("Wu", s))
            load_w(Wd[s], w_down_d[l, c0:c0 + 256, :].rearrange("(k p) n -> p k n", p=P), ("Wd", s))
        issue(0)
        for gi in range(NG):
            s = gi % 2
            if gi + 1 < NG:
                issue(gi + 1)
            if extra is not None:
                extra(gi)
            for j in range(2):
                for nb in range(NB):
                    sl = slice(nb * 512, (nb + 1) * 512)
                    bg_ = nextbank()
                    for k in range(KC):
                        mm(banks[bg_][:], Wg[s][:, k, j * P:(j + 1) * P], XN[:, k, sl], k == 0, k == KC - 1,
                           reads=[("Wg", s), ("XN", k, nb)], writes=[("ps", bg_)])
                    bu = nextbank()
                    for k in range(KC):
                        mm(banks[bu][:], Wu[s][:, k, j * P:(j + 1) * P], XN[:, k, sl], k == 0, k == KC - 1,
                           reads=[("Wu", s), ("XN", k, nb)], writes=[("ps", bu)])
                    q = (j * NB + nb) % 2
                    act(sg[q], banks[bg_][:], AF.Silu, [("ps", bg_)], [("sg", q)])
                    tt("dve", Hh[s][:, j, sl], banks[bu][:], sg[q], ALU.mult, [("ps", bu), ("sg", q)],
                       [("Hh", s, j, nb)])
            for nb in range(NB):
                for mi in range(KC):
                    b = nextbank()
                    for j in range(2):
                        mm(banks[b][:], Wd[s][:, j, mi * P:(mi + 1) * P], Hh[s][:, j, nb * 512:(nb + 1) * 512], j == 0,
                           j == 1, reads=[("Wd", s), ("Hh", s, j, nb)], writes=[("ps", b)])
                    resid_evac(l, 5)(b, mi, nb)
        ar.free(*Wg, *Wu, *Wd, *Hh, *sg)
        kb.barrier()

    WAs = [ar.alloc((KC, D), BF16) for _ in range(2)]
    stage = [ar.alloc((D,), F32) for _ in range(4)]
    ada_dma(0, 0, WAs[0], bg=True)
    ada_dma(0, 1, WAs[1], bg=True)
    for t in range(NT):
        s = t % 4
        kb.dma("sp", stage[s], x_d[t * P:(t + 1) * P, :], writes=[("stage", s)])
        for half in range(2):
            b = nextbank()
            for j in range(4):
                k = half * 4 + j
                tr(banks[b][:, j * P:(j + 1) * P], stage[s][:, k * P:(k + 1) * P], identf[:],
                   [("stage", s), "identf"], [("ps", b)])
            copy_to(evac_eng(), X[:, half * 4:(half + 1) * 4, t * P:(t + 1) * P],
                    banks[b][:].rearrange("p (j n) -> p j n", j=4), [("ps", b)], [("X", t // 4)])
    ar.free(*stage)
    ada_i = [0]

    def ada_tick():
        if ada_i[0] < 6:
            i_ = ada_i[0]
            ada_compute(0, i_, WAs[i_ % 2])
            if i_ + 2 < 6:
                ada_dma(0, i_ + 2, WAs[i_ % 2], bg=True)
            ada_i[0] += 1
    if flags["s5"]:
        prep_cache[0] = s5_prep(0, tick=ada_tick)
    while ada_i[0] < 6:
        ada_tick()
    ar.free(*WAs)
    kb.barrier()

    for l in range(NL):
        foT = moT = soT = None
        if flags["s5"]:
            soT = s5_all(l)
        if flags["fourier"] or flags["mlstm"]:
            XN = ar.alloc((KC, T), BF16)
            rms_norm(l, 0, XN)
            mb = mlstm_inproj(l, XN) if flags["mlstm"] else None
            zfT = fourier_inproj(l, XN) if flags["fourier"] else None
            ar.free(XN)
            kb.barrier()
            if flags["fourier"]:
                foT = fourier_core(l, zfT)
            if flags["mlstm"]:
                moT = mlstm_core(l, mb)
        if foT is not None or moT is not None or soT is not None:
            outproj(l, foT, moT, soT)
        if flags["ffn"]:
            XN = ar.alloc((KC, T), BF16)
            rms_norm(l, 1, XN)
            if l + 1 < NL:
                WAs = [ar.alloc((KC, D), BF16) for _ in range(2)]

                def extra(gi, l=l, WAs=WAs):
                    if gi < 6:
                        ada_dma(l + 1, gi, WAs[gi % 2])
                    if 1 <= gi < 7:
                        ada_compute(l + 1, gi - 1, WAs[(gi - 1) % 2])
                ffn(l, XN, extra)
                ar.free(*WAs)
            else:
                ffn(l, XN)
            ar.free(XN)
            kb.barrier()
        elif l + 1 < NL:
            WAs = [ar.alloc((KC, D), BF16) for _ in range(2)]
            for i in range(6):
                ada_dma(l + 1, i, WAs[i % 2])
                ada_compute(l + 1, i, WAs[i % 2])
            ar.free(*WAs)
            kb.barrier()

    rms_norm(None, None, None)
    stage = [ar.alloc((D,), F32) for _ in range(2)]
    for t in range(NT):
        s = t % 2
        for half in range(2):
            b = nextbank()
            for j in range(4):
                k = half * 4 + j
                tr(banks[b][:, j * P:(j + 1) * P], X[:, k, t * P:(t + 1) * P], identf[:], [("X", t // 4), "identf"],
                   [("ps", b)])
            copy_to(evac_eng(), stage[s][:, half * 512:(half + 1) * 512], banks[b][:], [("ps", b)], [("stage", s, half)])
        kb.dma("sp", y_d[t * P:(t + 1) * P, :], stage[s], reads=[("stage", s, 0), ("stage", s, 1)],
               writes=[("y", t)])
    nc_done = kb.finish()
    build_program.last_peak = ar.peak * 2
    build_program.ninstr = kb.ninstr
    build_program.counts = dict(kb.cnt)
    return nc_done


def fm(v):
    v = np.asarray(v, np.float32)
    n = v.shape[-1] // P
    r = v.reshape(v.shape[:-1] + (n, P))
    return np.ascontiguousarray(np.moveaxis(r, -1, 0))


def dft_consts():
    i = np.arange(64)
    ang = 2 * np.pi * np.outer(i, i) / 64.0
    C64, S64 = np.cos(ang), np.sin(ang)
    Z = np.zeros((64, 64))
    BDC = np.block([[C64, Z], [Z, C64]])
    BDS = np.block([[S64, Z], [Z, S64]])
    csc = np.concatenate([BDC, BDS], axis=1).astype(ml_dtypes.bfloat16)
    t = np.arange(T)
    pos = t % 256
    same = (t[:, None] // 256) == (t[None, :] // 256)
    angp = 2 * np.pi * np.outer(pos, pos) / 256.0
    nrm = 1.0 / math.sqrt(256 * 64)
    pc = np.where(same, np.cos(angp), 0.0) * nrm
    psn = np.where(same, -np.sin(angp), 0.0) * nrm
    pm_prompt = np.stack([pc, psn], axis=1).reshape(NT, P, 2, T).astype(ml_dtypes.bfloat16)
    r, c = t // 64, t % 64
    angs = 2 * np.pi * (np.outer(r, r) / 32.0 + np.outer(c, c) / 64.0)
    nrm = 1.0 / math.sqrt(32 * 64 * 64)
    pm_sample = np.stack([np.cos(angs) * nrm, -np.sin(angs) * nrm], axis=1).reshape(NT, P, 2, T).astype(ml_dtypes.bfloat16)
    return csc, pm_prompt, pm_sample


def mask_consts():
    i = np.arange(P)
    trile = (i[:, None] <= i[None, :]).astype(np.float32)
    trige = (i[:, None] >= i[None, :]).astype(np.float32)
    r = i // GC
    mF = (r[:, None] <= r[None, :]).astype(np.float32)
    mB = (r[:, None] >= r[None, :]).astype(np.float32)
    J = np.eye(P, dtype=np.float32)[::-1].copy()
    return np.ascontiguousarray(np.stack([trile, trige, mF, mB, J], axis=1))


def rep_dp(a):
    a = np.asarray(a, np.float32)
    return np.ascontiguousarray(a.transpose(1, 3, 0, 2).reshape(P, L, G))


def make_in_maps(inp):
    csc, pm_prompt, pm_sample = dft_consts()
    f32 = lambda a: np.ascontiguousarray(np.asarray(a, np.float32))
    lre, lim = rep_dp(inp["s5_lambda_re"]), rep_dp(inp["s5_lambda_im"])
    lst = np.ascontiguousarray(np.broadcast_to(np.asarray(inp["s5_log_step"], np.float32).transpose(1, 0, 2)[:, None],
                                               (2, 64, L, G)).reshape(P, L, G))
    s5p = np.ascontiguousarray(np.stack([lre, lim, lst], axis=2))

    def rep_b(a):
        a = np.asarray(a, np.float32).transpose(2, 0, 1, 3)
        return np.broadcast_to(a[None], (2, 64, L, G, GC)).reshape(P, L, G, GC)

    def rep_c(a):
        a = np.asarray(a, np.float32).transpose(3, 0, 1, 2)
        return np.broadcast_to(a[None], (2, 64, L, G, GC)).reshape(P, L, G, GC)
    s5bc = np.ascontiguousarray(np.stack([rep_b(inp["s5_b_re"]), rep_b(inp["s5_b_im"]), rep_c(inp["s5_c_re"]),
                                          rep_c(inp["s5_c_im"])], axis=2))
    dcol = np.asarray(inp["s5_d"], np.float32).transpose(2, 0, 1)
    s5dcol = np.ascontiguousarray(np.broadcast_to(dcol[None], (8, GC, L, G)).reshape(P, L, G))
    shared = {
        "w_ada": f32(inp["w_ada"]),
        "b_ada_fm": fm(inp["b_ada"]),
        "n1w": fm(inp["norm1_w"]), "n2w": fm(inp["norm2_w"]), "nfw": fm(inp["norm_f"]),
        "w_in": f32(inp["w_in"]), "w_fourier": f32(inp["w_fourier"]), "w_glu": f32(inp["w_glu"]),
        "w_out": f32(inp["w_out"]), "w_gate": f32(inp["w_gate"]), "w_up": f32(inp["w_up"]),
        "w_down": f32(inp["w_down"]),
        "identf": np.eye(P, dtype=np.float32), "csc": csc, "cmask": mask_consts(),
        "bg_bc": np.ascontiguousarray(np.broadcast_to(np.asarray(inp["b_gates"], np.float32)[None], (P, L, 16))),
        "mnw_bc": np.ascontiguousarray(np.broadcast_to(np.asarray(inp["mlstm_norm_w"], np.float32)[None], (P, L, 384))),
        "s5p": s5p, "s5bc": s5bc, "s5dcol": s5dcol,
    }
    maps = []
    xs = np.asarray(inp["x_sample"], np.float32)
    xp = np.asarray(inp["x_prompt"], np.float32)
    sC = np.asarray(inp["state_mlstm_C"], np.float32)
    sn = np.asarray(inp["state_mlstm_n"], np.float32)
    smm = np.asarray(inp["state_mlstm_m"], np.float32)
    sre = np.asarray(inp["state_s5_re"], np.float32)
    sim = np.asarray(inp["state_s5_im"], np.float32)
    for core in range(8):
        m = dict(shared)
        if core < 4:
            b = core
            m["x"] = np.ascontiguousarray(xs[b])
            m["cvec"] = fm(np.asarray(inp["c"])[b])
            m["posmat"] = pm_sample
            m["keepcol"] = np.ones((P, 1), np.float32)
            m["keeprow"] = np.ones((1, NT), np.float32)
            mC0 = np.concatenate([sC[b].transpose(3, 0, 1, 2, 4), sn[b].transpose(3, 0, 1, 2)[..., None]], axis=-1)
            m["mC0"] = np.ascontiguousarray(mC0)
            m["mM0"] = np.ascontiguousarray(smm[b].reshape(1, L, 8))
            st = np.stack([sre[b], sim[b]], axis=0)
            m["s5st0"] = np.ascontiguousarray(st.transpose(2, 4, 1, 0, 3).reshape(P, L, 2, G))
        else:
            j = core - 4
            m["x"] = np.ascontiguousarray(xp[8 * j:8 * j + 8].reshape(T, D))
            m["cvec"] = fm(inp["c_ctx"])
            m["posmat"] = pm_prompt
            m["keepcol"] = np.zeros((P, 1), np.float32)
            kr = np.zeros((1, NT), np.float32)
            kr[0, 1::2] = 1.0
            m["keeprow"] = kr
            m["mC0"] = np.zeros((DH, L, 2, H, 97), np.float32)
            m["mM0"] = np.zeros((1, L, 8), np.float32)
            m["s5st0"] = np.zeros((P, L, 2, G), np.float32)
        maps.append(m)
    return maps


_CACHE = {}
DEBUG = None
LAST = {}


def kernel(**inputs):
    import os
    maps = make_in_maps(inputs)
    if "nc" not in _CACHE:
        _CACHE["nc"] = build_program(debug=DEBUG)
    nc = _CACHE["nc"]
    dev_cores = os.environ.get("KDEV_CORES")
    if dev_cores:
        ids = [int(c) for c in dev_cores.split(",")]
        res = run_bass_kernel_spmd(nc, [maps[i] for i in ids], core_ids=list(range(len(ids))))
        outs = [None] * 8
        for j, i in enumerate(ids):
            outs[i] = res.results[j]
        for i in range(8):
            if outs[i] is None:
                outs[i] = outs[ids[0] if i < 4 else ids[-1]]
    else:
        res = run_bass_kernel_spmd(nc, maps, core_ids=list(range(8)))
        outs = res.results
    LAST["outs"] = outs
    f = lambda a: np.ascontiguousarray(np.asarray(a, dtype=np.float32))
    y_sample = f(np.stack([outs[c]["y"] for c in range(4)], axis=0))
    y_prompt = f(np.concatenate([np.asarray(outs[c]["y"]).reshape(8, 256, D) for c in range(4, 8)], axis=0))
    cat = lambda k: f(np.concatenate([np.asarray(outs[c][k]) for c in range(4, 8)], axis=0))
    return (y_prompt, y_sample, cat("newC"), cat("newn"), cat("newm"), cat("news5re"), cat("news5im"))
```
